# Optimizing a Trainium2 kernel written in Bass

```python
import jax, jax.numpy as jnp
from jax import lax
import numpy as np

D_MODEL = 1024
BATCH = 16
SEQ = 256
DEPTH = 2
DEC_BATCH = 2
DEC_SEQ = 4096
PAST_LEN = 512

GRID_W = 64
N_ADA = 6
D_FF = 4 * D_MODEL
EPS = 1e-6
NEG_INF = -1e30
Q_BLOCK = 128
ROPE_THETA = 10000.0
POOL_WIDTH = D_MODEL // 2
POOL_WINDOWS = (2, 4, 8, 16)
N_POOL_GROUPS = len(POOL_WINDOWS)
POOL_GROUP = POOL_WIDTH // N_POOL_GROUPS
MLA_HEADS = 8
MLA_NOPE = 64
MLA_ROPE = 32
MLA_QK = MLA_NOPE + MLA_ROPE
MLA_V = 64
Q_LORA = 256
KV_LORA = 128
SPLIT0 = (POOL_WIDTH, POOL_WIDTH + Q_LORA, POOL_WIDTH + Q_LORA + KV_LORA)
IN0_WIDTH = POOL_WIDTH + Q_LORA + KV_LORA + MLA_ROPE
OUT0_WIDTH = POOL_WIDTH + MLA_HEADS * MLA_V
NA_HEADS = 16
NA_HEAD_DIM = D_MODEL // NA_HEADS
NA_KH = 8
NA_KW = 16

kernel_name = "hybrid_pool_mla_natten_diffusion_step"


def rms_norm(x, w):
    xf = x.astype(jnp.float32)
    y = xf * lax.rsqrt(jnp.mean(xf * xf, axis=-1, keepdims=True) + EPS)
    return (y * w.astype(jnp.float32)).astype(x.dtype)


def ada_modulation(cond, w_ada, b_ada):
    m = jax.nn.silu(cond) @ w_ada + b_ada
    return jnp.split(m[:, None, :], N_ADA, axis=-1)


def modulate(x, norm_w, shift, scale):
    return rms_norm(x, norm_w) * (1 + scale) + shift


def channel_mixer(h, w1, w2):
    return jnp.square(jax.nn.relu(h @ w1)) @ w2


def axial_rope(x):
    S = x.shape[1]
    t = jnp.arange(S)
    half = x.shape[-1] // 2
    inv_freq = jnp.power(ROPE_THETA, -jnp.arange(0, half, 2, dtype=jnp.float32) / half)
    xf = x.astype(jnp.float32)

    def rotate(xa, pos):
        ang = pos.astype(jnp.float32)[:, None] * inv_freq[None, :]
        cos = jnp.cos(ang)[None, :, None, :]
        sin = jnp.sin(ang)[None, :, None, :]
        x1, x2 = xa[..., : half // 2], xa[..., half // 2:]
        return jnp.concatenate([x1 * cos - x2 * sin, x1 * sin + x2 * cos], axis=-1)

    y = jnp.concatenate([rotate(xf[..., :half], t // GRID_W), rotate(xf[..., half:], t % GRID_W)], axis=-1)
    return y.astype(x.dtype)


def blocked_attention(q, k, v, scale):
    B, S, H, Dq = q.shape
    nblk = S // Q_BLOCK
    qb = jnp.moveaxis(q.reshape(B, nblk, Q_BLOCK, H, Dq), 1, 0)

    def one_block(qblk):
        s = jnp.einsum('bqhd,bkhd->bhqk', qblk, k, preferred_element_type=jnp.float32) * scale
        p = jax.nn.softmax(s, axis=-1).astype(v.dtype)
        return jnp.einsum('bhqk,bkhd->bqhd', p, v)

    out = lax.map(one_block, qb)
    return jnp.moveaxis(out, 0, 1).reshape(B, S, H, v.shape[-1])


def pool_mixer(u, w_pool, pool_scale):
    B, S, _ = u.shape
    ug = u.astype(jnp.float32).reshape(B, S, N_POOL_GROUPS, POOL_GROUP)
    cs = jnp.concatenate([jnp.zeros((B, 1, N_POOL_GROUPS, POOL_GROUP), jnp.float32), jnp.cumsum(ug, axis=1)], axis=1)
    t = jnp.arange(S)
    outs = []
    for g, win in enumerate(POOL_WINDOWS):
        lo = jnp.clip(t - win // 2, 0, S)
        hi = jnp.clip(t - win // 2 + win, 0, S)
        csg = cs[:, :, g]
        mean = (csg[:, hi] - csg[:, lo]) / (hi - lo).astype(jnp.float32)[None, :, None]
        outs.append(mean - ug[:, :, g])
    pooled = jnp.stack(outs, axis=2)
    mixed = jnp.einsum('bsgc,gcd->bsgd', pooled, w_pool.astype(jnp.float32))
    return (mixed.reshape(B, S, POOL_WIDTH) * pool_scale.astype(jnp.float32)).astype(u.dtype)


def mla_split(h, w_in, q_lora_norm, kv_lora_norm, w_q_up, mla_q_norm):
    B, S, _ = h.shape
    a_in, q_lat, kv_lat, k_rope = jnp.split(h @ w_in, SPLIT0, axis=-1)
    q = (rms_norm(q_lat, q_lora_norm) @ w_q_up).reshape(B, S, MLA_HEADS, MLA_QK)
    q = rms_norm(q, mla_q_norm)
    ckv = rms_norm(kv_lat, kv_lora_norm)
    return a_in, q, ckv, k_rope


def mla_keys_values(ckv, k_rope, w_kv_up, mla_k_norm):
    B, L, _ = ckv.shape
    kv = (ckv @ w_kv_up).reshape(B, L, MLA_HEADS, MLA_NOPE + MLA_V)
    k_nope, v = kv[..., :MLA_NOPE], kv[..., MLA_NOPE:]
    k = jnp.concatenate([k_nope, jnp.broadcast_to(k_rope[:, :, None, :], (B, L, MLA_HEADS, MLA_ROPE))], axis=-1)
    return rms_norm(k, mla_k_norm), v


def rope_tail(x):
    return jnp.concatenate([x[..., :MLA_NOPE], axial_rope(x[..., MLA_NOPE:])], axis=-1)


def pool_mla_context(h, w_in, q_lora_norm, kv_lora_norm, w_q_up, w_kv_up, mla_q_norm, mla_k_norm,
                     w_pool, pool_scale, w_out):
    B, S, _ = h.shape
    a_in, q, ckv, k_rope = mla_split(h, w_in, q_lora_norm, kv_lora_norm, w_q_up, mla_q_norm)
    k, v = mla_keys_values(ckv, k_rope, w_kv_up, mla_k_norm)
    attn = blocked_attention(q, k, v, MLA_QK ** -0.5).reshape(B, S, MLA_HEADS * MLA_V)
    out = jnp.concatenate([pool_mixer(a_in, w_pool, pool_scale), attn], axis=-1) @ w_out
    return out, ckv, k_rope


def pool_mla_latent(h, ckv_ctx, krope_ctx, w_in, q_lora_norm, kv_lora_norm, w_q_up, w_kv_up, mla_q_norm,
                    mla_k_norm, w_pool, pool_scale, w_out):
    B, S, _ = h.shape
    a_in, q, ckv, k_rope = mla_split(h, w_in, q_lora_norm, kv_lora_norm, w_q_up, mla_q_norm)
    k_lat, v_lat = mla_keys_values(ckv, k_rope, w_kv_up, mla_k_norm)
    k_ctx, v_ctx = mla_keys_values(ckv_ctx, krope_ctx, w_kv_up, mla_k_norm)
    q = rope_tail(q)
    k = jnp.concatenate([rope_tail(k_lat), k_ctx], axis=1)
    v = jnp.concatenate([v_lat, v_ctx], axis=1)
    attn = blocked_attention(q, k, v, MLA_QK ** -0.5).reshape(B, S, MLA_HEADS * MLA_V)
    return jnp.concatenate([pool_mixer(a_in, w_pool, pool_scale), attn], axis=-1) @ w_out


def na_qkv(h, w_in, na_q_norm, na_k_norm):
    B, S, _ = h.shape
    q, k, v = jnp.split(h @ w_in, 3, axis=-1)
    shp = (B, S, NA_HEADS, NA_HEAD_DIM)
    return rms_norm(q.reshape(shp), na_q_norm), rms_norm(k.reshape(shp), na_k_norm), v.reshape(shp)


def neighbourhood_attention(q, k, v, k_ctx, v_ctx, rel_bias):
    B, S, H, D = q.shape
    rows = S // GRID_W
    kh = min(NA_KH, rows)
    kw = NA_KW
    scale = D ** -0.5
    qg = q.reshape(B, rows, GRID_W, H, D)
    kg = k.reshape(B, rows, GRID_W, H, D)
    vg = v.reshape(B, rows, GRID_W, H, D)
    qcol = jnp.arange(GRID_W)
    kcol = jnp.arange(GRID_W)
    col_start = jnp.clip(qcol - kw // 2, 0, GRID_W - kw)
    col_mask = (kcol[None, :] >= col_start[:, None]) & (kcol[None, :] < col_start[:, None] + kw)
    dc_idx = jnp.clip(kcol[None, :] - qcol[:, None] + NA_KW - 1, 0, 2 * NA_KW - 2)
    bias_cols = rel_bias.astype(jnp.float32)[:, :, dc_idx]

    def one_row(r):
        rs = jnp.clip(r - kh // 2, 0, rows - kh)
        kb = lax.dynamic_slice_in_dim(kg, rs, kh, axis=1)
        vb = lax.dynamic_slice_in_dim(vg, rs, kh, axis=1)
        qr = lax.dynamic_index_in_dim(qg, r, axis=1, keepdims=False)
        dr_idx = rs + jnp.arange(kh) - r + NA_KH - 1
        bias = jnp.take(bias_cols, dr_idx, axis=1)
        bias = jnp.where(col_mask[None, None], bias, NEG_INF).transpose(0, 2, 1, 3)
        s_loc = jnp.einsum('bqhd,bikhd->bhqik', qr, kb, preferred_element_type=jnp.float32) * scale + bias[None]
        s_ctx = jnp.einsum('bqhd,blhd->bhql', qr, k_ctx, preferred_element_type=jnp.float32) * scale
        s = jnp.concatenate([s_loc.reshape(B, H, GRID_W, kh * GRID_W), s_ctx], axis=-1)
        p = jax.nn.softmax(s, axis=-1).astype(v.dtype)
        p_loc = p[..., : kh * GRID_W].reshape(B, H, GRID_W, kh, GRID_W)
        p_ctx = p[..., kh * GRID_W:]
        return (jnp.einsum('bhqik,bikhd->bqhd', p_loc, vb)
                + jnp.einsum('bhql,blhd->bqhd', p_ctx, v_ctx))

    out = lax.map(one_row, jnp.arange(rows))
    return jnp.moveaxis(out, 0, 1).reshape(B, S, H, D)


def na_context(h, w_in, na_q_norm, na_k_norm, w_out):
    B, S, _ = h.shape
    q, k, v = na_qkv(h, w_in, na_q_norm, na_k_norm)
    attn = blocked_attention(q, k, v, NA_HEAD_DIM ** -0.5).reshape(B, S, D_MODEL)
    return attn @ w_out, k, v


def na_latent(h, k_ctx, v_ctx, w_in, na_q_norm, na_k_norm, rel_bias, w_out):
    B, S, _ = h.shape
    q, k, v = na_qkv(h, w_in, na_q_norm, na_k_norm)
    attn = neighbourhood_attention(q, k, v, k_ctx, v_ctx, rel_bias).reshape(B, S, D_MODEL)
    return attn @ w_out


def setup_inputs(seed: int = 0) -> dict:
    key = jax.random.key(seed)
    keys = jax.random.split(key, 48)
    counter = [0]

    def nk():
        counter[0] += 1
        return keys[counter[0] - 1]

    def nrm(shape, s=1.0):
        return s * jax.random.normal(nk(), shape, jnp.float32)

    def dense(shape, fan_in, s=1.0):
        return nrm(shape, s * fan_in ** -0.5)

    def gain(shape):
        return 1.0 + nrm(shape, 0.05)

    return {
        "x_prompt": nrm((BATCH, SEQ, D_MODEL)),
        "x_sample": nrm((DEC_BATCH, DEC_SEQ, D_MODEL)),
        "cache_l0_mla_ckv": nrm((DEC_BATCH, PAST_LEN, KV_LORA)),
        "cache_l0_mla_krope": nrm((DEC_BATCH, PAST_LEN, MLA_ROPE)),
        "cache_l1_na_k": nrm((DEC_BATCH, PAST_LEN, NA_HEADS, NA_HEAD_DIM)),
        "cache_l1_na_v": nrm((DEC_BATCH, PAST_LEN, NA_HEADS, NA_HEAD_DIM)),
        "c": nrm((DEC_BATCH, D_MODEL)),
        "c_ctx": nrm((D_MODEL,)),
        "w_ada_l0": dense((D_MODEL, N_ADA * D_MODEL), D_MODEL, 0.5),
        "b_ada_l0": nrm((N_ADA * D_MODEL,), 0.02),
        "norm_mix_l0": gain((D_MODEL,)),
        "norm_mlp_l0": gain((D_MODEL,)),
        "w_mlp1_l0": dense((D_MODEL, D_FF), D_MODEL),
        "w_mlp2_l0": dense((D_FF, D_MODEL), D_FF),
        "w_in_l0": dense((D_MODEL, IN0_WIDTH), D_MODEL),
        "q_lora_norm_l0": gain((Q_LORA,)),
        "kv_lora_norm_l0": gain((KV_LORA,)),
        "w_q_up_l0": dense((Q_LORA, MLA_HEADS * MLA_QK), Q_LORA),
        "w_kv_up_l0": dense((KV_LORA, MLA_HEADS * (MLA_NOPE + MLA_V)), KV_LORA),
        "mla_q_norm_l0": gain((MLA_QK,)),
        "mla_k_norm_l0": gain((MLA_QK,)),
        "w_pool_l0": dense((N_POOL_GROUPS, POOL_GROUP, POOL_GROUP), POOL_GROUP),
        "pool_scale_l0": gain((POOL_WIDTH,)),
        "w_out_l0": dense((OUT0_WIDTH, D_MODEL), OUT0_WIDTH),
        "w_ada_l1": dense((D_MODEL, N_ADA * D_MODEL), D_MODEL, 0.5),
        "b_ada_l1": nrm((N_ADA * D_MODEL,), 0.02),
        "norm_mix_l1": gain((D_MODEL,)),
        "norm_mlp_l1": gain((D_MODEL,)),
        "w_mlp1_l1": dense((D_MODEL, D_FF), D_MODEL),
        "w_mlp2_l1": dense((D_FF, D_MODEL), D_FF),
        "w_in_l1": dense((D_MODEL, 3 * D_MODEL), D_MODEL),
        "na_q_norm_l1": gain((NA_HEAD_DIM,)),
        "na_k_norm_l1": gain((NA_HEAD_DIM,)),
        "rel_bias_l1": nrm((NA_HEADS, 2 * NA_KH - 1, 2 * NA_KW - 1), 0.2),
        "w_out_l1": dense((D_MODEL, D_MODEL), D_MODEL),
    }


def reference(x_prompt, x_sample, cache_l0_mla_ckv, cache_l0_mla_krope, cache_l1_na_k, cache_l1_na_v, c, c_ctx,
              w_ada_l0, b_ada_l0, norm_mix_l0, norm_mlp_l0, w_mlp1_l0, w_mlp2_l0,
              w_in_l0, q_lora_norm_l0, kv_lora_norm_l0, w_q_up_l0, w_kv_up_l0, mla_q_norm_l0, mla_k_norm_l0,
              w_pool_l0, pool_scale_l0, w_out_l0,
              w_ada_l1, b_ada_l1, norm_mix_l1, norm_mlp_l1, w_mlp1_l1, w_mlp2_l1,
              w_in_l1, na_q_norm_l1, na_k_norm_l1, rel_bias_l1, w_out_l1):
    ada_w = (w_ada_l0, w_ada_l1)
    ada_b = (b_ada_l0, b_ada_l1)
    norm_mix = (norm_mix_l0, norm_mix_l1)
    norm_mlp = (norm_mlp_l0, norm_mlp_l1)
    mlp1 = (w_mlp1_l0, w_mlp1_l1)
    mlp2 = (w_mlp2_l0, w_mlp2_l1)
    p0 = (w_in_l0, q_lora_norm_l0, kv_lora_norm_l0, w_q_up_l0, w_kv_up_l0, mla_q_norm_l0, mla_k_norm_l0,
          w_pool_l0, pool_scale_l0, w_out_l0)

    xp = x_prompt
    xs = x_sample
    new_state = []
    for i in range(DEPTH):
        sh1_p, sc1_p, g1_p, sh2_p, sc2_p, g2_p = ada_modulation(c_ctx[None, :], ada_w[i], ada_b[i])
        sh1_s, sc1_s, g1_s, sh2_s, sc2_s, g2_s = ada_modulation(c, ada_w[i], ada_b[i])
        hp = modulate(xp, norm_mix[i], sh1_p, sc1_p)
        hs = modulate(xs, norm_mix[i], sh1_s, sc1_s)
        if i % 2 == 0:
            mp, ckv_new, krope_new = pool_mla_context(hp, *p0)
            ms = pool_mla_latent(hs, cache_l0_mla_ckv, cache_l0_mla_krope, *p0)
            new_state += [ckv_new, krope_new]
        else:
            mp, k_new, v_new = na_context(hp, w_in_l1, na_q_norm_l1, na_k_norm_l1, w_out_l1)
            ms = na_latent(hs, cache_l1_na_k, cache_l1_na_v, w_in_l1, na_q_norm_l1, na_k_norm_l1,
                           rel_bias_l1, w_out_l1)
            new_state += [k_new, v_new]
        xp = xp + g1_p * mp
        xs = xs + g1_s * ms
        xp = xp + g2_p * channel_mixer(modulate(xp, norm_mlp[i], sh2_p, sc2_p), mlp1[i], mlp2[i])
        xs = xs + g2_s * channel_mixer(modulate(xs, norm_mlp[i], sh2_s, sc2_s), mlp1[i], mlp2[i])
    y_prompt = xp
    y_sample = xs
    return (y_prompt, y_sample, *new_state)
```

```python
import numpy as np
from contextlib import ExitStack
import concourse.bass as bass
import concourse.mybir as mybir
from concourse.bass_utils import run_bass_kernel_spmd

F32 = mybir.dt.float32
BF16 = mybir.dt.bfloat16
AF = mybir.ActivationFunctionType
ALU = mybir.AluOpType

ENGS = ("tensor", "vector", "scalar", "gpsimd", "sync")
D = 1024
NCH = 8
EPS = 1e-6
THETA = 10000.0
NEGBIG = -30000.0


class Res:
    __slots__ = ("name", "last_w", "readers", "dsem_idx", "dcount", "excl")

    def __init__(self, name, excl=False):
        self.name = name
        self.excl = excl
        self.last_w = None
        self.readers = []
        self.dsem_idx = None
        self.dcount = 0

    def pending(self):
        r = list(self.readers)
        if self.last_w is not None:
            r.append(self.last_w)
        return r


class Op:
    __slots__ = ("eng", "fn", "deps", "signal", "sigval", "is_dma", "dsem_idx", "dval")

    def __init__(self, eng, fn, is_dma=False):
        self.eng = eng
        self.fn = fn
        self.deps = []
        self.signal = False
        self.sigval = 0
        self.is_dma = is_dma
        self.dsem_idx = None
        self.dval = 0


class Prog:
    def __init__(self):
        self.ops = []
        self.n_dsem = 0

    def _track(self, op, reads, writes):
        for r in reads:
            if r.last_w is not None:
                op.deps.append(r.last_w)
            if r.excl:
                for rd in r.readers:
                    if rd.eng != op.eng:
                        op.deps.append(rd)
            r.readers.append(op)
        for w in writes:
            if w.last_w is not None:
                op.deps.append(w.last_w)
            for rd in w.readers:
                if rd is not op:
                    op.deps.append(rd)
            w.last_w = op
            w.readers = []

    def op(self, eng, fn, reads=(), writes=()):
        o = Op(eng, fn)
        self._track(o, reads, writes)
        self.ops.append(o)
        return o

    def dma(self, eng, fn, res, reads=(), writes=()):
        o = Op(eng, fn, is_dma=True)
        if res.dsem_idx is None:
            res.dsem_idx = self.n_dsem
            self.n_dsem += 1
        res.dcount += 1
        o.dsem_idx = res.dsem_idx
        o.dval = 16 * res.dcount
        self._track(o, reads, writes)
        self.ops.append(o)
        return o

    def emit(self, nc, stack, final_eng="sync"):
        fin = Op(final_eng, None)
        last_by_sem = {}
        for o in self.ops:
            if o.is_dma:
                last_by_sem[o.dsem_idx] = o
        fin.deps = list(last_by_sem.values())
        ops = self.ops + [fin]
        for o in ops:
            for d in o.deps:
                if not d.is_dma:
                    if d.eng == "tensor" and o.eng == "tensor" and not o.is_dma:
                        continue
                    d.signal = True
        cnt = {e: 0 for e in ENGS}
        for o in ops:
            if o.signal:
                cnt[o.eng] += 1
                o.sigval = cnt[o.eng]
        esem = {e: stack.enter_context(nc.semaphore("es_" + e)) for e in ENGS}
        dsem = [stack.enter_context(nc.semaphore("ds_%d" % i)) for i in range(self.n_dsem)]
        per_eng = {e: [o for o in ops if o.eng == e] for e in ENGS}
        block = stack.enter_context(nc.Block())

        def run(e, engobj):
            waited = {}
            for o in per_eng[e]:
                need = {}
                for d in o.deps:
                    if d.is_dma:
                        key = ("d", d.dsem_idx)
                        val = d.dval
                    else:
                        if d.eng == "tensor" and e == "tensor" and not o.is_dma:
                            continue
                        key = ("e", d.eng)
                        val = d.sigval
                    if val > need.get(key, 0):
                        need[key] = val
                for key, val in need.items():
                    if waited.get(key, 0) >= val:
                        continue
                    waited[key] = val
                    s = dsem[key[1]] if key[0] == "d" else esem[key[1]]
                    engobj.wait_ge(s, val)
                if o.fn is None:
                    continue
                inst = o.fn(engobj)
                if o.is_dma:
                    inst.then_inc(dsem[o.dsem_idx], 16)
                elif o.signal:
                    inst.then_inc(esem[e], 1)

        @block.tensor
        def _(eng):
            run("tensor", eng)

        @block.vector
        def _(eng):
            run("vector", eng)

        @block.scalar
        def _(eng):
            run("scalar", eng)

        @block.gpsimd
        def _(eng):
            run("gpsimd", eng)

        @block.sync
        def _(eng):
            run("sync", eng)


class Buf:
    def __init__(self, ap, res, c0, c1):
        self.ap = ap
        self.res = res
        self.c0 = c0
        self.c1 = c1

    @property
    def r(self):
        return self.res[0]


class Arena:
    def __init__(self, ap, ncols):
        self.ap = ap
        self.ncols = ncols
        self.live = []
        self.freed = []

    def alloc(self, name, ncols, nres=1):
        ivs = sorted((c0, c1) for c0, c1, _ in self.live)
        pos = 0
        for c0, c1 in ivs:
            if c0 - pos >= ncols:
                break
            pos = max(pos, c1)
        if pos + ncols > self.ncols:
            lay = sorted((c0, c1, b.res[0].name) for c0, c1, b in self.live)
            raise AssertionError("SBUF arena overflow allocating %s (%d cols at %d); live=%s" % (name, ncols, pos, lay))
        res = [Res("%s_%d" % (name, i)) for i in range(nres)]
        keep = []
        for c0, c1, ops in self.freed:
            if c0 < pos + ncols and pos < c1:
                for r in res:
                    r.readers.extend(ops)
                if c0 < pos:
                    keep.append((c0, pos, ops))
                if c1 > pos + ncols:
                    keep.append((pos + ncols, c1, ops))
            else:
                keep.append((c0, c1, ops))
        self.freed = keep
        b = Buf(self.ap[:, pos:pos + ncols], res, pos, pos + ncols)
        self.live.append((pos, pos + ncols, b))
        return b

    def free(self, *bufs):
        for b in bufs:
            self.live = [t for t in self.live if t[2] is not b]
            ops = []
            for r in b.res:
                ops.extend(r.pending())
            self.freed.append((b.c0, b.c1, ops))


def build_program(stage=99, dbg=False, mini=None, skip=()):
    nc = bass.Bass("TRN2", target_bir_lowering=False)
    P = Prog()

    def din(name, shape, dt=F32):
        if mini is not None and name not in mini:
            shape = [1, 1]
        return nc.dram_tensor(name, list(shape), dt, kind="ExternalInput").ap()

    def dout(name, shape):
        return nc.dram_tensor(name, list(shape), F32, kind="ExternalOutput").ap()

    d_xp = din("xp", [512, D])
    d_xw = din("xw", [1536, D])
    d_xpre = din("xpre", [128, D])
    d_xb = din("xb", [4096, D])
    d_cond = din("condT", [128, 16])
    d_wada = [din("w_ada0", [D, 6 * D]), din("w_ada1", [D, 6 * D])]
    d_bada = [din("b_ada0", [128, 48]), din("b_ada1", [128, 48])]
    d_vec = din("vecs", [128, 64])
    d_win0_pool = din("w_in0_pool", [D, 512])
    d_win0_q = din("w_in0_q", [D, 256])
    d_win0_kv = din("w_in0_kv", [D, 320])
    d_wqup = din("w_qup_aug", [256, 1536])
    d_wkvup = din("w_kvup", [128, 1024])
    d_wpool = din("w_pool", [128, 512])
    d_wout0 = din("w_out0", [D, D])
    d_mlp1 = [din("w_mlp1_0", [D, 4 * D]), din("w_mlp1_1", [D, 4 * D])]
    d_mlp2 = [din("w_mlp2_0", [4 * D, D]), din("w_mlp2_1", [4 * D, D])]
    d_ropek = din("rope_k", [64, 4096])
    d_ropeq = din("rope_q", [64, 1536])
    d_cckv = din("cache_ckvT", [128, 512])
    d_ckr = din("cache_kropeT", [32, 512])
    d_pmask = din("pool_mask", [128, 1600])
    d_pinv = din("pool_inv", [128, 128])
    d_win1 = din("w_in1", [D, 8 * 384])
    d_wout1 = din("w_out1", [D, D])
    d_rbt = din("rbt", [16, 64, 960])
    d_cmask = din("cmask", [64, 64])
    d_pen = din("pen", [16, 1536])
    d_qoh = din("qoh", [16, 1024])
    d_ck1 = din("cache_k1T", [64, 16 * 512])
    d_cv1 = din("cache_v1", [512, 1024])

    o_yp = dout("yp", [512, D])
    o_ys = dout("ys", [1024, D])
    o_ckv = dout("ckv_new", [512, 128])
    o_kr = dout("krope_new", [512, 32])
    o_k1 = dout("k_new", [512, D])
    o_v1 = dout("v_new", [512, D])
    o_dbg = dout("dbg", [2048, D]) if dbg else None

    with ExitStack() as st:
        NCOLS = 53000
        arena_t = st.enter_context(nc.sbuf_tensor("arena", [128, NCOLS], F32))
        psum_t = st.enter_context(nc.psum_tensor("psum", [128, 8 * 512], F32))
        A = Arena(arena_t, NCOLS)
        PB = [psum_t[:, i * 512:(i + 1) * 512] for i in range(8)]
        RB = [Res("bank%d" % i, excl=True) for i in range(8)]
        bank_rr = {}

        def nb(group, banks):
            i = bank_rr.get(group, 0)
            bank_rr[group] = i + 1
            return banks[i % len(banks)]

        def X(eng, meth, reads, writes, *a, **kw):
            def fn(e):
                try:
                    return getattr(e, meth)(*a, **kw)
                except Exception as ex:
                    desc = [getattr(v, "shape", v) for v in a] + ["%s=%s" % (k, getattr(v, "shape", v)) for k, v in kw.items()]
                    raise RuntimeError("op %s.%s failed: %s | %s" % (eng, meth, desc, ex))
            return P.op(eng, fn, reads, writes)

        def DMA(eng, out, in_, res, reads=(), writes=()):
            return P.dma(eng, lambda e: e.dma_start(out=out, in_=in_), res, reads, writes)

        def bf(buf_ap):
            return buf_ap.bitcast(BF16)

        cst = A.alloc("consts", 128 + 64 + 64 + 64 + 128 + 1 + 64 + 16 + 48 * 2 + 48 * 2 + 2 * 6 * 16)
        cc = [0]

        def ccarve(n):
            a = cst.ap[:, cc[0]:cc[0] + n]
            cc[0] += n
            return a
        ident_f = ccarve(128)
        ident_b = bf(ccarve(64))
        ones_b = bf(ccarve(64))
        blk_b = bf(ccarve(64))
        ones_f = ccarve(128)
        eps_t = ccarve(1)
        vec = ccarve(64)
        cond_f = ccarve(16)
        RC = cst.r
        X("gpsimd", "memset", [], [RC], ident_f, 1.0)
        X("gpsimd", "affine_select", [RC], [RC], out=ident_f, in_=ident_f, pattern=[[-1, 128]],
          compare_op=ALU.is_equal, fill=0.0, base=0, channel_multiplier=1)
        X("gpsimd", "memset", [], [RC], ones_f, 1.0)
        X("gpsimd", "memset", [], [RC], eps_t, EPS)
        X("vector", "tensor_copy", [RC], [RC], out=ident_b, in_=ident_f)
        X("vector", "tensor_copy", [RC], [RC], out=ones_b, in_=ones_f)
        X("gpsimd", "memset", [], [RC], blk_b, 0.0)
        X("gpsimd", "memset", [], [RC], blk_b[0:64, 0:64], 1.0)
        X("gpsimd", "memset", [], [RC], blk_b[64:128, 64:128], 1.0)
        DMA("sync", vec, d_vec, RC, writes=[RC])
        DMA("sync", cond_f, d_cond, RC, writes=[RC])
        V_NMIX = [vec[:, 0:8], vec[:, 16:24]]
        V_NMLP = [vec[:, 8:16], vec[:, 24:32]]
        V_QL = vec[:, 32:34]
        V_KVL = vec[:, 34:35]
        V_GQ, V_GQS, V_GK, V_GKS = vec[:, 35:36], vec[:, 36:37], vec[:, 37:38], vec[:, 38:39]
        V_PSC = vec[:, 39:43]
        V_NGQ, V_NGK = vec[:, 43:44], vec[:, 44:45]

        if stage == -3:
            P.emit(nc, st)
            return nc
        NSLOT = 4
        slots = [A.alloc("wslot%d" % i, 2048) for i in range(NSLOT)]
        slot_rr = [0]

        pinned = set()

        def wload(dram_ap, shape_str=None, parts=128, **kw):
            while (slot_rr[0] % NSLOT) in pinned:
                slot_rr[0] += 1
            wload.last = slot_rr[0] % NSLOT
            s = slots[slot_rr[0] % NSLOT]
            slot_rr[0] += 1
            shp = list(dram_ap.shape)
            n = int(np.prod(shp[1:]))
            assert n <= 4096 and shp[0] == parts
            flat = bf(s.ap)[0:parts, 0:n]
            if len(shp) == 3:
                view = flat.rearrange("p (a b) -> p a b", a=shp[1])
            else:
                view = flat
            DMA("gpsimd", view, dram_ap, s.r, writes=[s.r])
            return view, s.r

        mods_b = A.alloc("mods", 2 * 6 * 16)
        RM = mods_b.r
        xT_b = A.alloc("xT", 8 * 2048, nres=4)
        xT = xT_b.ap.rearrange("p (c t) -> p c t", c=8)
        RX = xT_b.res
        hT_b = A.alloc("hT", 8 * 2048 // 2, nres=4)
        hT = bf(hT_b.ap).rearrange("p (c t) -> p c t", c=8)
        RH = hT_b.res

        def blk(tb):
            return slice(tb * 512, (tb + 1) * 512)

        def rstd_from(ps_ap, out_ap, inv_n, rows, rps, rout):
            X("scalar", "activation", [rps, RC], [rout], out=out_ap, in_=ps_ap, func=AF.Ln,
              bias=eps_t[rows, 0:1], scale=inv_n)
            X("scalar", "activation", [rout], [rout], out=out_ap, in_=out_ap, func=AF.Exp, scale=-0.5)

        def load_xT(dram_ap, n, dst, rdst, stg, eng_alt=0):
            nt = max(1, n // 128)
            pp = min(n, 128)
            sv = stg.ap.rearrange("p (a d) -> p a d", a=4)
            if n >= 128:
                DMA("sync", sv[:, 0:nt, :], dram_ap.rearrange("(a p) d -> p a d", p=128), stg.r, writes=[stg.r])
            else:
                DMA("sync", sv[0:pp, 0, :], dram_ap, stg.r, writes=[stg.r])
            for c in range(8):
                b = nb("ld", [0, 1, 2, 3])
                for a in range(nt):
                    X("tensor", "transpose", [stg.r, RC], [RB[b]], out=PB[b][:, a * pp:(a + 1) * pp],
                      in_=sv[0:pp, a, c * 128:(c + 1) * 128], identity=ident_f[0:pp, 0:pp])
                if (c + eng_alt) % 2 == 0:
                    X("vector", "tensor_copy", [RB[b]], [rdst], out=dst[:, c, 0:n], in_=PB[b][:, 0:n])
                else:
                    X("scalar", "copy", [RB[b]], [rdst], out=dst[:, c, 0:n], in_=PB[b][:, 0:n])

        def normmod(xsrc, rx, n, Avec, shvec, hdst, rh, sq, tmp, rstd):
            sqv = bf(sq.ap)[:, 0:8 * n].rearrange("p (c t) -> p c t", c=8)
            rxl = list(rx) if isinstance(rx, (list, tuple)) else [rx]
            b = nb("nm", [4, 5])
            for c in range(8):
                X("scalar", "activation", rxl, [sq.res[c % len(sq.res)]], out=sqv[:, c, :], in_=xsrc[:, c, :], func=AF.Square)
                X("tensor", "matmul", [sq.res[c % len(sq.res)], RC], [RB[b]], PB[b][:, 0:n], lhsT=ones_b, rhs=sqv[:, c, :],
                  start=(c == 0), stop=(c == 7))
            rs = rstd.ap[:, 0:n]
            rstd_from(PB[b][:, 0:n], rs, 1.0 / D, slice(0, 128), RB[b], rstd.r)
            for c in range(8):
                t = tmp[c % len(tmp)]
                X("vector", "scalar_tensor_tensor", rxl + [rstd.r, RM], [t.r], out=t.ap[:, 0:n], in0=xsrc[:, c, :],
                  scalar=Avec[:, c:c + 1], in1=rs, op0=ALU.mult, op1=ALU.mult)
                X("scalar", "activation", [t.r, RM], [rh], out=hdst[:, c, :], in_=t.ap[:, 0:n], func=AF.Identity,
                  bias=shvec[:, c:c + 1], scale=1.0)

        modT = [mods_b.ap[:, l * 96:(l + 1) * 96].rearrange("p (j k) -> p j k", k=2) for l in range(2)]
        cs_b = A.alloc("cond_silu", 8)
        csb = bf(cs_b.ap).rearrange("p (c k) -> p c k", k=2)
        X("scalar", "activation", [RC], [cs_b.r], out=csb, in_=cond_f.rearrange("p (c k) -> p c k", k=2), func=AF.Silu)
        mrow = [A.alloc("mrow%d" % i, 512) for i in range(2)]
        bada_b = A.alloc("bada", 96)
        for l in range(2):
            DMA("sync", bada_b.ap[:, l * 48:(l + 1) * 48], d_bada[l], bada_b.r, writes=[bada_b.r])
        if mini is not None:
            X("vector", "memset", [], [RM], mods_b.ap, 0.5)
        def ada_piece(l, pc):
            tbk = 6 + l
            wv, wr = wload(d_wada[l][:, pc * 512:(pc + 1) * 512].rearrange("(c p) n -> p c n", p=128))
            b = nb("ada", [0, 1, 2, 3])
            for kc in range(8):
                X("tensor", "matmul", [cs_b.r, wr], [RB[b]], PB[b][0:2, :], lhsT=csb[:, kc, :], rhs=wv[:, kc, :],
                  start=(kc == 0), stop=(kc == 7))
            mr = mrow[pc % 2]
            X("vector", "tensor_copy", [RB[b]], [mr.r], out=mr.ap[0:2, :], in_=PB[b][0:2, :])
            for j in range(4):
                jj = pc * 4 + j
                X("tensor", "matmul", [mr.r, RC], [RB[tbk]], PB[tbk][:, jj * 2:jj * 2 + 2],
                  lhsT=mr.ap[0:2, j * 128:(j + 1) * 128], rhs=ident_f[0:2, 0:2], start=True, stop=True)

        def ada_finish(l):
            tbk = 6 + l
            for k in range(2):
                X("vector", "tensor_tensor", [RB[tbk], bada_b.r], [RM], out=modT[l][:, :, k],
                  in0=PB[tbk][:, 0:96].rearrange("p (j k) -> p j k", k=2)[:, :, k], in1=bada_b.ap[:, l * 48:(l + 1) * 48],
                  op=ALU.add)
            for k in range(2):
                X("vector", "scalar_tensor_tensor", [RM, RC], [RM], out=modT[l][:, 8:16, k], in0=modT[l][:, 8:16, k],
                  scalar=1.0, in1=V_NMIX[l], op0=ALU.add, op1=ALU.mult)
                X("vector", "scalar_tensor_tensor", [RM, RC], [RM], out=modT[l][:, 32:40, k], in0=modT[l][:, 32:40, k],
                  scalar=1.0, in1=V_NMLP[l], op0=ALU.add, op1=ALU.mult)

        ada_deferred = (mini is None) and stage >= 3

        def MOD(l, kind, cond):
            return modT[l][:, kind * 8:(kind + 1) * 8, cond]

        def cond_of(tb):
            return 0 if tb == 0 else 1

        def store_tokens(xsrc, rsrc, n, dram_ap, ostg, rextra=()):
            ov = ostg.ap.rearrange("p (a d) -> p a d", a=4)
            nt = n // 128
            assert nt <= 4
            for a in range(nt):
                for half in range(2):
                    b = nb("st", [0, 1, 2, 3])
                    for cc_ in range(4):
                        c = half * 4 + cc_
                        X("tensor", "transpose", [rsrc, RC] + list(rextra), [RB[b]], out=PB[b][:, cc_ * 128:(cc_ + 1) * 128],
                          in_=xsrc[:, c, a * 128:(a + 1) * 128], identity=ident_f)
                    if half == 0:
                        X("vector", "tensor_copy", [RB[b]], [ostg.r], out=ov[:, a, 0:512], in_=PB[b])
                    else:
                        X("scalar", "copy", [RB[b]], [ostg.r], out=ov[:, a, 512:1024], in_=PB[b])
            DMA("sync", dram_ap.rearrange("(a p) d -> p a d", p=128), ov[:, 0:nt, :], ostg.r, reads=[ostg.r])

        def epilogue(ps, rps, l, kind, tb, m):
            X("vector", "scalar_tensor_tensor", [rps, RM, RX[tb]], [RX[tb]], out=xT[:, m, blk(tb)], in0=ps,
              scalar=MOD(l, kind, cond_of(tb))[:, m:m + 1], in1=xT[:, m, blk(tb)], op0=ALU.mult, op1=ALU.add)

        def mlp(l, segs):
            ns = len(segs)
            hb = A.alloc("hT_mlp", 8 * 512 * ns // 2, nres=ns)
            hv = bf(hb.ap).rearrange("p (c t) -> p c t", c=8)
            sq = A.alloc("sq_m", 2048, nres=8)
            tmps = [A.alloc("tmp_m%d" % i, 512) for i in range(2)]
            rstd = A.alloc("rstd_m", 512)
            for si, (c0, rl, cd) in enumerate(segs):
                normmod(xT[:, :, c0:c0 + 512], rl, 512, MOD(l, 4, cd), MOD(l, 3, cd),
                        hv[:, :, si * 512:(si + 1) * 512], hb.res[si], sq, tmps, rstd)
            A.free(sq, rstd)
            ag = [A.alloc("ag%d" % i, 4 * 512 * ns // 2, nres=ns) for i in range(2)]
            agv = [bf(a_.ap).rearrange("p (j t) -> p j t", j=4) for a_ in ag]
            wts = {}

            def ph1(g):
                w1, r1 = wload(d_mlp1[l][:, g * 512:(g + 1) * 512].rearrange("(c p) n -> p c n", p=128))
                w2, r2 = wload(d_mlp2[l][g * 512:(g + 1) * 512, :].rearrange("(j p) n -> p j n", p=128))
                wts[g] = (w2, r2)
                for si in range(ns):
                    ss = slice(si * 512, (si + 1) * 512)
                    for j in range(4):
                        b = nb("m1", [0, 1, 2, 3])
                        for c in range(8):
                            X("tensor", "matmul", [hb.res[si], r1], [RB[b]], PB[b], lhsT=w1[:, c, j * 128:(j + 1) * 128],
                              rhs=hv[:, c, ss], start=(c == 0), stop=(c == 7))
                        t = tmps[(si * 4 + j) % 2]
                        X("scalar", "activation", [RB[b]], [t.r], out=t.ap, in_=PB[b], func=AF.Relu)
                        X("vector", "tensor_tensor", [t.r], [ag[g % 2].res[si]], out=agv[g % 2][:, j, ss], in0=t.ap,
                          in1=t.ap, op=ALU.mult)

            def ph2(g):
                w2, r2 = wts.pop(g)
                for si, (c0, rl, cd) in enumerate(segs):
                    ss = slice(si * 512, (si + 1) * 512)
                    for m in range(8):
                        b = nb("m2", [4, 5, 6, 7])
                        for j in range(4):
                            X("tensor", "matmul", [ag[g % 2].res[si], r2], [RB[b]], PB[b], lhsT=w2[:, j, m * 128:(m + 1) * 128],
                              rhs=agv[g % 2][:, j, ss], start=(j == 0), stop=(j == 3))
                        X("vector", "scalar_tensor_tensor", [RB[b], RM] + list(rl), list(rl), out=xT[:, m, c0:c0 + 512], in0=PB[b],
                          scalar=MOD(l, 5, cd)[:, m:m + 1], in1=xT[:, m, c0:c0 + 512], op0=ALU.mult, op1=ALU.add)
            ph1(0)
            for g in range(1, 8):
                ph1(g)
                ph2(g - 1)
            ph2(7)
            A.free(hb, *tmps, *ag)

        stg = A.alloc("stg", 4096)
        load_xT(d_xp, 512, xT[:, :, blk(0)], RX[0], stg)
        for i in range(3):
            load_xT(d_xw[i * 512:(i + 1) * 512, :], 512, xT[:, :, blk(i + 1)], RX[i + 1], stg, eng_alt=i)
        if "x1" in skip:
            P.emit(nc, st)
            return nc
        xpre_b = A.alloc("xpreT", 8 * 128)
        xpreT = xpre_b.ap.rearrange("p (c t) -> p c t", c=8)
        load_xT(d_xpre, 128, xpreT, xpre_b.r, stg)
        hpre_b = A.alloc("hpreT", 8 * 128 // 2)
        hpreT = bf(hpre_b.ap).rearrange("p (c t) -> p c t", c=8)
        if "x2" in skip:
            P.emit(nc, st)
            return nc
        if mini is None:
            for pc in range(12):
                ada_piece(0, pc)
            ada_finish(0)
            if not ada_deferred:
                for pc in range(12):
                    ada_piece(1, pc)
                ada_finish(1)
        if not ada_deferred:
            A.free(cs_b, bada_b, *mrow)

        sq_b = A.alloc("sq", 2048, nres=8)
        tmpb = [A.alloc("tmp%d" % i, 512) for i in range(2)]
        rstd_b = A.alloc("rstd", 512)
        for tb in range(4):
            normmod(xT[:, :, blk(tb)], RX[tb], 512, MOD(0, 1, cond_of(tb)), MOD(0, 0, cond_of(tb)),
                    hT[:, :, blk(tb)], RH[tb], sq_b, tmpb, rstd_b)
        normmod(xpreT, xpre_b.r, 128, MOD(0, 1, 1), MOD(0, 0, 1), hpreT, hpre_b.r, sq_b, tmpb, rstd_b)
        if "x3" in skip:
            P.emit(nc, st)
            return nc
        A.free(stg, xpre_b)

        if stage >= 2:
            LA = 2144
            B_S0, B_S1, B_PRE, B_W = 8, 272, 536, 600
            wpin_v, wpin_r = wload(d_win0_pool.rearrange("(c p) n -> p c n", p=128))
            wpl_v, wpl_r = wload(d_wpool)
            wop_v, wop_r = wload(d_wout0[0:512, :].rearrange("(c p) n -> p c n", p=128))
            mask_b = A.alloc("pmask", 1600)
            pinv_b = A.alloc("pinv", 128)
            DMA("sync", mask_b.ap, d_pmask, mask_b.r, writes=[mask_b.r])
            DMA("sync", pinv_b.ap, d_pinv, pinv_b.r, writes=[pinv_b.r])
            abuf = [A.alloc("abuf%d" % i, LA) for i in range(1)]
            sbuf_ = [A.alloc("sbuf%d" % i, LA) for i in range(2)]
            pld = [A.alloc("pooled%d" % i, LA // 2) for i in range(1)]
            t8 = A.alloc("t8", 8)
            po_b = A.alloc("poolout", 4 * 2048 // 2, nres=4)
            pov = bf(po_b.ap).rearrange("p (g t) -> p g t", g=4)
            for b_ in abuf + sbuf_:
                X("vector", "memset", [], [b_.r], b_.ap, 0.0)
            for g in range(4):
                w = (2, 4, 8, 16)[g]
                ab = abuf[0]
                av = ab.ap
                for tb in range(4):
                    b = nb("pl", [0, 1, 2, 3])
                    for c in range(8):
                        X("tensor", "matmul", [RH[tb], wpin_r], [RB[b]], PB[b], lhsT=wpin_v[:, c, g * 128:(g + 1) * 128],
                          rhs=hT[:, c, blk(tb)], start=(c == 0), stop=(c == 7))
                    if tb == 0:
                        X("vector", "tensor_copy", [RB[b]], [ab.r], out=av[:, B_S0:B_S0 + 256], in_=PB[b][:, 0:256])
                        X("vector", "tensor_copy", [RB[b]], [ab.r], out=av[:, B_S1:B_S1 + 256], in_=PB[b][:, 256:512])
                    else:
                        X("vector", "tensor_tensor", [RB[b], mask_b.r], [ab.r],
                          out=av[:, B_W + (tb - 1) * 512:B_W + tb * 512], in0=PB[b],
                          in1=mask_b.ap[:, 64 + (tb - 1) * 512:64 + tb * 512], op=ALU.mult)
                b = nb("pl", [0, 1, 2, 3])
                for c in range(8):
                    X("tensor", "matmul", [hpre_b.r, wpin_r], [RB[b]], PB[b][:, 0:64], lhsT=wpin_v[:, c, g * 128:(g + 1) * 128],
                      rhs=hpreT[:, c, 0:64], start=(c == 0), stop=(c == 7))
                X("vector", "tensor_tensor", [RB[b], mask_b.r], [ab.r], out=av[:, B_PRE:B_PRE + 64], in0=PB[b][:, 0:64],
                  in1=mask_b.ap[:, 0:64], op=ALU.mult)
                cur, rcur = av, ab.r
                sh = 1
                k = 0
                while sh < w:
                    dstb = sbuf_[k % 2]
                    X("vector", "tensor_tensor", [rcur], [dstb.r], out=dstb.ap[:, sh:LA], in0=cur[:, sh:LA], in1=cur[:, 0:LA - sh],
                      op=ALU.add)
                    cur, rcur = dstb.ap, dstb.r
                    sh *= 2
                    k += 1
                pb_ = pld[0]
                pv = bf(pb_.ap)
                off = w // 2 - 1
                for (base, L) in ((B_S0, 256), (B_S1, 256), (B_W, 1536)):
                    X("vector", "scalar_tensor_tensor", [rcur, ab.r], [pb_.r], out=pv[:, base:base + L],
                      in0=cur[:, base + off:base + off + L], scalar=1.0 / w, in1=av[:, base:base + L], op0=ALU.mult,
                      op1=ALU.subtract)
                for (pos, bd) in ((B_S0, 0), (B_S0 + 248, 1), (B_S1, 0), (B_S1 + 248, 1), (B_W + 256, 2), (B_W + 1272, 3)):
                    X("vector", "tensor_tensor", [rcur, pinv_b.r], [t8.r], out=t8.ap, in0=cur[:, pos + off:pos + off + 8],
                      in1=pinv_b.ap[:, g * 32 + bd * 8:g * 32 + bd * 8 + 8], op=ALU.mult)
                    X("vector", "tensor_tensor", [t8.r, ab.r], [pb_.r], out=pv[:, pos:pos + 8], in0=t8.ap, in1=av[:, pos:pos + 8],
                      op=ALU.subtract)
                for tb in range(4):
                    b = nb("pl", [0, 1, 2, 3])
                    if tb == 0:
                        for s_, base in enumerate((B_S0, B_S1)):
                            X("tensor", "matmul", [pb_.r, wpl_r], [RB[b]], PB[b][:, s_ * 256:(s_ + 1) * 256],
                              lhsT=wpl_v[:, g * 128:(g + 1) * 128], rhs=pv[:, base:base + 256], start=True, stop=True)
                    else:
                        X("tensor", "matmul", [pb_.r, wpl_r], [RB[b]], PB[b], lhsT=wpl_v[:, g * 128:(g + 1) * 128],
                          rhs=pv[:, B_W + (tb - 1) * 512:B_W + tb * 512], start=True, stop=True)
                    X("vector", "tensor_scalar", [RB[b], RC], [po_b.res[tb]], out=pov[:, g, blk(tb)], in0=PB[b],
                      scalar1=V_PSC[:, g:g + 1], scalar2=None, op0=ALU.mult)
            for tb in range(4):
                for m in range(8):
                    b = nb("po", [4, 5, 6, 7])
                    for g in range(4):
                        X("tensor", "matmul", [po_b.res[tb], wop_r], [RB[b]], PB[b], lhsT=wop_v[:, g, m * 128:(m + 1) * 128],
                          rhs=pov[:, g, blk(tb)], start=(g == 0), stop=(g == 3))
                    epilogue(PB[b], RB[b], 0, 2, tb, m)
            A.free(mask_b, pinv_b, t8, po_b, hpre_b, *abuf, *sbuf_, *pld)

        wkv_v, wkv_r = wload(d_win0_kv.rearrange("(c p) n -> p c n", p=128))
        pinned.add(wload.last)
        NKEY = 5120
        ckvn_b = A.alloc("ckvnT", NKEY // 2, nres=10)
        ckvnT = bf(ckvn_b.ap)
        kr_b = A.alloc("KRsq", NKEY, nres=10)
        KRv = bf(kr_b.ap)[:, 0:NKEY]
        KSQv = bf(kr_b.ap)[:, NKEY:2 * NKEY]

        def kv_block(hsrc, rh, kb, rope_tile=None, r_rope=None, ckv32=None, kr32=None, r32=None):
            ks = slice(kb * 512, (kb + 1) * 512)
            rr = slice(64, 96)
            ba = nb("kv", [0, 1, 2, 3])
            for c in range(8):
                X("tensor", "matmul", [rh, wkv_r], [RB[ba]], PB[ba], lhsT=wkv_v[:, c, 0:128], rhs=hsrc[:, c, :],
                  start=(c == 0), stop=(c == 7))
            sqv = bf(sq_b.ap)[:, 0:512]
            X("scalar", "activation", [RB[ba]], [sq_b.r], out=sqv, in_=PB[ba], func=AF.Square)
            bs = nb("kv", [0, 1, 2, 3])
            X("tensor", "matmul", [sq_b.r, RC], [RB[bs]], PB[bs], lhsT=ones_b, rhs=sqv, start=True, stop=True)
            rstd_from(PB[bs], rstd_b.ap, 1.0 / 128, slice(0, 128), RB[bs], rstd_b.r)
            X("vector", "scalar_tensor_tensor", [RB[ba], rstd_b.r, RC], [ckvn_b.res[kb]], out=ckvnT[:, ks], in0=PB[ba],
              scalar=V_KVL, in1=rstd_b.ap, op0=ALU.mult, op1=ALU.mult)
            if ckv32 is not None:
                X("vector", "scalar_tensor_tensor", [RB[ba], rstd_b.r, RC], [r32], out=ckv32, in0=PB[ba],
                  scalar=V_KVL, in1=rstd_b.ap, op0=ALU.mult, op1=ALU.mult)
            bb = nb("kv", [0, 1, 2, 3])
            for c in range(8):
                X("tensor", "matmul", [rh, wkv_r], [RB[bb]], PB[bb][0:96, :], lhsT=wkv_v[:, c, 128:224], rhs=hsrc[:, c, :],
                  start=(c == 0), stop=(c == 7))
            if "k2" not in skip:
                X("scalar", "activation", [RB[bb]], [kr_b.res[kb]], out=KSQv[rr, ks], in_=PB[bb][rr, :], func=AF.Square)
            if kr32 is not None and "k3" not in skip:
                X("vector", "tensor_copy", [RB[bb]], [r32], out=kr32, in_=PB[bb][rr, :])
            if rope_tile is None:
                if "k1" not in skip:
                    X("vector", "tensor_scalar", [RB[bb], RC], [kr_b.res[kb]], out=KRv[rr, ks], in0=PB[bb][rr, :],
                      scalar1=V_GK[rr, :], scalar2=None, op0=ALU.mult)
            else:
                bc = nb("kv", [0, 1, 2, 3])
                for c in range(8):
                    X("tensor", "matmul", [rh, wkv_r], [RB[bc]], PB[bc][0:96, :], lhsT=wkv_v[:, c, 224:320],
                      rhs=hsrc[:, c, :], start=(c == 0), stop=(c == 7))
                t1, t2 = tmpb[0], tmpb[1]
                X("vector", "scalar_tensor_tensor", [RB[bb], r_rope, RC], [t1.r], out=t1.ap[rr, :], in0=PB[bb][rr, :],
                  scalar=V_GK[rr, :], in1=rope_tile[rr, 0:512], op0=ALU.mult, op1=ALU.mult)
                X("vector", "scalar_tensor_tensor", [RB[bc], r_rope, RC], [t2.r], out=t2.ap[rr, :], in0=PB[bc][rr, :],
                  scalar=V_GKS[rr, :], in1=rope_tile[rr, 512:1024], op0=ALU.mult, op1=ALU.mult)
                X("vector", "tensor_tensor", [t1.r, t2.r], [kr_b.res[kb]], out=KRv[rr, ks], in0=t1.ap[rr, :], in1=t2.ap[rr, :],
                  op=ALU.add)

        o32_b = A.alloc("kvout32", 1024)
        ckv32 = o32_b.ap[:, 0:512]
        kr32 = o32_b.ap[:, 512:1024]
        kv_block(hT[:, :, blk(0)], RH[0], 9, ckv32=ckv32, kr32=kr32[64:96, :], r32=o32_b.r)
        if "x4" in skip:
            P.emit(nc, st)
            return nc
        ost_b = A.alloc("ostage", 4 * 160)
        ostv = ost_b.ap.rearrange("p (a d) -> p a d", a=4)
        for a in range(4):
            b = nb("ld", [0, 1, 2, 3])
            X("tensor", "transpose", [o32_b.r, RC], [RB[b]], out=PB[b][:, 0:128], in_=ckv32[:, a * 128:(a + 1) * 128],
              identity=ident_f)
            X("tensor", "matmul", [o32_b.r, RC], [RB[b]], PB[b][:, 128:160], lhsT=kr32[64:96, a * 128:(a + 1) * 128],
              rhs=ident_f[64:96, 64:96], start=True, stop=True)
            X("vector", "tensor_copy", [RB[b]], [ost_b.r], out=ostv[:, a, :], in_=PB[b][:, 0:160])
        DMA("sync", o_ckv.rearrange("(a p) d -> p a d", p=128), ostv[:, :, 0:128], ost_b.r, reads=[ost_b.r])
        DMA("sync", o_kr.rearrange("(a p) d -> p a d", p=128), ostv[:, :, 128:160], ost_b.r, reads=[ost_b.r])
        A.free(o32_b, ost_b)

        if stage >= 3:
            wq_v, wq_r = wload(d_win0_q.rearrange("(c p) n -> p c n", p=128))
            qln_b = A.alloc("qlnT", 2 * 2048 // 2, nres=4)
            qln = bf(qln_b.ap).rearrange("p (j t) -> p j t", j=2)
            sq2 = bf(sq_b.ap)[:, 0:1024].rearrange("p (j t) -> p j t", j=2)
            for tb in range(4):
                bq = [nb("q", [0, 1, 2, 3]), nb("q", [0, 1, 2, 3])]
                for j in range(2):
                    for c in range(8):
                        X("tensor", "matmul", [RH[tb], wq_r], [RB[bq[j]]], PB[bq[j]], lhsT=wq_v[:, c, j * 128:(j + 1) * 128],
                          rhs=hT[:, c, blk(tb)], start=(c == 0), stop=(c == 7))
                    X("scalar", "activation", [RB[bq[j]]], [sq_b.res[j]], out=sq2[:, j, :], in_=PB[bq[j]], func=AF.Square)
                bs = nb("qs", [4, 5])
                for j in range(2):
                    X("tensor", "matmul", [sq_b.res[j], RC], [RB[bs]], PB[bs], lhsT=ones_b, rhs=sq2[:, j, :], start=(j == 0), stop=(j == 1))
                rstd_from(PB[bs], rstd_b.ap, 1.0 / 256, slice(0, 128), RB[bs], rstd_b.r)
                for j in range(2):
                    X("vector", "scalar_tensor_tensor", [RB[bq[j]], rstd_b.r, RC], [qln_b.res[tb]], out=qln[:, j, blk(tb)],
                      in0=PB[bq[j]], scalar=V_QL[:, j:j + 1], in1=rstd_b.ap, op0=ALU.mult, op1=ALU.mult)
        A.free(hT_b)

        if stage >= 3:
            stg = A.alloc("stg", 4096)
            xtmp_b = A.alloc("xtmp", 4096)
            xtmp = xtmp_b.ap.rearrange("p (c t) -> p c t", c=8)
            htmp_b = [A.alloc("htmp%d" % i, 2048) for i in range(1)]
            ropet = [A.alloc("ropet%d" % i, 1024) for i in range(1)]
            for kb in range(8):
                load_xT(d_xb[kb * 512:(kb + 1) * 512, :], 512, xtmp, xtmp_b.r, stg, eng_alt=kb)
                hb_ = htmp_b[0]
                hv_ = bf(hb_.ap).rearrange("p (c t) -> p c t", c=8)
                normmod(xtmp, xtmp_b.r, 512, MOD(0, 1, 1), MOD(0, 0, 1), hv_, hb_.r, sq_b, tmpb, rstd_b)
                rt = ropet[0]
                DMA("sync", rt.ap[64:96, 0:512], d_ropek[0:32, kb * 512:(kb + 1) * 512], rt.r, writes=[rt.r])
                DMA("sync", rt.ap[64:96, 512:1024], d_ropek[32:64, kb * 512:(kb + 1) * 512], rt.r, writes=[rt.r])
                kv_block(hv_, hb_.r, kb, rope_tile=rt.ap, r_rope=rt.r)
                if ada_deferred:
                    for pc in range((kb * 12) // 8, ((kb + 1) * 12) // 8):
                        ada_piece(1, pc)
            DMA("sync", xtmp_b.ap[:, 0:512], d_cckv, xtmp_b.r, writes=[xtmp_b.r])
            X("vector", "tensor_copy", [xtmp_b.r], [ckvn_b.res[8]], out=ckvnT[:, 4096:4608], in_=xtmp_b.ap[:, 0:512])
            DMA("sync", xtmp_b.ap[64:96, 512:1024], d_ckr, xtmp_b.r, writes=[xtmp_b.r])
            X("scalar", "activation", [xtmp_b.r], [kr_b.res[8]], out=KSQv[64:96, 4096:4608], in_=xtmp_b.ap[64:96, 512:1024],
              func=AF.Square)
            X("vector", "tensor_scalar", [xtmp_b.r, RC], [kr_b.res[8]], out=KRv[64:96, 4096:4608], in0=xtmp_b.ap[64:96, 512:1024],
              scalar1=V_GK[64:96, :], scalar2=None, op0=ALU.mult)
            A.free(stg, xtmp_b, *htmp_b, *ropet)
            A.free(sq_b, rstd_b)
            if ada_deferred:
                ada_finish(1)
                A.free(cs_b, bada_b, *mrow)
            pinned.clear()

        if stage >= 3:
            SC0 = 96.0 ** -0.5
            wqup_v, wqup_r = wload(d_wqup.rearrange("(c p) n -> p c n", p=128))
            wkvup_v, wkvup_r = wload(d_wkvup)
            KT_b = A.alloc("KT", NKEY // 2, nres=10)
            KT = bf(KT_b.ap)
            X("vector", "memset", [], KT_b.res, KT[64:128, :], 0.0)
            V_b = A.alloc("Vh", 40 * 128 // 2, nres=10)
            Vv = bf(V_b.ap).rearrange("p (t d) -> p t d", d=128)
            X("vector", "memset", [], V_b.res, Vv[:, :, 64:128], 1.0)
            QT_b = [A.alloc("QT%d" % i, 2048 // 2, nres=4) for i in range(2)]
            for qb_ in QT_b:
                X("vector", "memset", [], qb_.res, bf(qb_.ap)[64:128, :], 0.0)
            PT_b = [A.alloc("PT%d" % i, 256) for i in range(4)]
            rden_l = [A.alloc("rden%d" % i, 512) for i in range(2)]
            sqs_b = [A.alloc("sqs%d" % i, 256) for i in range(2)]
            r96_b = [A.alloc("r96_%d" % i, 512) for i in range(2)]
            rq_b = [A.alloc("ropeq%d" % i, 1024) for i in range(1)]
            rsk_b = [A.alloc("rsk%d" % i, 40, nres=10) for i in range(2)]
            lnsc_b = A.alloc("lnsc", 2)
            X("vector", "memset", [], [lnsc_b.r], lnsc_b.ap[:, 0:1], float(np.log(SC0)))
            X("vector", "tensor_tensor", [RC], [lnsc_b.r], out=lnsc_b.ap[:, 1:2], in0=V_GQ, in1=V_GK, op=ALU.mult)
            V_GQK = lnsc_b.ap[:, 1:2]
            for kb_ in range(10):
                X("vector", "tensor_copy", [kr_b.res[kb_]], [KT_b.res[kb_]], out=KT[64:96, kb_ * 512:(kb_ + 1) * 512],
                  in_=KRv[64:96, kb_ * 512:(kb_ + 1) * 512])
            attn_b = A.alloc("attnT", 2 * 2048 // 2, nres=4)
            attnT = bf(attn_b.ap).rearrange("p (h t) -> p h t", h=2)
            cnt = {"sq": 0, "r96": 0, "pt": 0, "rden": 0, "rq": 0}

            def rot(key, lst):
                i = cnt[key]
                cnt[key] = i + 1
                return lst[i % len(lst)]

            def normalize(ob, dst, rdst, nq):
                rd = rot("rden", rden_l)
                X("vector", "reciprocal", [RB[ob]], [rd.r], out=rd.ap[0:64, 0:nq], in_=PB[ob][64:128, 0:nq])
                X("vector", "tensor_tensor", [RB[ob], rd.r], [rdst], out=dst, in0=PB[ob][0:64, 0:nq], in1=rd.ap[0:64, 0:nq],
                  op=ALU.mult)

            def q_a(h, tb, stt_):
                for kc in range(2):
                    X("tensor", "matmul", [qln_b.res[tb], wqup_r], [RB[5]], PB[5][0:96, :], lhsT=wqup_v[:, kc, h * 192:h * 192 + 96],
                      rhs=qln[:, kc, blk(tb)], start=(kc == 0), stop=(kc == 1))
                if tb > 0:
                    for kc in range(2):
                        X("tensor", "matmul", [qln_b.res[tb], wqup_r], [RB[6]], PB[6][0:96, :],
                          lhsT=wqup_v[:, kc, h * 192 + 96:h * 192 + 192], rhs=qln[:, kc, blk(tb)], start=(kc == 0), stop=(kc == 1))
                sq = rot("sq", sqs_b)
                X("scalar", "activation", [RB[5]], [sq.r], out=bf(sq.ap)[0:96, :], in_=PB[5][0:96, :], func=AF.Square)
                stt_["sq"] = sq

            def q_b(h, tb, stt_):
                QB_ = QT_b[h % 2]
                QTv = bf(QB_.ap)
                rq = QB_.res[tb]
                sq = stt_["sq"]
                bq_ = nb("S", [3, 4, 7])
                X("tensor", "matmul", [sq.r, RC], [RB[bq_]], PB[bq_][0:96, :], lhsT=ones_b[0:96, 0:96], rhs=bf(sq.ap)[0:96, :],
                  start=True, stop=True)
                r96 = rot("r96", r96_b)
                rstd_from(PB[bq_][0:96, :], r96.ap[0:96, :], 1.0 / 96, slice(0, 96), RB[bq_], r96.r)
                if tb == 0:
                    X("vector", "scalar_tensor_tensor", [RB[5], r96.r, lnsc_b.r], [rq], out=QTv[0:64, blk(tb)], in0=PB[5][0:64, :],
                      scalar=V_GQK[0:64, :], in1=r96.ap[0:64, :], op0=ALU.mult, op1=ALU.mult)
                    X("vector", "scalar_tensor_tensor", [RB[5], r96.r, RC], [rq], out=QTv[64:96, blk(tb)], in0=PB[5][64:96, :],
                      scalar=V_GQ[64:96, :], in1=r96.ap[64:96, :], op0=ALU.mult, op1=ALU.mult)
                    return
                X("vector", "scalar_tensor_tensor", [RB[5], r96.r, lnsc_b.r], [rq], out=QTv[0:64, blk(tb)], in0=PB[5][0:64, :],
                  scalar=V_GQK[0:64, :], in1=r96.ap[0:64, :], op0=ALU.mult, op1=ALU.mult)
                rt = rot("rq", rq_b)
                ws = (tb - 1) * 512
                DMA("sync", rt.ap[64:96, 0:512], d_ropeq[0:32, ws:ws + 512], rt.r, writes=[rt.r])
                DMA("sync", rt.ap[64:96, 512:1024], d_ropeq[32:64, ws:ws + 512], rt.r, writes=[rt.r])
                rr = slice(64, 96)
                t1, t2 = tmpb[0], tmpb[1]
                X("vector", "scalar_tensor_tensor", [RB[5], rt.r, RC], [t1.r], out=t1.ap[rr, :], in0=PB[5][rr, :],
                  scalar=V_GQ[rr, :], in1=rt.ap[rr, 0:512], op0=ALU.mult, op1=ALU.mult)
                X("vector", "scalar_tensor_tensor", [RB[6], rt.r, RC], [t2.r], out=t2.ap[rr, :], in0=PB[6][rr, :],
                  scalar=V_GQS[rr, :], in1=rt.ap[rr, 512:1024], op0=ALU.mult, op1=ALU.mult)
                X("vector", "tensor_tensor", [t1.r, t2.r], [t1.r], out=t1.ap[rr, :], in0=t1.ap[rr, :], in1=t2.ap[rr, :], op=ALU.add)
                X("vector", "tensor_tensor", [t1.r, r96.r], [rq], out=QTv[rr, blk(tb)], in0=t1.ap[rr, :], in1=r96.ap[rr, :], op=ALU.mult)

            def k_s0(h, kb, stt_):
                ks = slice(kb * 512, (kb + 1) * 512)
                X("tensor", "matmul", [ckvn_b.res[kb], wkvup_r], [RB[5]], PB[5][0:64, :], lhsT=wkvup_v[:, h * 128:h * 128 + 64],
                  rhs=ckvnT[:, ks], start=True, stop=True)
                sq = rot("sq", sqs_b)
                X("vector", "tensor_copy", [RB[5]], [KT_b.res[kb]], out=KT[0:64, ks], in_=PB[5][0:64, :])
                X("vector", "tensor_tensor", [KT_b.res[kb]], [sq.r], out=bf(sq.ap)[0:64, :], in0=KT[0:64, ks], in1=KT[0:64, ks],
                  op=ALU.mult)
                stt_["sq"] = sq

            def k_s1(h, kb, stt_):
                sq = stt_["sq"]
                for a in range(4):
                    X("tensor", "matmul", [sq.r, RC], [RB[6]], PB[6][:, a:a + 1], lhsT=bf(sq.ap)[0:64, a * 128:(a + 1) * 128],
                      rhs=ones_b[0:64, 0:1], start=True, stop=False)
                    X("tensor", "matmul", [kr_b.res[kb], RC], [RB[6]], PB[6][:, a:a + 1],
                      lhsT=KSQv[64:96, kb * 512 + a * 128:kb * 512 + (a + 1) * 128], rhs=ones_b[64:96, 0:1], start=False, stop=True)
                rk = rsk_b[h % 2]
                X("scalar", "activation", [RB[6], RC], [rk.res[kb]], out=rk.ap[:, kb * 4:kb * 4 + 4], in_=PB[6][:, 0:4], func=AF.Ln,
                  bias=eps_t[:, 0:1], scale=1.0 / 96)
                X("scalar", "activation", [rk.res[kb], lnsc_b.r], [rk.res[kb]], out=rk.ap[:, kb * 4:kb * 4 + 4],
                  in_=rk.ap[:, kb * 4:kb * 4 + 4], func=AF.Exp, bias=lnsc_b.ap[:, 0:1], scale=-0.5)

            def k_s2(h, kb, stt_):
                pass

            def v_all(h, kb, stt_):
                for a in range(4):
                    t = kb * 4 + a
                    X("tensor", "matmul", [ckvn_b.res[kb], wkvup_r], [RB[6]], PB[6][:, a * 64:(a + 1) * 64],
                      lhsT=ckvnT[:, t * 128:(t + 1) * 128], rhs=wkvup_v[:, h * 128 + 64:h * 128 + 128], start=True, stop=True)
                X("vector", "tensor_copy", [RB[6]], [V_b.res[kb]], out=Vv[:, kb * 4:kb * 4 + 4, 0:64],
                  in_=PB[6][:, 0:256].rearrange("p (a d) -> p a d", a=4))

            items = [(h, kb) for h in range(8) for kb in (list(range(9)) + [9])]
            nitem = len(items)
            sched = [dict() for _ in range(nitem)]

            def add(n, step, fn, *a):
                sched[n].setdefault(step, []).append((fn, a))

            for n, (h, kb) in enumerate(items):
                if n + 1 >= nitem:
                    break
                h2, kb2 = items[n + 1]
                st_ = {}
                if kb == 9:
                    continue
                base = 1
                add(n, base, k_s0, h2, kb2, st_)
                add(n, base + 2, k_s1, h2, kb2, st_)
                add(n, base + 4, k_s2, h2, kb2, st_)
                add(n, base + 5, v_all, h2, kb2, st_)
                if kb == 8 and n + 2 < nitem:
                    h3, kb3 = items[n + 2]
                    st3 = {}
                    add(n, 7, k_s0, h3, kb3, st3)
                    add(n, 9, k_s1, h3, kb3, st3)
                    add(n, 11, k_s2, h3, kb3, st3)
                    add(n, 11, v_all, h3, kb3, st3)
                if 2 <= kb <= 5 and h + 1 < 8:
                    stq = {}
                    add(n, 7, q_a, h + 1, kb - 2, stq)
                    add(n, 9, q_b, h + 1, kb - 2, stq)
            for tb in range(4):
                stq = {}
                q_a(0, tb, stq)
                q_b(0, tb, stq)
            st0 = {}
            k_s0(0, 0, st0)
            k_s1(0, 0, st0)
            k_s2(0, 0, st0)
            v_all(0, 0, st0)

            def run_sched(n, step):
                for fn, a in sched[n].get(step, []):
                    fn(*a)

            woa = {}
            pend = []

            def flush(keep):
                while len(pend) > keep:
                    pt_, kc_, qb_ = pend.pop(0)
                    X("tensor", "matmul", [V_b.res[kc_ // 4], pt_.r], [RB[qb_ - 1]], PB[qb_ - 1],
                      lhsT=Vv[:, kc_, :], rhs=bf(pt_.ap), start=(kc_ == 0), stop=(kc_ == 35))

            for n, (h, kb) in enumerate(items):
                hg, hh = h // 4, h % 4
                po = (hh % 2) * 64
                QB = QT_b[h % 2]
                QTv = bf(QB.ap)
                if kb == 0 and hh == 0:
                    woa[hg] = wload(d_wout0[512 + hg * 256:512 + (hg + 1) * 256, :].rearrange("(c p) n -> p c n", p=128))
                if kb < 9:
                    step = 0
                    for c in range(4):
                        kc = kb * 4 + c
                        for qb in (1, 2, 3):
                            sb = nb("S", [3, 4, 7])
                            X("tensor", "matmul", [KT_b.res[kb], QB.res[qb]], [RB[sb]], PB[sb], lhsT=KT[:, kc * 128:(kc + 1) * 128],
                              rhs=QTv[:, blk(qb)], start=True, stop=True)
                            pt = rot("pt", PT_b)
                            X("scalar", "activation", [RB[sb], rsk_b[h % 2].res[kb]], [pt.r], out=bf(pt.ap), in_=PB[sb], func=AF.Exp,
                              scale=rsk_b[h % 2].ap[:, kc:kc + 1])
                            pend.append((pt, kc, qb))
                            flush(2)
                            run_sched(n, step)
                            step += 1
                    if kb == 8:
                        flush(0)
                        for qb in (1, 2, 3):
                            normalize(qb - 1, attnT[po:po + 64, hh // 2, blk(qb)], attn_b.res[qb], 512)
                else:
                    for s_ in range(2):
                        sb = nb("S", [3, 4, 7])
                        for c in range(2):
                            k0 = 4608 + s_ * 256 + c * 128
                            X("tensor", "matmul", [KT_b.res[9], QB.res[0]], [RB[sb]], PB[sb][:, c * 256:(c + 1) * 256],
                              lhsT=KT[:, k0:k0 + 128], rhs=QTv[:, s_ * 256:(s_ + 1) * 256], start=True, stop=True)
                        pt = rot("pt", PT_b)
                        ptv = bf(pt.ap)
                        for c in range(2):
                            X("scalar", "activation", [RB[sb], rsk_b[h % 2].res[9]], [pt.r], out=ptv[:, c * 256:(c + 1) * 256],
                              in_=PB[sb][:, c * 256:(c + 1) * 256], func=AF.Exp, scale=rsk_b[h % 2].ap[:, 36 + 2 * s_ + c:37 + 2 * s_ + c])
                        for c in range(2):
                            X("tensor", "matmul", [V_b.res[9], pt.r], [RB[0]], PB[0][:, s_ * 256:(s_ + 1) * 256],
                              lhsT=Vv[:, 36 + 2 * s_ + c, :], rhs=ptv[:, c * 256:(c + 1) * 256], start=(c == 0), stop=(c == 1))
                    normalize(0, attnT[po:po + 64, hh // 2, blk(0)], attn_b.res[0], 512)
                    if hh == 3:
                        woa_v, woa_r = woa.pop(hg)
                        for tb in range(4):
                            for m in range(8):
                                b = nb("po2", [5, 6])
                                for pr in range(2):
                                    X("tensor", "matmul", [attn_b.res[tb], woa_r], [RB[b]], PB[b], lhsT=woa_v[:, pr, m * 128:(m + 1) * 128],
                                      rhs=attnT[:, pr, blk(tb)], start=(pr == 0), stop=(pr == 1))
                                epilogue(PB[b], RB[b], 0, 2, tb, m)
            A.free(KT_b, V_b, attn_b, qln_b, ckvn_b, kr_b, lnsc_b, *rsk_b, *QT_b, *PT_b, *rden_l, *sqs_b, *r96_b, *rq_b)
        if stage < 3:
            A.free(sq_b, rstd_b)
        A.free(*tmpb)

        if stage >= 4:
            mlp(0, [(tb * 512, [RX[tb]], cond_of(tb)) for tb in range(4)])

        if dbg and stage < 5:
            ostg_d = A.alloc("ostg_dbg", 4096)
            for tb in range(4):
                store_tokens(xT[:, :, blk(tb)], RX[tb], 512, o_dbg[tb * 512:(tb + 1) * 512, :], ostg_d)
            A.free(ostg_d)

        if stage >= 5:
            hT_b = A.alloc("hT1", 8 * 2048 // 2, nres=4)
            hT = bf(hT_b.ap).rearrange("p (c t) -> p c t", c=8)
            RH = hT_b.res
            sq_b = A.alloc("sq1", 2048, nres=8)
            tmpb = [A.alloc("tmp1_%d" % i, 512) for i in range(2)]
            rstd_b = A.alloc("rstd1", 512)
            for tb in range(4):
                normmod(xT[:, :, blk(tb)], RX[tb], 512, MOD(1, 1, cond_of(tb)), MOD(1, 0, cond_of(tb)),
                        hT[:, :, blk(tb)], RH[tb], sq_b, tmpb, rstd_b)
            A.free(sq_b)
            SC1 = 0.125
            QCOL = [0, 768, 1280]
            sqp = [A.alloc("sqp%d" % i, 256) for i in range(2)]
            KTp = [A.alloc("KT1_%d" % i, 1024, nres=4) for i in range(2)]
            QTp = [A.alloc("QT1_%d" % i, 768, nres=3) for i in range(2)]
            Vw_b = A.alloc("Vw", 12 * 2 * 128 // 2, nres=3)
            Vw = bf(Vw_b.ap).rearrange("p (r h d) -> p r h d", r=12, h=2)
            Vp_b = A.alloc("Vp", 4 * 2 * 128 // 2)
            Vp = bf(Vp_b.ap).rearrange("p (r h d) -> p r h d", r=4, h=2)
            Kc_b = A.alloc("Kc", 512)
            Kc = bf(Kc_b.ap).rearrange("p (h k) -> p h k", h=2)
            X("vector", "memset", [], [Kc_b.r], Kc[64:128, :, :], 0.0)
            Vc_b = A.alloc("Vc", 4 * 2 * 128 // 2)
            Vc = bf(Vc_b.ap).rearrange("p (r h d) -> p r h d", r=4, h=2)
            T_b = [A.alloc("Ttab%d" % i, 1024) for i in range(2)]
            cm_b = A.alloc("cmask", 64)
            DMA("sync", cm_b.ap[0:64, :], d_cmask, cm_b.r, writes=[cm_b.r])
            DMA("sync", cm_b.ap[64:128, :], d_cmask, cm_b.r, writes=[cm_b.r])
            for tb_ in T_b:
                X("vector", "memset", [], [tb_.r], tb_.ap, 0.0)
            vT_l = [A.alloc("vT%d" % i, 256) for i in range(2)]
            kf_b = A.alloc("kf32", 512)
            kst_b = [A.alloc("kstage%d" % i, 512) for i in range(1)]
            E32_b = [A.alloc("E32_%d" % i, 768) for i in range(2)]
            PL_b = [A.alloc("PL%d" % i, 384) for i in range(3)]
            PT1_b = [A.alloc("PT1_%d" % i, 256) for i in range(2)]
            rden1 = [A.alloc("rden1_%d" % i, 512) for i in range(1)]
            attn1_b = A.alloc("attn1", 2 * 1536 // 2, nres=3)
            attn1 = bf(attn1_b.ap).rearrange("p (h t) -> p h t", h=2)
            X("vector", "memset", [], Vw_b.res, Vw[:, :, :, 64:128], 1.0)
            X("vector", "memset", [], [Vp_b.r], Vp[:, :, :, 64:128], 1.0)
            X("vector", "memset", [], [Vc_b.r], Vc[:, :, :, 64:128], 1.0)
            cnt1 = {}

            def rot1(key, lst):
                i = cnt1.get(key, 0)
                cnt1[key] = i + 1
                return lst[i % len(lst)]

            def normalize1(ob, dst, rdst, nq):
                rd = rot1("rden", rden1)
                X("scalar", "activation", [RB[ob]], [rd.r], out=rd.ap[0:64, 0:nq], in_=PB[ob][64:128, 0:nq], func=AF.Ln)
                X("scalar", "activation", [rd.r], [rd.r], out=rd.ap[0:64, 0:nq], in_=rd.ap[0:64, 0:nq], func=AF.Exp, scale=-1.0)
                X("vector", "tensor_tensor", [RB[ob], rd.r], [rdst], out=dst, in0=PB[ob][0:64, 0:nq], in1=rd.ap[0:64, 0:nq],
                  op=ALU.mult)

            def headnorm(bp, gvec):
                sq = rot1("sqp", sqp)
                sqv = bf(sq.ap)
                X("scalar", "activation", [RB[bp]], [sq.r], out=sqv, in_=PB[bp], func=AF.Square)
                bs = nb("prep1", [2, 3, 4, 5, 6, 7])
                X("tensor", "matmul", [sq.r, RC], [RB[bs]], PB[bs], lhsT=blk_b, rhs=sqv, start=True, stop=True)
                rb_ = rot1("rstd3", [rstd_b, tmpb[0], tmpb[1]])
                rstd_from(PB[bs], rb_.ap, 1.0 / 64, slice(0, 128), RB[bs], rb_.r)
                return rb_

            deferred = []

            def defer(fn):
                deferred.append(fn)
                while len(deferred) > 1:
                    deferred.pop(0)()

            def drain():
                while deferred:
                    deferred.pop(0)()

            def own_rows(i):
                if i < 4:
                    return i, 12
                if i > 12:
                    return 12, i + 8
                return i, i + 8

            for hh in range(2):
                X("vector", "memset", [], KTp[hh].res, bf(KTp[hh].ap)[64:128, :], 0.0)
                X("vector", "memset", [], QTp[hh].res, bf(QTp[hh].ap)[64:128, :], 0.0)
            for hh in range(2):
                DMA("gpsimd", bf(KTp[hh].ap)[64:80, 512:2048], d_pen, KTp[hh].res[1], writes=KTp[hh].res[1:4])
                DMA("gpsimd", bf(QTp[hh].ap)[64:80, 512:1536], d_qoh, QTp[hh].res[1], writes=QTp[hh].res[1:3])
            for c in range(8):
                w1v, w1r = wload(d_win1[:, c * 384:(c + 1) * 384].rearrange("(kc p) n -> p kc n", p=128))
                KTv = [bf(KTp[i].ap) for i in range(2)]
                QTv = [bf(QTp[i].ap) for i in range(2)]
                DMA("gpsimd", Kc[0:64, :, :], d_ck1[:, c * 1024:(c + 1) * 1024].rearrange("p (h k) -> p h k", h=2), Kc_b.r,
                    writes=[Kc_b.r])
                for t in range(4):
                    DMA("gpsimd", Vc[:, t, :, 0:64], d_cv1[t * 128:(t + 1) * 128, c * 128:(c + 1) * 128].rearrange("p (h d) -> p h d", h=2),
                        Vc_b.r, writes=[Vc_b.r])
                Tv = []
                for hh in range(2):
                    h = 2 * c + hh
                    tb_ = T_b[hh]
                    DMA("sync", tb_.ap[0:64, 0:960], d_rbt[h], tb_.r, writes=[tb_.r])
                    DMA("sync", tb_.ap[64:128, 64:1024], d_rbt[h], tb_.r, writes=[tb_.r])
                    X("vector", "memset", [], [tb_.r], tb_.ap[0:64, 960:1024], 0.0)
                    X("vector", "memset", [], [tb_.r], tb_.ap[64:128, 0:64], 0.0)
                    X("scalar", "activation", [tb_.r], [tb_.r], out=tb_.ap, in_=tb_.ap, func=AF.Exp)
                    t3 = tb_.ap.rearrange("p (r q) -> p r q", r=16)
                    X("vector", "tensor_tensor", [tb_.r, cm_b.r], [tb_.r], out=t3, in0=t3,
                      in1=cm_b.ap.unsqueeze(1).to_broadcast([128, 16, 64]), op=ALU.mult)
                    Tv.append(t3)
                for tb in range(4):
                    bp = nb("prep1", [2, 3, 4, 5, 6, 7])
                    for kc in range(8):
                        X("tensor", "matmul", [RH[tb], w1r], [RB[bp]], PB[bp], lhsT=w1v[:, kc, 128:256], rhs=hT[:, kc, blk(tb)],
                          start=(kc == 0), stop=(kc == 7))
                    def post_k(tb=tb, bp=bp):
                        rb_ = headnorm(bp, V_NGK)
                        for hh in range(2):
                            ps = slice(hh * 64, hh * 64 + 64)
                            X("vector", "scalar_tensor_tensor", [RB[bp], rb_.r, RC], [KTp[hh].res[tb]], out=KTv[hh][0:64, blk(tb)],
                              in0=PB[bp][ps, :], scalar=V_NGK[ps, :], in1=rb_.ap[ps, :], op0=ALU.mult, op1=ALU.mult)
                        if tb == 0:
                            X("vector", "scalar_tensor_tensor", [RB[bp], rb_.r, RC], [kf_b.r], out=kf_b.ap, in0=PB[bp],
                              scalar=V_NGK, in1=rb_.ap, op0=ALU.mult, op1=ALU.mult)
                            ks_ = rot1("kst", kst_b)
                            ksv = ks_.ap.rearrange("p (a d) -> p a d", a=4)
                            bt = nb("prep1", [2, 3, 4, 5, 6, 7])
                            for a in range(4):
                                X("tensor", "transpose", [kf_b.r, RC], [RB[bt]], out=PB[bt][:, a * 128:(a + 1) * 128],
                                  in_=kf_b.ap[:, a * 128:(a + 1) * 128], identity=ident_f)
                            X("vector", "tensor_copy", [RB[bt]], [ks_.r], out=ksv, in_=PB[bt].rearrange("p (a d) -> p a d", a=4))
                            DMA("sync", o_k1[:, c * 128:(c + 1) * 128].rearrange("(a p) d -> p a d", p=128), ksv, ks_.r, reads=[ks_.r])
                    defer(post_k)
                for qi in range(3):
                    q0 = QCOL[qi]
                    rq_ = [RH[0]] if qi == 0 else ([RH[1], RH[2]] if qi == 1 else [RH[2], RH[3]])
                    bp = nb("prep1", [2, 3, 4, 5, 6, 7])
                    for kc in range(8):
                        X("tensor", "matmul", rq_ + [w1r], [RB[bp]], PB[bp], lhsT=w1v[:, kc, 0:128], rhs=hT[:, kc, q0:q0 + 512],
                          start=(kc == 0), stop=(kc == 7))
                    def post_q(qi=qi, bp=bp):
                        rb_ = headnorm(bp, V_NGQ)
                        for hh in range(2):
                            ps = slice(hh * 64, hh * 64 + 64)
                            X("vector", "scalar_tensor_tensor", [RB[bp], rb_.r, RC], [QTp[hh].res[qi]],
                              out=QTv[hh][0:64, qi * 512:(qi + 1) * 512], in0=PB[bp][ps, :], scalar=V_NGQ[ps, :], in1=rb_.ap[ps, :],
                              op0=ALU.mult, op1=ALU.mult)
                    defer(post_q)
                for tb in range(4):
                    bp = nb("prep1", [2, 3, 4, 5, 6, 7])
                    for kc in range(8):
                        X("tensor", "matmul", [RH[tb], w1r], [RB[bp]], PB[bp], lhsT=w1v[:, kc, 256:384], rhs=hT[:, kc, blk(tb)],
                          start=(kc == 0), stop=(kc == 7))

                    def post_v(tb=tb, bp=bp):
                        vT_b = rot1("vT", vT_l)
                        vTv = bf(vT_b.ap)
                        X("scalar", "copy", [RB[bp]], [vT_b.r], out=vTv, in_=PB[bp])
                        bt = nb("prep1", [2, 3, 4, 5, 6, 7])
                        ptb = PB[bt].bitcast(BF16)
                        if tb == 0:
                            X("vector", "tensor_copy", [RB[bp]], [kf_b.r], out=kf_b.ap, in_=PB[bp])
                            for a in range(4):
                                X("tensor", "transpose", [vT_b.r, RC], [RB[bt]], out=ptb[:, a * 128:(a + 1) * 128],
                                  in_=vTv[:, a * 128:(a + 1) * 128], identity=ident_b)
                            X("vector", "tensor_copy", [RB[bt]], [Vp_b.r], out=Vp[:, :, :, 0:64],
                              in_=ptb[:, 0:512].rearrange("p (a h d) -> p a h d", a=4, h=2))
                            ks_ = rot1("kst", kst_b)
                            ksv = ks_.ap.rearrange("p (a d) -> p a d", a=4)
                            bt2 = nb("prep1", [2, 3, 4, 5, 6, 7])
                            for a in range(4):
                                X("tensor", "transpose", [kf_b.r, RC], [RB[bt2]], out=PB[bt2][:, a * 128:(a + 1) * 128],
                                  in_=kf_b.ap[:, a * 128:(a + 1) * 128], identity=ident_f)
                            X("vector", "tensor_copy", [RB[bt2]], [ks_.r], out=ksv, in_=PB[bt2].rearrange("p (a d) -> p a d", a=4))
                            DMA("sync", o_v1[:, c * 128:(c + 1) * 128].rearrange("(a p) d -> p a d", p=128), ksv, ks_.r, reads=[ks_.r])
                        else:
                            for a in range(4):
                                X("tensor", "transpose", [vT_b.r, RC], [RB[bt]], out=ptb[:, a * 128:(a + 1) * 128],
                                  in_=vTv[:, a * 128:(a + 1) * 128], identity=ident_b)
                            X("vector", "tensor_copy", [RB[bt]], [Vw_b.res[tb - 1]], out=Vw[:, (tb - 1) * 4:tb * 4, :, 0:64],
                              in_=ptb[:, 0:512].rearrange("p (a h d) -> p a h d", a=4, h=2))
                    defer(post_v)
                drain()
                for hh in range(2):
                    hidx = (c % 2) * 2 + hh
                    K_, Q_ = KTv[hh], QTv[hh]
                    rK, rQ = KTp[hh].res, QTp[hh].res
                    po1 = (hidx % 2) * 64
                    obp = nb("O1", [0, 1])
                    obs = [nb("O1", [0, 1]), nb("O1", [0, 1])]
                    cp = []

                    def cflush(keep):
                        while len(cp) > keep:
                            fn = cp.pop(0)
                            fn()
                    for s_ in range(2):
                        sb = nb("SC", [2, 3, 4, 5, 6, 7])
                        for cc_ in range(2):
                            k0 = s_ * 256 + cc_ * 128
                            X("tensor", "matmul", [rK[0], rQ[0]], [RB[sb]], PB[sb][:, cc_ * 256:(cc_ + 1) * 256], lhsT=K_[:, k0:k0 + 128],
                              rhs=Q_[:, s_ * 256:(s_ + 1) * 256], start=True, stop=True)
                        pt = rot1("pl", PL_b)
                        ptv = bf(pt.ap)[:, 0:512]
                        X("scalar", "activation", [RB[sb]], [pt.r], out=ptv, in_=PB[sb], func=AF.Exp, scale=SC1)

                        def pv_p(s_=s_, pt=pt, ptv=ptv):
                            for cc_ in range(2):
                                X("tensor", "matmul", [Vp_b.r, pt.r], [RB[obp]], PB[obp][:, s_ * 256:(s_ + 1) * 256],
                                  lhsT=Vp[:, 2 * s_ + cc_, hh, :], rhs=ptv[:, cc_ * 256:(cc_ + 1) * 256], start=(cc_ == 0), stop=(cc_ == 1))
                        cp.append(pv_p)
                        cflush(2)
                    cflush(0)
                    normalize1(obp, attn1[po1:po1 + 64, hidx // 2, 0:512], attn1_b.res[0], 512)
                    obs = [obp, obs[0]] if False else obs
                    for qb in range(2):
                        ob = obs[qb]
                        qs = slice(512 + qb * 512, 1024 + qb * 512)
                        for c4 in range(4):
                            sb = nb("SC", [2, 3, 4, 5, 6, 7])
                            X("tensor", "matmul", [Kc_b.r, rQ[1 + qb]], [RB[sb]], PB[sb], lhsT=Kc[:, hh, c4 * 128:(c4 + 1) * 128],
                              rhs=Q_[:, qs], start=True, stop=True)
                            pt = rot1("pl", PL_b)
                            ptv = bf(pt.ap)[:, 0:512]
                            X("scalar", "activation", [RB[sb]], [pt.r], out=ptv, in_=PB[sb], func=AF.Exp, scale=SC1)

                            def pv_c(ob=ob, c4=c4, pt=pt, ptv=ptv):
                                X("tensor", "matmul", [Vc_b.r, pt.r], [RB[ob]], PB[ob], lhsT=Vc[:, c4, hh, :], rhs=ptv,
                                  start=(c4 == 0), stop=False)
                            cp.append(pv_c)
                            cflush(2)
                    cflush(0)
                    pend = []

                    def rows_of(wr):
                        if wr > 22:
                            return None
                        return (0 if wr <= 11 else wr - 7), (min(15, wr) if wr < 12 else 15)
                    chunks = []
                    for m_ in range(12):
                        ra, rb_ = rows_of(2 * m_), rows_of(2 * m_ + 1)
                        ilo_c = min(ra[0], rb_[0]) if rb_ else ra[0]
                        ihi_c = max(ra[1], rb_[1]) if rb_ else ra[1]
                        chunks.append((m_, ilo_c, ihi_c))
                    last_m = {}
                    for (m_, ilo_c, ihi_c) in chunks:
                        for qb_ in range(2):
                            if max(ilo_c, qb_ * 8) <= min(ihi_c, qb_ * 8 + 7):
                                last_m[qb_] = m_

                    def flush1(keep):
                        while len(pend) > keep:
                            pl_, plv_, m_, ilo_, ihi_ = pend.pop(0)
                            for qb_ in range(2):
                                a_, b_ = max(ilo_, qb_ * 8), min(ihi_, qb_ * 8 + 7)
                                if a_ > b_:
                                    continue
                                X("tensor", "matmul", [Vw_b.res[m_ // 4], pl_.r], [RB[obs[qb_]]],
                                  PB[obs[qb_]][:, (a_ - qb_ * 8) * 64:(b_ - qb_ * 8 + 1) * 64], lhsT=Vw[:, m_, hh, :],
                                  rhs=plv_[:, (a_ - ilo_) * 64:(b_ - ilo_ + 1) * 64], start=False, stop=(m_ == last_m[qb_]))
                    for (m_, ilo, ihi) in chunks:
                        nq = ihi - ilo + 1
                        jj0 = ilo - 2 * m_ + 11
                        pair = nb("SL", [2, 4, 6])
                        sl = psum_t[:, pair * 512:pair * 512 + 1024]
                        kcol = 512 + m_ * 128
                        for part in range(2):
                            a_, b_ = part * 8, min(nq, part * 8 + 8)
                            if a_ >= b_:
                                continue
                            qc0 = 512 + (ilo + a_) * 64
                            rqs = sorted(set([1 + (ilo + a_) // 8, 1 + (ilo + b_ - 1) // 8]))
                            X("tensor", "matmul", [rK[1 + m_ // 4]] + [rQ[x_] for x_ in rqs], [RB[pair + part]],
                              sl[:, a_ * 64:b_ * 64], lhsT=K_[:, kcol:kcol + 128], rhs=Q_[:, qc0:qc0 + (b_ - a_) * 64],
                              start=True, stop=True)
                        e32 = rot1("e32", E32_b)
                        rbs = [RB[pair]] + ([RB[pair + 1]] if nq > 8 else [])
                        X("scalar", "activation", rbs, [e32.r], out=e32.ap[:, 0:nq * 64], in_=sl[:, 0:nq * 64], func=AF.Exp,
                          scale=SC1)
                        pl = rot1("pl", PL_b)
                        plv2 = bf(pl.ap)[:, 0:nq * 64]
                        X("vector", "tensor_tensor", [e32.r, T_b[hh].r], [pl.r], out=plv2.rearrange("p (r q) -> p r q", q=64),
                          in0=e32.ap[:, 0:nq * 64].rearrange("p (r q) -> p r q", q=64), in1=Tv[hh][:, jj0:jj0 + nq, :],
                          op=ALU.mult)
                        pend.append((pl, plv2, m_, ilo, ihi))
                        flush1(2)
                    flush1(0)
                    for qb in range(2):
                        qs = slice(512 + qb * 512, 1024 + qb * 512)
                        normalize1(obs[qb], attn1[po1:po1 + 64, hidx // 2, qs], attn1_b.res[1 + qb], 512)
                if c % 2 == 1:
                    grp = c // 2
                    wo1v, wo1r = wload(d_wout1[grp * 256:(grp + 1) * 256, :].rearrange("(c p) n -> p c n", p=128))
                    for qi in range(3):
                        q0 = QCOL[qi]
                        rxs = [RX[0]] if qi == 0 else ([RX[1], RX[2]] if qi == 1 else [RX[2], RX[3]])
                        cd = 0 if qi == 0 else 1
                        for m in range(8):
                            b = nb("prep1", [2, 3, 4, 5, 6, 7])
                            for hq in range(2):
                                X("tensor", "matmul", [attn1_b.res[qi], wo1r], [RB[b]], PB[b], lhsT=wo1v[:, hq, m * 128:(m + 1) * 128],
                                  rhs=attn1[:, hq, qi * 512:(qi + 1) * 512], start=(hq == 0), stop=(hq == 3 - 2))
                            X("vector", "scalar_tensor_tensor", [RB[b], RM] + rxs, rxs, out=xT[:, m, q0:q0 + 512], in0=PB[b],
                              scalar=MOD(1, 2, cd)[:, m:m + 1], in1=xT[:, m, q0:q0 + 512], op0=ALU.mult, op1=ALU.add)
            A.free(hT_b, rstd_b, Vw_b, Vp_b, Kc_b, Vc_b, cm_b, kf_b, attn1_b, *vT_l, *tmpb, *sqp, *KTp, *QTp, *T_b, *kst_b,
                   *E32_b, *PL_b, *PT1_b, *rden1)

        if dbg and stage >= 5:
            ostg_d = A.alloc("ostg_dbg", 4096)
            for tb in range(4):
                store_tokens(xT[:, :, blk(tb)], RX[tb], 512, o_dbg[tb * 512:(tb + 1) * 512, :], ostg_d)
            A.free(ostg_d)

        if stage >= 6:
            mlp(1, [(0, [RX[0]], 0), (768, [RX[1], RX[2]], 1), (1280, [RX[2], RX[3]], 1)])
            ostg = A.alloc("ostg", 4096)
            store_tokens(xT[:, :, 0:512], RX[0], 512, o_yp, ostg)
            store_tokens(xT[:, :, 768:1280], RX[1], 512, o_ys[0:512, :], ostg, rextra=[RX[2]])
            store_tokens(xT[:, :, 1280:1792], RX[2], 512, o_ys[512:1024, :], ostg, rextra=[RX[3]])
            A.free(ostg)

        P.emit(nc, st)
    return nc


def _pvec(v, n):
    return np.ascontiguousarray(np.asarray(v, np.float32).reshape(n, 128).T)


_SWAP = np.array([(j + 8) if (j % 16) < 8 else (j - 8) for j in range(32)])
_SGN = np.array([-1.0 if (j % 16) < 8 else 1.0 for j in range(32)])


def _rope_tables(rows, cols):
    f = THETA ** (-np.arange(0, 16, 2, dtype=np.float64) / 16.0)
    n = len(rows)
    ang = np.zeros((32, n))
    for j in range(32):
        pos = rows if j < 16 else cols
        ang[j] = pos.astype(np.float32).astype(np.float64) * np.float32(f[j % 8]).astype(np.float64)
    ang = ang.astype(np.float32).astype(np.float64)
    return np.concatenate([np.cos(ang), _SGN[:, None] * np.sin(ang)], axis=0).astype(np.float32)


def _prep_inputs(inp):
    f32 = np.float32
    g = {k: np.asarray(v) for k, v in inp.items()}
    shared = {}
    shared["w_ada0"] = g["w_ada_l0"]
    shared["w_ada1"] = g["w_ada_l1"]
    shared["b_ada0"] = _pvec(g["b_ada_l0"], 48)
    shared["b_ada1"] = _pvec(g["b_ada_l1"], 48)
    shared["w_mlp1_0"] = g["w_mlp1_l0"]
    shared["w_mlp1_1"] = g["w_mlp1_l1"]
    shared["w_mlp2_0"] = g["w_mlp2_l0"]
    shared["w_mlp2_1"] = g["w_mlp2_l1"]
    vec = np.zeros((128, 64), f32)
    vec[:, 0:8] = _pvec(g["norm_mix_l0"], 8)
    vec[:, 8:16] = _pvec(g["norm_mlp_l0"], 8)
    vec[:, 16:24] = _pvec(g["norm_mix_l1"], 8)
    vec[:, 24:32] = _pvec(g["norm_mlp_l1"], 8)
    vec[:, 32:34] = _pvec(g["q_lora_norm_l0"], 2)
    vec[:, 34] = g["kv_lora_norm_l0"]
    gq, gk = g["mla_q_norm_l0"], g["mla_k_norm_l0"]
    vec[0:96, 35] = gq
    vec[64:96, 36] = gq[64 + _SWAP]
    vec[0:96, 37] = gk
    vec[64:96, 38] = gk[64 + _SWAP]
    vec[:, 39:43] = _pvec(g["pool_scale_l0"], 4)
    vec[:, 43] = np.tile(g["na_q_norm_l1"], 2)
    vec[:, 44] = np.tile(g["na_k_norm_l1"], 2)
    shared["vecs"] = vec
    w_in0 = g["w_in_l0"]
    shared["w_in0_pool"] = np.ascontiguousarray(w_in0[:, 0:512])
    shared["w_in0_q"] = np.ascontiguousarray(w_in0[:, 512:768])
    kr = w_in0[:, 896:928]
    dummy = w_in0[:, 768:832]
    shared["w_in0_kv"] = np.ascontiguousarray(np.concatenate([w_in0[:, 768:896], dummy, kr, dummy, kr[:, _SWAP]], axis=1))
    wq = g["w_q_up_l0"].reshape(256, 8, 96)
    wq_aug = np.concatenate([wq, wq[:, :, 0:64], wq[:, :, 64 + _SWAP]], axis=2)
    shared["w_qup_aug"] = np.ascontiguousarray(wq_aug.reshape(256, 1536))
    shared["w_kvup"] = g["w_kv_up_l0"]
    shared["w_pool"] = np.ascontiguousarray(g["w_pool_l0"].transpose(1, 0, 2).reshape(128, 512))
    shared["w_out0"] = g["w_out_l0"]
    t = np.arange(4096)
    shared["rope_k"] = _rope_tables(t // 64, t % 64)
    w1 = g["w_in_l1"]
    pieces = []
    for c in range(8):
        pieces += [w1[:, c * 128:(c + 1) * 128], w1[:, 1024 + c * 128:1024 + (c + 1) * 128],
                   w1[:, 2048 + c * 128:2048 + (c + 1) * 128]]
    shared["w_in1"] = np.ascontiguousarray(np.concatenate(pieces, axis=1))
    shared["w_out1"] = g["w_out_l1"]
    rb = g["rel_bias_l1"]
    kc = np.arange(64)
    dc = np.clip(kc[:, None] - kc[None, :] + 15, 0, 30)
    rbt = rb[:, ::-1, :][:, :, dc]
    shared["rbt"] = np.ascontiguousarray(rbt.transpose(0, 2, 1, 3).reshape(16, 64, 960))
    cs = np.clip(kc - 8, 0, 48)
    cm = (kc[:, None] >= cs[None, :]) & (kc[:, None] < cs[None, :] + 16)
    shared["cmask"] = cm.astype(f32)
    maps = []
    xs = g["x_sample"]
    for core in range(8):
        b, q = core // 4, core % 4
        m = dict(shared)
        m["xp"] = np.ascontiguousarray(g["x_prompt"][2 * core:2 * core + 2].reshape(512, D))
        r0 = 16 * q - 4
        xw = np.zeros((24, 64, D), f32)
        xpre = np.zeros((128, D), f32)
        valid = np.zeros(25, f32)
        xg = xs[b].reshape(64, 64, D)
        for wr in range(24):
            if 0 <= r0 + wr < 64:
                xw[wr] = xg[r0 + wr]
                valid[1 + wr] = 1.0
        if 0 <= r0 - 1 < 64:
            xpre[0:64] = xg[r0 - 1]
            valid[0] = 1.0
        m["xw"] = xw.reshape(1536, D)
        m["xpre"] = xpre
        m["xb"] = np.ascontiguousarray(xs[b])
        cond = np.stack([g["c_ctx"], g["c"][b]], axis=1)
        m["condT"] = np.ascontiguousarray(cond.reshape(8, 128, 2).transpose(1, 0, 2).reshape(128, 16))
        wt = np.arange(1536)
        m["rope_q"] = _rope_tables(r0 + wt // 64, wt % 64)
        m["cache_ckvT"] = np.ascontiguousarray(g["cache_l0_mla_ckv"][b].T)
        m["cache_kropeT"] = np.ascontiguousarray(g["cache_l0_mla_krope"][b].T)
        m["pool_mask"] = np.ascontiguousarray(np.broadcast_to(np.repeat(valid, 64)[None, :], (128, 1600))).astype(f32)
        pinv = np.zeros((4, 4, 8), f32)
        for gi, w in enumerate((2, 4, 8, 16)):
            for bd, (S, first, exists) in enumerate([(256, True, True), (256, False, True),
                                                    (4096, True, q == 0), (4096, False, q == 3)]):
                for j in range(8):
                    tt = j if first else S - 8 + j
                    if exists:
                        lo = np.clip(tt - w // 2, 0, S)
                        hi = np.clip(tt - w // 2 + w, 0, S)
                        pinv[gi, bd, j] = 1.0 / float(hi - lo)
                    else:
                        pinv[gi, bd, j] = 1.0 / w
        m["pool_inv"] = np.ascontiguousarray(np.broadcast_to(pinv.reshape(1, 128), (128, 128))).astype(f32)
        pen = np.full((16, 24), NEGBIG, f32)
        for i in range(16):
            gr = 16 * q + i
            rs = int(np.clip(gr - 4, 0, 56))
            for kr_ in range(rs, rs + 8):
                pen[i, kr_ - r0] = 0.0
        m["pen"] = np.ascontiguousarray(np.repeat(pen, 64, axis=1))
        m["qoh"] = np.ascontiguousarray(np.repeat(np.eye(16, dtype=f32), 64, axis=1))
        m["cache_k1T"] = np.ascontiguousarray(g["cache_l1_na_k"][b].transpose(2, 1, 0).reshape(64, 16 * 512))
        m["cache_v1"] = np.ascontiguousarray(g["cache_l1_na_v"][b].reshape(512, 1024))
        maps.append(m)
    return maps


_NC_CACHE = {}
_BUILD_ARGS = {}


def kernel(**inputs):
    maps = _prep_inputs(inputs)
    if "nc" not in _NC_CACHE:
        _NC_CACHE["nc"] = build_program(**_BUILD_ARGS)
    nc = _NC_CACHE["nc"]
    res = run_bass_kernel_spmd(nc, maps, core_ids=list(range(8)))
    R = res.results
    yp = np.concatenate([R[c]["yp"].reshape(2, 256, D) for c in range(8)], axis=0)
    ys = np.stack([np.concatenate([R[b * 4 + q]["ys"] for q in range(4)], axis=0) for b in range(2)], axis=0)
    ckv = np.concatenate([R[c]["ckv_new"].reshape(2, 256, 128) for c in range(8)], axis=0)
    kr = np.concatenate([R[c]["krope_new"].reshape(2, 256, 32) for c in range(8)], axis=0)
    k1 = np.concatenate([R[c]["k_new"].reshape(2, 256, 16, 64) for c in range(8)], axis=0)
    v1 = np.concatenate([R[c]["v_new"].reshape(2, 256, 16, 64) for c in range(8)], axis=0)
    return (yp.astype(np.float32), ys.astype(np.float32), ckv.astype(np.float32), kr.astype(np.float32),
            k1.astype(np.float32), v1.astype(np.float32))
```

```python
import numpy as np
from contextlib import ExitStack
import concourse.bass as bass
import concourse.mybir as mybir
from concourse.bass_utils import run_bass_kernel_spmd

F32 = mybir.dt.float32
BF16 = mybir.dt.bfloat16
AF = mybir.ActivationFunctionType
ALU = mybir.AluOpType

ENGS = ("tensor", "vector", "scalar", "gpsimd", "sync")
D = 1024
NCH = 8
EPS = 1e-6
THETA = 10000.0
NEGBIG = -30000.0


class Res:
    __slots__ = ("name", "last_w", "readers", "dsem_idx", "dcount", "excl")

    def __init__(self, name, excl=False):
        self.name = name
        self.excl = excl
        self.last_w = None
        self.readers = []
        self.dsem_idx = None
        self.dcount = 0

    def pending(self):
        r = list(self.readers)
        if self.last_w is not None:
            r.append(self.last_w)
        return r


class Op:
    __slots__ = ("eng", "fn", "deps", "signal", "sigval", "is_dma", "dsem_idx", "dval")

    def __init__(self, eng, fn, is_dma=False):
        self.eng = eng
        self.fn = fn
        self.deps = []
        self.signal = False
        self.sigval = 0
        self.is_dma = is_dma
        self.dsem_idx = None
        self.dval = 0


class Prog:
    def __init__(self):
        self.ops = []
        self.n_dsem = 0

    def _track(self, op, reads, writes):
        for r in reads:
            if r.last_w is not None:
                op.deps.append(r.last_w)
            if r.excl:
                for rd in r.readers:
                    if rd.eng != op.eng:
                        op.deps.append(rd)
            r.readers.append(op)
        for w in writes:
            if w.last_w is not None:
                op.deps.append(w.last_w)
            for rd in w.readers:
                if rd is not op:
                    op.deps.append(rd)
            w.last_w = op
            w.readers = []

    def op(self, eng, fn, reads=(), writes=()):
        o = Op(eng, fn)
        self._track(o, reads, writes)
        self.ops.append(o)
        return o

    def dma(self, eng, fn, res, reads=(), writes=()):
        o = Op(eng, fn, is_dma=True)
        if res.dsem_idx is None:
            res.dsem_idx = self.n_dsem
            self.n_dsem += 1
        res.dcount += 1
        o.dsem_idx = res.dsem_idx
        o.dval = 16 * res.dcount
        self._track(o, reads, writes)
        self.ops.append(o)
        return o

    def emit(self, nc, stack, final_eng="sync"):
        fin = Op(final_eng, None)
        last_by_sem = {}
        for o in self.ops:
            if o.is_dma:
                last_by_sem[o.dsem_idx] = o
        fin.deps = list(last_by_sem.values())
        ops = self.ops + [fin]
        for o in ops:
            for d in o.deps:
                if not d.is_dma:
                    if d.eng == "tensor" and o.eng == "tensor" and not o.is_dma:
                        continue
                    d.signal = True
        cnt = {e: 0 for e in ENGS}
        for o in ops:
            if o.signal:
                cnt[o.eng] += 1
                o.sigval = cnt[o.eng]
        esem = {e: stack.enter_context(nc.semaphore("es_" + e)) for e in ENGS}
        dsem = [stack.enter_context(nc.semaphore("ds_%d" % i)) for i in range(self.n_dsem)]
        per_eng = {e: [o for o in ops if o.eng == e] for e in ENGS}
        block = stack.enter_context(nc.Block())

        def run(e, engobj):
            waited = {}
            for o in per_eng[e]:
                need = {}
                for d in o.deps:
                    if d.is_dma:
                        key = ("d", d.dsem_idx)
                        val = d.dval
                    else:
                        if d.eng == "tensor" and e == "tensor" and not o.is_dma:
                            continue
                        key = ("e", d.eng)
                        val = d.sigval
                    if val > need.get(key, 0):
                        need[key] = val
                for key, val in need.items():
                    if waited.get(key, 0) >= val:
                        continue
                    waited[key] = val
                    s = dsem[key[1]] if key[0] == "d" else esem[key[1]]
                    engobj.wait_ge(s, val)
                if o.fn is None:
                    continue
                inst = o.fn(engobj)
                if o.is_dma:
                    inst.then_inc(dsem[o.dsem_idx], 16)
                elif o.signal:
                    inst.then_inc(esem[e], 1)

        @block.tensor
        def _(eng):
            run("tensor", eng)

        @block.vector
        def _(eng):
            run("vector", eng)

        @block.scalar
        def _(eng):
            run("scalar", eng)

        @block.gpsimd
        def _(eng):
            run("gpsimd", eng)

        @block.sync
        def _(eng):
            run("sync", eng)


class Buf:
    def __init__(self, ap, res, c0, c1):
        self.ap = ap
        self.res = res
        self.c0 = c0
        self.c1 = c1

    @property
    def r(self):
        return self.res[0]


class Arena:
    def __init__(self, ap, ncols):
        self.ap = ap
        self.ncols = ncols
        self.live = []
        self.freed = []

    def alloc(self, name, ncols, nres=1):
        ivs = sorted((c0, c1) for c0, c1, _ in self.live)
        pos = 0
        for c0, c1 in ivs:
            if c0 - pos >= ncols:
                break
            pos = max(pos, c1)
        if pos + ncols > self.ncols:
            lay = sorted((c0, c1, b.res[0].name) for c0, c1, b in self.live)
            raise AssertionError("SBUF arena overflow allocating %s (%d cols at %d); live=%s" % (name, ncols, pos, lay))
        res = [Res("%s_%d" % (name, i)) for i in range(nres)]
        keep = []
        for c0, c1, ops in self.freed:
            if c0 < pos + ncols and pos < c1:
                for r in res:
                    r.readers.extend(ops)
                if c0 < pos:
                    keep.append((c0, pos, ops))
                if c1 > pos + ncols:
                    keep.append((pos + ncols, c1, ops))
            else:
                keep.append((c0, c1, ops))
        self.freed = keep
        b = Buf(self.ap[:, pos:pos + ncols], res, pos, pos + ncols)
        self.live.append((pos, pos + ncols, b))
        return b

    def free(self, *bufs):
        for b in bufs:
            self.live = [t for t in self.live if t[2] is not b]
            ops = []
            for r in b.res:
                ops.extend(r.pending())
            self.freed.append((b.c0, b.c1, ops))


def build_program(stage=99, dbg=False, mini=None, skip=()):
    nc = bass.Bass("TRN2", target_bir_lowering=False)
    P = Prog()

    def din(name, shape, dt=F32):
        if mini is not None and name not in mini:
            shape = [1, 1]
        return nc.dram_tensor(name, list(shape), dt, kind="ExternalInput").ap()

    def dout(name, shape):
        return nc.dram_tensor(name, list(shape), F32, kind="ExternalOutput").ap()

    d_xp = din("xp", [512, D])
    d_xw = din("xw", [1536, D])
    d_xpre = din("xpre", [128, D])
    d_xb = din("xb", [4096, D])
    d_cond = din("condT", [128, 16])
    d_wada = [din("w_ada0", [D, 6 * D]), din("w_ada1", [D, 6 * D])]
    d_bada = [din("b_ada0", [128, 48]), din("b_ada1", [128, 48])]
    d_vec = din("vecs", [128, 64])
    d_win0_pool = din("w_in0_pool", [D, 512])
    d_win0_q = din("w_in0_q", [D, 256])
    d_win0_kv = din("w_in0_kv", [D, 320])
    d_wqup = din("w_qup_aug", [256, 1536])
    d_wkvup = din("w_kvup", [128, 1024])
    d_wpool = din("w_pool", [128, 512])
    d_wout0 = din("w_out0", [D, D])
    d_mlp1 = [din("w_mlp1_0", [D, 4 * D]), din("w_mlp1_1", [D, 4 * D])]
    d_mlp2 = [din("w_mlp2_0", [4 * D, D]), din("w_mlp2_1", [4 * D, D])]
    d_ropek = din("rope_k", [64, 4096])
    d_ropeq = din("rope_q", [64, 1536])
    d_cckv = din("cache_ckvT", [128, 512])
    d_ckr = din("cache_kropeT", [32, 512])
    d_pmask = din("pool_mask", [128, 1600])
    d_pinv = din("pool_inv", [128, 128])
    d_win1 = din("w_in1", [D, 8 * 384])
    d_wout1 = din("w_out1", [D, D])
    d_rbt = din("rbt", [16, 64, 960])
    d_cmask = din("cmask", [64, 64])
    d_pen = din("pen", [16, 1536])
    d_qoh = din("qoh", [16, 1024])
    d_ck1 = din("cache_k1T", [64, 16 * 512])
    d_cv1 = din("cache_v1", [512, 1024])

    o_yp = dout("yp", [512, D])
    o_ys = dout("ys", [1024, D])
    o_ckv = dout("ckv_new", [512, 128])
    o_kr = dout("krope_new", [512, 32])
    o_k1 = dout("k_new", [512, D])
    o_v1 = dout("v_new", [512, D])
    o_dbg = dout("dbg", [2048, D]) if dbg else None

    with ExitStack() as st:
        NCOLS = 53000
        arena_t = st.enter_context(nc.sbuf_tensor("arena", [128, NCOLS], F32))
        psum_t = st.enter_context(nc.psum_tensor("psum", [128, 8 * 512], F32))
        A = Arena(arena_t, NCOLS)
        PB = [psum_t[:, i * 512:(i + 1) * 512] for i in range(8)]
        RB = [Res("bank%d" % i, excl=True) for i in range(8)]
        bank_rr = {}

        def nb(group, banks):
            i = bank_rr.get(group, 0)
            bank_rr[group] = i + 1
            return banks[i % len(banks)]

        def X(eng, meth, reads, writes, *a, **kw):
            def fn(e):
                try:
                    return getattr(e, meth)(*a, **kw)
                except Exception as ex:
                    desc = [getattr(v, "shape", v) for v in a] + ["%s=%s" % (k, getattr(v, "shape", v)) for k, v in kw.items()]
                    raise RuntimeError("op %s.%s failed: %s | %s" % (eng, meth, desc, ex))
            return P.op(eng, fn, reads, writes)

        def DMA(eng, out, in_, res, reads=(), writes=()):
            return P.dma(eng, lambda e: e.dma_start(out=out, in_=in_), res, reads, writes)

        def bf(buf_ap):
            return buf_ap.bitcast(BF16)

        cst = A.alloc("consts", 128 + 64 + 64 + 64 + 128 + 1 + 64 + 16 + 48 * 2 + 48 * 2 + 2 * 6 * 16)
        cc = [0]

        def ccarve(n):
            a = cst.ap[:, cc[0]:cc[0] + n]
            cc[0] += n
            return a
        ident_f = ccarve(128)
        ident_b = bf(ccarve(64))
        ones_b = bf(ccarve(64))
        blk_b = bf(ccarve(64))
        ones_f = ccarve(128)
        eps_t = ccarve(1)
        vec = ccarve(64)
        cond_f = ccarve(16)
        RC = cst.r
        X("gpsimd", "memset", [], [RC], ident_f, 1.0)
        X("gpsimd", "affine_select", [RC], [RC], out=ident_f, in_=ident_f, pattern=[[-1, 128]],
          compare_op=ALU.is_equal, fill=0.0, base=0, channel_multiplier=1)
        X("gpsimd", "memset", [], [RC], ones_f, 1.0)
        X("gpsimd", "memset", [], [RC], eps_t, EPS)
        X("vector", "tensor_copy", [RC], [RC], out=ident_b, in_=ident_f)
        X("vector", "tensor_copy", [RC], [RC], out=ones_b, in_=ones_f)
        X("gpsimd", "memset", [], [RC], blk_b, 0.0)
        X("gpsimd", "memset", [], [RC], blk_b[0:64, 0:64], 1.0)
        X("gpsimd", "memset", [], [RC], blk_b[64:128, 64:128], 1.0)
        DMA("sync", vec, d_vec, RC, writes=[RC])
        DMA("sync", cond_f, d_cond, RC, writes=[RC])
        V_NMIX = [vec[:, 0:8], vec[:, 16:24]]
        V_NMLP = [vec[:, 8:16], vec[:, 24:32]]
        V_QL = vec[:, 32:34]
        V_KVL = vec[:, 34:35]
        V_GQ, V_GQS, V_GK, V_GKS = vec[:, 35:36], vec[:, 36:37], vec[:, 37:38], vec[:, 38:39]
        V_PSC = vec[:, 39:43]
        V_NGQ, V_NGK = vec[:, 43:44], vec[:, 44:45]

        if stage == -3:
            P.emit(nc, st)
            return nc
        NSLOT = 4
        slots = [A.alloc("wslot%d" % i, 2048) for i in range(NSLOT)]
        slot_rr = [0]

        pinned = set()

        def wload(dram_ap, shape_str=None, parts=128, **kw):
            while (slot_rr[0] % NSLOT) in pinned:
                slot_rr[0] += 1
            wload.last = slot_rr[0] % NSLOT
            s = slots[slot_rr[0] % NSLOT]
            slot_rr[0] += 1
            shp = list(dram_ap.shape)
            n = int(np.prod(shp[1:]))
            assert n <= 4096 and shp[0] == parts
            flat = bf(s.ap)[0:parts, 0:n]
            if len(shp) == 3:
                view = flat.rearrange("p (a b) -> p a b", a=shp[1])
            else:
                view = flat
            DMA("gpsimd", view, dram_ap, s.r, writes=[s.r])
            return view, s.r

        mods_b = A.alloc("mods", 2 * 6 * 16)
        RM = mods_b.r
        xT_b = A.alloc("xT", 8 * 2048, nres=4)
        xT = xT_b.ap.rearrange("p (c t) -> p c t", c=8)
        RX = xT_b.res
        hT_b = A.alloc("hT", 8 * 2048 // 2, nres=4)
        hT = bf(hT_b.ap).rearrange("p (c t) -> p c t", c=8)
        RH = hT_b.res

        def blk(tb):
            return slice(tb * 512, (tb + 1) * 512)

        def rstd_from(ps_ap, out_ap, inv_n, rows, rps, rout):
            X("scalar", "activation", [rps, RC], [rout], out=out_ap, in_=ps_ap, func=AF.Ln,
              bias=eps_t[rows, 0:1], scale=inv_n)
            X("scalar", "activation", [rout], [rout], out=out_ap, in_=out_ap, func=AF.Exp, scale=-0.5)

        def load_xT(dram_ap, n, dst, rdst, stg, eng_alt=0):
            nt = max(1, n // 128)
            pp = min(n, 128)
            sv = stg.ap.rearrange("p (a d) -> p a d", a=4)
            if n >= 128:
                DMA("sync", sv[:, 0:nt, :], dram_ap.rearrange("(a p) d -> p a d", p=128), stg.r, writes=[stg.r])
            else:
                DMA("sync", sv[0:pp, 0, :], dram_ap, stg.r, writes=[stg.r])
            for c in range(8):
                b = nb("ld", [0, 1, 2, 3])
                for a in range(nt):
                    X("tensor", "transpose", [stg.r, RC], [RB[b]], out=PB[b][:, a * pp:(a + 1) * pp],
                      in_=sv[0:pp, a, c * 128:(c + 1) * 128], identity=ident_f[0:pp, 0:pp])
                if (c + eng_alt) % 2 == 0:
                    X("vector", "tensor_copy", [RB[b]], [rdst], out=dst[:, c, 0:n], in_=PB[b][:, 0:n])
                else:
                    X("scalar", "copy", [RB[b]], [rdst], out=dst[:, c, 0:n], in_=PB[b][:, 0:n])

        def normmod(xsrc, rx, n, Avec, shvec, hdst, rh, sq, tmp, rstd):
            sqv = bf(sq.ap)[:, 0:8 * n].rearrange("p (c t) -> p c t", c=8)
            rxl = list(rx) if isinstance(rx, (list, tuple)) else [rx]
            b = nb("nm", [4, 5])
            for c in range(8):
                X("scalar", "activation", rxl, [sq.res[c % len(sq.res)]], out=sqv[:, c, :], in_=xsrc[:, c, :], func=AF.Square)
                X("tensor", "matmul", [sq.res[c % len(sq.res)], RC], [RB[b]], PB[b][:, 0:n], lhsT=ones_b, rhs=sqv[:, c, :],
                  start=(c == 0), stop=(c == 7))
            rs = rstd.ap[:, 0:n]
            rstd_from(PB[b][:, 0:n], rs, 1.0 / D, slice(0, 128), RB[b], rstd.r)
            for c in range(8):
                t = tmp[c % len(tmp)]
                X("vector", "scalar_tensor_tensor", rxl + [rstd.r, RM], [t.r], out=t.ap[:, 0:n], in0=xsrc[:, c, :],
                  scalar=Avec[:, c:c + 1], in1=rs, op0=ALU.mult, op1=ALU.mult)
                X("scalar", "activation", [t.r, RM], [rh], out=hdst[:, c, :], in_=t.ap[:, 0:n], func=AF.Identity,
                  bias=shvec[:, c:c + 1], scale=1.0)

        modT = [mods_b.ap[:, l * 96:(l + 1) * 96].rearrange("p (j k) -> p j k", k=2) for l in range(2)]
        cs_b = A.alloc("cond_silu", 8)
        csb = bf(cs_b.ap).rearrange("p (c k) -> p c k", k=2)
        X("scalar", "activation", [RC], [cs_b.r], out=csb, in_=cond_f.rearrange("p (c k) -> p c k", k=2), func=AF.Silu)
        mrow = [A.alloc("mrow%d" % i, 512) for i in range(2)]
        bada_b = A.alloc("bada", 96)
        for l in range(2):
            DMA("sync", bada_b.ap[:, l * 48:(l + 1) * 48], d_bada[l], bada_b.r, writes=[bada_b.r])
        if mini is not None:
            X("vector", "memset", [], [RM], mods_b.ap, 0.5)
        def ada_piece(l, pc):
            tbk = 6 + l
            wv, wr = wload(d_wada[l][:, pc * 512:(pc + 1) * 512].rearrange("(c p) n -> p c n", p=128))
            b = nb("ada", [0, 1, 2, 3])
            for kc in range(8):
                X("tensor", "matmul", [cs_b.r, wr], [RB[b]], PB[b][0:2, :], lhsT=csb[:, kc, :], rhs=wv[:, kc, :],
                  start=(kc == 0), stop=(kc == 7))
            mr = mrow[pc % 2]
            X("vector", "tensor_copy", [RB[b]], [mr.r], out=mr.ap[0:2, :], in_=PB[b][0:2, :])
            for j in range(4):
                jj = pc * 4 + j
                X("tensor", "matmul", [mr.r, RC], [RB[tbk]], PB[tbk][:, jj * 2:jj * 2 + 2],
                  lhsT=mr.ap[0:2, j * 128:(j + 1) * 128], rhs=ident_f[0:2, 0:2], start=True, stop=True)

        def ada_finish(l):
            tbk = 6 + l
            for k in range(2):
                X("vector", "tensor_tensor", [RB[tbk], bada_b.r], [RM], out=modT[l][:, :, k],
                  in0=PB[tbk][:, 0:96].rearrange("p (j k) -> p j k", k=2)[:, :, k], in1=bada_b.ap[:, l * 48:(l + 1) * 48],
                  op=ALU.add)
            for k in range(2):
                X("vector", "scalar_tensor_tensor", [RM, RC], [RM], out=modT[l][:, 8:16, k], in0=modT[l][:, 8:16, k],
                  scalar=1.0, in1=V_NMIX[l], op0=ALU.add, op1=ALU.mult)
                X("vector", "scalar_tensor_tensor", [RM, RC], [RM], out=modT[l][:, 32:40, k], in0=modT[l][:, 32:40, k],
                  scalar=1.0, in1=V_NMLP[l], op0=ALU.add, op1=ALU.mult)

        ada_deferred = (mini is None) and stage >= 3

        def MOD(l, kind, cond):
            return modT[l][:, kind * 8:(kind + 1) * 8, cond]

        def cond_of(tb):
            return 0 if tb == 0 else 1

        def store_tokens(xsrc, rsrc, n, dram_ap, ostg, rextra=()):
            ov = ostg.ap.rearrange("p (a d) -> p a d", a=4)
            nt = n // 128
            assert nt <= 4
            for a in range(nt):
                for half in range(2):
                    b = nb("st", [0, 1, 2, 3])
                    for cc_ in range(4):
                        c = half * 4 + cc_
                        X("tensor", "transpose", [rsrc, RC] + list(rextra), [RB[b]], out=PB[b][:, cc_ * 128:(cc_ + 1) * 128],
                          in_=xsrc[:, c, a * 128:(a + 1) * 128], identity=ident_f)
                    if half == 0:
                        X("vector", "tensor_copy", [RB[b]], [ostg.r], out=ov[:, a, 0:512], in_=PB[b])
                    else:
                        X("scalar", "copy", [RB[b]], [ostg.r], out=ov[:, a, 512:1024], in_=PB[b])
            DMA("sync", dram_ap.rearrange("(a p) d -> p a d", p=128), ov[:, 0:nt, :], ostg.r, reads=[ostg.r])

        def epilogue(ps, rps, l, kind, tb, m):
            X("vector", "scalar_tensor_tensor", [rps, RM, RX[tb]], [RX[tb]], out=xT[:, m, blk(tb)], in0=ps,
              scalar=MOD(l, kind, cond_of(tb))[:, m:m + 1], in1=xT[:, m, blk(tb)], op0=ALU.mult, op1=ALU.add)

        def mlp(l, segs):
            ns = len(segs)
            hb = A.alloc("hT_mlp", 8 * 512 * ns // 2, nres=ns)
            hv = bf(hb.ap).rearrange("p (c t) -> p c t", c=8)
            sq = A.alloc("sq_m", 2048, nres=8)
            tmps = [A.alloc("tmp_m%d" % i, 512) for i in range(2)]
            rstd = A.alloc("rstd_m", 512)
            for si, (c0, rl, cd) in enumerate(segs):
                normmod(xT[:, :, c0:c0 + 512], rl, 512, MOD(l, 4, cd), MOD(l, 3, cd),
                        hv[:, :, si * 512:(si + 1) * 512], hb.res[si], sq, tmps, rstd)
            A.free(sq, rstd)
            ag = [A.alloc("ag%d" % i, 4 * 512 * ns // 2, nres=ns) for i in range(2)]
            agv = [bf(a_.ap).rearrange("p (j t) -> p j t", j=4) for a_ in ag]
            wts = {}

            def ph1(g):
                w1, r1 = wload(d_mlp1[l][:, g * 512:(g + 1) * 512].rearrange("(c p) n -> p c n", p=128))
                w2, r2 = wload(d_mlp2[l][g * 512:(g + 1) * 512, :].rearrange("(j p) n -> p j n", p=128))
                wts[g] = (w2, r2)
                for si in range(ns):
                    ss = slice(si * 512, (si + 1) * 512)
                    for j in range(4):
                        b = nb("m1", [0, 1, 2, 3])
                        for c in range(8):
                            X("tensor", "matmul", [hb.res[si], r1], [RB[b]], PB[b], lhsT=w1[:, c, j * 128:(j + 1) * 128],
                              rhs=hv[:, c, ss], start=(c == 0), stop=(c == 7))
                        t = tmps[(si * 4 + j) % 2]
                        X("scalar", "activation", [RB[b]], [t.r], out=t.ap, in_=PB[b], func=AF.Relu)
                        X("vector", "tensor_tensor", [t.r], [ag[g % 2].res[si]], out=agv[g % 2][:, j, ss], in0=t.ap,
                          in1=t.ap, op=ALU.mult)

            def ph2(g):
                w2, r2 = wts.pop(g)
                for si, (c0, rl, cd) in enumerate(segs):
                    ss = slice(si * 512, (si + 1) * 512)
                    for m in range(8):
                        b = nb("m2", [4, 5, 6, 7])
                        for j in range(4):
                            X("tensor", "matmul", [ag[g % 2].res[si], r2], [RB[b]], PB[b], lhsT=w2[:, j, m * 128:(m + 1) * 128],
                              rhs=agv[g % 2][:, j, ss], start=(j == 0), stop=(j == 3))
                        X("vector", "scalar_tensor_tensor", [RB[b], RM] + list(rl), list(rl), out=xT[:, m, c0:c0 + 512], in0=PB[b],
                          scalar=MOD(l, 5, cd)[:, m:m + 1], in1=xT[:, m, c0:c0 + 512], op0=ALU.mult, op1=ALU.add)
            ph1(0)
            for g in range(1, 8):
                ph1(g)
                ph2(g - 1)
            ph2(7)
            A.free(hb, *tmps, *ag)

        stg = A.alloc("stg", 4096)
        load_xT(d_xp, 512, xT[:, :, blk(0)], RX[0], stg)
        for i in range(3):
            load_xT(d_xw[i * 512:(i + 1) * 512, :], 512, xT[:, :, blk(i + 1)], RX[i + 1], stg, eng_alt=i)
        if "x1" in skip:
            P.emit(nc, st)
            return nc
        xpre_b = A.alloc("xpreT", 8 * 128)
        xpreT = xpre_b.ap.rearrange("p (c t) -> p c t", c=8)
        load_xT(d_xpre, 128, xpreT, xpre_b.r, stg)
        hpre_b = A.alloc("hpreT", 8 * 128 // 2)
        hpreT = bf(hpre_b.ap).rearrange("p (c t) -> p c t", c=8)
        if "x2" in skip:
            P.emit(nc, st)
            return nc
        if mini is None:
            for pc in range(12):
                ada_piece(0, pc)
            ada_finish(0)
            if not ada_deferred:
                for pc in range(12):
                    ada_piece(1, pc)
                ada_finish(1)
        if not ada_deferred:
            A.free(cs_b, bada_b, *mrow)

        sq_b = A.alloc("sq", 2048, nres=8)
        tmpb = [A.alloc("tmp%d" % i, 512) for i in range(2)]
        rstd_b = A.alloc("rstd", 512)
        for tb in range(4):
            normmod(xT[:, :, blk(tb)], RX[tb], 512, MOD(0, 1, cond_of(tb)), MOD(0, 0, cond_of(tb)),
                    hT[:, :, blk(tb)], RH[tb], sq_b, tmpb, rstd_b)
        normmod(xpreT, xpre_b.r, 128, MOD(0, 1, 1), MOD(0, 0, 1), hpreT, hpre_b.r, sq_b, tmpb, rstd_b)
        if "x3" in skip:
            P.emit(nc, st)
            return nc
        A.free(stg, xpre_b)

        if stage >= 2:
            LA = 2144
            B_S0, B_S1, B_PRE, B_W = 8, 272, 536, 600
            wpin_v, wpin_r = wload(d_win0_pool.rearrange("(c p) n -> p c n", p=128))
            wpl_v, wpl_r = wload(d_wpool)
            wop_v, wop_r = wload(d_wout0[0:512, :].rearrange("(c p) n -> p c n", p=128))
            mask_b = A.alloc("pmask", 1600)
            pinv_b = A.alloc("pinv", 128)
            DMA("sync", mask_b.ap, d_pmask, mask_b.r, writes=[mask_b.r])
            DMA("sync", pinv_b.ap, d_pinv, pinv_b.r, writes=[pinv_b.r])
            abuf = [A.alloc("abuf%d" % i, LA) for i in range(1)]
            sbuf_ = [A.alloc("sbuf%d" % i, LA) for i in range(2)]
            pld = [A.alloc("pooled%d" % i, LA // 2) for i in range(1)]
            t8 = A.alloc("t8", 8)
            po_b = A.alloc("poolout", 4 * 2048 // 2, nres=4)
            pov = bf(po_b.ap).rearrange("p (g t) -> p g t", g=4)
            for b_ in abuf + sbuf_:
                X("vector", "memset", [], [b_.r], b_.ap, 0.0)
            for g in range(4):
                w = (2, 4, 8, 16)[g]
                ab = abuf[0]
                av = ab.ap
                for tb in range(4):
                    b = nb("pl", [0, 1, 2, 3])
                    for c in range(8):
                        X("tensor", "matmul", [RH[tb], wpin_r], [RB[b]], PB[b], lhsT=wpin_v[:, c, g * 128:(g + 1) * 128],
                          rhs=hT[:, c, blk(tb)], start=(c == 0), stop=(c == 7))
                    if tb == 0:
                        X("vector", "tensor_copy", [RB[b]], [ab.r], out=av[:, B_S0:B_S0 + 256], in_=PB[b][:, 0:256])
                        X("vector", "tensor_copy", [RB[b]], [ab.r], out=av[:, B_S1:B_S1 + 256], in_=PB[b][:, 256:512])
                    else:
                        X("vector", "tensor_tensor", [RB[b], mask_b.r], [ab.r],
                          out=av[:, B_W + (tb - 1) * 512:B_W + tb * 512], in0=PB[b],
                          in1=mask_b.ap[:, 64 + (tb - 1) * 512:64 + tb * 512], op=ALU.mult)
                b = nb("pl", [0, 1, 2, 3])
                for c in range(8):
                    X("tensor", "matmul", [hpre_b.r, wpin_r], [RB[b]], PB[b][:, 0:64], lhsT=wpin_v[:, c, g * 128:(g + 1) * 128],
                      rhs=hpreT[:, c, 0:64], start=(c == 0), stop=(c == 7))
                X("vector", "tensor_tensor", [RB[b], mask_b.r], [ab.r], out=av[:, B_PRE:B_PRE + 64], in0=PB[b][:, 0:64],
                  in1=mask_b.ap[:, 0:64], op=ALU.mult)
                cur, rcur = av, ab.r
                sh = 1
                k = 0
                while sh < w:
                    dstb = sbuf_[k % 2]
                    X("vector", "tensor_tensor", [rcur], [dstb.r], out=dstb.ap[:, sh:LA], in0=cur[:, sh:LA], in1=cur[:, 0:LA - sh],
                      op=ALU.add)
                    cur, rcur = dstb.ap, dstb.r
                    sh *= 2
                    k += 1
                pb_ = pld[0]
                pv = bf(pb_.ap)
                off = w // 2 - 1
                for (base, L) in ((B_S0, 256), (B_S1, 256), (B_W, 1536)):
                    X("vector", "scalar_tensor_tensor", [rcur, ab.r], [pb_.r], out=pv[:, base:base + L],
                      in0=cur[:, base + off:base + off + L], scalar=1.0 / w, in1=av[:, base:base + L], op0=ALU.mult,
                      op1=ALU.subtract)
                for (pos, bd) in ((B_S0, 0), (B_S0 + 248, 1), (B_S1, 0), (B_S1 + 248, 1), (B_W + 256, 2), (B_W + 1272, 3)):
                    X("vector", "tensor_tensor", [rcur, pinv_b.r], [t8.r], out=t8.ap, in0=cur[:, pos + off:pos + off + 8],
                      in1=pinv_b.ap[:, g * 32 + bd * 8:g * 32 + bd * 8 + 8], op=ALU.mult)
                    X("vector", "tensor_tensor", [t8.r, ab.r], [pb_.r], out=pv[:, pos:pos + 8], in0=t8.ap, in1=av[:, pos:pos + 8],
                      op=ALU.subtract)
                for tb in range(4):
                    b = nb("pl", [0, 1, 2, 3])
                    if tb == 0:
                        for s_, base in enumerate((B_S0, B_S1)):
                            X("tensor", "matmul", [pb_.r, wpl_r], [RB[b]], PB[b][:, s_ * 256:(s_ + 1) * 256],
                              lhsT=wpl_v[:, g * 128:(g + 1) * 128], rhs=pv[:, base:base + 256], start=True, stop=True)
                    else:
                        X("tensor", "matmul", [pb_.r, wpl_r], [RB[b]], PB[b], lhsT=wpl_v[:, g * 128:(g + 1) * 128],
                          rhs=pv[:, B_W + (tb - 1) * 512:B_W + tb * 512], start=True, stop=True)
                    X("vector", "tensor_scalar", [RB[b], RC], [po_b.res[tb]], out=pov[:, g, blk(tb)], in0=PB[b],
                      scalar1=V_PSC[:, g:g + 1], scalar2=None, op0=ALU.mult)
            for tb in range(4):
                for m in range(8):
                    b = nb("po", [4, 5, 6, 7])
                    for g in range(4):
                        X("tensor", "matmul", [po_b.res[tb], wop_r], [RB[b]], PB[b], lhsT=wop_v[:, g, m * 128:(m + 1) * 128],
                          rhs=pov[:, g, blk(tb)], start=(g == 0), stop=(g == 3))
                    epilogue(PB[b], RB[b], 0, 2, tb, m)
            A.free(mask_b, pinv_b, t8, po_b, hpre_b, *abuf, *sbuf_, *pld)

        wkv_v, wkv_r = wload(d_win0_kv.rearrange("(c p) n -> p c n", p=128))
        pinned.add(wload.last)
        NKEY = 5120
        ckvn_b = A.alloc("ckvnT", NKEY // 2, nres=10)
        ckvnT = bf(ckvn_b.ap)
        kr_b = A.alloc("KRsq", NKEY, nres=10)
        KRv = bf(kr_b.ap)[:, 0:NKEY]
        KSQv = bf(kr_b.ap)[:, NKEY:2 * NKEY]

        def kv_block(hsrc, rh, kb, rope_tile=None, r_rope=None, ckv32=None, kr32=None, r32=None):
            ks = slice(kb * 512, (kb + 1) * 512)
            rr = slice(64, 96)
            ba = nb("kv", [0, 1, 2, 3])
            for c in range(8):
                X("tensor", "matmul", [rh, wkv_r], [RB[ba]], PB[ba], lhsT=wkv_v[:, c, 0:128], rhs=hsrc[:, c, :],
                  start=(c == 0), stop=(c == 7))
            sqv = bf(sq_b.ap)[:, 0:512]
            X("scalar", "activation", [RB[ba]], [sq_b.r], out=sqv, in_=PB[ba], func=AF.Square)
            bs = nb("kv", [0, 1, 2, 3])
            X("tensor", "matmul", [sq_b.r, RC], [RB[bs]], PB[bs], lhsT=ones_b, rhs=sqv, start=True, stop=True)
            rstd_from(PB[bs], rstd_b.ap, 1.0 / 128, slice(0, 128), RB[bs], rstd_b.r)
            X("vector", "scalar_tensor_tensor", [RB[ba], rstd_b.r, RC], [ckvn_b.res[kb]], out=ckvnT[:, ks], in0=PB[ba],
              scalar=V_KVL, in1=rstd_b.ap, op0=ALU.mult, op1=ALU.mult)
            if ckv32 is not None:
                X("vector", "scalar_tensor_tensor", [RB[ba], rstd_b.r, RC], [r32], out=ckv32, in0=PB[ba],
                  scalar=V_KVL, in1=rstd_b.ap, op0=ALU.mult, op1=ALU.mult)
            bb = nb("kv", [0, 1, 2, 3])
            for c in range(8):
                X("tensor", "matmul", [rh, wkv_r], [RB[bb]], PB[bb][0:96, :], lhsT=wkv_v[:, c, 128:224], rhs=hsrc[:, c, :],
                  start=(c == 0), stop=(c == 7))
            if "k2" not in skip:
                X("scalar", "activation", [RB[bb]], [kr_b.res[kb]], out=KSQv[rr, ks], in_=PB[bb][rr, :], func=AF.Square)
            if kr32 is not None and "k3" not in skip:
                X("vector", "tensor_copy", [RB[bb]], [r32], out=kr32, in_=PB[bb][rr, :])
            if rope_tile is None:
                if "k1" not in skip:
                    X("vector", "tensor_scalar", [RB[bb], RC], [kr_b.res[kb]], out=KRv[rr, ks], in0=PB[bb][rr, :],
                      scalar1=V_GK[rr, :], scalar2=None, op0=ALU.mult)
            else:
                bc = nb("kv", [0, 1, 2, 3])
                for c in range(8):
                    X("tensor", "matmul", [rh, wkv_r], [RB[bc]], PB[bc][0:96, :], lhsT=wkv_v[:, c, 224:320],
                      rhs=hsrc[:, c, :], start=(c == 0), stop=(c == 7))
                t1, t2 = tmpb[0], tmpb[1]
                X("vector", "scalar_tensor_tensor", [RB[bb], r_rope, RC], [t1.r], out=t1.ap[rr, :], in0=PB[bb][rr, :],
                  scalar=V_GK[rr, :], in1=rope_tile[rr, 0:512], op0=ALU.mult, op1=ALU.mult)
                X("vector", "scalar_tensor_tensor", [RB[bc], r_rope, RC], [t2.r], out=t2.ap[rr, :], in0=PB[bc][rr, :],
                  scalar=V_GKS[rr, :], in1=rope_tile[rr, 512:1024], op0=ALU.mult, op1=ALU.mult)
                X("vector", "tensor_tensor", [t1.r, t2.r], [kr_b.res[kb]], out=KRv[rr, ks], in0=t1.ap[rr, :], in1=t2.ap[rr, :],
                  op=ALU.add)

        o32_b = A.alloc("kvout32", 1024)
        ckv32 = o32_b.ap[:, 0:512]
        kr32 = o32_b.ap[:, 512:1024]
        kv_block(hT[:, :, blk(0)], RH[0], 9, ckv32=ckv32, kr32=kr32[64:96, :], r32=o32_b.r)
        if "x4" in skip:
            P.emit(nc, st)
            return nc
        ost_b = A.alloc("ostage", 4 * 160)
        ostv = ost_b.ap.rearrange("p (a d) -> p a d", a=4)
        for a in range(4):
            b = nb("ld", [0, 1, 2, 3])
            X("tensor", "transpose", [o32_b.r, RC], [RB[b]], out=PB[b][:, 0:128], in_=ckv32[:, a * 128:(a + 1) * 128],
              identity=ident_f)
            X("tensor", "matmul", [o32_b.r, RC], [RB[b]], PB[b][:, 128:160], lhsT=kr32[64:96, a * 128:(a + 1) * 128],
              rhs=ident_f[64:96, 64:96], start=True, stop=True)
            X("vector", "tensor_copy", [RB[b]], [ost_b.r], out=ostv[:, a, :], in_=PB[b][:, 0:160])
        DMA("sync", o_ckv.rearrange("(a p) d -> p a d", p=128), ostv[:, :, 0:128], ost_b.r, reads=[ost_b.r])
        DMA("sync", o_kr.rearrange("(a p) d -> p a d", p=128), ostv[:, :, 128:160], ost_b.r, reads=[ost_b.r])
        A.free(o32_b, ost_b)

        if stage >= 3:
            wq_v, wq_r = wload(d_win0_q.rearrange("(c p) n -> p c n", p=128))
            qln_b = A.alloc("qlnT", 2 * 2048 // 2, nres=4)
            qln = bf(qln_b.ap).rearrange("p (j t) -> p j t", j=2)
            sq2 = bf(sq_b.ap)[:, 0:1024].rearrange("p (j t) -> p j t", j=2)
            for tb in range(4):
                bq = [nb("q", [0, 1, 2, 3]), nb("q", [0, 1, 2, 3])]
                for j in range(2):
                    for c in range(8):
                        X("tensor", "matmul", [RH[tb], wq_r], [RB[bq[j]]], PB[bq[j]], lhsT=wq_v[:, c, j * 128:(j + 1) * 128],
                          rhs=hT[:, c, blk(tb)], start=(c == 0), stop=(c == 7))
                    X("scalar", "activation", [RB[bq[j]]], [sq_b.res[j]], out=sq2[:, j, :], in_=PB[bq[j]], func=AF.Square)
                bs = nb("qs", [4, 5])
                for j in range(2):
                    X("tensor", "matmul", [sq_b.res[j], RC], [RB[bs]], PB[bs], lhsT=ones_b, rhs=sq2[:, j, :], start=(j == 0), stop=(j == 1))
                rstd_from(PB[bs], rstd_b.ap, 1.0 / 256, slice(0, 128), RB[bs], rstd_b.r)
                for j in range(2):
                    X("vector", "scalar_tensor_tensor", [RB[bq[j]], rstd_b.r, RC], [qln_b.res[tb]], out=qln[:, j, blk(tb)],
                      in0=PB[bq[j]], scalar=V_QL[:, j:j + 1], in1=rstd_b.ap, op0=ALU.mult, op1=ALU.mult)
        A.free(hT_b)

        if stage >= 3:
            stg = A.alloc("stg", 4096)
            xtmp_b = A.alloc("xtmp", 4096)
            xtmp = xtmp_b.ap.rearrange("p (c t) -> p c t", c=8)
            htmp_b = [A.alloc("htmp%d" % i, 2048) for i in range(1)]
            ropet = [A.alloc("ropet%d" % i, 1024) for i in range(1)]
            for kb in range(8):
                load_xT(d_xb[kb * 512:(kb + 1) * 512, :], 512, xtmp, xtmp_b.r, stg, eng_alt=kb)
                hb_ = htmp_b[0]
                hv_ = bf(hb_.ap).rearrange("p (c t) -> p c t", c=8)
                normmod(xtmp, xtmp_b.r, 512, MOD(0, 1, 1), MOD(0, 0, 1), hv_, hb_.r, sq_b, tmpb, rstd_b)
                rt = ropet[0]
                DMA("sync", rt.ap[64:96, 0:512], d_ropek[0:32, kb * 512:(kb + 1) * 512], rt.r, writes=[rt.r])
                DMA("sync", rt.ap[64:96, 512:1024], d_ropek[32:64, kb * 512:(kb + 1) * 512], rt.r, writes=[rt.r])
                kv_block(hv_, hb_.r, kb, rope_tile=rt.ap, r_rope=rt.r)
                if ada_deferred:
                    for pc in range((kb * 12) // 8, ((kb + 1) * 12) // 8):
                        ada_piece(1, pc)
            DMA("sync", xtmp_b.ap[:, 0:512], d_cckv, xtmp_b.r, writes=[xtmp_b.r])
            X("vector", "tensor_copy", [xtmp_b.r], [ckvn_b.res[8]], out=ckvnT[:, 4096:4608], in_=xtmp_b.ap[:, 0:512])
            DMA("sync", xtmp_b.ap[64:96, 512:1024], d_ckr, xtmp_b.r, writes=[xtmp_b.r])
            X("scalar", "activation", [xtmp_b.r], [kr_b.res[8]], out=KSQv[64:96, 4096:4608], in_=xtmp_b.ap[64:96, 512:1024],
              func=AF.Square)
            X("vector", "tensor_scalar", [xtmp_b.r, RC], [kr_b.res[8]], out=KRv[64:96, 4096:4608], in0=xtmp_b.ap[64:96, 512:1024],
              scalar1=V_GK[64:96, :], scalar2=None, op0=ALU.mult)
            A.free(stg, xtmp_b, *htmp_b, *ropet)
            A.free(sq_b, rstd_b)
            if ada_deferred:
                ada_finish(1)
                A.free(cs_b, bada_b, *mrow)
            pinned.clear()

        if stage >= 3:
            SC0 = 96.0 ** -0.5
            wqup_v, wqup_r = wload(d_wqup.rearrange("(c p) n -> p c n", p=128))
            wkvup_v, wkvup_r = wload(d_wkvup)
            KT_b = A.alloc("KT", NKEY // 2, nres=10)
            KT = bf(KT_b.ap)
            X("vector", "memset", [], KT_b.res, KT[64:128, :], 0.0)
            V_b = A.alloc("Vh", 40 * 128 // 2, nres=10)
            Vv = bf(V_b.ap).rearrange("p (t d) -> p t d", d=128)
            X("vector", "memset", [], V_b.res, Vv[:, :, 64:128], 1.0)
            QT_b = [A.alloc("QT%d" % i, 2048 // 2, nres=4) for i in range(2)]
            for qb_ in QT_b:
                X("vector", "memset", [], qb_.res, bf(qb_.ap)[64:128, :], 0.0)
            PT_b = [A.alloc("PT%d" % i, 256) for i in range(4)]
            rden_l = [A.alloc("rden%d" % i, 512) for i in range(1)]
            osb_l = [A.alloc("osbm%d" % i, 512) for i in range(3)]
            sqs_b = [A.alloc("sqs%d" % i, 256) for i in range(2)]
            r96_b = [A.alloc("r96_%d" % i, 512) for i in range(2)]
            rq_b = [A.alloc("ropeq%d" % i, 1024) for i in range(1)]
            rsk_b = [A.alloc("rsk%d" % i, 40, nres=10) for i in range(2)]
            lnsc_b = A.alloc("lnsc", 2)
            X("vector", "memset", [], [lnsc_b.r], lnsc_b.ap[:, 0:1], float(np.log(SC0)))
            X("vector", "tensor_tensor", [RC], [lnsc_b.r], out=lnsc_b.ap[:, 1:2], in0=V_GQ, in1=V_GK, op=ALU.mult)
            V_GQK = lnsc_b.ap[:, 1:2]
            for kb_ in range(10):
                X("vector", "tensor_copy", [kr_b.res[kb_]], [KT_b.res[kb_]], out=KT[64:96, kb_ * 512:(kb_ + 1) * 512],
                  in_=KRv[64:96, kb_ * 512:(kb_ + 1) * 512])
            attn_b = A.alloc("attnT", 2 * 2048 // 2, nres=4)
            attnT = bf(attn_b.ap).rearrange("p (h t) -> p h t", h=2)
            cnt = {"sq": 0, "r96": 0, "pt": 0, "rden": 0, "rq": 0, "osb": 0}

            def rot(key, lst):
                i = cnt[key]
                cnt[key] = i + 1
                return lst[i % len(lst)]

            def evac(ob, nq):
                ob_ = rot("osb", osb_l)
                X("vector", "tensor_copy", [RB[ob]], [ob_.r], out=ob_.ap[:, 0:nq], in_=PB[ob][:, 0:nq])
                return ob_

            def normalize_sb(ob_, dst, rdst, nq):
                rd = rot("rden", rden_l)
                X("vector", "reciprocal", [ob_.r], [rd.r], out=rd.ap[0:64, 0:nq], in_=ob_.ap[64:128, 0:nq])
                X("vector", "tensor_tensor", [ob_.r, rd.r], [rdst], out=dst, in0=ob_.ap[0:64, 0:nq], in1=rd.ap[0:64, 0:nq],
                  op=ALU.mult)

            def normalize(ob, dst, rdst, nq):
                normalize_sb(evac(ob, nq), dst, rdst, nq)

            def q_a(h, tb, stt_):
                for kc in range(2):
                    X("tensor", "matmul", [qln_b.res[tb], wqup_r], [RB[5]], PB[5][0:96, :], lhsT=wqup_v[:, kc, h * 192:h * 192 + 96],
                      rhs=qln[:, kc, blk(tb)], start=(kc == 0), stop=(kc == 1))
                if tb > 0:
                    for kc in range(2):
                        X("tensor", "matmul", [qln_b.res[tb], wqup_r], [RB[6]], PB[6][0:96, :],
                          lhsT=wqup_v[:, kc, h * 192 + 96:h * 192 + 192], rhs=qln[:, kc, blk(tb)], start=(kc == 0), stop=(kc == 1))
                sq = rot("sq", sqs_b)
                X("scalar", "activation", [RB[5]], [sq.r], out=bf(sq.ap)[0:96, :], in_=PB[5][0:96, :], func=AF.Square)
                stt_["sq"] = sq

            def q_b(h, tb, stt_):
                QB_ = QT_b[h % 2]
                QTv = bf(QB_.ap)
                rq = QB_.res[tb]
                sq = stt_["sq"]
                bq_ = nb("S", [3, 4, 7])
                X("tensor", "matmul", [sq.r, RC], [RB[bq_]], PB[bq_][0:96, :], lhsT=ones_b[0:96, 0:96], rhs=bf(sq.ap)[0:96, :],
                  start=True, stop=True)
                r96 = rot("r96", r96_b)
                rstd_from(PB[bq_][0:96, :], r96.ap[0:96, :], 1.0 / 96, slice(0, 96), RB[bq_], r96.r)
                if tb == 0:
                    X("vector", "scalar_tensor_tensor", [RB[5], r96.r, lnsc_b.r], [rq], out=QTv[0:64, blk(tb)], in0=PB[5][0:64, :],
                      scalar=V_GQK[0:64, :], in1=r96.ap[0:64, :], op0=ALU.mult, op1=ALU.mult)
                    X("vector", "scalar_tensor_tensor", [RB[5], r96.r, RC], [rq], out=QTv[64:96, blk(tb)], in0=PB[5][64:96, :],
                      scalar=V_GQ[64:96, :], in1=r96.ap[64:96, :], op0=ALU.mult, op1=ALU.mult)
                    return
                X("vector", "scalar_tensor_tensor", [RB[5], r96.r, lnsc_b.r], [rq], out=QTv[0:64, blk(tb)], in0=PB[5][0:64, :],
                  scalar=V_GQK[0:64, :], in1=r96.ap[0:64, :], op0=ALU.mult, op1=ALU.mult)
                rt = rot("rq", rq_b)
                ws = (tb - 1) * 512
                DMA("sync", rt.ap[64:96, 0:512], d_ropeq[0:32, ws:ws + 512], rt.r, writes=[rt.r])
                DMA("sync", rt.ap[64:96, 512:1024], d_ropeq[32:64, ws:ws + 512], rt.r, writes=[rt.r])
                rr = slice(64, 96)
                t1, t2 = tmpb[0], tmpb[1]
                X("vector", "scalar_tensor_tensor", [RB[5], rt.r, RC], [t1.r], out=t1.ap[rr, :], in0=PB[5][rr, :],
                  scalar=V_GQ[rr, :], in1=rt.ap[rr, 0:512], op0=ALU.mult, op1=ALU.mult)
                X("vector", "scalar_tensor_tensor", [RB[6], rt.r, RC], [t2.r], out=t2.ap[rr, :], in0=PB[6][rr, :],
                  scalar=V_GQS[rr, :], in1=rt.ap[rr, 512:1024], op0=ALU.mult, op1=ALU.mult)
                X("vector", "tensor_tensor", [t1.r, t2.r], [t1.r], out=t1.ap[rr, :], in0=t1.ap[rr, :], in1=t2.ap[rr, :], op=ALU.add)
                X("vector", "tensor_tensor", [t1.r, r96.r], [rq], out=QTv[rr, blk(tb)], in0=t1.ap[rr, :], in1=r96.ap[rr, :], op=ALU.mult)

            def k_s0(h, kb, stt_):
                ks = slice(kb * 512, (kb + 1) * 512)
                X("tensor", "matmul", [ckvn_b.res[kb], wkvup_r], [RB[5]], PB[5][0:64, :], lhsT=wkvup_v[:, h * 128:h * 128 + 64],
                  rhs=ckvnT[:, ks], start=True, stop=True)
                sq = rot("sq", sqs_b)
                X("scalar", "activation", [RB[5]], [sq.r], out=bf(sq.ap)[0:64, :], in_=PB[5][0:64, :], func=AF.Square)
                X("vector", "tensor_copy", [RB[5]], [KT_b.res[kb]], out=KT[0:64, ks], in_=PB[5][0:64, :])
                stt_["sq"] = sq

            def k_s1(h, kb, stt_):
                sq = stt_["sq"]
                for a in range(4):
                    X("tensor", "matmul", [sq.r, RC], [RB[6]], PB[6][:, a:a + 1], lhsT=bf(sq.ap)[0:64, a * 128:(a + 1) * 128],
                      rhs=ones_b[0:64, 0:1], start=True, stop=False)
                    X("tensor", "matmul", [kr_b.res[kb], RC], [RB[6]], PB[6][:, a:a + 1],
                      lhsT=KSQv[64:96, kb * 512 + a * 128:kb * 512 + (a + 1) * 128], rhs=ones_b[64:96, 0:1], start=False, stop=True)
                rk = rsk_b[h % 2]
                X("scalar", "activation", [RB[6], RC], [rk.res[kb]], out=rk.ap[:, kb * 4:kb * 4 + 4], in_=PB[6][:, 0:4], func=AF.Ln,
                  bias=eps_t[:, 0:1], scale=1.0 / 96)
                X("scalar", "activation", [rk.res[kb], lnsc_b.r], [rk.res[kb]], out=rk.ap[:, kb * 4:kb * 4 + 4],
                  in_=rk.ap[:, kb * 4:kb * 4 + 4], func=AF.Exp, bias=lnsc_b.ap[:, 0:1], scale=-0.5)

            def k_s2(h, kb, stt_):
                pass

            def v_all(h, kb, stt_):
                for a in range(4):
                    t = kb * 4 + a
                    X("tensor", "matmul", [ckvn_b.res[kb], wkvup_r], [RB[6]], PB[6][:, a * 64:(a + 1) * 64],
                      lhsT=ckvnT[:, t * 128:(t + 1) * 128], rhs=wkvup_v[:, h * 128 + 64:h * 128 + 128], start=True, stop=True)
                X("vector", "tensor_copy", [RB[6]], [V_b.res[kb]], out=Vv[:, kb * 4:kb * 4 + 4, 0:64],
                  in_=PB[6][:, 0:256].rearrange("p (a d) -> p a d", a=4))

            items = [(h, kb) for h in range(8) for kb in (list(range(9)) + [9])]
            nitem = len(items)
            sched = [dict() for _ in range(nitem)]

            def add(n, step, fn, *a):
                sched[n].setdefault(step, []).append((fn, a))

            for n, (h, kb) in enumerate(items):
                if n + 1 >= nitem:
                    break
                h2, kb2 = items[n + 1]
                st_ = {}
                if kb == 9:
                    continue
                base = 1
                add(n, base, k_s0, h2, kb2, st_)
                add(n, base + 2, k_s1, h2, kb2, st_)
                add(n, base + 4, k_s2, h2, kb2, st_)
                add(n, base + 5, v_all, h2, kb2, st_)
                if kb == 8 and n + 2 < nitem:
                    h3, kb3 = items[n + 2]
                    st3 = {}
                    add(n, 7, k_s0, h3, kb3, st3)
                    add(n, 9, k_s1, h3, kb3, st3)
                    add(n, 11, k_s2, h3, kb3, st3)
                    add(n, 11, v_all, h3, kb3, st3)
                if 2 <= kb <= 5 and h + 1 < 8:
                    stq = {}
                    add(n, 7, q_a, h + 1, kb - 2, stq)
                    add(n, 9, q_b, h + 1, kb - 2, stq)
            for tb in range(4):
                stq = {}
                q_a(0, tb, stq)
                q_b(0, tb, stq)
            st0 = {}
            k_s0(0, 0, st0)
            k_s1(0, 0, st0)
            k_s2(0, 0, st0)
            v_all(0, 0, st0)

            def run_sched(n, step):
                for fn, a in sched[n].get(step, []):
                    fn(*a)

            woa = {}
            pend = []

            def flush(keep):
                while len(pend) > keep:
                    pt_, kc_, qb_ = pend.pop(0)
                    X("tensor", "matmul", [V_b.res[kc_ // 4], pt_.r], [RB[qb_ - 1]], PB[qb_ - 1],
                      lhsT=Vv[:, kc_, :], rhs=bf(pt_.ap), start=(kc_ == 0), stop=(kc_ == 35))

            for n, (h, kb) in enumerate(items):
                hg, hh = h // 4, h % 4
                po = (hh % 2) * 64
                QB = QT_b[h % 2]
                QTv = bf(QB.ap)
                if kb == 0 and hh == 0:
                    woa[hg] = wload(d_wout0[512 + hg * 256:512 + (hg + 1) * 256, :].rearrange("(c p) n -> p c n", p=128))
                if kb < 9:
                    step = 0
                    for c in range(4):
                        kc = kb * 4 + c
                        for qb in (1, 2, 3):
                            sb = nb("S", [3, 4, 7])
                            X("tensor", "matmul", [KT_b.res[kb], QB.res[qb]], [RB[sb]], PB[sb], lhsT=KT[:, kc * 128:(kc + 1) * 128],
                              rhs=QTv[:, blk(qb)], start=True, stop=True)
                            pt = rot("pt", PT_b)
                            X("scalar", "activation", [RB[sb], rsk_b[h % 2].res[kb]], [pt.r], out=bf(pt.ap), in_=PB[sb], func=AF.Exp,
                              scale=rsk_b[h % 2].ap[:, kc:kc + 1])
                            pend.append((pt, kc, qb))
                            flush(2)
                            run_sched(n, step)
                            step += 1
                    if kb == 8:
                        flush(0)
                        evs = [evac(qb - 1, 512) for qb in (1, 2, 3)]
                        for qb in (1, 2, 3):
                            normalize_sb(evs[qb - 1], attnT[po:po + 64, hh // 2, blk(qb)], attn_b.res[qb], 512)
                else:
                    for s_ in range(2):
                        sb = nb("S", [3, 4, 7])
                        for c in range(2):
                            k0 = 4608 + s_ * 256 + c * 128
                            X("tensor", "matmul", [KT_b.res[9], QB.res[0]], [RB[sb]], PB[sb][:, c * 256:(c + 1) * 256],
                              lhsT=KT[:, k0:k0 + 128], rhs=QTv[:, s_ * 256:(s_ + 1) * 256], start=True, stop=True)
                        pt = rot("pt", PT_b)
                        ptv = bf(pt.ap)
                        for c in range(2):
                            X("scalar", "activation", [RB[sb], rsk_b[h % 2].res[9]], [pt.r], out=ptv[:, c * 256:(c + 1) * 256],
                              in_=PB[sb][:, c * 256:(c + 1) * 256], func=AF.Exp, scale=rsk_b[h % 2].ap[:, 36 + 2 * s_ + c:37 + 2 * s_ + c])
                        for c in range(2):
                            X("tensor", "matmul", [V_b.res[9], pt.r], [RB[0]], PB[0][:, s_ * 256:(s_ + 1) * 256],
                              lhsT=Vv[:, 36 + 2 * s_ + c, :], rhs=ptv[:, c * 256:(c + 1) * 256], start=(c == 0), stop=(c == 1))
                    normalize(0, attnT[po:po + 64, hh // 2, blk(0)], attn_b.res[0], 512)
                    if hh == 3:
                        woa_v, woa_r = woa.pop(hg)
                        for tb in range(4):
                            for m in range(8):
                                b = nb("po2", [5, 6, 3, 4, 7])
                                for pr in range(2):
                                    X("tensor", "matmul", [attn_b.res[tb], woa_r], [RB[b]], PB[b], lhsT=woa_v[:, pr, m * 128:(m + 1) * 128],
                                      rhs=attnT[:, pr, blk(tb)], start=(pr == 0), stop=(pr == 1))
                                epilogue(PB[b], RB[b], 0, 2, tb, m)
            A.free(KT_b, V_b, attn_b, qln_b, ckvn_b, kr_b, lnsc_b, *rsk_b, *QT_b, *PT_b, *rden_l, *osb_l, *sqs_b, *r96_b, *rq_b)
        if stage < 3:
            A.free(sq_b, rstd_b)
        A.free(*tmpb)

        if stage >= 4:
            mlp(0, [(tb * 512, [RX[tb]], cond_of(tb)) for tb in range(4)])

        if dbg and stage < 5:
            ostg_d = A.alloc("ostg_dbg", 4096)
            for tb in range(4):
                store_tokens(xT[:, :, blk(tb)], RX[tb], 512, o_dbg[tb * 512:(tb + 1) * 512, :], ostg_d)
            A.free(ostg_d)

        if stage >= 5:
            hT_b = A.alloc("hT1", 8 * 2048 // 2, nres=4)
            hT = bf(hT_b.ap).rearrange("p (c t) -> p c t", c=8)
            RH = hT_b.res
            sq_b = A.alloc("sq1", 2048, nres=8)
            tmpb = [A.alloc("tmp1_%d" % i, 512) for i in range(2)]
            rstd_b = A.alloc("rstd1", 512)
            for tb in range(4):
                normmod(xT[:, :, blk(tb)], RX[tb], 512, MOD(1, 1, cond_of(tb)), MOD(1, 0, cond_of(tb)),
                        hT[:, :, blk(tb)], RH[tb], sq_b, tmpb, rstd_b)
            A.free(sq_b)
            SC1 = 0.125
            QCOL = [0, 768, 1280]
            sqp = [A.alloc("sqp%d" % i, 256) for i in range(2)]
            KTp = [A.alloc("KT1_%d" % i, 1024, nres=4) for i in range(2)]
            QTp = [A.alloc("QT1_%d" % i, 768, nres=3) for i in range(2)]
            Vw_b = A.alloc("Vw", 12 * 2 * 128 // 2, nres=3)
            Vw = bf(Vw_b.ap).rearrange("p (r h d) -> p r h d", r=12, h=2)
            Vp_b = A.alloc("Vp", 4 * 2 * 128 // 2)
            Vp = bf(Vp_b.ap).rearrange("p (r h d) -> p r h d", r=4, h=2)
            Kc_b = A.alloc("Kc", 512)
            Kc = bf(Kc_b.ap).rearrange("p (h k) -> p h k", h=2)
            X("vector", "memset", [], [Kc_b.r], Kc[64:128, :, :], 0.0)
            Vc_b = A.alloc("Vc", 4 * 2 * 128 // 2)
            Vc = bf(Vc_b.ap).rearrange("p (r h d) -> p r h d", r=4, h=2)
            T_b = [A.alloc("Ttab%d" % i, 1024) for i in range(2)]
            cm_b = A.alloc("cmask", 64)
            DMA("sync", cm_b.ap[0:64, :], d_cmask, cm_b.r, writes=[cm_b.r])
            DMA("sync", cm_b.ap[64:128, :], d_cmask, cm_b.r, writes=[cm_b.r])
            for tb_ in T_b:
                X("vector", "memset", [], [tb_.r], tb_.ap, 0.0)
            vT_l = [A.alloc("vT%d" % i, 256) for i in range(2)]
            kf_b = A.alloc("kf32", 512)
            kst_b = [A.alloc("kstage%d" % i, 512) for i in range(1)]
            E32_b = [A.alloc("E32_%d" % i, 768) for i in range(2)]
            PL_b = [A.alloc("PL%d" % i, 384) for i in range(3)]
            PT1_b = [A.alloc("PT1_%d" % i, 256) for i in range(2)]
            rden1 = [A.alloc("rden1_%d" % i, 512) for i in range(1)]
            attn1_b = A.alloc("attn1", 2 * 1536 // 2, nres=3)
            attn1 = bf(attn1_b.ap).rearrange("p (h t) -> p h t", h=2)
            X("vector", "memset", [], Vw_b.res, Vw[:, :, :, 64:128], 1.0)
            X("vector", "memset", [], [Vp_b.r], Vp[:, :, :, 64:128], 1.0)
            X("vector", "memset", [], [Vc_b.r], Vc[:, :, :, 64:128], 1.0)
            cnt1 = {}

            def rot1(key, lst):
                i = cnt1.get(key, 0)
                cnt1[key] = i + 1
                return lst[i % len(lst)]

            def normalize1(ob, dst, rdst, nq):
                rd = rot1("rden", rden1)
                X("scalar", "activation", [RB[ob]], [rd.r], out=rd.ap[0:64, 0:nq], in_=PB[ob][64:128, 0:nq], func=AF.Ln)
                X("scalar", "activation", [rd.r], [rd.r], out=rd.ap[0:64, 0:nq], in_=rd.ap[0:64, 0:nq], func=AF.Exp, scale=-1.0)
                X("vector", "tensor_tensor", [RB[ob], rd.r], [rdst], out=dst, in0=PB[ob][0:64, 0:nq], in1=rd.ap[0:64, 0:nq],
                  op=ALU.mult)

            def headnorm(bp, gvec):
                sq = rot1("sqp", sqp)
                sqv = bf(sq.ap)
                X("scalar", "activation", [RB[bp]], [sq.r], out=sqv, in_=PB[bp], func=AF.Square)
                bs = nb("prep1", [2, 3, 4, 5, 6, 7])
                X("tensor", "matmul", [sq.r, RC], [RB[bs]], PB[bs], lhsT=blk_b, rhs=sqv, start=True, stop=True)
                rb_ = rot1("rstd3", [rstd_b, tmpb[0], tmpb[1]])
                rstd_from(PB[bs], rb_.ap, 1.0 / 64, slice(0, 128), RB[bs], rb_.r)
                return rb_

            deferred = []

            def defer(fn):
                deferred.append(fn)
                while len(deferred) > 1:
                    deferred.pop(0)()

            def drain():
                while deferred:
                    deferred.pop(0)()

            def own_rows(i):
                if i < 4:
                    return i, 12
                if i > 12:
                    return 12, i + 8
                return i, i + 8

            for hh in range(2):
                X("vector", "memset", [], KTp[hh].res, bf(KTp[hh].ap)[64:128, :], 0.0)
                X("vector", "memset", [], QTp[hh].res, bf(QTp[hh].ap)[64:128, :], 0.0)
            for hh in range(2):
                DMA("gpsimd", bf(KTp[hh].ap)[64:80, 512:2048], d_pen, KTp[hh].res[1], writes=KTp[hh].res[1:4])
                DMA("gpsimd", bf(QTp[hh].ap)[64:80, 512:1536], d_qoh, QTp[hh].res[1], writes=QTp[hh].res[1:3])
            for c in range(8):
                w1v, w1r = wload(d_win1[:, c * 384:(c + 1) * 384].rearrange("(kc p) n -> p kc n", p=128))
                KTv = [bf(KTp[i].ap) for i in range(2)]
                QTv = [bf(QTp[i].ap) for i in range(2)]
                DMA("gpsimd", Kc[0:64, :, :], d_ck1[:, c * 1024:(c + 1) * 1024].rearrange("p (h k) -> p h k", h=2), Kc_b.r,
                    writes=[Kc_b.r])
                for t in range(4):
                    DMA("gpsimd", Vc[:, t, :, 0:64], d_cv1[t * 128:(t + 1) * 128, c * 128:(c + 1) * 128].rearrange("p (h d) -> p h d", h=2),
                        Vc_b.r, writes=[Vc_b.r])
                Tv = []
                for hh in range(2):
                    h = 2 * c + hh
                    tb_ = T_b[hh]
                    DMA("sync", tb_.ap[0:64, 0:960], d_rbt[h], tb_.r, writes=[tb_.r])
                    DMA("sync", tb_.ap[64:128, 64:1024], d_rbt[h], tb_.r, writes=[tb_.r])
                    X("vector", "memset", [], [tb_.r], tb_.ap[0:64, 960:1024], 0.0)
                    X("vector", "memset", [], [tb_.r], tb_.ap[64:128, 0:64], 0.0)
                    X("scalar", "activation", [tb_.r], [tb_.r], out=tb_.ap, in_=tb_.ap, func=AF.Exp)
                    t3 = tb_.ap.rearrange("p (r q) -> p r q", r=16)
                    X("vector", "tensor_tensor", [tb_.r, cm_b.r], [tb_.r], out=t3, in0=t3,
                      in1=cm_b.ap.unsqueeze(1).to_broadcast([128, 16, 64]), op=ALU.mult)
                    Tv.append(t3)
                for tb in range(4):
                    bp = nb("prep1", [2, 3, 4, 5, 6, 7])
                    for kc in range(8):
                        X("tensor", "matmul", [RH[tb], w1r], [RB[bp]], PB[bp], lhsT=w1v[:, kc, 128:256], rhs=hT[:, kc, blk(tb)],
                          start=(kc == 0), stop=(kc == 7))
                    def post_k(tb=tb, bp=bp):
                        rb_ = headnorm(bp, V_NGK)
                        for hh in range(2):
                            ps = slice(hh * 64, hh * 64 + 64)
                            X("vector", "scalar_tensor_tensor", [RB[bp], rb_.r, RC], [KTp[hh].res[tb]], out=KTv[hh][0:64, blk(tb)],
                              in0=PB[bp][ps, :], scalar=V_NGK[ps, :], in1=rb_.ap[ps, :], op0=ALU.mult, op1=ALU.mult)
                        if tb == 0:
                            X("vector", "scalar_tensor_tensor", [RB[bp], rb_.r, RC], [kf_b.r], out=kf_b.ap, in0=PB[bp],
                              scalar=V_NGK, in1=rb_.ap, op0=ALU.mult, op1=ALU.mult)
                            ks_ = rot1("kst", kst_b)
                            ksv = ks_.ap.rearrange("p (a d) -> p a d", a=4)
                            bt = nb("prep1", [2, 3, 4, 5, 6, 7])
                            for a in range(4):
                                X("tensor", "transpose", [kf_b.r, RC], [RB[bt]], out=PB[bt][:, a * 128:(a + 1) * 128],
                                  in_=kf_b.ap[:, a * 128:(a + 1) * 128], identity=ident_f)
                            X("vector", "tensor_copy", [RB[bt]], [ks_.r], out=ksv, in_=PB[bt].rearrange("p (a d) -> p a d", a=4))
                            DMA("sync", o_k1[:, c * 128:(c + 1) * 128].rearrange("(a p) d -> p a d", p=128), ksv, ks_.r, reads=[ks_.r])
                    defer(post_k)
                for qi in range(3):
                    q0 = QCOL[qi]
                    rq_ = [RH[0]] if qi == 0 else ([RH[1], RH[2]] if qi == 1 else [RH[2], RH[3]])
                    bp = nb("prep1", [2, 3, 4, 5, 6, 7])
                    for kc in range(8):
                        X("tensor", "matmul", rq_ + [w1r], [RB[bp]], PB[bp], lhsT=w1v[:, kc, 0:128], rhs=hT[:, kc, q0:q0 + 512],
                          start=(kc == 0), stop=(kc == 7))
                    def post_q(qi=qi, bp=bp):
                        rb_ = headnorm(bp, V_NGQ)
                        for hh in range(2):
                            ps = slice(hh * 64, hh * 64 + 64)
                            X("vector", "scalar_tensor_tensor", [RB[bp], rb_.r, RC], [QTp[hh].res[qi]],
                              out=QTv[hh][0:64, qi * 512:(qi + 1) * 512], in0=PB[bp][ps, :], scalar=V_NGQ[ps, :], in1=rb_.ap[ps, :],
                              op0=ALU.mult, op1=ALU.mult)
                    defer(post_q)
                for tb in range(4):
                    bp = nb("prep1", [2, 3, 4, 5, 6, 7])
                    for kc in range(8):
                        X("tensor", "matmul", [RH[tb], w1r], [RB[bp]], PB[bp], lhsT=w1v[:, kc, 256:384], rhs=hT[:, kc, blk(tb)],
                          start=(kc == 0), stop=(kc == 7))

                    def post_v(tb=tb, bp=bp):
                        vT_b = rot1("vT", vT_l)
                        vTv = bf(vT_b.ap)
                        X("scalar", "copy", [RB[bp]], [vT_b.r], out=vTv, in_=PB[bp])
                        bt = nb("prep1", [2, 3, 4, 5, 6, 7])
                        ptb = PB[bt].bitcast(BF16)
                        if tb == 0:
                            X("vector", "tensor_copy", [RB[bp]], [kf_b.r], out=kf_b.ap, in_=PB[bp])
                            for a in range(4):
                                X("tensor", "transpose", [vT_b.r, RC], [RB[bt]], out=ptb[:, a * 128:(a + 1) * 128],
                                  in_=vTv[:, a * 128:(a + 1) * 128], identity=ident_b)
                            X("vector", "tensor_copy", [RB[bt]], [Vp_b.r], out=Vp[:, :, :, 0:64],
                              in_=ptb[:, 0:512].rearrange("p (a h d) -> p a h d", a=4, h=2))
                            ks_ = rot1("kst", kst_b)
                            ksv = ks_.ap.rearrange("p (a d) -> p a d", a=4)
                            bt2 = nb("prep1", [2, 3, 4, 5, 6, 7])
                            for a in range(4):
                                X("tensor", "transpose", [kf_b.r, RC], [RB[bt2]], out=PB[bt2][:, a * 128:(a + 1) * 128],
                                  in_=kf_b.ap[:, a * 128:(a + 1) * 128], identity=ident_f)
                            X("vector", "tensor_copy", [RB[bt2]], [ks_.r], out=ksv, in_=PB[bt2].rearrange("p (a d) -> p a d", a=4))
                            DMA("sync", o_v1[:, c * 128:(c + 1) * 128].rearrange("(a p) d -> p a d", p=128), ksv, ks_.r, reads=[ks_.r])
                        else:
                            for a in range(4):
                                X("tensor", "transpose", [vT_b.r, RC], [RB[bt]], out=ptb[:, a * 128:(a + 1) * 128],
                                  in_=vTv[:, a * 128:(a + 1) * 128], identity=ident_b)
                            X("vector", "tensor_copy", [RB[bt]], [Vw_b.res[tb - 1]], out=Vw[:, (tb - 1) * 4:tb * 4, :, 0:64],
                              in_=ptb[:, 0:512].rearrange("p (a h d) -> p a h d", a=4, h=2))
                    defer(post_v)
                drain()
                for hh in range(2):
                    hidx = (c % 2) * 2 + hh
                    K_, Q_ = KTv[hh], QTv[hh]
                    rK, rQ = KTp[hh].res, QTp[hh].res
                    po1 = (hidx % 2) * 64
                    obp = nb("O1", [0, 1])
                    obs = [nb("O1", [0, 1]), nb("O1", [0, 1])]
                    cp = []

                    def cflush(keep):
                        while len(cp) > keep:
                            fn = cp.pop(0)
                            fn()
                    for s_ in range(2):
                        sb = nb("SC", [2, 3, 4, 5, 6, 7])
                        for cc_ in range(2):
                            k0 = s_ * 256 + cc_ * 128
                            X("tensor", "matmul", [rK[0], rQ[0]], [RB[sb]], PB[sb][:, cc_ * 256:(cc_ + 1) * 256], lhsT=K_[:, k0:k0 + 128],
                              rhs=Q_[:, s_ * 256:(s_ + 1) * 256], start=True, stop=True)
                        pt = rot1("pl", PL_b)
                        ptv = bf(pt.ap)[:, 0:512]
                        X("scalar", "activation", [RB[sb]], [pt.r], out=ptv, in_=PB[sb], func=AF.Exp, scale=SC1)

                        def pv_p(s_=s_, pt=pt, ptv=ptv):
                            for cc_ in range(2):
                                X("tensor", "matmul", [Vp_b.r, pt.r], [RB[obp]], PB[obp][:, s_ * 256:(s_ + 1) * 256],
                                  lhsT=Vp[:, 2 * s_ + cc_, hh, :], rhs=ptv[:, cc_ * 256:(cc_ + 1) * 256], start=(cc_ == 0), stop=(cc_ == 1))
                        cp.append(pv_p)
                        cflush(2)
                    cflush(0)
                    normalize1(obp, attn1[po1:po1 + 64, hidx // 2, 0:512], attn1_b.res[0], 512)
                    obs = [obp, obs[0]] if False else obs
                    for qb in range(2):
                        ob = obs[qb]
                        qs = slice(512 + qb * 512, 1024 + qb * 512)
                        for c4 in range(4):
                            sb = nb("SC", [2, 3, 4, 5, 6, 7])
                            X("tensor", "matmul", [Kc_b.r, rQ[1 + qb]], [RB[sb]], PB[sb], lhsT=Kc[:, hh, c4 * 128:(c4 + 1) * 128],
                              rhs=Q_[:, qs], start=True, stop=True)
                            pt = rot1("pl", PL_b)
                            ptv = bf(pt.ap)[:, 0:512]
                            X("scalar", "activation", [RB[sb]], [pt.r], out=ptv, in_=PB[sb], func=AF.Exp, scale=SC1)

                            def pv_c(ob=ob, c4=c4, pt=pt, ptv=ptv):
                                X("tensor", "matmul", [Vc_b.r, pt.r], [RB[ob]], PB[ob], lhsT=Vc[:, c4, hh, :], rhs=ptv,
                                  start=(c4 == 0), stop=False)
                            cp.append(pv_c)
                            cflush(2)
                    cflush(0)
                    pend = []

                    def rows_of(wr):
                        if wr > 22:
                            return None
                        return (0 if wr <= 11 else wr - 7), (min(15, wr) if wr < 12 else 15)
                    chunks = []
                    for m_ in range(12):
                        ra, rb_ = rows_of(2 * m_), rows_of(2 * m_ + 1)
                        ilo_c = min(ra[0], rb_[0]) if rb_ else ra[0]
                        ihi_c = max(ra[1], rb_[1]) if rb_ else ra[1]
                        chunks.append((m_, ilo_c, ihi_c))
                    last_m = {}
                    for (m_, ilo_c, ihi_c) in chunks:
                        for qb_ in range(2):
                            if max(ilo_c, qb_ * 8) <= min(ihi_c, qb_ * 8 + 7):
                                last_m[qb_] = m_

                    def flush1(keep):
                        while len(pend) > keep:
                            pl_, plv_, m_, ilo_, ihi_ = pend.pop(0)
                            for qb_ in range(2):
                                a_, b_ = max(ilo_, qb_ * 8), min(ihi_, qb_ * 8 + 7)
                                if a_ > b_:
                                    continue
                                X("tensor", "matmul", [Vw_b.res[m_ // 4], pl_.r], [RB[obs[qb_]]],
                                  PB[obs[qb_]][:, (a_ - qb_ * 8) * 64:(b_ - qb_ * 8 + 1) * 64], lhsT=Vw[:, m_, hh, :],
                                  rhs=plv_[:, (a_ - ilo_) * 64:(b_ - ilo_ + 1) * 64], start=False, stop=(m_ == last_m[qb_]))
                    for (m_, ilo, ihi) in chunks:
                        nq = ihi - ilo + 1
                        jj0 = ilo - 2 * m_ + 11
                        pair = nb("SL", [2, 4, 6])
                        sl = psum_t[:, pair * 512:pair * 512 + 1024]
                        kcol = 512 + m_ * 128
                        for part in range(2):
                            a_, b_ = part * 8, min(nq, part * 8 + 8)
                            if a_ >= b_:
                                continue
                            qc0 = 512 + (ilo + a_) * 64
                            rqs = sorted(set([1 + (ilo + a_) // 8, 1 + (ilo + b_ - 1) // 8]))
                            X("tensor", "matmul", [rK[1 + m_ // 4]] + [rQ[x_] for x_ in rqs], [RB[pair + part]],
                              sl[:, a_ * 64:b_ * 64], lhsT=K_[:, kcol:kcol + 128], rhs=Q_[:, qc0:qc0 + (b_ - a_) * 64],
                              start=True, stop=True)
                        e32 = rot1("e32", E32_b)
                        rbs = [RB[pair]] + ([RB[pair + 1]] if nq > 8 else [])
                        X("scalar", "activation", rbs, [e32.r], out=e32.ap[:, 0:nq * 64], in_=sl[:, 0:nq * 64], func=AF.Exp,
                          scale=SC1)
                        pl = rot1("pl", PL_b)
                        plv2 = bf(pl.ap)[:, 0:nq * 64]
                        X("vector", "tensor_tensor", [e32.r, T_b[hh].r], [pl.r], out=plv2.rearrange("p (r q) -> p r q", q=64),
                          in0=e32.ap[:, 0:nq * 64].rearrange("p (r q) -> p r q", q=64), in1=Tv[hh][:, jj0:jj0 + nq, :],
                          op=ALU.mult)
                        pend.append((pl, plv2, m_, ilo, ihi))
                        flush1(2)
                    flush1(0)
                    for qb in range(2):
                        qs = slice(512 + qb * 512, 1024 + qb * 512)
                        normalize1(obs[qb], attn1[po1:po1 + 64, hidx // 2, qs], attn1_b.res[1 + qb], 512)
                if c % 2 == 1:
                    grp = c // 2
                    wo1v, wo1r = wload(d_wout1[grp * 256:(grp + 1) * 256, :].rearrange("(c p) n -> p c n", p=128))
                    for qi in range(3):
                        q0 = QCOL[qi]
                        rxs = [RX[0]] if qi == 0 else ([RX[1], RX[2]] if qi == 1 else [RX[2], RX[3]])
                        cd = 0 if qi == 0 else 1
                        for m in range(8):
                            b = nb("prep1", [2, 3, 4, 5, 6, 7])
                            for hq in range(2):
                                X("tensor", "matmul", [attn1_b.res[qi], wo1r], [RB[b]], PB[b], lhsT=wo1v[:, hq, m * 128:(m + 1) * 128],
                                  rhs=attn1[:, hq, qi * 512:(qi + 1) * 512], start=(hq == 0), stop=(hq == 3 - 2))
                            X("vector", "scalar_tensor_tensor", [RB[b], RM] + rxs, rxs, out=xT[:, m, q0:q0 + 512], in0=PB[b],
                              scalar=MOD(1, 2, cd)[:, m:m + 1], in1=xT[:, m, q0:q0 + 512], op0=ALU.mult, op1=ALU.add)
            A.free(hT_b, rstd_b, Vw_b, Vp_b, Kc_b, Vc_b, cm_b, kf_b, attn1_b, *vT_l, *tmpb, *sqp, *KTp, *QTp, *T_b, *kst_b,
                   *E32_b, *PL_b, *PT1_b, *rden1)

        if dbg and stage >= 5:
            ostg_d = A.alloc("ostg_dbg", 4096)
            for tb in range(4):
                store_tokens(xT[:, :, blk(tb)], RX[tb], 512, o_dbg[tb * 512:(tb + 1) * 512, :], ostg_d)
            A.free(ostg_d)

        if stage >= 6:
            mlp(1, [(0, [RX[0]], 0), (768, [RX[1], RX[2]], 1), (1280, [RX[2], RX[3]], 1)])
            ostg = A.alloc("ostg", 4096)
            store_tokens(xT[:, :, 0:512], RX[0], 512, o_yp, ostg)
            store_tokens(xT[:, :, 768:1280], RX[1], 512, o_ys[0:512, :], ostg, rextra=[RX[2]])
            store_tokens(xT[:, :, 1280:1792], RX[2], 512, o_ys[512:1024, :], ostg, rextra=[RX[3]])
            A.free(ostg)

        P.emit(nc, st)
    return nc


def _pvec(v, n):
    return np.ascontiguousarray(np.asarray(v, np.float32).reshape(n, 128).T)


_SWAP = np.array([(j + 8) if (j % 16) < 8 else (j - 8) for j in range(32)])
_SGN = np.array([-1.0 if (j % 16) < 8 else 1.0 for j in range(32)])


def _rope_tables(rows, cols):
    f = THETA ** (-np.arange(0, 16, 2, dtype=np.float64) / 16.0)
    n = len(rows)
    ang = np.zeros((32, n))
    for j in range(32):
        pos = rows if j < 16 else cols
        ang[j] = pos.astype(np.float32).astype(np.float64) * np.float32(f[j % 8]).astype(np.float64)
    ang = ang.astype(np.float32).astype(np.float64)
    return np.concatenate([np.cos(ang), _SGN[:, None] * np.sin(ang)], axis=0).astype(np.float32)


def _prep_inputs(inp):
    f32 = np.float32
    g = {k: np.asarray(v) for k, v in inp.items()}
    shared = {}
    shared["w_ada0"] = g["w_ada_l0"]
    shared["w_ada1"] = g["w_ada_l1"]
    shared["b_ada0"] = _pvec(g["b_ada_l0"], 48)
    shared["b_ada1"] = _pvec(g["b_ada_l1"], 48)
    shared["w_mlp1_0"] = g["w_mlp1_l0"]
    shared["w_mlp1_1"] = g["w_mlp1_l1"]
    shared["w_mlp2_0"] = g["w_mlp2_l0"]
    shared["w_mlp2_1"] = g["w_mlp2_l1"]
    vec = np.zeros((128, 64), f32)
    vec[:, 0:8] = _pvec(g["norm_mix_l0"], 8)
    vec[:, 8:16] = _pvec(g["norm_mlp_l0"], 8)
    vec[:, 16:24] = _pvec(g["norm_mix_l1"], 8)
    vec[:, 24:32] = _pvec(g["norm_mlp_l1"], 8)
    vec[:, 32:34] = _pvec(g["q_lora_norm_l0"], 2)
    vec[:, 34] = g["kv_lora_norm_l0"]
    gq, gk = g["mla_q_norm_l0"], g["mla_k_norm_l0"]
    vec[0:96, 35] = gq
    vec[64:96, 36] = gq[64 + _SWAP]
    vec[0:96, 37] = gk
    vec[64:96, 38] = gk[64 + _SWAP]
    vec[:, 39:43] = _pvec(g["pool_scale_l0"], 4)
    vec[:, 43] = np.tile(g["na_q_norm_l1"], 2)
    vec[:, 44] = np.tile(g["na_k_norm_l1"], 2)
    shared["vecs"] = vec
    w_in0 = g["w_in_l0"]
    shared["w_in0_pool"] = np.ascontiguousarray(w_in0[:, 0:512])
    shared["w_in0_q"] = np.ascontiguousarray(w_in0[:, 512:768])
    kr = w_in0[:, 896:928]
    dummy = w_in0[:, 768:832]
    shared["w_in0_kv"] = np.ascontiguousarray(np.concatenate([w_in0[:, 768:896], dummy, kr, dummy, kr[:, _SWAP]], axis=1))
    wq = g["w_q_up_l0"].reshape(256, 8, 96)
    wq_aug = np.concatenate([wq, wq[:, :, 0:64], wq[:, :, 64 + _SWAP]], axis=2)
    shared["w_qup_aug"] = np.ascontiguousarray(wq_aug.reshape(256, 1536))
    shared["w_kvup"] = g["w_kv_up_l0"]
    shared["w_pool"] = np.ascontiguousarray(g["w_pool_l0"].transpose(1, 0, 2).reshape(128, 512))
    shared["w_out0"] = g["w_out_l0"]
    t = np.arange(4096)
    shared["rope_k"] = _rope_tables(t // 64, t % 64)
    w1 = g["w_in_l1"]
    pieces = []
    for c in range(8):
        pieces += [w1[:, c * 128:(c + 1) * 128], w1[:, 1024 + c * 128:1024 + (c + 1) * 128],
                   w1[:, 2048 + c * 128:2048 + (c + 1) * 128]]
    shared["w_in1"] = np.ascontiguousarray(np.concatenate(pieces, axis=1))
    shared["w_out1"] = g["w_out_l1"]
    rb = g["rel_bias_l1"]
    kc = np.arange(64)
    dc = np.clip(kc[:, None] - kc[None, :] + 15, 0, 30)
    rbt = rb[:, ::-1, :][:, :, dc]
    shared["rbt"] = np.ascontiguousarray(rbt.transpose(0, 2, 1, 3).reshape(16, 64, 960))
    cs = np.clip(kc - 8, 0, 48)
    cm = (kc[:, None] >= cs[None, :]) & (kc[:, None] < cs[None, :] + 16)
    shared["cmask"] = cm.astype(f32)
    maps = []
    xs = g["x_sample"]
    for core in range(8):
        b, q = core // 4, core % 4
        m = dict(shared)
        m["xp"] = np.ascontiguousarray(g["x_prompt"][2 * core:2 * core + 2].reshape(512, D))
        r0 = 16 * q - 4
        xw = np.zeros((24, 64, D), f32)
        xpre = np.zeros((128, D), f32)
        valid = np.zeros(25, f32)
        xg = xs[b].reshape(64, 64, D)
        for wr in range(24):
            if 0 <= r0 + wr < 64:
                xw[wr] = xg[r0 + wr]
                valid[1 + wr] = 1.0
        if 0 <= r0 - 1 < 64:
            xpre[0:64] = xg[r0 - 1]
            valid[0] = 1.0
        m["xw"] = xw.reshape(1536, D)
        m["xpre"] = xpre
        m["xb"] = np.ascontiguousarray(xs[b])
        cond = np.stack([g["c_ctx"], g["c"][b]], axis=1)
        m["condT"] = np.ascontiguousarray(cond.reshape(8, 128, 2).transpose(1, 0, 2).reshape(128, 16))
        wt = np.arange(1536)
        m["rope_q"] = _rope_tables(r0 + wt // 64, wt % 64)
        m["cache_ckvT"] = np.ascontiguousarray(g["cache_l0_mla_ckv"][b].T)
        m["cache_kropeT"] = np.ascontiguousarray(g["cache_l0_mla_krope"][b].T)
        m["pool_mask"] = np.ascontiguousarray(np.broadcast_to(np.repeat(valid, 64)[None, :], (128, 1600))).astype(f32)
        pinv = np.zeros((4, 4, 8), f32)
        for gi, w in enumerate((2, 4, 8, 16)):
            for bd, (S, first, exists) in enumerate([(256, True, True), (256, False, True),
                                                    (4096, True, q == 0), (4096, False, q == 3)]):
                for j in range(8):
                    tt = j if first else S - 8 + j
                    if exists:
                        lo = np.clip(tt - w // 2, 0, S)
                        hi = np.clip(tt - w // 2 + w, 0, S)
                        pinv[gi, bd, j] = 1.0 / float(hi - lo)
                    else:
                        pinv[gi, bd, j] = 1.0 / w
        m["pool_inv"] = np.ascontiguousarray(np.broadcast_to(pinv.reshape(1, 128), (128, 128))).astype(f32)
        pen = np.full((16, 24), NEGBIG, f32)
        for i in range(16):
            gr = 16 * q + i
            rs = int(np.clip(gr - 4, 0, 56))
            for kr_ in range(rs, rs + 8):
                pen[i, kr_ - r0] = 0.0
        m["pen"] = np.ascontiguousarray(np.repeat(pen, 64, axis=1))
        m["qoh"] = np.ascontiguousarray(np.repeat(np.eye(16, dtype=f32), 64, axis=1))
        m["cache_k1T"] = np.ascontiguousarray(g["cache_l1_na_k"][b].transpose(2, 1, 0).reshape(64, 16 * 512))
        m["cache_v1"] = np.ascontiguousarray(g["cache_l1_na_v"][b].reshape(512, 1024))
        maps.append(m)
    return maps


_NC_CACHE = {}
_BUILD_ARGS = {}


def kernel(**inputs):
    maps = _prep_inputs(inputs)
    if "nc" not in _NC_CACHE:
        _NC_CACHE["nc"] = build_program(**_BUILD_ARGS)
    nc = _NC_CACHE["nc"]
    res = run_bass_kernel_spmd(nc, maps, core_ids=list(range(8)))
    R = res.results
    yp = np.concatenate([R[c]["yp"].reshape(2, 256, D) for c in range(8)], axis=0)
    ys = np.stack([np.concatenate([R[b * 4 + q]["ys"] for q in range(4)], axis=0) for b in range(2)], axis=0)
    ckv = np.concatenate([R[c]["ckv_new"].reshape(2, 256, 128) for c in range(8)], axis=0)
    kr = np.concatenate([R[c]["krope_new"].reshape(2, 256, 32) for c in range(8)], axis=0)
    k1 = np.concatenate([R[c]["k_new"].reshape(2, 256, 16, 64) for c in range(8)], axis=0)
    v1 = np.concatenate([R[c]["v_new"].reshape(2, 256, 16, 64) for c in range(8)], axis=0)
    return (yp.astype(np.float32), ys.astype(np.float32), ckv.astype(np.float32), kr.astype(np.float32),
            k1.astype(np.float32), v1.astype(np.float32))
```

```python
import numpy as np
from contextlib import ExitStack
import concourse.bass as bass
import concourse.mybir as mybir
from concourse.bass_utils import run_bass_kernel_spmd

F32 = mybir.dt.float32
BF16 = mybir.dt.bfloat16
AF = mybir.ActivationFunctionType
ALU = mybir.AluOpType

ENGS = ("tensor", "vector", "scalar", "gpsimd", "sync")
D = 1024
NCH = 8
EPS = 1e-6
THETA = 10000.0
NEGBIG = -30000.0


class Res:
    __slots__ = ("name", "last_w", "readers", "dsem_idx", "dcount", "excl")

    def __init__(self, name, excl=False):
        self.name = name
        self.excl = excl
        self.last_w = None
        self.readers = []
        self.dsem_idx = None
        self.dcount = 0

    def pending(self):
        r = list(self.readers)
        if self.last_w is not None:
            r.append(self.last_w)
        return r


class Op:
    __slots__ = ("eng", "fn", "deps", "signal", "sigval", "is_dma", "dsem_idx", "dval")

    def __init__(self, eng, fn, is_dma=False):
        self.eng = eng
        self.fn = fn
        self.deps = []
        self.signal = False
        self.sigval = 0
        self.is_dma = is_dma
        self.dsem_idx = None
        self.dval = 0


class Prog:
    def __init__(self):
        self.ops = []
        self.n_dsem = 0

    def _track(self, op, reads, writes):
        for r in reads:
            if r.last_w is not None:
                op.deps.append(r.last_w)
            if r.excl:
                for rd in r.readers:
                    if rd.eng != op.eng:
                        op.deps.append(rd)
            r.readers.append(op)
        for w in writes:
            if w.last_w is not None:
                op.deps.append(w.last_w)
            for rd in w.readers:
                if rd is not op:
                    op.deps.append(rd)
            w.last_w = op
            w.readers = []

    def op(self, eng, fn, reads=(), writes=()):
        o = Op(eng, fn)
        self._track(o, reads, writes)
        self.ops.append(o)
        return o

    def dma(self, eng, fn, res, reads=(), writes=()):
        o = Op(eng, fn, is_dma=True)
        if res.dsem_idx is None:
            res.dsem_idx = self.n_dsem
            self.n_dsem += 1
        res.dcount += 1
        o.dsem_idx = res.dsem_idx
        o.dval = 16 * res.dcount
        self._track(o, reads, writes)
        self.ops.append(o)
        return o

    def emit(self, nc, stack, final_eng="sync"):
        fin = Op(final_eng, None)
        last_by_sem = {}
        for o in self.ops:
            if o.is_dma:
                last_by_sem[o.dsem_idx] = o
        fin.deps = list(last_by_sem.values())
        ops = self.ops + [fin]
        for o in ops:
            for d in o.deps:
                if not d.is_dma:
                    if d.eng == "tensor" and o.eng == "tensor" and not o.is_dma:
                        continue
                    d.signal = True
        cnt = {e: 0 for e in ENGS}
        for o in ops:
            if o.signal:
                cnt[o.eng] += 1
                o.sigval = cnt[o.eng]
        esem = {e: stack.enter_context(nc.semaphore("es_" + e)) for e in ENGS}
        dsem = [stack.enter_context(nc.semaphore("ds_%d" % i)) for i in range(self.n_dsem)]
        per_eng = {e: [o for o in ops if o.eng == e] for e in ENGS}
        block = stack.enter_context(nc.Block())

        def run(e, engobj):
            waited = {}
            for o in per_eng[e]:
                need = {}
                for d in o.deps:
                    if d.is_dma:
                        key = ("d", d.dsem_idx)
                        val = d.dval
                    else:
                        if d.eng == "tensor" and e == "tensor" and not o.is_dma:
                            continue
                        key = ("e", d.eng)
                        val = d.sigval
                    if val > need.get(key, 0):
                        need[key] = val
                for key, val in need.items():
                    if waited.get(key, 0) >= val:
                        continue
                    waited[key] = val
                    s = dsem[key[1]] if key[0] == "d" else esem[key[1]]
                    engobj.wait_ge(s, val)
                if o.fn is None:
                    continue
                inst = o.fn(engobj)
                if o.is_dma:
                    inst.then_inc(dsem[o.dsem_idx], 16)
                elif o.signal:
                    inst.then_inc(esem[e], 1)

        @block.tensor
        def _(eng):
            run("tensor", eng)

        @block.vector
        def _(eng):
            run("vector", eng)

        @block.scalar
        def _(eng):
            run("scalar", eng)

        @block.gpsimd
        def _(eng):
            run("gpsimd", eng)

        @block.sync
        def _(eng):
            run("sync", eng)


class Buf:
    def __init__(self, ap, res, c0, c1):
        self.ap = ap
        self.res = res
        self.c0 = c0
        self.c1 = c1

    @property
    def r(self):
        return self.res[0]


class Arena:
    def __init__(self, ap, ncols):
        self.ap = ap
        self.ncols = ncols
        self.live = []
        self.freed = []

    def alloc(self, name, ncols, nres=1):
        ivs = sorted((c0, c1) for c0, c1, _ in self.live)
        pos = 0
        for c0, c1 in ivs:
            if c0 - pos >= ncols:
                break
            pos = max(pos, c1)
        if pos + ncols > self.ncols:
            lay = sorted((c0, c1, b.res[0].name) for c0, c1, b in self.live)
            raise AssertionError("SBUF arena overflow allocating %s (%d cols at %d); live=%s" % (name, ncols, pos, lay))
        res = [Res("%s_%d" % (name, i)) for i in range(nres)]
        keep = []
        for c0, c1, ops in self.freed:
            if c0 < pos + ncols and pos < c1:
                for r in res:
                    r.readers.extend(ops)
                if c0 < pos:
                    keep.append((c0, pos, ops))
                if c1 > pos + ncols:
                    keep.append((pos + ncols, c1, ops))
            else:
                keep.append((c0, c1, ops))
        self.freed = keep
        b = Buf(self.ap[:, pos:pos + ncols], res, pos, pos + ncols)
        self.live.append((pos, pos + ncols, b))
        return b

    def free(self, *bufs):
        for b in bufs:
            self.live = [t for t in self.live if t[2] is not b]
            ops = []
            for r in b.res:
                ops.extend(r.pending())
            self.freed.append((b.c0, b.c1, ops))


def build_program(stage=99, dbg=False, mini=None, skip=()):
    nc = bass.Bass("TRN2", target_bir_lowering=False)
    P = Prog()

    def din(name, shape, dt=F32):
        if mini is not None and name not in mini:
            shape = [1, 1]
        return nc.dram_tensor(name, list(shape), dt, kind="ExternalInput").ap()

    def dout(name, shape):
        return nc.dram_tensor(name, list(shape), F32, kind="ExternalOutput").ap()

    d_xp = din("xp", [512, D])
    d_xw = din("xw", [1536, D])
    d_xpre = din("xpre", [128, D])
    d_xb = din("xb", [4096, D])
    d_cond = din("condT", [128, 16])
    d_wada = [din("w_ada0", [D, 6 * D]), din("w_ada1", [D, 6 * D])]
    d_bada = [din("b_ada0", [128, 48]), din("b_ada1", [128, 48])]
    d_vec = din("vecs", [128, 64])
    d_win0_pool = din("w_in0_pool", [D, 512])
    d_win0_q = din("w_in0_q", [D, 256])
    d_win0_kv = din("w_in0_kv", [D, 320])
    d_wqup = din("w_qup_aug", [256, 1536])
    d_wkvup = din("w_kvup", [128, 1024])
    d_wpool = din("w_pool", [128, 512])
    d_wout0 = din("w_out0", [D, D])
    d_mlp1 = [din("w_mlp1_0", [D, 4 * D]), din("w_mlp1_1", [D, 4 * D])]
    d_mlp2 = [din("w_mlp2_0", [4 * D, D]), din("w_mlp2_1", [4 * D, D])]
    d_ropek = din("rope_k", [64, 4096])
    d_ropeq = din("rope_q", [64, 1536])
    d_cckv = din("cache_ckvT", [128, 512])
    d_ckr = din("cache_kropeT", [32, 512])
    d_pmask = din("pool_mask", [128, 1600])
    d_pinv = din("pool_inv", [128, 128])
    d_win1 = din("w_in1", [D, 8 * 384])
    d_wout1 = din("w_out1", [D, D])
    d_rbt = din("rbt", [16, 64, 960])
    d_cmask = din("cmask", [64, 64])
    d_pen = din("pen", [16, 1536])
    d_qoh = din("qoh", [16, 1024])
    d_ck1 = din("cache_k1T", [64, 16 * 512])
    d_cv1 = din("cache_v1", [512, 1024])

    o_yp = dout("yp", [512, D])
    o_ys = dout("ys", [1024, D])
    o_ckv = dout("ckv_new", [512, 128])
    o_kr = dout("krope_new", [512, 32])
    o_k1 = dout("k_new", [512, D])
    o_v1 = dout("v_new", [512, D])
    o_dbg = dout("dbg", [2048, D]) if dbg else None

    with ExitStack() as st:
        NCOLS = 53000
        arena_t = st.enter_context(nc.sbuf_tensor("arena", [128, NCOLS], F32))
        psum_t = st.enter_context(nc.psum_tensor("psum", [128, 8 * 512], F32))
        A = Arena(arena_t, NCOLS)
        PB = [psum_t[:, i * 512:(i + 1) * 512] for i in range(8)]
        RB = [Res("bank%d" % i, excl=True) for i in range(8)]
        bank_rr = {}

        def nb(group, banks):
            i = bank_rr.get(group, 0)
            bank_rr[group] = i + 1
            return banks[i % len(banks)]

        def X(eng, meth, reads, writes, *a, **kw):
            def fn(e):
                try:
                    return getattr(e, meth)(*a, **kw)
                except Exception as ex:
                    desc = [getattr(v, "shape", v) for v in a] + ["%s=%s" % (k, getattr(v, "shape", v)) for k, v in kw.items()]
                    raise RuntimeError("op %s.%s failed: %s | %s" % (eng, meth, desc, ex))
            return P.op(eng, fn, reads, writes)

        def DMA(eng, out, in_, res, reads=(), writes=()):
            return P.dma(eng, lambda e: e.dma_start(out=out, in_=in_), res, reads, writes)

        def bf(buf_ap):
            return buf_ap.bitcast(BF16)

        cst = A.alloc("consts", 128 + 64 + 64 + 64 + 128 + 1 + 64 + 16 + 48 * 2 + 48 * 2 + 2 * 6 * 16)
        cc = [0]

        def ccarve(n):
            a = cst.ap[:, cc[0]:cc[0] + n]
            cc[0] += n
            return a
        ident_f = ccarve(128)
        ident_b = bf(ccarve(64))
        ones_b = bf(ccarve(64))
        blk_b = bf(ccarve(64))
        ones_f = ccarve(128)
        eps_t = ccarve(1)
        vec = ccarve(64)
        cond_f = ccarve(16)
        RC = cst.r
        X("gpsimd", "memset", [], [RC], ident_f, 1.0)
        X("gpsimd", "affine_select", [RC], [RC], out=ident_f, in_=ident_f, pattern=[[-1, 128]],
          compare_op=ALU.is_equal, fill=0.0, base=0, channel_multiplier=1)
        X("gpsimd", "memset", [], [RC], ones_f, 1.0)
        X("gpsimd", "memset", [], [RC], eps_t, EPS)
        X("vector", "tensor_copy", [RC], [RC], out=ident_b, in_=ident_f)
        X("vector", "tensor_copy", [RC], [RC], out=ones_b, in_=ones_f)
        X("gpsimd", "memset", [], [RC], blk_b, 0.0)
        X("gpsimd", "memset", [], [RC], blk_b[0:64, 0:64], 1.0)
        X("gpsimd", "memset", [], [RC], blk_b[64:128, 64:128], 1.0)
        DMA("sync", vec, d_vec, RC, writes=[RC])
        DMA("sync", cond_f, d_cond, RC, writes=[RC])
        V_NMIX = [vec[:, 0:8], vec[:, 16:24]]
        V_NMLP = [vec[:, 8:16], vec[:, 24:32]]
        V_QL = vec[:, 32:34]
        V_KVL = vec[:, 34:35]
        V_GQ, V_GQS, V_GK, V_GKS = vec[:, 35:36], vec[:, 36:37], vec[:, 37:38], vec[:, 38:39]
        V_PSC = vec[:, 39:43]
        V_NGQ, V_NGK = vec[:, 43:44], vec[:, 44:45]

        if stage == -3:
            P.emit(nc, st)
            return nc
        NSLOT = 4
        slots = [A.alloc("wslot%d" % i, 2048) for i in range(NSLOT)]
        slot_rr = [0]

        pinned = set()

        def wload(dram_ap, shape_str=None, parts=128, **kw):
            while (slot_rr[0] % NSLOT) in pinned:
                slot_rr[0] += 1
            wload.last = slot_rr[0] % NSLOT
            s = slots[slot_rr[0] % NSLOT]
            slot_rr[0] += 1
            shp = list(dram_ap.shape)
            n = int(np.prod(shp[1:]))
            assert n <= 4096 and shp[0] == parts
            flat = bf(s.ap)[0:parts, 0:n]
            if len(shp) == 3:
                view = flat.rearrange("p (a b) -> p a b", a=shp[1])
            else:
                view = flat
            DMA("gpsimd", view, dram_ap, s.r, writes=[s.r])
            return view, s.r

        mods_b = A.alloc("mods", 2 * 6 * 16)
        RM = mods_b.r
        xT_b = A.alloc("xT", 8 * 2048, nres=4)
        xT = xT_b.ap.rearrange("p (c t) -> p c t", c=8)
        RX = xT_b.res
        hT_b = A.alloc("hT", 8 * 2048 // 2, nres=4)
        hT = bf(hT_b.ap).rearrange("p (c t) -> p c t", c=8)
        RH = hT_b.res

        def blk(tb):
            return slice(tb * 512, (tb + 1) * 512)

        def rstd_from(ps_ap, out_ap, inv_n, rows, rps, rout):
            X("scalar", "activation", [rps, RC], [rout], out=out_ap, in_=ps_ap, func=AF.Ln,
              bias=eps_t[rows, 0:1], scale=inv_n)
            X("scalar", "activation", [rout], [rout], out=out_ap, in_=out_ap, func=AF.Exp, scale=-0.5)

        def load_xT(dram_ap, n, dst, rdst, stg, eng_alt=0):
            nt = max(1, n // 128)
            pp = min(n, 128)
            sv = stg.ap.rearrange("p (a d) -> p a d", a=4)
            if n >= 128:
                DMA("sync", sv[:, 0:nt, :], dram_ap.rearrange("(a p) d -> p a d", p=128), stg.r, writes=[stg.r])
            else:
                DMA("sync", sv[0:pp, 0, :], dram_ap, stg.r, writes=[stg.r])
            for c in range(8):
                b = nb("ld", [0, 1, 2, 3])
                for a in range(nt):
                    X("tensor", "transpose", [stg.r, RC], [RB[b]], out=PB[b][:, a * pp:(a + 1) * pp],
                      in_=sv[0:pp, a, c * 128:(c + 1) * 128], identity=ident_f[0:pp, 0:pp])
                if (c + eng_alt) % 2 == 0:
                    X("vector", "tensor_copy", [RB[b]], [rdst], out=dst[:, c, 0:n], in_=PB[b][:, 0:n])
                else:
                    X("scalar", "copy", [RB[b]], [rdst], out=dst[:, c, 0:n], in_=PB[b][:, 0:n])

        def normmod(xsrc, rx, n, Avec, shvec, hdst, rh, sq, tmp, rstd):
            sqv = bf(sq.ap)[:, 0:8 * n].rearrange("p (c t) -> p c t", c=8)
            rxl = list(rx) if isinstance(rx, (list, tuple)) else [rx]
            b = nb("nm", [4, 5])
            for c in range(8):
                X("scalar", "activation", rxl, [sq.res[c % len(sq.res)]], out=sqv[:, c, :], in_=xsrc[:, c, :], func=AF.Square)
                X("tensor", "matmul", [sq.res[c % len(sq.res)], RC], [RB[b]], PB[b][:, 0:n], lhsT=ones_b, rhs=sqv[:, c, :],
                  start=(c == 0), stop=(c == 7))
            rs = rstd.ap[:, 0:n]
            rstd_from(PB[b][:, 0:n], rs, 1.0 / D, slice(0, 128), RB[b], rstd.r)
            for c in range(8):
                t = tmp[c % len(tmp)]
                X("vector", "scalar_tensor_tensor", rxl + [rstd.r, RM], [t.r], out=t.ap[:, 0:n], in0=xsrc[:, c, :],
                  scalar=Avec[:, c:c + 1], in1=rs, op0=ALU.mult, op1=ALU.mult)
                X("scalar", "activation", [t.r, RM], [rh], out=hdst[:, c, :], in_=t.ap[:, 0:n], func=AF.Identity,
                  bias=shvec[:, c:c + 1], scale=1.0)

        modT = [mods_b.ap[:, l * 96:(l + 1) * 96].rearrange("p (j k) -> p j k", k=2) for l in range(2)]
        cs_b = A.alloc("cond_silu", 8)
        csb = bf(cs_b.ap).rearrange("p (c k) -> p c k", k=2)
        X("scalar", "activation", [RC], [cs_b.r], out=csb, in_=cond_f.rearrange("p (c k) -> p c k", k=2), func=AF.Silu)
        mrow = [A.alloc("mrow%d" % i, 512) for i in range(2)]
        bada_b = A.alloc("bada", 96)
        for l in range(2):
            DMA("sync", bada_b.ap[:, l * 48:(l + 1) * 48], d_bada[l], bada_b.r, writes=[bada_b.r])
        if mini is not None:
            X("vector", "memset", [], [RM], mods_b.ap, 0.5)
        def ada_piece(l, pc):
            tbk = 6 + l
            wv, wr = wload(d_wada[l][:, pc * 512:(pc + 1) * 512].rearrange("(c p) n -> p c n", p=128))
            b = nb("ada", [0, 1, 2, 3])
            for kc in range(8):
                X("tensor", "matmul", [cs_b.r, wr], [RB[b]], PB[b][0:2, :], lhsT=csb[:, kc, :], rhs=wv[:, kc, :],
                  start=(kc == 0), stop=(kc == 7))
            mr = mrow[pc % 2]
            X("vector", "tensor_copy", [RB[b]], [mr.r], out=mr.ap[0:2, :], in_=PB[b][0:2, :])
            for j in range(4):
                jj = pc * 4 + j
                X("tensor", "matmul", [mr.r, RC], [RB[tbk]], PB[tbk][:, jj * 2:jj * 2 + 2],
                  lhsT=mr.ap[0:2, j * 128:(j + 1) * 128], rhs=ident_f[0:2, 0:2], start=True, stop=True)

        def ada_finish(l):
            tbk = 6 + l
            for k in range(2):
                X("vector", "tensor_tensor", [RB[tbk], bada_b.r], [RM], out=modT[l][:, :, k],
                  in0=PB[tbk][:, 0:96].rearrange("p (j k) -> p j k", k=2)[:, :, k], in1=bada_b.ap[:, l * 48:(l + 1) * 48],
                  op=ALU.add)
            for k in range(2):
                X("vector", "scalar_tensor_tensor", [RM, RC], [RM], out=modT[l][:, 8:16, k], in0=modT[l][:, 8:16, k],
                  scalar=1.0, in1=V_NMIX[l], op0=ALU.add, op1=ALU.mult)
                X("vector", "scalar_tensor_tensor", [RM, RC], [RM], out=modT[l][:, 32:40, k], in0=modT[l][:, 32:40, k],
                  scalar=1.0, in1=V_NMLP[l], op0=ALU.add, op1=ALU.mult)

        ada_deferred = (mini is None) and stage >= 3

        def MOD(l, kind, cond):
            return modT[l][:, kind * 8:(kind + 1) * 8, cond]

        def cond_of(tb):
            return 0 if tb == 0 else 1

        def store_tokens(xsrc, rsrc, n, dram_ap, ostg, rextra=()):
            ov = ostg.ap.rearrange("p (a d) -> p a d", a=4)
            nt = n // 128
            assert nt <= 4
            for a in range(nt):
                for half in range(2):
                    b = nb("st", [0, 1, 2, 3])
                    for cc_ in range(4):
                        c = half * 4 + cc_
                        X("tensor", "transpose", [rsrc, RC] + list(rextra), [RB[b]], out=PB[b][:, cc_ * 128:(cc_ + 1) * 128],
                          in_=xsrc[:, c, a * 128:(a + 1) * 128], identity=ident_f)
                    if half == 0:
                        X("vector", "tensor_copy", [RB[b]], [ostg.r], out=ov[:, a, 0:512], in_=PB[b])
                    else:
                        X("scalar", "copy", [RB[b]], [ostg.r], out=ov[:, a, 512:1024], in_=PB[b])
            DMA("sync", dram_ap.rearrange("(a p) d -> p a d", p=128), ov[:, 0:nt, :], ostg.r, reads=[ostg.r])

        def epilogue(ps, rps, l, kind, tb, m):
            X("vector", "scalar_tensor_tensor", [rps, RM, RX[tb]], [RX[tb]], out=xT[:, m, blk(tb)], in0=ps,
              scalar=MOD(l, kind, cond_of(tb))[:, m:m + 1], in1=xT[:, m, blk(tb)], op0=ALU.mult, op1=ALU.add)

        def mlp(l, segs):
            ns = len(segs)
            hb = A.alloc("hT_mlp", 8 * 512 * ns // 2, nres=ns)
            hv = bf(hb.ap).rearrange("p (c t) -> p c t", c=8)
            sq = A.alloc("sq_m", 2048, nres=8)
            tmps = [A.alloc("tmp_m%d" % i, 512) for i in range(2)]
            rstd = A.alloc("rstd_m", 512)
            for si, (c0, rl, cd) in enumerate(segs):
                normmod(xT[:, :, c0:c0 + 512], rl, 512, MOD(l, 4, cd), MOD(l, 3, cd),
                        hv[:, :, si * 512:(si + 1) * 512], hb.res[si], sq, tmps, rstd)
            A.free(sq, rstd)
            ag = [A.alloc("ag%d" % i, 4 * 512 * ns // 2, nres=ns) for i in range(2)]
            agv = [bf(a_.ap).rearrange("p (j t) -> p j t", j=4) for a_ in ag]
            wts = {}

            def ph1(g):
                w1, r1 = wload(d_mlp1[l][:, g * 512:(g + 1) * 512].rearrange("(c p) n -> p c n", p=128))
                w2, r2 = wload(d_mlp2[l][g * 512:(g + 1) * 512, :].rearrange("(j p) n -> p j n", p=128))
                wts[g] = (w2, r2)
                for si in range(ns):
                    ss = slice(si * 512, (si + 1) * 512)
                    for j in range(4):
                        b = nb("m1", [0, 1, 2, 3])
                        for c in range(8):
                            X("tensor", "matmul", [hb.res[si], r1], [RB[b]], PB[b], lhsT=w1[:, c, j * 128:(j + 1) * 128],
                              rhs=hv[:, c, ss], start=(c == 0), stop=(c == 7))
                        t = tmps[(si * 4 + j) % 2]
                        X("scalar", "activation", [RB[b]], [t.r], out=t.ap, in_=PB[b], func=AF.Relu)
                        X("vector", "tensor_tensor", [t.r], [ag[g % 2].res[si]], out=agv[g % 2][:, j, ss], in0=t.ap,
                          in1=t.ap, op=ALU.mult)

            def ph2(g):
                w2, r2 = wts.pop(g)
                for si, (c0, rl, cd) in enumerate(segs):
                    ss = slice(si * 512, (si + 1) * 512)
                    for m in range(8):
                        b = nb("m2", [4, 5, 6, 7])
                        for j in range(4):
                            X("tensor", "matmul", [ag[g % 2].res[si], r2], [RB[b]], PB[b], lhsT=w2[:, j, m * 128:(m + 1) * 128],
                              rhs=agv[g % 2][:, j, ss], start=(j == 0), stop=(j == 3))
                        X("vector", "scalar_tensor_tensor", [RB[b], RM] + list(rl), list(rl), out=xT[:, m, c0:c0 + 512], in0=PB[b],
                          scalar=MOD(l, 5, cd)[:, m:m + 1], in1=xT[:, m, c0:c0 + 512], op0=ALU.mult, op1=ALU.add)
            ph1(0)
            for g in range(1, 8):
                ph1(g)
                ph2(g - 1)
            ph2(7)
            A.free(hb, *tmps, *ag)

        stg = A.alloc("stg", 4096)
        load_xT(d_xp, 512, xT[:, :, blk(0)], RX[0], stg)
        for i in range(3):
            load_xT(d_xw[i * 512:(i + 1) * 512, :], 512, xT[:, :, blk(i + 1)], RX[i + 1], stg, eng_alt=i)
        if "x1" in skip:
            P.emit(nc, st)
            return nc
        xpre_b = A.alloc("xpreT", 8 * 128)
        xpreT = xpre_b.ap.rearrange("p (c t) -> p c t", c=8)
        load_xT(d_xpre, 128, xpreT, xpre_b.r, stg)
        hpre_b = A.alloc("hpreT", 8 * 128 // 2)
        hpreT = bf(hpre_b.ap).rearrange("p (c t) -> p c t", c=8)
        if "x2" in skip:
            P.emit(nc, st)
            return nc
        if mini is None:
            for pc in range(12):
                ada_piece(0, pc)
            ada_finish(0)
            if not ada_deferred:
                for pc in range(12):
                    ada_piece(1, pc)
                ada_finish(1)
        if not ada_deferred:
            A.free(cs_b, bada_b, *mrow)

        sq_b = A.alloc("sq", 2048, nres=8)
        tmpb = [A.alloc("tmp%d" % i, 512) for i in range(2)]
        rstd_b = A.alloc("rstd", 512)
        for tb in range(4):
            normmod(xT[:, :, blk(tb)], RX[tb], 512, MOD(0, 1, cond_of(tb)), MOD(0, 0, cond_of(tb)),
                    hT[:, :, blk(tb)], RH[tb], sq_b, tmpb, rstd_b)
        normmod(xpreT, xpre_b.r, 128, MOD(0, 1, 1), MOD(0, 0, 1), hpreT, hpre_b.r, sq_b, tmpb, rstd_b)
        if "x3" in skip:
            P.emit(nc, st)
            return nc
        A.free(stg, xpre_b)

        if stage >= 2:
            LA = 2144
            B_S0, B_S1, B_PRE, B_W = 8, 272, 536, 600
            wpin_v, wpin_r = wload(d_win0_pool.rearrange("(c p) n -> p c n", p=128))
            wpl_v, wpl_r = wload(d_wpool)
            wop_v, wop_r = wload(d_wout0[0:512, :].rearrange("(c p) n -> p c n", p=128))
            mask_b = A.alloc("pmask", 1600)
            pinv_b = A.alloc("pinv", 128)
            DMA("sync", mask_b.ap, d_pmask, mask_b.r, writes=[mask_b.r])
            DMA("sync", pinv_b.ap, d_pinv, pinv_b.r, writes=[pinv_b.r])
            abuf = [A.alloc("abuf%d" % i, LA) for i in range(1)]
            sbuf_ = [A.alloc("sbuf%d" % i, LA) for i in range(2)]
            pld = [A.alloc("pooled%d" % i, LA // 2) for i in range(1)]
            t8 = A.alloc("t8", 8)
            po_b = A.alloc("poolout", 4 * 2048 // 2, nres=4)
            pov = bf(po_b.ap).rearrange("p (g t) -> p g t", g=4)
            for b_ in abuf + sbuf_:
                X("vector", "memset", [], [b_.r], b_.ap, 0.0)
            for g in range(4):
                w = (2, 4, 8, 16)[g]
                ab = abuf[0]
                av = ab.ap
                for tb in range(4):
                    b = nb("pl", [0, 1, 2, 3])
                    for c in range(8):
                        X("tensor", "matmul", [RH[tb], wpin_r], [RB[b]], PB[b], lhsT=wpin_v[:, c, g * 128:(g + 1) * 128],
                          rhs=hT[:, c, blk(tb)], start=(c == 0), stop=(c == 7))
                    if tb == 0:
                        X("vector", "tensor_copy", [RB[b]], [ab.r], out=av[:, B_S0:B_S0 + 256], in_=PB[b][:, 0:256])
                        X("vector", "tensor_copy", [RB[b]], [ab.r], out=av[:, B_S1:B_S1 + 256], in_=PB[b][:, 256:512])
                    else:
                        X("vector", "tensor_tensor", [RB[b], mask_b.r], [ab.r],
                          out=av[:, B_W + (tb - 1) * 512:B_W + tb * 512], in0=PB[b],
                          in1=mask_b.ap[:, 64 + (tb - 1) * 512:64 + tb * 512], op=ALU.mult)
                b = nb("pl", [0, 1, 2, 3])
                for c in range(8):
                    X("tensor", "matmul", [hpre_b.r, wpin_r], [RB[b]], PB[b][:, 0:64], lhsT=wpin_v[:, c, g * 128:(g + 1) * 128],
                      rhs=hpreT[:, c, 0:64], start=(c == 0), stop=(c == 7))
                X("vector", "tensor_tensor", [RB[b], mask_b.r], [ab.r], out=av[:, B_PRE:B_PRE + 64], in0=PB[b][:, 0:64],
                  in1=mask_b.ap[:, 0:64], op=ALU.mult)
                cur, rcur = av, ab.r
                sh = 1
                k = 0
                while sh < w:
                    dstb = sbuf_[k % 2]
                    X("vector", "tensor_tensor", [rcur], [dstb.r], out=dstb.ap[:, sh:LA], in0=cur[:, sh:LA], in1=cur[:, 0:LA - sh],
                      op=ALU.add)
                    cur, rcur = dstb.ap, dstb.r
                    sh *= 2
                    k += 1
                pb_ = pld[0]
                pv = bf(pb_.ap)
                off = w // 2 - 1
                for (base, L) in ((B_S0, 256), (B_S1, 256), (B_W, 1536)):
                    X("vector", "scalar_tensor_tensor", [rcur, ab.r], [pb_.r], out=pv[:, base:base + L],
                      in0=cur[:, base + off:base + off + L], scalar=1.0 / w, in1=av[:, base:base + L], op0=ALU.mult,
                      op1=ALU.subtract)
                for (pos, bd) in ((B_S0, 0), (B_S0 + 248, 1), (B_S1, 0), (B_S1 + 248, 1), (B_W + 256, 2), (B_W + 1272, 3)):
                    X("vector", "tensor_tensor", [rcur, pinv_b.r], [t8.r], out=t8.ap, in0=cur[:, pos + off:pos + off + 8],
                      in1=pinv_b.ap[:, g * 32 + bd * 8:g * 32 + bd * 8 + 8], op=ALU.mult)
                    X("vector", "tensor_tensor", [t8.r, ab.r], [pb_.r], out=pv[:, pos:pos + 8], in0=t8.ap, in1=av[:, pos:pos + 8],
                      op=ALU.subtract)
                for tb in range(4):
                    b = nb("pl", [0, 1, 2, 3])
                    if tb == 0:
                        for s_, base in enumerate((B_S0, B_S1)):
                            X("tensor", "matmul", [pb_.r, wpl_r], [RB[b]], PB[b][:, s_ * 256:(s_ + 1) * 256],
                              lhsT=wpl_v[:, g * 128:(g + 1) * 128], rhs=pv[:, base:base + 256], start=True, stop=True)
                    else:
                        X("tensor", "matmul", [pb_.r, wpl_r], [RB[b]], PB[b], lhsT=wpl_v[:, g * 128:(g + 1) * 128],
                          rhs=pv[:, B_W + (tb - 1) * 512:B_W + tb * 512], start=True, stop=True)
                    X("vector", "tensor_scalar", [RB[b], RC], [po_b.res[tb]], out=pov[:, g, blk(tb)], in0=PB[b],
                      scalar1=V_PSC[:, g:g + 1], scalar2=None, op0=ALU.mult)
            for tb in range(4):
                for m in range(8):
                    b = nb("po", [4, 5, 6, 7])
                    for g in range(4):
                        X("tensor", "matmul", [po_b.res[tb], wop_r], [RB[b]], PB[b], lhsT=wop_v[:, g, m * 128:(m + 1) * 128],
                          rhs=pov[:, g, blk(tb)], start=(g == 0), stop=(g == 3))
                    epilogue(PB[b], RB[b], 0, 2, tb, m)
            A.free(mask_b, pinv_b, t8, po_b, hpre_b, *abuf, *sbuf_, *pld)

        wkv_v, wkv_r = wload(d_win0_kv.rearrange("(c p) n -> p c n", p=128))
        pinned.add(wload.last)
        NKEY = 5120
        ckvn_b = A.alloc("ckvnT", NKEY // 2, nres=10)
        ckvnT = bf(ckvn_b.ap)
        kr_b = A.alloc("KRsq", NKEY, nres=10)
        KRv = bf(kr_b.ap)[:, 0:NKEY]
        KSQv = bf(kr_b.ap)[:, NKEY:2 * NKEY]

        def kv_block(hsrc, rh, kb, rope_tile=None, r_rope=None, ckv32=None, kr32=None, r32=None):
            ks = slice(kb * 512, (kb + 1) * 512)
            rr = slice(64, 96)
            ba = nb("kv", [0, 1, 2, 3])
            for c in range(8):
                X("tensor", "matmul", [rh, wkv_r], [RB[ba]], PB[ba], lhsT=wkv_v[:, c, 0:128], rhs=hsrc[:, c, :],
                  start=(c == 0), stop=(c == 7))
            sqv = bf(sq_b.ap)[:, 0:512]
            X("scalar", "activation", [RB[ba]], [sq_b.r], out=sqv, in_=PB[ba], func=AF.Square)
            bs = nb("kv", [0, 1, 2, 3])
            X("tensor", "matmul", [sq_b.r, RC], [RB[bs]], PB[bs], lhsT=ones_b, rhs=sqv, start=True, stop=True)
            rstd_from(PB[bs], rstd_b.ap, 1.0 / 128, slice(0, 128), RB[bs], rstd_b.r)
            X("vector", "scalar_tensor_tensor", [RB[ba], rstd_b.r, RC], [ckvn_b.res[kb]], out=ckvnT[:, ks], in0=PB[ba],
              scalar=V_KVL, in1=rstd_b.ap, op0=ALU.mult, op1=ALU.mult)
            if ckv32 is not None:
                X("vector", "scalar_tensor_tensor", [RB[ba], rstd_b.r, RC], [r32], out=ckv32, in0=PB[ba],
                  scalar=V_KVL, in1=rstd_b.ap, op0=ALU.mult, op1=ALU.mult)
            bb = nb("kv", [0, 1, 2, 3])
            for c in range(8):
                X("tensor", "matmul", [rh, wkv_r], [RB[bb]], PB[bb][0:96, :], lhsT=wkv_v[:, c, 128:224], rhs=hsrc[:, c, :],
                  start=(c == 0), stop=(c == 7))
            if "k2" not in skip:
                X("scalar", "activation", [RB[bb]], [kr_b.res[kb]], out=KSQv[rr, ks], in_=PB[bb][rr, :], func=AF.Square)
            if kr32 is not None and "k3" not in skip:
                X("vector", "tensor_copy", [RB[bb]], [r32], out=kr32, in_=PB[bb][rr, :])
            if rope_tile is None:
                if "k1" not in skip:
                    X("vector", "tensor_scalar", [RB[bb], RC], [kr_b.res[kb]], out=KRv[rr, ks], in0=PB[bb][rr, :],
                      scalar1=V_GK[rr, :], scalar2=None, op0=ALU.mult)
            else:
                bc = nb("kv", [0, 1, 2, 3])
                for c in range(8):
                    X("tensor", "matmul", [rh, wkv_r], [RB[bc]], PB[bc][0:96, :], lhsT=wkv_v[:, c, 224:320],
                      rhs=hsrc[:, c, :], start=(c == 0), stop=(c == 7))
                t1, t2 = tmpb[0], tmpb[1]
                X("vector", "scalar_tensor_tensor", [RB[bb], r_rope, RC], [t1.r], out=t1.ap[rr, :], in0=PB[bb][rr, :],
                  scalar=V_GK[rr, :], in1=rope_tile[rr, 0:512], op0=ALU.mult, op1=ALU.mult)
                X("vector", "scalar_tensor_tensor", [RB[bc], r_rope, RC], [t2.r], out=t2.ap[rr, :], in0=PB[bc][rr, :],
                  scalar=V_GKS[rr, :], in1=rope_tile[rr, 512:1024], op0=ALU.mult, op1=ALU.mult)
                X("vector", "tensor_tensor", [t1.r, t2.r], [kr_b.res[kb]], out=KRv[rr, ks], in0=t1.ap[rr, :], in1=t2.ap[rr, :],
                  op=ALU.add)

        o32_b = A.alloc("kvout32", 1024)
        ckv32 = o32_b.ap[:, 0:512]
        kr32 = o32_b.ap[:, 512:1024]
        kv_block(hT[:, :, blk(0)], RH[0], 9, ckv32=ckv32, kr32=kr32[64:96, :], r32=o32_b.r)
        if "x4" in skip:
            P.emit(nc, st)
            return nc
        ost_b = A.alloc("ostage", 4 * 160)
        ostv = ost_b.ap.rearrange("p (a d) -> p a d", a=4)
        for a in range(4):
            b = nb("ld", [0, 1, 2, 3])
            X("tensor", "transpose", [o32_b.r, RC], [RB[b]], out=PB[b][:, 0:128], in_=ckv32[:, a * 128:(a + 1) * 128],
              identity=ident_f)
            X("tensor", "matmul", [o32_b.r, RC], [RB[b]], PB[b][:, 128:160], lhsT=kr32[64:96, a * 128:(a + 1) * 128],
              rhs=ident_f[64:96, 64:96], start=True, stop=True)
            X("vector", "tensor_copy", [RB[b]], [ost_b.r], out=ostv[:, a, :], in_=PB[b][:, 0:160])
        DMA("sync", o_ckv.rearrange("(a p) d -> p a d", p=128), ostv[:, :, 0:128], ost_b.r, reads=[ost_b.r])
        DMA("sync", o_kr.rearrange("(a p) d -> p a d", p=128), ostv[:, :, 128:160], ost_b.r, reads=[ost_b.r])
        A.free(o32_b, ost_b)

        if stage >= 3:
            wq_v, wq_r = wload(d_win0_q.rearrange("(c p) n -> p c n", p=128))
            qln_b = A.alloc("qlnT", 2 * 2048 // 2, nres=4)
            qln = bf(qln_b.ap).rearrange("p (j t) -> p j t", j=2)
            sq2 = bf(sq_b.ap)[:, 0:1024].rearrange("p (j t) -> p j t", j=2)
            for tb in range(4):
                bq = [nb("q", [0, 1, 2, 3]), nb("q", [0, 1, 2, 3])]
                for j in range(2):
                    for c in range(8):
                        X("tensor", "matmul", [RH[tb], wq_r], [RB[bq[j]]], PB[bq[j]], lhsT=wq_v[:, c, j * 128:(j + 1) * 128],
                          rhs=hT[:, c, blk(tb)], start=(c == 0), stop=(c == 7))
                    X("scalar", "activation", [RB[bq[j]]], [sq_b.res[j]], out=sq2[:, j, :], in_=PB[bq[j]], func=AF.Square)
                bs = nb("qs", [4, 5])
                for j in range(2):
                    X("tensor", "matmul", [sq_b.res[j], RC], [RB[bs]], PB[bs], lhsT=ones_b, rhs=sq2[:, j, :], start=(j == 0), stop=(j == 1))
                rstd_from(PB[bs], rstd_b.ap, 1.0 / 256, slice(0, 128), RB[bs], rstd_b.r)
                for j in range(2):
                    X("vector", "scalar_tensor_tensor", [RB[bq[j]], rstd_b.r, RC], [qln_b.res[tb]], out=qln[:, j, blk(tb)],
                      in0=PB[bq[j]], scalar=V_QL[:, j:j + 1], in1=rstd_b.ap, op0=ALU.mult, op1=ALU.mult)
        A.free(hT_b)

        if stage >= 3:
            stg = A.alloc("stg", 4096)
            xtmp_b = A.alloc("xtmp", 4096)
            xtmp = xtmp_b.ap.rearrange("p (c t) -> p c t", c=8)
            htmp_b = [A.alloc("htmp%d" % i, 2048) for i in range(1)]
            ropet = [A.alloc("ropet%d" % i, 1024) for i in range(1)]
            for kb in range(8):
                load_xT(d_xb[kb * 512:(kb + 1) * 512, :], 512, xtmp, xtmp_b.r, stg, eng_alt=kb)
                hb_ = htmp_b[0]
                hv_ = bf(hb_.ap).rearrange("p (c t) -> p c t", c=8)
                normmod(xtmp, xtmp_b.r, 512, MOD(0, 1, 1), MOD(0, 0, 1), hv_, hb_.r, sq_b, tmpb, rstd_b)
                rt = ropet[0]
                DMA("sync", rt.ap[64:96, 0:512], d_ropek[0:32, kb * 512:(kb + 1) * 512], rt.r, writes=[rt.r])
                DMA("sync", rt.ap[64:96, 512:1024], d_ropek[32:64, kb * 512:(kb + 1) * 512], rt.r, writes=[rt.r])
                kv_block(hv_, hb_.r, kb, rope_tile=rt.ap, r_rope=rt.r)
                if ada_deferred:
                    for pc in range((kb * 12) // 8, ((kb + 1) * 12) // 8):
                        ada_piece(1, pc)
            DMA("sync", xtmp_b.ap[:, 0:512], d_cckv, xtmp_b.r, writes=[xtmp_b.r])
            X("vector", "tensor_copy", [xtmp_b.r], [ckvn_b.res[8]], out=ckvnT[:, 4096:4608], in_=xtmp_b.ap[:, 0:512])
            DMA("sync", xtmp_b.ap[64:96, 512:1024], d_ckr, xtmp_b.r, writes=[xtmp_b.r])
            X("scalar", "activation", [xtmp_b.r], [kr_b.res[8]], out=KSQv[64:96, 4096:4608], in_=xtmp_b.ap[64:96, 512:1024],
              func=AF.Square)
            X("vector", "tensor_scalar", [xtmp_b.r, RC], [kr_b.res[8]], out=KRv[64:96, 4096:4608], in0=xtmp_b.ap[64:96, 512:1024],
              scalar1=V_GK[64:96, :], scalar2=None, op0=ALU.mult)
            A.free(stg, xtmp_b, *htmp_b, *ropet)
            A.free(sq_b, rstd_b)
            if ada_deferred:
                ada_finish(1)
                A.free(cs_b, bada_b, *mrow)
            pinned.clear()

        if stage >= 3:
            SC0 = 96.0 ** -0.5
            wqup_v, wqup_r = wload(d_wqup.rearrange("(c p) n -> p c n", p=128))
            wkvup_v, wkvup_r = wload(d_wkvup)
            KT_b = A.alloc("KT", NKEY // 2, nres=10)
            KT = bf(KT_b.ap)
            X("vector", "memset", [], KT_b.res, KT[64:128, :], 0.0)
            V_b = A.alloc("Vh", 40 * 128 // 2, nres=10)
            Vv = bf(V_b.ap).rearrange("p (t d) -> p t d", d=128)
            X("vector", "memset", [], V_b.res, Vv[:, :, 64:128], 1.0)
            QT_b = [A.alloc("QT%d" % i, 2048 // 2, nres=4) for i in range(2)]
            for qb_ in QT_b:
                X("vector", "memset", [], qb_.res, bf(qb_.ap)[64:128, :], 0.0)
            PT_b = [A.alloc("PT%d" % i, 256) for i in range(4)]
            rden_l = [A.alloc("rden%d" % i, 512) for i in range(1)]
            osb_l = [A.alloc("osbm%d" % i, 512) for i in range(4)]
            sqs_b = [A.alloc("sqs%d" % i, 256) for i in range(2)]
            r96_b = [A.alloc("r96_%d" % i, 512) for i in range(2)]
            rq_b = [A.alloc("ropeq%d" % i, 1024) for i in range(1)]
            rsk_b = [A.alloc("rsk%d" % i, 40, nres=10) for i in range(2)]
            lnsc_b = A.alloc("lnsc", 2)
            X("vector", "memset", [], [lnsc_b.r], lnsc_b.ap[:, 0:1], float(np.log(SC0)))
            X("vector", "tensor_tensor", [RC], [lnsc_b.r], out=lnsc_b.ap[:, 1:2], in0=V_GQ, in1=V_GK, op=ALU.mult)
            V_GQK = lnsc_b.ap[:, 1:2]
            for kb_ in range(10):
                X("vector", "tensor_copy", [kr_b.res[kb_]], [KT_b.res[kb_]], out=KT[64:96, kb_ * 512:(kb_ + 1) * 512],
                  in_=KRv[64:96, kb_ * 512:(kb_ + 1) * 512])
            attn_b = A.alloc("attnT", 2 * 2048 // 2, nres=4)
            attnT = bf(attn_b.ap).rearrange("p (h t) -> p h t", h=2)
            cnt = {"sq": 0, "r96": 0, "pt": 0, "rden": 0, "rq": 0, "osb": 0}

            def rot(key, lst):
                i = cnt[key]
                cnt[key] = i + 1
                return lst[i % len(lst)]

            def evac(ob, nq):
                ob_ = rot("osb", osb_l)
                X("vector", "tensor_copy", [RB[ob]], [ob_.r], out=ob_.ap[:, 0:nq], in_=PB[ob][:, 0:nq])
                return ob_

            def normalize_sb(ob_, dst, rdst, nq):
                rd = rot("rden", rden_l)
                X("vector", "reciprocal", [ob_.r], [rd.r], out=rd.ap[0:64, 0:nq], in_=ob_.ap[64:128, 0:nq])
                X("vector", "tensor_tensor", [ob_.r, rd.r], [rdst], out=dst, in0=ob_.ap[0:64, 0:nq], in1=rd.ap[0:64, 0:nq],
                  op=ALU.mult)

            def normalize(ob, dst, rdst, nq):
                normalize_sb(evac(ob, nq), dst, rdst, nq)

            def q_a(h, tb, stt_):
                for kc in range(2):
                    X("tensor", "matmul", [qln_b.res[tb], wqup_r], [RB[5]], PB[5][0:96, :], lhsT=wqup_v[:, kc, h * 192:h * 192 + 96],
                      rhs=qln[:, kc, blk(tb)], start=(kc == 0), stop=(kc == 1))
                if tb > 0:
                    for kc in range(2):
                        X("tensor", "matmul", [qln_b.res[tb], wqup_r], [RB[6]], PB[6][0:96, :],
                          lhsT=wqup_v[:, kc, h * 192 + 96:h * 192 + 192], rhs=qln[:, kc, blk(tb)], start=(kc == 0), stop=(kc == 1))
                sq = rot("sq", sqs_b)
                X("scalar", "activation", [RB[5]], [sq.r], out=bf(sq.ap)[0:96, :], in_=PB[5][0:96, :], func=AF.Square)
                stt_["sq"] = sq

            def q_b(h, tb, stt_):
                QB_ = QT_b[h % 2]
                QTv = bf(QB_.ap)
                rq = QB_.res[tb]
                sq = stt_["sq"]
                bq_ = nb("S", [3, 4, 7])
                X("tensor", "matmul", [sq.r, RC], [RB[bq_]], PB[bq_][0:96, :], lhsT=ones_b[0:96, 0:96], rhs=bf(sq.ap)[0:96, :],
                  start=True, stop=True)
                r96 = rot("r96", r96_b)
                rstd_from(PB[bq_][0:96, :], r96.ap[0:96, :], 1.0 / 96, slice(0, 96), RB[bq_], r96.r)
                if tb == 0:
                    X("vector", "scalar_tensor_tensor", [RB[5], r96.r, lnsc_b.r], [rq], out=QTv[0:64, blk(tb)], in0=PB[5][0:64, :],
                      scalar=V_GQK[0:64, :], in1=r96.ap[0:64, :], op0=ALU.mult, op1=ALU.mult)
                    X("vector", "scalar_tensor_tensor", [RB[5], r96.r, RC], [rq], out=QTv[64:96, blk(tb)], in0=PB[5][64:96, :],
                      scalar=V_GQ[64:96, :], in1=r96.ap[64:96, :], op0=ALU.mult, op1=ALU.mult)
                    return
                X("vector", "scalar_tensor_tensor", [RB[5], r96.r, lnsc_b.r], [rq], out=QTv[0:64, blk(tb)], in0=PB[5][0:64, :],
                  scalar=V_GQK[0:64, :], in1=r96.ap[0:64, :], op0=ALU.mult, op1=ALU.mult)
                rt = rot("rq", rq_b)
                ws = (tb - 1) * 512
                DMA("sync", rt.ap[64:96, 0:512], d_ropeq[0:32, ws:ws + 512], rt.r, writes=[rt.r])
                DMA("sync", rt.ap[64:96, 512:1024], d_ropeq[32:64, ws:ws + 512], rt.r, writes=[rt.r])
                rr = slice(64, 96)
                t1, t2 = tmpb[0], tmpb[1]
                X("vector", "scalar_tensor_tensor", [RB[5], rt.r, RC], [t1.r], out=t1.ap[rr, :], in0=PB[5][rr, :],
                  scalar=V_GQ[rr, :], in1=rt.ap[rr, 0:512], op0=ALU.mult, op1=ALU.mult)
                X("vector", "scalar_tensor_tensor", [RB[6], rt.r, RC], [t2.r], out=t2.ap[rr, :], in0=PB[6][rr, :],
                  scalar=V_GQS[rr, :], in1=rt.ap[rr, 512:1024], op0=ALU.mult, op1=ALU.mult)
                X("vector", "tensor_tensor", [t1.r, t2.r], [t1.r], out=t1.ap[rr, :], in0=t1.ap[rr, :], in1=t2.ap[rr, :], op=ALU.add)
                X("vector", "tensor_tensor", [t1.r, r96.r], [rq], out=QTv[rr, blk(tb)], in0=t1.ap[rr, :], in1=r96.ap[rr, :], op=ALU.mult)

            def k_s0(h, kb, stt_):
                ks = slice(kb * 512, (kb + 1) * 512)
                X("tensor", "matmul", [ckvn_b.res[kb], wkvup_r], [RB[5]], PB[5][0:64, :], lhsT=wkvup_v[:, h * 128:h * 128 + 64],
                  rhs=ckvnT[:, ks], start=True, stop=True)
                sq = rot("sq", sqs_b)
                X("scalar", "activation", [RB[5]], [sq.r], out=bf(sq.ap)[0:64, :], in_=PB[5][0:64, :], func=AF.Square)
                X("vector", "tensor_copy", [RB[5]], [KT_b.res[kb]], out=KT[0:64, ks], in_=PB[5][0:64, :])
                stt_["sq"] = sq

            def k_s1(h, kb, stt_):
                sq = stt_["sq"]
                for a in range(4):
                    X("tensor", "matmul", [sq.r, RC], [RB[6]], PB[6][:, a:a + 1], lhsT=bf(sq.ap)[0:64, a * 128:(a + 1) * 128],
                      rhs=ones_b[0:64, 0:1], start=True, stop=False)
                    X("tensor", "matmul", [kr_b.res[kb], RC], [RB[6]], PB[6][:, a:a + 1],
                      lhsT=KSQv[64:96, kb * 512 + a * 128:kb * 512 + (a + 1) * 128], rhs=ones_b[64:96, 0:1], start=False, stop=True)
                rk = rsk_b[h % 2]
                X("scalar", "activation", [RB[6], RC], [rk.res[kb]], out=rk.ap[:, kb * 4:kb * 4 + 4], in_=PB[6][:, 0:4], func=AF.Ln,
                  bias=eps_t[:, 0:1], scale=1.0 / 96)
                X("scalar", "activation", [rk.res[kb], lnsc_b.r], [rk.res[kb]], out=rk.ap[:, kb * 4:kb * 4 + 4],
                  in_=rk.ap[:, kb * 4:kb * 4 + 4], func=AF.Exp, bias=lnsc_b.ap[:, 0:1], scale=-0.5)

            def k_s2(h, kb, stt_):
                pass

            def v_all(h, kb, stt_):
                for a in range(4):
                    t = kb * 4 + a
                    X("tensor", "matmul", [ckvn_b.res[kb], wkvup_r], [RB[6]], PB[6][:, a * 64:(a + 1) * 64],
                      lhsT=ckvnT[:, t * 128:(t + 1) * 128], rhs=wkvup_v[:, h * 128 + 64:h * 128 + 128], start=True, stop=True)
                X("vector", "tensor_copy", [RB[6]], [V_b.res[kb]], out=Vv[:, kb * 4:kb * 4 + 4, 0:64],
                  in_=PB[6][:, 0:256].rearrange("p (a d) -> p a d", a=4))

            items = [(h, kb) for h in range(8) for kb in (list(range(9)) + [9])]
            nitem = len(items)
            sched = [dict() for _ in range(nitem)]

            def add(n, step, fn, *a):
                sched[n].setdefault(step, []).append((fn, a))

            for n, (h, kb) in enumerate(items):
                if n + 1 >= nitem:
                    break
                h2, kb2 = items[n + 1]
                st_ = {}
                if kb == 9:
                    continue
                base = 1
                add(n, base, k_s0, h2, kb2, st_)
                add(n, base + 2, k_s1, h2, kb2, st_)
                add(n, base + 4, k_s2, h2, kb2, st_)
                add(n, base + 5, v_all, h2, kb2, st_)
                if kb == 8 and n + 2 < nitem:
                    h3, kb3 = items[n + 2]
                    st3 = {}
                    add(n, 7, k_s0, h3, kb3, st3)
                    add(n, 9, k_s1, h3, kb3, st3)
                    add(n, 11, k_s2, h3, kb3, st3)
                    add(n, 11, v_all, h3, kb3, st3)
                if 2 <= kb <= 5 and h + 1 < 8:
                    stq = {}
                    add(n, 7, q_a, h + 1, kb - 2, stq)
                    add(n, 9, q_b, h + 1, kb - 2, stq)
            for tb in range(4):
                stq = {}
                q_a(0, tb, stq)
                q_b(0, tb, stq)
            st0 = {}
            k_s0(0, 0, st0)
            k_s1(0, 0, st0)
            k_s2(0, 0, st0)
            v_all(0, 0, st0)

            def run_sched(n, step):
                for fn, a in sched[n].get(step, []):
                    fn(*a)

            woa = {}
            pend = []

            def flush(keep):
                while len(pend) > keep:
                    pt_, kc_, qb_ = pend.pop(0)
                    X("tensor", "matmul", [V_b.res[kc_ // 4], pt_.r], [RB[qb_ - 1]], PB[qb_ - 1],
                      lhsT=Vv[:, kc_, :], rhs=bf(pt_.ap), start=(kc_ == 0), stop=(kc_ == 35))

            for n, (h, kb) in enumerate(items):
                hg, hh = h // 4, h % 4
                po = (hh % 2) * 64
                QB = QT_b[h % 2]
                QTv = bf(QB.ap)
                if kb == 0 and hh == 0:
                    woa[hg] = wload(d_wout0[512 + hg * 256:512 + (hg + 1) * 256, :].rearrange("(c p) n -> p c n", p=128))
                if kb < 9:
                    step = 0
                    for c in range(4):
                        kc = kb * 4 + c
                        for qb in (1, 2, 3):
                            sb = nb("S", [3, 4, 7])
                            X("tensor", "matmul", [KT_b.res[kb], QB.res[qb]], [RB[sb]], PB[sb], lhsT=KT[:, kc * 128:(kc + 1) * 128],
                              rhs=QTv[:, blk(qb)], start=True, stop=True)
                            pt = rot("pt", PT_b)
                            X("scalar", "activation", [RB[sb], rsk_b[h % 2].res[kb]], [pt.r], out=bf(pt.ap), in_=PB[sb], func=AF.Exp,
                              scale=rsk_b[h % 2].ap[:, kc:kc + 1])
                            pend.append((pt, kc, qb))
                            flush(2)
                            run_sched(n, step)
                            step += 1
                    if kb == 8:
                        flush(0)
                        pending_norm = [(evac(qb - 1, 512), attnT[po:po + 64, hh // 2, blk(qb)], attn_b.res[qb]) for qb in (1, 2, 3)]
                else:
                    for s_ in range(2):
                        sb = nb("S", [3, 4, 7])
                        for c in range(2):
                            k0 = 4608 + s_ * 256 + c * 128
                            X("tensor", "matmul", [KT_b.res[9], QB.res[0]], [RB[sb]], PB[sb][:, c * 256:(c + 1) * 256],
                              lhsT=KT[:, k0:k0 + 128], rhs=QTv[:, s_ * 256:(s_ + 1) * 256], start=True, stop=True)
                        pt = rot("pt", PT_b)
                        ptv = bf(pt.ap)
                        for c in range(2):
                            X("scalar", "activation", [RB[sb], rsk_b[h % 2].res[9]], [pt.r], out=ptv[:, c * 256:(c + 1) * 256],
                              in_=PB[sb][:, c * 256:(c + 1) * 256], func=AF.Exp, scale=rsk_b[h % 2].ap[:, 36 + 2 * s_ + c:37 + 2 * s_ + c])
                        for c in range(2):
                            X("tensor", "matmul", [V_b.res[9], pt.r], [RB[0]], PB[0][:, s_ * 256:(s_ + 1) * 256],
                              lhsT=Vv[:, 36 + 2 * s_ + c, :], rhs=ptv[:, c * 256:(c + 1) * 256], start=(c == 0), stop=(c == 1))
                    ev0 = evac(0, 512)
                    todo = pending_norm + [(ev0, attnT[po:po + 64, hh // 2, blk(0)], attn_b.res[0])]
                    if hh == 3 or n + 4 >= nitem:
                        for (ev_, dst_, rr_) in todo:
                            normalize_sb(ev_, dst_, rr_, 512)
                    else:
                        for j_, (ev_, dst_, rr_) in enumerate(todo):
                            add(n + 1 + j_, 10, normalize_sb, ev_, dst_, rr_, 512)
                    if hh == 3:
                        woa_v, woa_r = woa.pop(hg)
                        for tb in range(4):
                            for m in range(8):
                                b = nb("po2", [5, 6, 3, 4, 7])
                                for pr in range(2):
                                    X("tensor", "matmul", [attn_b.res[tb], woa_r], [RB[b]], PB[b], lhsT=woa_v[:, pr, m * 128:(m + 1) * 128],
                                      rhs=attnT[:, pr, blk(tb)], start=(pr == 0), stop=(pr == 1))
                                epilogue(PB[b], RB[b], 0, 2, tb, m)
            A.free(KT_b, V_b, attn_b, qln_b, ckvn_b, kr_b, lnsc_b, *rsk_b, *QT_b, *PT_b, *rden_l, *osb_l, *sqs_b, *r96_b, *rq_b)
        if stage < 3:
            A.free(sq_b, rstd_b)
        A.free(*tmpb)

        if stage >= 4:
            mlp(0, [(tb * 512, [RX[tb]], cond_of(tb)) for tb in range(4)])

        if dbg and stage < 5:
            ostg_d = A.alloc("ostg_dbg", 4096)
            for tb in range(4):
                store_tokens(xT[:, :, blk(tb)], RX[tb], 512, o_dbg[tb * 512:(tb + 1) * 512, :], ostg_d)
            A.free(ostg_d)

        if stage >= 5:
            hT_b = A.alloc("hT1", 8 * 2048 // 2, nres=4)
            hT = bf(hT_b.ap).rearrange("p (c t) -> p c t", c=8)
            RH = hT_b.res
            sq_b = A.alloc("sq1", 2048, nres=8)
            tmpb = [A.alloc("tmp1_%d" % i, 512) for i in range(2)]
            rstd_b = A.alloc("rstd1", 512)
            for tb in range(4):
                normmod(xT[:, :, blk(tb)], RX[tb], 512, MOD(1, 1, cond_of(tb)), MOD(1, 0, cond_of(tb)),
                        hT[:, :, blk(tb)], RH[tb], sq_b, tmpb, rstd_b)
            A.free(sq_b)
            SC1 = 0.125
            QCOL = [0, 768, 1280]
            sqp = [A.alloc("sqp%d" % i, 256) for i in range(2)]
            KTp = [A.alloc("KT1_%d" % i, 1024, nres=4) for i in range(2)]
            QTp = [A.alloc("QT1_%d" % i, 768, nres=3) for i in range(2)]
            Vw_b = A.alloc("Vw", 12 * 2 * 128 // 2, nres=3)
            Vw = bf(Vw_b.ap).rearrange("p (r h d) -> p r h d", r=12, h=2)
            Vp_b = A.alloc("Vp", 4 * 2 * 128 // 2)
            Vp = bf(Vp_b.ap).rearrange("p (r h d) -> p r h d", r=4, h=2)
            Kc_b = A.alloc("Kc", 512)
            Kc = bf(Kc_b.ap).rearrange("p (h k) -> p h k", h=2)
            X("vector", "memset", [], [Kc_b.r], Kc[64:128, :, :], 0.0)
            Vc_b = A.alloc("Vc", 4 * 2 * 128 // 2)
            Vc = bf(Vc_b.ap).rearrange("p (r h d) -> p r h d", r=4, h=2)
            T_b = [A.alloc("Ttab%d" % i, 1024) for i in range(2)]
            cm_b = A.alloc("cmask", 64)
            DMA("sync", cm_b.ap[0:64, :], d_cmask, cm_b.r, writes=[cm_b.r])
            DMA("sync", cm_b.ap[64:128, :], d_cmask, cm_b.r, writes=[cm_b.r])
            for tb_ in T_b:
                X("vector", "memset", [], [tb_.r], tb_.ap, 0.0)
            vT_l = [A.alloc("vT%d" % i, 256) for i in range(2)]
            kf_b = A.alloc("kf32", 512)
            kst_b = [A.alloc("kstage%d" % i, 512) for i in range(1)]
            E32_b = [A.alloc("E32_%d" % i, 768) for i in range(2)]
            PL_b = [A.alloc("PL%d" % i, 384) for i in range(3)]
            PT1_b = [A.alloc("PT1_%d" % i, 256) for i in range(2)]
            rden1 = [A.alloc("rden1_%d" % i, 512) for i in range(1)]
            attn1_b = A.alloc("attn1", 2 * 1536 // 2, nres=3)
            attn1 = bf(attn1_b.ap).rearrange("p (h t) -> p h t", h=2)
            X("vector", "memset", [], Vw_b.res, Vw[:, :, :, 64:128], 1.0)
            X("vector", "memset", [], [Vp_b.r], Vp[:, :, :, 64:128], 1.0)
            X("vector", "memset", [], [Vc_b.r], Vc[:, :, :, 64:128], 1.0)
            cnt1 = {}

            def rot1(key, lst):
                i = cnt1.get(key, 0)
                cnt1[key] = i + 1
                return lst[i % len(lst)]

            def normalize1(ob, dst, rdst, nq):
                rd = rot1("rden", rden1)
                X("scalar", "activation", [RB[ob]], [rd.r], out=rd.ap[0:64, 0:nq], in_=PB[ob][64:128, 0:nq], func=AF.Ln)
                X("scalar", "activation", [rd.r], [rd.r], out=rd.ap[0:64, 0:nq], in_=rd.ap[0:64, 0:nq], func=AF.Exp, scale=-1.0)
                X("vector", "tensor_tensor", [RB[ob], rd.r], [rdst], out=dst, in0=PB[ob][0:64, 0:nq], in1=rd.ap[0:64, 0:nq],
                  op=ALU.mult)

            def headnorm(bp, gvec):
                sq = rot1("sqp", sqp)
                sqv = bf(sq.ap)
                X("scalar", "activation", [RB[bp]], [sq.r], out=sqv, in_=PB[bp], func=AF.Square)
                bs = nb("prep1", [2, 3, 4, 5, 6, 7])
                X("tensor", "matmul", [sq.r, RC], [RB[bs]], PB[bs], lhsT=blk_b, rhs=sqv, start=True, stop=True)
                rb_ = rot1("rstd3", [rstd_b, tmpb[0], tmpb[1]])
                rstd_from(PB[bs], rb_.ap, 1.0 / 64, slice(0, 128), RB[bs], rb_.r)
                return rb_

            deferred = []

            def defer(fn):
                deferred.append(fn)
                while len(deferred) > 1:
                    deferred.pop(0)()

            def drain():
                while deferred:
                    deferred.pop(0)()

            def own_rows(i):
                if i < 4:
                    return i, 12
                if i > 12:
                    return 12, i + 8
                return i, i + 8

            for hh in range(2):
                X("vector", "memset", [], KTp[hh].res, bf(KTp[hh].ap)[64:128, :], 0.0)
                X("vector", "memset", [], QTp[hh].res, bf(QTp[hh].ap)[64:128, :], 0.0)
            for hh in range(2):
                DMA("gpsimd", bf(KTp[hh].ap)[64:80, 512:2048], d_pen, KTp[hh].res[1], writes=KTp[hh].res[1:4])
                DMA("gpsimd", bf(QTp[hh].ap)[64:80, 512:1536], d_qoh, QTp[hh].res[1], writes=QTp[hh].res[1:3])
            for c in range(8):
                w1v, w1r = wload(d_win1[:, c * 384:(c + 1) * 384].rearrange("(kc p) n -> p kc n", p=128))
                KTv = [bf(KTp[i].ap) for i in range(2)]
                QTv = [bf(QTp[i].ap) for i in range(2)]
                DMA("gpsimd", Kc[0:64, :, :], d_ck1[:, c * 1024:(c + 1) * 1024].rearrange("p (h k) -> p h k", h=2), Kc_b.r,
                    writes=[Kc_b.r])
                for t in range(4):
                    DMA("gpsimd", Vc[:, t, :, 0:64], d_cv1[t * 128:(t + 1) * 128, c * 128:(c + 1) * 128].rearrange("p (h d) -> p h d", h=2),
                        Vc_b.r, writes=[Vc_b.r])
                Tv = []
                for hh in range(2):
                    h = 2 * c + hh
                    tb_ = T_b[hh]
                    DMA("sync", tb_.ap[0:64, 0:960], d_rbt[h], tb_.r, writes=[tb_.r])
                    DMA("sync", tb_.ap[64:128, 64:1024], d_rbt[h], tb_.r, writes=[tb_.r])
                    X("vector", "memset", [], [tb_.r], tb_.ap[0:64, 960:1024], 0.0)
                    X("vector", "memset", [], [tb_.r], tb_.ap[64:128, 0:64], 0.0)
                    X("scalar", "activation", [tb_.r], [tb_.r], out=tb_.ap, in_=tb_.ap, func=AF.Exp)
                    t3 = tb_.ap.rearrange("p (r q) -> p r q", r=16)
                    X("vector", "tensor_tensor", [tb_.r, cm_b.r], [tb_.r], out=t3, in0=t3,
                      in1=cm_b.ap.unsqueeze(1).to_broadcast([128, 16, 64]), op=ALU.mult)
                    Tv.append(t3)
                for tb in range(4):
                    bp = nb("prep1", [2, 3, 4, 5, 6, 7])
                    for kc in range(8):
                        X("tensor", "matmul", [RH[tb], w1r], [RB[bp]], PB[bp], lhsT=w1v[:, kc, 128:256], rhs=hT[:, kc, blk(tb)],
                          start=(kc == 0), stop=(kc == 7))
                    def post_k(tb=tb, bp=bp):
                        rb_ = headnorm(bp, V_NGK)
                        for hh in range(2):
                            ps = slice(hh * 64, hh * 64 + 64)
                            X("vector", "scalar_tensor_tensor", [RB[bp], rb_.r, RC], [KTp[hh].res[tb]], out=KTv[hh][0:64, blk(tb)],
                              in0=PB[bp][ps, :], scalar=V_NGK[ps, :], in1=rb_.ap[ps, :], op0=ALU.mult, op1=ALU.mult)
                        if tb == 0:
                            X("vector", "scalar_tensor_tensor", [RB[bp], rb_.r, RC], [kf_b.r], out=kf_b.ap, in0=PB[bp],
                              scalar=V_NGK, in1=rb_.ap, op0=ALU.mult, op1=ALU.mult)
                            ks_ = rot1("kst", kst_b)
                            ksv = ks_.ap.rearrange("p (a d) -> p a d", a=4)
                            bt = nb("prep1", [2, 3, 4, 5, 6, 7])
                            for a in range(4):
                                X("tensor", "transpose", [kf_b.r, RC], [RB[bt]], out=PB[bt][:, a * 128:(a + 1) * 128],
                                  in_=kf_b.ap[:, a * 128:(a + 1) * 128], identity=ident_f)
                            X("vector", "tensor_copy", [RB[bt]], [ks_.r], out=ksv, in_=PB[bt].rearrange("p (a d) -> p a d", a=4))
                            DMA("sync", o_k1[:, c * 128:(c + 1) * 128].rearrange("(a p) d -> p a d", p=128), ksv, ks_.r, reads=[ks_.r])
                    defer(post_k)
                for qi in range(3):
                    q0 = QCOL[qi]
                    rq_ = [RH[0]] if qi == 0 else ([RH[1], RH[2]] if qi == 1 else [RH[2], RH[3]])
                    bp = nb("prep1", [2, 3, 4, 5, 6, 7])
                    for kc in range(8):
                        X("tensor", "matmul", rq_ + [w1r], [RB[bp]], PB[bp], lhsT=w1v[:, kc, 0:128], rhs=hT[:, kc, q0:q0 + 512],
                          start=(kc == 0), stop=(kc == 7))
                    def post_q(qi=qi, bp=bp):
                        rb_ = headnorm(bp, V_NGQ)
                        for hh in range(2):
                            ps = slice(hh * 64, hh * 64 + 64)
                            X("vector", "scalar_tensor_tensor", [RB[bp], rb_.r, RC], [QTp[hh].res[qi]],
                              out=QTv[hh][0:64, qi * 512:(qi + 1) * 512], in0=PB[bp][ps, :], scalar=V_NGQ[ps, :], in1=rb_.ap[ps, :],
                              op0=ALU.mult, op1=ALU.mult)
                    defer(post_q)
                for tb in range(4):
                    bp = nb("prep1", [2, 3, 4, 5, 6, 7])
                    for kc in range(8):
                        X("tensor", "matmul", [RH[tb], w1r], [RB[bp]], PB[bp], lhsT=w1v[:, kc, 256:384], rhs=hT[:, kc, blk(tb)],
                          start=(kc == 0), stop=(kc == 7))

                    def post_v(tb=tb, bp=bp):
                        vT_b = rot1("vT", vT_l)
                        vTv = bf(vT_b.ap)
                        X("scalar", "copy", [RB[bp]], [vT_b.r], out=vTv, in_=PB[bp])
                        bt = nb("prep1", [2, 3, 4, 5, 6, 7])
                        ptb = PB[bt].bitcast(BF16)
                        if tb == 0:
                            X("vector", "tensor_copy", [RB[bp]], [kf_b.r], out=kf_b.ap, in_=PB[bp])
                            for a in range(4):
                                X("tensor", "transpose", [vT_b.r, RC], [RB[bt]], out=ptb[:, a * 128:(a + 1) * 128],
                                  in_=vTv[:, a * 128:(a + 1) * 128], identity=ident_b)
                            X("vector", "tensor_copy", [RB[bt]], [Vp_b.r], out=Vp[:, :, :, 0:64],
                              in_=ptb[:, 0:512].rearrange("p (a h d) -> p a h d", a=4, h=2))
                            ks_ = rot1("kst", kst_b)
                            ksv = ks_.ap.rearrange("p (a d) -> p a d", a=4)
                            bt2 = nb("prep1", [2, 3, 4, 5, 6, 7])
                            for a in range(4):
                                X("tensor", "transpose", [kf_b.r, RC], [RB[bt2]], out=PB[bt2][:, a * 128:(a + 1) * 128],
                                  in_=kf_b.ap[:, a * 128:(a + 1) * 128], identity=ident_f)
                            X("vector", "tensor_copy", [RB[bt2]], [ks_.r], out=ksv, in_=PB[bt2].rearrange("p (a d) -> p a d", a=4))
                            DMA("sync", o_v1[:, c * 128:(c + 1) * 128].rearrange("(a p) d -> p a d", p=128), ksv, ks_.r, reads=[ks_.r])
                        else:
                            for a in range(4):
                                X("tensor", "transpose", [vT_b.r, RC], [RB[bt]], out=ptb[:, a * 128:(a + 1) * 128],
                                  in_=vTv[:, a * 128:(a + 1) * 128], identity=ident_b)
                            X("vector", "tensor_copy", [RB[bt]], [Vw_b.res[tb - 1]], out=Vw[:, (tb - 1) * 4:tb * 4, :, 0:64],
                              in_=ptb[:, 0:512].rearrange("p (a h d) -> p a h d", a=4, h=2))
                    defer(post_v)
                drain()
                for hh in range(2):
                    hidx = (c % 2) * 2 + hh
                    K_, Q_ = KTv[hh], QTv[hh]
                    rK, rQ = KTp[hh].res, QTp[hh].res
                    po1 = (hidx % 2) * 64
                    obp = nb("O1", [0, 1])
                    obs = [nb("O1", [0, 1]), nb("O1", [0, 1])]
                    cp = []

                    def cflush(keep):
                        while len(cp) > keep:
                            fn = cp.pop(0)
                            fn()
                    for s_ in range(2):
                        sb = nb("SC", [2, 3, 4, 5, 6, 7])
                        for cc_ in range(2):
                            k0 = s_ * 256 + cc_ * 128
                            X("tensor", "matmul", [rK[0], rQ[0]], [RB[sb]], PB[sb][:, cc_ * 256:(cc_ + 1) * 256], lhsT=K_[:, k0:k0 + 128],
                              rhs=Q_[:, s_ * 256:(s_ + 1) * 256], start=True, stop=True)
                        pt = rot1("pl", PL_b)
                        ptv = bf(pt.ap)[:, 0:512]
                        X("scalar", "activation", [RB[sb]], [pt.r], out=ptv, in_=PB[sb], func=AF.Exp, scale=SC1)

                        def pv_p(s_=s_, pt=pt, ptv=ptv):
                            for cc_ in range(2):
                                X("tensor", "matmul", [Vp_b.r, pt.r], [RB[obp]], PB[obp][:, s_ * 256:(s_ + 1) * 256],
                                  lhsT=Vp[:, 2 * s_ + cc_, hh, :], rhs=ptv[:, cc_ * 256:(cc_ + 1) * 256], start=(cc_ == 0), stop=(cc_ == 1))
                        cp.append(pv_p)
                        cflush(2)
                    cflush(0)
                    normalize1(obp, attn1[po1:po1 + 64, hidx // 2, 0:512], attn1_b.res[0], 512)
                    obs = [obp, obs[0]] if False else obs
                    for qb in range(2):
                        ob = obs[qb]
                        qs = slice(512 + qb * 512, 1024 + qb * 512)
                        for c4 in range(4):
                            sb = nb("SC", [2, 3, 4, 5, 6, 7])
                            X("tensor", "matmul", [Kc_b.r, rQ[1 + qb]], [RB[sb]], PB[sb], lhsT=Kc[:, hh, c4 * 128:(c4 + 1) * 128],
                              rhs=Q_[:, qs], start=True, stop=True)
                            pt = rot1("pl", PL_b)
                            ptv = bf(pt.ap)[:, 0:512]
                            X("scalar", "activation", [RB[sb]], [pt.r], out=ptv, in_=PB[sb], func=AF.Exp, scale=SC1)

                            def pv_c(ob=ob, c4=c4, pt=pt, ptv=ptv):
                                X("tensor", "matmul", [Vc_b.r, pt.r], [RB[ob]], PB[ob], lhsT=Vc[:, c4, hh, :], rhs=ptv,
                                  start=(c4 == 0), stop=False)
                            cp.append(pv_c)
                            cflush(2)
                    cflush(0)
                    pend = []

                    def rows_of(wr):
                        if wr > 22:
                            return None
                        return (0 if wr <= 11 else wr - 7), (min(15, wr) if wr < 12 else 15)
                    chunks = []
                    for m_ in range(12):
                        ra, rb_ = rows_of(2 * m_), rows_of(2 * m_ + 1)
                        ilo_c = min(ra[0], rb_[0]) if rb_ else ra[0]
                        ihi_c = max(ra[1], rb_[1]) if rb_ else ra[1]
                        chunks.append((m_, ilo_c, ihi_c))
                    last_m = {}
                    for (m_, ilo_c, ihi_c) in chunks:
                        for qb_ in range(2):
                            if max(ilo_c, qb_ * 8) <= min(ihi_c, qb_ * 8 + 7):
                                last_m[qb_] = m_

                    def flush1(keep):
                        while len(pend) > keep:
                            pl_, plv_, m_, ilo_, ihi_ = pend.pop(0)
                            for qb_ in range(2):
                                a_, b_ = max(ilo_, qb_ * 8), min(ihi_, qb_ * 8 + 7)
                                if a_ > b_:
                                    continue
                                X("tensor", "matmul", [Vw_b.res[m_ // 4], pl_.r], [RB[obs[qb_]]],
                                  PB[obs[qb_]][:, (a_ - qb_ * 8) * 64:(b_ - qb_ * 8 + 1) * 64], lhsT=Vw[:, m_, hh, :],
                                  rhs=plv_[:, (a_ - ilo_) * 64:(b_ - ilo_ + 1) * 64], start=False, stop=(m_ == last_m[qb_]))
                    for (m_, ilo, ihi) in chunks:
                        nq = ihi - ilo + 1
                        jj0 = ilo - 2 * m_ + 11
                        pair = nb("SL", [2, 4, 6])
                        sl = psum_t[:, pair * 512:pair * 512 + 1024]
                        kcol = 512 + m_ * 128
                        for part in range(2):
                            a_, b_ = part * 8, min(nq, part * 8 + 8)
                            if a_ >= b_:
                                continue
                            qc0 = 512 + (ilo + a_) * 64
                            rqs = sorted(set([1 + (ilo + a_) // 8, 1 + (ilo + b_ - 1) // 8]))
                            X("tensor", "matmul", [rK[1 + m_ // 4]] + [rQ[x_] for x_ in rqs], [RB[pair + part]],
                              sl[:, a_ * 64:b_ * 64], lhsT=K_[:, kcol:kcol + 128], rhs=Q_[:, qc0:qc0 + (b_ - a_) * 64],
                              start=True, stop=True)
                        e32 = rot1("e32", E32_b)
                        rbs = [RB[pair]] + ([RB[pair + 1]] if nq > 8 else [])
                        X("scalar", "activation", rbs, [e32.r], out=e32.ap[:, 0:nq * 64], in_=sl[:, 0:nq * 64], func=AF.Exp,
                          scale=SC1)
                        pl = rot1("pl", PL_b)
                        plv2 = bf(pl.ap)[:, 0:nq * 64]
                        X("vector", "tensor_tensor", [e32.r, T_b[hh].r], [pl.r], out=plv2.rearrange("p (r q) -> p r q", q=64),
                          in0=e32.ap[:, 0:nq * 64].rearrange("p (r q) -> p r q", q=64), in1=Tv[hh][:, jj0:jj0 + nq, :],
                          op=ALU.mult)
                        pend.append((pl, plv2, m_, ilo, ihi))
                        flush1(2)
                    flush1(0)
                    for qb in range(2):
                        qs = slice(512 + qb * 512, 1024 + qb * 512)
                        normalize1(obs[qb], attn1[po1:po1 + 64, hidx // 2, qs], attn1_b.res[1 + qb], 512)
                if c % 2 == 1:
                    grp = c // 2
                    wo1v, wo1r = wload(d_wout1[grp * 256:(grp + 1) * 256, :].rearrange("(c p) n -> p c n", p=128))
                    for qi in range(3):
                        q0 = QCOL[qi]
                        rxs = [RX[0]] if qi == 0 else ([RX[1], RX[2]] if qi == 1 else [RX[2], RX[3]])
                        cd = 0 if qi == 0 else 1
                        for m in range(8):
                            b = nb("prep1", [2, 3, 4, 5, 6, 7])
                            for hq in range(2):
                                X("tensor", "matmul", [attn1_b.res[qi], wo1r], [RB[b]], PB[b], lhsT=wo1v[:, hq, m * 128:(m + 1) * 128],
                                  rhs=attn1[:, hq, qi * 512:(qi + 1) * 512], start=(hq == 0), stop=(hq == 3 - 2))
                            X("vector", "scalar_tensor_tensor", [RB[b], RM] + rxs, rxs, out=xT[:, m, q0:q0 + 512], in0=PB[b],
                              scalar=MOD(1, 2, cd)[:, m:m + 1], in1=xT[:, m, q0:q0 + 512], op0=ALU.mult, op1=ALU.add)
            A.free(hT_b, rstd_b, Vw_b, Vp_b, Kc_b, Vc_b, cm_b, kf_b, attn1_b, *vT_l, *tmpb, *sqp, *KTp, *QTp, *T_b, *kst_b,
                   *E32_b, *PL_b, *PT1_b, *rden1)

        if dbg and stage >= 5:
            ostg_d = A.alloc("ostg_dbg", 4096)
            for tb in range(4):
                store_tokens(xT[:, :, blk(tb)], RX[tb], 512, o_dbg[tb * 512:(tb + 1) * 512, :], ostg_d)
            A.free(ostg_d)

        if stage >= 6:
            mlp(1, [(0, [RX[0]], 0), (768, [RX[1], RX[2]], 1), (1280, [RX[2], RX[3]], 1)])
            ostg = A.alloc("ostg", 4096)
            store_tokens(xT[:, :, 0:512], RX[0], 512, o_yp, ostg)
            store_tokens(xT[:, :, 768:1280], RX[1], 512, o_ys[0:512, :], ostg, rextra=[RX[2]])
            store_tokens(xT[:, :, 1280:1792], RX[2], 512, o_ys[512:1024, :], ostg, rextra=[RX[3]])
            A.free(ostg)

        P.emit(nc, st)
    return nc


def _pvec(v, n):
    return np.ascontiguousarray(np.asarray(v, np.float32).reshape(n, 128).T)


_SWAP = np.array([(j + 8) if (j % 16) < 8 else (j - 8) for j in range(32)])
_SGN = np.array([-1.0 if (j % 16) < 8 else 1.0 for j in range(32)])


def _rope_tables(rows, cols):
    f = THETA ** (-np.arange(0, 16, 2, dtype=np.float64) / 16.0)
    n = len(rows)
    ang = np.zeros((32, n))
    for j in range(32):
        pos = rows if j < 16 else cols
        ang[j] = pos.astype(np.float32).astype(np.float64) * np.float32(f[j % 8]).astype(np.float64)
    ang = ang.astype(np.float32).astype(np.float64)
    return np.concatenate([np.cos(ang), _SGN[:, None] * np.sin(ang)], axis=0).astype(np.float32)


def _prep_inputs(inp):
    f32 = np.float32
    g = {k: np.asarray(v) for k, v in inp.items()}
    shared = {}
    shared["w_ada0"] = g["w_ada_l0"]
    shared["w_ada1"] = g["w_ada_l1"]
    shared["b_ada0"] = _pvec(g["b_ada_l0"], 48)
    shared["b_ada1"] = _pvec(g["b_ada_l1"], 48)
    shared["w_mlp1_0"] = g["w_mlp1_l0"]
    shared["w_mlp1_1"] = g["w_mlp1_l1"]
    shared["w_mlp2_0"] = g["w_mlp2_l0"]
    shared["w_mlp2_1"] = g["w_mlp2_l1"]
    vec = np.zeros((128, 64), f32)
    vec[:, 0:8] = _pvec(g["norm_mix_l0"], 8)
    vec[:, 8:16] = _pvec(g["norm_mlp_l0"], 8)
    vec[:, 16:24] = _pvec(g["norm_mix_l1"], 8)
    vec[:, 24:32] = _pvec(g["norm_mlp_l1"], 8)
    vec[:, 32:34] = _pvec(g["q_lora_norm_l0"], 2)
    vec[:, 34] = g["kv_lora_norm_l0"]
    gq, gk = g["mla_q_norm_l0"], g["mla_k_norm_l0"]
    vec[0:96, 35] = gq
    vec[64:96, 36] = gq[64 + _SWAP]
    vec[0:96, 37] = gk
    vec[64:96, 38] = gk[64 + _SWAP]
    vec[:, 39:43] = _pvec(g["pool_scale_l0"], 4)
    vec[:, 43] = np.tile(g["na_q_norm_l1"], 2)
    vec[:, 44] = np.tile(g["na_k_norm_l1"], 2)
    shared["vecs"] = vec
    w_in0 = g["w_in_l0"]
    shared["w_in0_pool"] = np.ascontiguousarray(w_in0[:, 0:512])
    shared["w_in0_q"] = np.ascontiguousarray(w_in0[:, 512:768])
    kr = w_in0[:, 896:928]
    dummy = w_in0[:, 768:832]
    shared["w_in0_kv"] = np.ascontiguousarray(np.concatenate([w_in0[:, 768:896], dummy, kr, dummy, kr[:, _SWAP]], axis=1))
    wq = g["w_q_up_l0"].reshape(256, 8, 96)
    wq_aug = np.concatenate([wq, wq[:, :, 0:64], wq[:, :, 64 + _SWAP]], axis=2)
    shared["w_qup_aug"] = np.ascontiguousarray(wq_aug.reshape(256, 1536))
    shared["w_kvup"] = g["w_kv_up_l0"]
    shared["w_pool"] = np.ascontiguousarray(g["w_pool_l0"].transpose(1, 0, 2).reshape(128, 512))
    shared["w_out0"] = g["w_out_l0"]
    t = np.arange(4096)
    shared["rope_k"] = _rope_tables(t // 64, t % 64)
    w1 = g["w_in_l1"]
    pieces = []
    for c in range(8):
        pieces += [w1[:, c * 128:(c + 1) * 128], w1[:, 1024 + c * 128:1024 + (c + 1) * 128],
                   w1[:, 2048 + c * 128:2048 + (c + 1) * 128]]
    shared["w_in1"] = np.ascontiguousarray(np.concatenate(pieces, axis=1))
    shared["w_out1"] = g["w_out_l1"]
    rb = g["rel_bias_l1"]
    kc = np.arange(64)
    dc = np.clip(kc[:, None] - kc[None, :] + 15, 0, 30)
    rbt = rb[:, ::-1, :][:, :, dc]
    shared["rbt"] = np.ascontiguousarray(rbt.transpose(0, 2, 1, 3).reshape(16, 64, 960))
    cs = np.clip(kc - 8, 0, 48)
    cm = (kc[:, None] >= cs[None, :]) & (kc[:, None] < cs[None, :] + 16)
    shared["cmask"] = cm.astype(f32)
    maps = []
    xs = g["x_sample"]
    for core in range(8):
        b, q = core // 4, core % 4
        m = dict(shared)
        m["xp"] = np.ascontiguousarray(g["x_prompt"][2 * core:2 * core + 2].reshape(512, D))
        r0 = 16 * q - 4
        xw = np.zeros((24, 64, D), f32)
        xpre = np.zeros((128, D), f32)
        valid = np.zeros(25, f32)
        xg = xs[b].reshape(64, 64, D)
        for wr in range(24):
            if 0 <= r0 + wr < 64:
                xw[wr] = xg[r0 + wr]
                valid[1 + wr] = 1.0
        if 0 <= r0 - 1 < 64:
            xpre[0:64] = xg[r0 - 1]
            valid[0] = 1.0
        m["xw"] = xw.reshape(1536, D)
        m["xpre"] = xpre
        m["xb"] = np.ascontiguousarray(xs[b])
        cond = np.stack([g["c_ctx"], g["c"][b]], axis=1)
        m["condT"] = np.ascontiguousarray(cond.reshape(8, 128, 2).transpose(1, 0, 2).reshape(128, 16))
        wt = np.arange(1536)
        m["rope_q"] = _rope_tables(r0 + wt // 64, wt % 64)
        m["cache_ckvT"] = np.ascontiguousarray(g["cache_l0_mla_ckv"][b].T)
        m["cache_kropeT"] = np.ascontiguousarray(g["cache_l0_mla_krope"][b].T)
        m["pool_mask"] = np.ascontiguousarray(np.broadcast_to(np.repeat(valid, 64)[None, :], (128, 1600))).astype(f32)
        pinv = np.zeros((4, 4, 8), f32)
        for gi, w in enumerate((2, 4, 8, 16)):
            for bd, (S, first, exists) in enumerate([(256, True, True), (256, False, True),
                                                    (4096, True, q == 0), (4096, False, q == 3)]):
                for j in range(8):
                    tt = j if first else S - 8 + j
                    if exists:
                        lo = np.clip(tt - w // 2, 0, S)
                        hi = np.clip(tt - w // 2 + w, 0, S)
                        pinv[gi, bd, j] = 1.0 / float(hi - lo)
                    else:
                        pinv[gi, bd, j] = 1.0 / w
        m["pool_inv"] = np.ascontiguousarray(np.broadcast_to(pinv.reshape(1, 128), (128, 128))).astype(f32)
        pen = np.full((16, 24), NEGBIG, f32)
        for i in range(16):
            gr = 16 * q + i
            rs = int(np.clip(gr - 4, 0, 56))
            for kr_ in range(rs, rs + 8):
                pen[i, kr_ - r0] = 0.0
        m["pen"] = np.ascontiguousarray(np.repeat(pen, 64, axis=1))
        m["qoh"] = np.ascontiguousarray(np.repeat(np.eye(16, dtype=f32), 64, axis=1))
        m["cache_k1T"] = np.ascontiguousarray(g["cache_l1_na_k"][b].transpose(2, 1, 0).reshape(64, 16 * 512))
        m["cache_v1"] = np.ascontiguousarray(g["cache_l1_na_v"][b].reshape(512, 1024))
        maps.append(m)
    return maps


_NC_CACHE = {}
_BUILD_ARGS = {}


def kernel(**inputs):
    maps = _prep_inputs(inputs)
    if "nc" not in _NC_CACHE:
        _NC_CACHE["nc"] = build_program(**_BUILD_ARGS)
    nc = _NC_CACHE["nc"]
    res = run_bass_kernel_spmd(nc, maps, core_ids=list(range(8)))
    R = res.results
    yp = np.concatenate([R[c]["yp"].reshape(2, 256, D) for c in range(8)], axis=0)
    ys = np.stack([np.concatenate([R[b * 4 + q]["ys"] for q in range(4)], axis=0) for b in range(2)], axis=0)
    ckv = np.concatenate([R[c]["ckv_new"].reshape(2, 256, 128) for c in range(8)], axis=0)
    kr = np.concatenate([R[c]["krope_new"].reshape(2, 256, 32) for c in range(8)], axis=0)
    k1 = np.concatenate([R[c]["k_new"].reshape(2, 256, 16, 64) for c in range(8)], axis=0)
    v1 = np.concatenate([R[c]["v_new"].reshape(2, 256, 16, 64) for c in range(8)], axis=0)
    return (yp.astype(np.float32), ys.astype(np.float32), ckv.astype(np.float32), kr.astype(np.float32),
            k1.astype(np.float32), v1.astype(np.float32))
```

```python
import numpy as np
from contextlib import ExitStack
import concourse.bass as bass
import concourse.mybir as mybir
from concourse.bass_utils import run_bass_kernel_spmd

F32 = mybir.dt.float32
BF16 = mybir.dt.bfloat16
AF = mybir.ActivationFunctionType
ALU = mybir.AluOpType

ENGS = ("tensor", "vector", "scalar", "gpsimd", "sync")
D = 1024
NCH = 8
EPS = 1e-6
THETA = 10000.0
NEGBIG = -30000.0


class Res:
    __slots__ = ("name", "last_w", "readers", "dsem_idx", "dcount", "excl")

    def __init__(self, name, excl=False):
        self.name = name
        self.excl = excl
        self.last_w = None
        self.readers = []
        self.dsem_idx = None
        self.dcount = 0

    def pending(self):
        r = list(self.readers)
        if self.last_w is not None:
            r.append(self.last_w)
        return r


class Op:
    __slots__ = ("eng", "fn", "deps", "signal", "sigval", "is_dma", "dsem_idx", "dval")

    def __init__(self, eng, fn, is_dma=False):
        self.eng = eng
        self.fn = fn
        self.deps = []
        self.signal = False
        self.sigval = 0
        self.is_dma = is_dma
        self.dsem_idx = None
        self.dval = 0


class Prog:
    def __init__(self):
        self.ops = []
        self.n_dsem = 0

    def _track(self, op, reads, writes):
        for r in reads:
            if r.last_w is not None:
                op.deps.append(r.last_w)
            if r.excl:
                for rd in r.readers:
                    if rd.eng != op.eng:
                        op.deps.append(rd)
            r.readers.append(op)
        for w in writes:
            if w.last_w is not None:
                op.deps.append(w.last_w)
            for rd in w.readers:
                if rd is not op:
                    op.deps.append(rd)
            w.last_w = op
            w.readers = []

    def op(self, eng, fn, reads=(), writes=()):
        o = Op(eng, fn)
        self._track(o, reads, writes)
        self.ops.append(o)
        return o

    def dma(self, eng, fn, res, reads=(), writes=()):
        o = Op(eng, fn, is_dma=True)
        if res.dsem_idx is None:
            res.dsem_idx = self.n_dsem
            self.n_dsem += 1
        res.dcount += 1
        o.dsem_idx = res.dsem_idx
        o.dval = 16 * res.dcount
        self._track(o, reads, writes)
        self.ops.append(o)
        return o

    def emit(self, nc, stack, final_eng="sync"):
        fin = Op(final_eng, None)
        last_by_sem = {}
        for o in self.ops:
            if o.is_dma:
                last_by_sem[o.dsem_idx] = o
        fin.deps = list(last_by_sem.values())
        ops = self.ops + [fin]
        for o in ops:
            for d in o.deps:
                if not d.is_dma:
                    if d.eng == "tensor" and o.eng == "tensor" and not o.is_dma:
                        continue
                    d.signal = True
        cnt = {e: 0 for e in ENGS}
        for o in ops:
            if o.signal:
                cnt[o.eng] += 1
                o.sigval = cnt[o.eng]
        esem = {e: stack.enter_context(nc.semaphore("es_" + e)) for e in ENGS}
        dsem = [stack.enter_context(nc.semaphore("ds_%d" % i)) for i in range(self.n_dsem)]
        per_eng = {e: [o for o in ops if o.eng == e] for e in ENGS}
        block = stack.enter_context(nc.Block())

        def run(e, engobj):
            waited = {}
            for o in per_eng[e]:
                need = {}
                for d in o.deps:
                    if d.is_dma:
                        key = ("d", d.dsem_idx)
                        val = d.dval
                    else:
                        if d.eng == "tensor" and e == "tensor" and not o.is_dma:
                            continue
                        key = ("e", d.eng)
                        val = d.sigval
                    if val > need.get(key, 0):
                        need[key] = val
                for key, val in need.items():
                    if waited.get(key, 0) >= val:
                        continue
                    waited[key] = val
                    s = dsem[key[1]] if key[0] == "d" else esem[key[1]]
                    engobj.wait_ge(s, val)
                if o.fn is None:
                    continue
                inst = o.fn(engobj)
                if o.is_dma:
                    inst.then_inc(dsem[o.dsem_idx], 16)
                elif o.signal:
                    inst.then_inc(esem[e], 1)

        @block.tensor
        def _(eng):
            run("tensor", eng)

        @block.vector
        def _(eng):
            run("vector", eng)

        @block.scalar
        def _(eng):
            run("scalar", eng)

        @block.gpsimd
        def _(eng):
            run("gpsimd", eng)

        @block.sync
        def _(eng):
            run("sync", eng)


class Buf:
    def __init__(self, ap, res, c0, c1):
        self.ap = ap
        self.res = res
        self.c0 = c0
        self.c1 = c1

    @property
    def r(self):
        return self.res[0]


class Arena:
    def __init__(self, ap, ncols):
        self.ap = ap
        self.ncols = ncols
        self.live = []
        self.freed = []

    def alloc(self, name, ncols, nres=1):
        ivs = sorted((c0, c1) for c0, c1, _ in self.live)
        pos = 0
        for c0, c1 in ivs:
            if c0 - pos >= ncols:
                break
            pos = max(pos, c1)
        if pos + ncols > self.ncols:
            lay = sorted((c0, c1, b.res[0].name) for c0, c1, b in self.live)
            raise AssertionError("SBUF arena overflow allocating %s (%d cols at %d); live=%s" % (name, ncols, pos, lay))
        res = [Res("%s_%d" % (name, i)) for i in range(nres)]
        keep = []
        for c0, c1, ops in self.freed:
            if c0 < pos + ncols and pos < c1:
                for r in res:
                    r.readers.extend(ops)
                if c0 < pos:
                    keep.append((c0, pos, ops))
                if c1 > pos + ncols:
                    keep.append((pos + ncols, c1, ops))
            else:
                keep.append((c0, c1, ops))
        self.freed = keep
        b = Buf(self.ap[:, pos:pos + ncols], res, pos, pos + ncols)
        self.live.append((pos, pos + ncols, b))
        return b

    def free(self, *bufs):
        for b in bufs:
            self.live = [t for t in self.live if t[2] is not b]
            ops = []
            for r in b.res:
                ops.extend(r.pending())
            self.freed.append((b.c0, b.c1, ops))


def build_program(stage=99, dbg=False, mini=None, skip=()):
    nc = bass.Bass("TRN2", target_bir_lowering=False)
    P = Prog()

    def din(name, shape, dt=F32):
        if mini is not None and name not in mini:
            shape = [1, 1]
        return nc.dram_tensor(name, list(shape), dt, kind="ExternalInput").ap()

    def dout(name, shape):
        return nc.dram_tensor(name, list(shape), F32, kind="ExternalOutput").ap()

    d_xp = din("xp", [512, D])
    d_xw = din("xw", [1536, D])
    d_xpre = din("xpre", [128, D])
    d_xb = din("xb", [4096, D])
    d_cond = din("condT", [128, 16])
    d_wada = [din("w_ada0", [D, 6 * D]), din("w_ada1", [D, 6 * D])]
    d_bada = [din("b_ada0", [128, 48]), din("b_ada1", [128, 48])]
    d_vec = din("vecs", [128, 64])
    d_win0_pool = din("w_in0_pool", [D, 512])
    d_win0_q = din("w_in0_q", [D, 256])
    d_win0_kv = din("w_in0_kv", [D, 320])
    d_wqup = din("w_qup_aug", [256, 1536])
    d_wkvup = din("w_kvup", [128, 1024])
    d_wpool = din("w_pool", [128, 512])
    d_wout0 = din("w_out0", [D, D])
    d_mlp1 = [din("w_mlp1_0", [D, 4 * D]), din("w_mlp1_1", [D, 4 * D])]
    d_mlp2 = [din("w_mlp2_0", [4 * D, D]), din("w_mlp2_1", [4 * D, D])]
    d_ropek = din("rope_k", [64, 4096])
    d_ropeq = din("rope_q", [64, 1536])
    d_cckv = din("cache_ckvT", [128, 512])
    d_ckr = din("cache_kropeT", [32, 512])
    d_pmask = din("pool_mask", [128, 1600])
    d_pinv = din("pool_inv", [128, 128])
    d_win1 = din("w_in1", [D, 8 * 384])
    d_wout1 = din("w_out1", [D, D])
    d_rbt = din("rbt", [16, 64, 960])
    d_cmask = din("cmask", [64, 64])
    d_pen = din("pen", [16, 1536])
    d_qoh = din("qoh", [16, 1024])
    d_ck1 = din("cache_k1T", [64, 16 * 512])
    d_cv1 = din("cache_v1", [512, 1024])

    o_yp = dout("yp", [512, D])
    o_ys = dout("ys", [1024, D])
    o_ckv = dout("ckv_new", [512, 128])
    o_kr = dout("krope_new", [512, 32])
    o_k1 = dout("k_new", [512, D])
    o_v1 = dout("v_new", [512, D])
    o_dbg = dout("dbg", [2048, D]) if dbg else None

    with ExitStack() as st:
        NCOLS = 53000
        arena_t = st.enter_context(nc.sbuf_tensor("arena", [128, NCOLS], F32))
        psum_t = st.enter_context(nc.psum_tensor("psum", [128, 8 * 512], F32))
        A = Arena(arena_t, NCOLS)
        PB = [psum_t[:, i * 512:(i + 1) * 512] for i in range(8)]
        RB = [Res("bank%d" % i, excl=True) for i in range(8)]
        bank_rr = {}

        def nb(group, banks):
            i = bank_rr.get(group, 0)
            bank_rr[group] = i + 1
            return banks[i % len(banks)]

        def X(eng, meth, reads, writes, *a, **kw):
            def fn(e):
                try:
                    return getattr(e, meth)(*a, **kw)
                except Exception as ex:
                    desc = [getattr(v, "shape", v) for v in a] + ["%s=%s" % (k, getattr(v, "shape", v)) for k, v in kw.items()]
                    raise RuntimeError("op %s.%s failed: %s | %s" % (eng, meth, desc, ex))
            return P.op(eng, fn, reads, writes)

        def DMA(eng, out, in_, res, reads=(), writes=()):
            return P.dma(eng, lambda e: e.dma_start(out=out, in_=in_), res, reads, writes)

        def bf(buf_ap):
            return buf_ap.bitcast(BF16)

        cst = A.alloc("consts", 128 + 64 + 64 + 64 + 128 + 1 + 64 + 16 + 48 * 2 + 48 * 2 + 2 * 6 * 16)
        cc = [0]

        def ccarve(n):
            a = cst.ap[:, cc[0]:cc[0] + n]
            cc[0] += n
            return a
        ident_f = ccarve(128)
        ident_b = bf(ccarve(64))
        ones_b = bf(ccarve(64))
        blk_b = bf(ccarve(64))
        ones_f = ccarve(128)
        eps_t = ccarve(1)
        vec = ccarve(64)
        cond_f = ccarve(16)
        RC = cst.r
        X("gpsimd", "memset", [], [RC], ident_f, 1.0)
        X("gpsimd", "affine_select", [RC], [RC], out=ident_f, in_=ident_f, pattern=[[-1, 128]],
          compare_op=ALU.is_equal, fill=0.0, base=0, channel_multiplier=1)
        X("gpsimd", "memset", [], [RC], ones_f, 1.0)
        X("gpsimd", "memset", [], [RC], eps_t, EPS)
        X("vector", "tensor_copy", [RC], [RC], out=ident_b, in_=ident_f)
        X("vector", "tensor_copy", [RC], [RC], out=ones_b, in_=ones_f)
        X("gpsimd", "memset", [], [RC], blk_b, 0.0)
        X("gpsimd", "memset", [], [RC], blk_b[0:64, 0:64], 1.0)
        X("gpsimd", "memset", [], [RC], blk_b[64:128, 64:128], 1.0)
        DMA("sync", vec, d_vec, RC, writes=[RC])
        DMA("sync", cond_f, d_cond, RC, writes=[RC])
        V_NMIX = [vec[:, 0:8], vec[:, 16:24]]
        V_NMLP = [vec[:, 8:16], vec[:, 24:32]]
        V_QL = vec[:, 32:34]
        V_KVL = vec[:, 34:35]
        V_GQ, V_GQS, V_GK, V_GKS = vec[:, 35:36], vec[:, 36:37], vec[:, 37:38], vec[:, 38:39]
        V_PSC = vec[:, 39:43]
        V_NGQ, V_NGK = vec[:, 43:44], vec[:, 44:45]

        if stage == -3:
            P.emit(nc, st)
            return nc
        NSLOT = 4
        slots = [A.alloc("wslot%d" % i, 2048) for i in range(NSLOT)]
        slot_rr = [0]

        pinned = set()

        def wload(dram_ap, shape_str=None, parts=128, **kw):
            while (slot_rr[0] % NSLOT) in pinned:
                slot_rr[0] += 1
            wload.last = slot_rr[0] % NSLOT
            s = slots[slot_rr[0] % NSLOT]
            slot_rr[0] += 1
            shp = list(dram_ap.shape)
            n = int(np.prod(shp[1:]))
            assert n <= 4096 and shp[0] == parts
            flat = bf(s.ap)[0:parts, 0:n]
            if len(shp) == 3:
                view = flat.rearrange("p (a b) -> p a b", a=shp[1])
            else:
                view = flat
            DMA("gpsimd", view, dram_ap, s.r, writes=[s.r])
            return view, s.r

        mods_b = A.alloc("mods", 2 * 6 * 16)
        RM = mods_b.r
        xT_b = A.alloc("xT", 8 * 2048, nres=4)
        xT = xT_b.ap.rearrange("p (c t) -> p c t", c=8)
        RX = xT_b.res
        hT_b = A.alloc("hT", 8 * 2048 // 2, nres=4)
        hT = bf(hT_b.ap).rearrange("p (c t) -> p c t", c=8)
        RH = hT_b.res

        def blk(tb):
            return slice(tb * 512, (tb + 1) * 512)

        def rstd_from(ps_ap, out_ap, inv_n, rows, rps, rout):
            X("scalar", "activation", [rps, RC], [rout], out=out_ap, in_=ps_ap, func=AF.Ln,
              bias=eps_t[rows, 0:1], scale=inv_n)
            X("scalar", "activation", [rout], [rout], out=out_ap, in_=out_ap, func=AF.Exp, scale=-0.5)

        def load_xT(dram_ap, n, dst, rdst, stg, eng_alt=0):
            nt = max(1, n // 128)
            pp = min(n, 128)
            sv = stg.ap.rearrange("p (a d) -> p a d", a=4)
            if n >= 128:
                DMA("sync", sv[:, 0:nt, :], dram_ap.rearrange("(a p) d -> p a d", p=128), stg.r, writes=[stg.r])
            else:
                DMA("sync", sv[0:pp, 0, :], dram_ap, stg.r, writes=[stg.r])
            for c in range(8):
                b = nb("ld", [0, 1, 2, 3])
                for a in range(nt):
                    X("tensor", "transpose", [stg.r, RC], [RB[b]], out=PB[b][:, a * pp:(a + 1) * pp],
                      in_=sv[0:pp, a, c * 128:(c + 1) * 128], identity=ident_f[0:pp, 0:pp])
                if (c + eng_alt) % 2 == 0:
                    X("vector", "tensor_copy", [RB[b]], [rdst], out=dst[:, c, 0:n], in_=PB[b][:, 0:n])
                else:
                    X("scalar", "copy", [RB[b]], [rdst], out=dst[:, c, 0:n], in_=PB[b][:, 0:n])

        def normmod(xsrc, rx, n, Avec, shvec, hdst, rh, sq, tmp, rstd):
            sqv = bf(sq.ap)[:, 0:8 * n].rearrange("p (c t) -> p c t", c=8)
            rxl = list(rx) if isinstance(rx, (list, tuple)) else [rx]
            b = nb("nm", [4, 5])
            for c in range(8):
                X("scalar", "activation", rxl, [sq.res[c % len(sq.res)]], out=sqv[:, c, :], in_=xsrc[:, c, :], func=AF.Square)
                X("tensor", "matmul", [sq.res[c % len(sq.res)], RC], [RB[b]], PB[b][:, 0:n], lhsT=ones_b, rhs=sqv[:, c, :],
                  start=(c == 0), stop=(c == 7))
            rs = rstd.ap[:, 0:n]
            rstd_from(PB[b][:, 0:n], rs, 1.0 / D, slice(0, 128), RB[b], rstd.r)
            for c in range(8):
                t = tmp[c % len(tmp)]
                X("vector", "scalar_tensor_tensor", rxl + [rstd.r, RM], [t.r], out=t.ap[:, 0:n], in0=xsrc[:, c, :],
                  scalar=Avec[:, c:c + 1], in1=rs, op0=ALU.mult, op1=ALU.mult)
                X("scalar", "activation", [t.r, RM], [rh], out=hdst[:, c, :], in_=t.ap[:, 0:n], func=AF.Identity,
                  bias=shvec[:, c:c + 1], scale=1.0)

        modT = [mods_b.ap[:, l * 96:(l + 1) * 96].rearrange("p (j k) -> p j k", k=2) for l in range(2)]
        cs_b = A.alloc("cond_silu", 8)
        csb = bf(cs_b.ap).rearrange("p (c k) -> p c k", k=2)
        X("scalar", "activation", [RC], [cs_b.r], out=csb, in_=cond_f.rearrange("p (c k) -> p c k", k=2), func=AF.Silu)
        mrow = [A.alloc("mrow%d" % i, 512) for i in range(2)]
        bada_b = A.alloc("bada", 96)
        for l in range(2):
            DMA("sync", bada_b.ap[:, l * 48:(l + 1) * 48], d_bada[l], bada_b.r, writes=[bada_b.r])
        if mini is not None:
            X("vector", "memset", [], [RM], mods_b.ap, 0.5)
        def ada_piece(l, pc):
            tbk = 6 + l
            wv, wr = wload(d_wada[l][:, pc * 512:(pc + 1) * 512].rearrange("(c p) n -> p c n", p=128))
            b = nb("ada", [0, 1, 2, 3])
            for kc in range(8):
                X("tensor", "matmul", [cs_b.r, wr], [RB[b]], PB[b][0:2, :], lhsT=csb[:, kc, :], rhs=wv[:, kc, :],
                  start=(kc == 0), stop=(kc == 7))
            mr = mrow[pc % 2]
            X("vector", "tensor_copy", [RB[b]], [mr.r], out=mr.ap[0:2, :], in_=PB[b][0:2, :])
            for j in range(4):
                jj = pc * 4 + j
                X("tensor", "matmul", [mr.r, RC], [RB[tbk]], PB[tbk][:, jj * 2:jj * 2 + 2],
                  lhsT=mr.ap[0:2, j * 128:(j + 1) * 128], rhs=ident_f[0:2, 0:2], start=True, stop=True)

        def ada_finish(l):
            tbk = 6 + l
            for k in range(2):
                X("vector", "tensor_tensor", [RB[tbk], bada_b.r], [RM], out=modT[l][:, :, k],
                  in0=PB[tbk][:, 0:96].rearrange("p (j k) -> p j k", k=2)[:, :, k], in1=bada_b.ap[:, l * 48:(l + 1) * 48],
                  op=ALU.add)
            for k in range(2):
                X("vector", "scalar_tensor_tensor", [RM, RC], [RM], out=modT[l][:, 8:16, k], in0=modT[l][:, 8:16, k],
                  scalar=1.0, in1=V_NMIX[l], op0=ALU.add, op1=ALU.mult)
                X("vector", "scalar_tensor_tensor", [RM, RC], [RM], out=modT[l][:, 32:40, k], in0=modT[l][:, 32:40, k],
                  scalar=1.0, in1=V_NMLP[l], op0=ALU.add, op1=ALU.mult)

        ada_deferred = (mini is None) and stage >= 3

        def MOD(l, kind, cond):
            return modT[l][:, kind * 8:(kind + 1) * 8, cond]

        def cond_of(tb):
            return 0 if tb == 0 else 1

        def store_tokens(xsrc, rsrc, n, dram_ap, ostg, rextra=()):
            ov = ostg.ap.rearrange("p (a d) -> p a d", a=4)
            nt = n // 128
            assert nt <= 4
            for a in range(nt):
                for half in range(2):
                    b = nb("st", [0, 1, 2, 3])
                    for cc_ in range(4):
                        c = half * 4 + cc_
                        X("tensor", "transpose", [rsrc, RC] + list(rextra), [RB[b]], out=PB[b][:, cc_ * 128:(cc_ + 1) * 128],
                          in_=xsrc[:, c, a * 128:(a + 1) * 128], identity=ident_f)
                    if half == 0:
                        X("vector", "tensor_copy", [RB[b]], [ostg.r], out=ov[:, a, 0:512], in_=PB[b])
                    else:
                        X("scalar", "copy", [RB[b]], [ostg.r], out=ov[:, a, 512:1024], in_=PB[b])
            DMA("sync", dram_ap.rearrange("(a p) d -> p a d", p=128), ov[:, 0:nt, :], ostg.r, reads=[ostg.r])

        def epilogue(ps, rps, l, kind, tb, m):
            X("vector", "scalar_tensor_tensor", [rps, RM, RX[tb]], [RX[tb]], out=xT[:, m, blk(tb)], in0=ps,
              scalar=MOD(l, kind, cond_of(tb))[:, m:m + 1], in1=xT[:, m, blk(tb)], op0=ALU.mult, op1=ALU.add)

        def mlp(l, segs):
            ns = len(segs)
            hb = A.alloc("hT_mlp", 8 * 512 * ns // 2, nres=ns)
            hv = bf(hb.ap).rearrange("p (c t) -> p c t", c=8)
            sq = A.alloc("sq_m", 2048, nres=8)
            tmps = [A.alloc("tmp_m%d" % i, 512) for i in range(2)]
            rstd = A.alloc("rstd_m", 512)
            for si, (c0, rl, cd) in enumerate(segs):
                normmod(xT[:, :, c0:c0 + 512], rl, 512, MOD(l, 4, cd), MOD(l, 3, cd),
                        hv[:, :, si * 512:(si + 1) * 512], hb.res[si], sq, tmps, rstd)
            A.free(sq, rstd)
            ag = [A.alloc("ag%d" % i, 4 * 512 * ns // 2, nres=ns) for i in range(2)]
            agv = [bf(a_.ap).rearrange("p (j t) -> p j t", j=4) for a_ in ag]
            wts = {}

            def ph1(g):
                w1, r1 = wload(d_mlp1[l][:, g * 512:(g + 1) * 512].rearrange("(c p) n -> p c n", p=128))
                w2, r2 = wload(d_mlp2[l][g * 512:(g + 1) * 512, :].rearrange("(j p) n -> p j n", p=128))
                wts[g] = (w2, r2)
                for si in range(ns):
                    ss = slice(si * 512, (si + 1) * 512)
                    for j in range(4):
                        b = nb("m1", [0, 1, 2, 3])
                        for c in range(8):
                            X("tensor", "matmul", [hb.res[si], r1], [RB[b]], PB[b], lhsT=w1[:, c, j * 128:(j + 1) * 128],
                              rhs=hv[:, c, ss], start=(c == 0), stop=(c == 7))
                        t = tmps[(si * 4 + j) % 2]
                        X("scalar", "activation", [RB[b]], [t.r], out=t.ap, in_=PB[b], func=AF.Relu)
                        X("vector", "tensor_tensor", [t.r], [ag[g % 2].res[si]], out=agv[g % 2][:, j, ss], in0=t.ap,
                          in1=t.ap, op=ALU.mult)

            def ph2(g):
                w2, r2 = wts.pop(g)
                for si, (c0, rl, cd) in enumerate(segs):
                    ss = slice(si * 512, (si + 1) * 512)
                    for m in range(8):
                        b = nb("m2", [4, 5, 6, 7])
                        for j in range(4):
                            X("tensor", "matmul", [ag[g % 2].res[si], r2], [RB[b]], PB[b], lhsT=w2[:, j, m * 128:(m + 1) * 128],
                              rhs=agv[g % 2][:, j, ss], start=(j == 0), stop=(j == 3))
                        X("vector", "scalar_tensor_tensor", [RB[b], RM] + list(rl), list(rl), out=xT[:, m, c0:c0 + 512], in0=PB[b],
                          scalar=MOD(l, 5, cd)[:, m:m + 1], in1=xT[:, m, c0:c0 + 512], op0=ALU.mult, op1=ALU.add)
            ph1(0)
            for g in range(1, 8):
                ph1(g)
                ph2(g - 1)
            ph2(7)
            A.free(hb, *tmps, *ag)

        stg = A.alloc("stg", 4096)
        load_xT(d_xp, 512, xT[:, :, blk(0)], RX[0], stg)
        for i in range(3):
            load_xT(d_xw[i * 512:(i + 1) * 512, :], 512, xT[:, :, blk(i + 1)], RX[i + 1], stg, eng_alt=i)
        if "x1" in skip:
            P.emit(nc, st)
            return nc
        xpre_b = A.alloc("xpreT", 8 * 128)
        xpreT = xpre_b.ap.rearrange("p (c t) -> p c t", c=8)
        load_xT(d_xpre, 128, xpreT, xpre_b.r, stg)
        hpre_b = A.alloc("hpreT", 8 * 128 // 2)
        hpreT = bf(hpre_b.ap).rearrange("p (c t) -> p c t", c=8)
        if "x2" in skip:
            P.emit(nc, st)
            return nc
        if mini is None:
            for pc in range(12):
                ada_piece(0, pc)
            ada_finish(0)
            if not ada_deferred:
                for pc in range(12):
                    ada_piece(1, pc)
                ada_finish(1)
        if not ada_deferred:
            A.free(cs_b, bada_b, *mrow)

        sq_b = A.alloc("sq", 2048, nres=8)
        tmpb = [A.alloc("tmp%d" % i, 512) for i in range(2)]
        rstd_b = A.alloc("rstd", 512)
        for tb in range(4):
            normmod(xT[:, :, blk(tb)], RX[tb], 512, MOD(0, 1, cond_of(tb)), MOD(0, 0, cond_of(tb)),
                    hT[:, :, blk(tb)], RH[tb], sq_b, tmpb, rstd_b)
        normmod(xpreT, xpre_b.r, 128, MOD(0, 1, 1), MOD(0, 0, 1), hpreT, hpre_b.r, sq_b, tmpb, rstd_b)
        if "x3" in skip:
            P.emit(nc, st)
            return nc
        A.free(stg, xpre_b)

        if stage >= 2:
            LA = 2144
            B_S0, B_S1, B_PRE, B_W = 8, 272, 536, 600
            wpin_v, wpin_r = wload(d_win0_pool.rearrange("(c p) n -> p c n", p=128))
            wpl_v, wpl_r = wload(d_wpool)
            wop_v, wop_r = wload(d_wout0[0:512, :].rearrange("(c p) n -> p c n", p=128))
            mask_b = A.alloc("pmask", 1600)
            pinv_b = A.alloc("pinv", 128)
            DMA("sync", mask_b.ap, d_pmask, mask_b.r, writes=[mask_b.r])
            DMA("sync", pinv_b.ap, d_pinv, pinv_b.r, writes=[pinv_b.r])
            abuf = [A.alloc("abuf%d" % i, LA) for i in range(1)]
            sbuf_ = [A.alloc("sbuf%d" % i, LA) for i in range(2)]
            pld = [A.alloc("pooled%d" % i, LA // 2) for i in range(1)]
            t8 = A.alloc("t8", 8)
            po_b = A.alloc("poolout", 4 * 2048 // 2, nres=4)
            pov = bf(po_b.ap).rearrange("p (g t) -> p g t", g=4)
            for b_ in abuf + sbuf_:
                X("vector", "memset", [], [b_.r], b_.ap, 0.0)
            for g in range(4):
                w = (2, 4, 8, 16)[g]
                ab = abuf[0]
                av = ab.ap
                for tb in range(4):
                    b = nb("pl", [0, 1, 2, 3])
                    for c in range(8):
                        X("tensor", "matmul", [RH[tb], wpin_r], [RB[b]], PB[b], lhsT=wpin_v[:, c, g * 128:(g + 1) * 128],
                          rhs=hT[:, c, blk(tb)], start=(c == 0), stop=(c == 7))
                    if tb == 0:
                        X("vector", "tensor_copy", [RB[b]], [ab.r], out=av[:, B_S0:B_S0 + 256], in_=PB[b][:, 0:256])
                        X("vector", "tensor_copy", [RB[b]], [ab.r], out=av[:, B_S1:B_S1 + 256], in_=PB[b][:, 256:512])
                    else:
                        X("vector", "tensor_tensor", [RB[b], mask_b.r], [ab.r],
                          out=av[:, B_W + (tb - 1) * 512:B_W + tb * 512], in0=PB[b],
                          in1=mask_b.ap[:, 64 + (tb - 1) * 512:64 + tb * 512], op=ALU.mult)
                b = nb("pl", [0, 1, 2, 3])
                for c in range(8):
                    X("tensor", "matmul", [hpre_b.r, wpin_r], [RB[b]], PB[b][:, 0:64], lhsT=wpin_v[:, c, g * 128:(g + 1) * 128],
                      rhs=hpreT[:, c, 0:64], start=(c == 0), stop=(c == 7))
                X("vector", "tensor_tensor", [RB[b], mask_b.r], [ab.r], out=av[:, B_PRE:B_PRE + 64], in0=PB[b][:, 0:64],
                  in1=mask_b.ap[:, 0:64], op=ALU.mult)
                cur, rcur = av, ab.r
                sh = 1
                k = 0
                while sh < w:
                    dstb = sbuf_[k % 2]
                    X("vector", "tensor_tensor", [rcur], [dstb.r], out=dstb.ap[:, sh:LA], in0=cur[:, sh:LA], in1=cur[:, 0:LA - sh],
                      op=ALU.add)
                    cur, rcur = dstb.ap, dstb.r
                    sh *= 2
                    k += 1
                pb_ = pld[0]
                pv = bf(pb_.ap)
                off = w // 2 - 1
                for (base, L) in ((B_S0, 256), (B_S1, 256), (B_W, 1536)):
                    X("vector", "scalar_tensor_tensor", [rcur, ab.r], [pb_.r], out=pv[:, base:base + L],
                      in0=cur[:, base + off:base + off + L], scalar=1.0 / w, in1=av[:, base:base + L], op0=ALU.mult,
                      op1=ALU.subtract)
                for (pos, bd) in ((B_S0, 0), (B_S0 + 248, 1), (B_S1, 0), (B_S1 + 248, 1), (B_W + 256, 2), (B_W + 1272, 3)):
                    X("vector", "tensor_tensor", [rcur, pinv_b.r], [t8.r], out=t8.ap, in0=cur[:, pos + off:pos + off + 8],
                      in1=pinv_b.ap[:, g * 32 + bd * 8:g * 32 + bd * 8 + 8], op=ALU.mult)
                    X("vector", "tensor_tensor", [t8.r, ab.r], [pb_.r], out=pv[:, pos:pos + 8], in0=t8.ap, in1=av[:, pos:pos + 8],
                      op=ALU.subtract)
                for tb in range(4):
                    b = nb("pl", [0, 1, 2, 3])
                    if tb == 0:
                        for s_, base in enumerate((B_S0, B_S1)):
                            X("tensor", "matmul", [pb_.r, wpl_r], [RB[b]], PB[b][:, s_ * 256:(s_ + 1) * 256],
                              lhsT=wpl_v[:, g * 128:(g + 1) * 128], rhs=pv[:, base:base + 256], start=True, stop=True)
                    else:
                        X("tensor", "matmul", [pb_.r, wpl_r], [RB[b]], PB[b], lhsT=wpl_v[:, g * 128:(g + 1) * 128],
                          rhs=pv[:, B_W + (tb - 1) * 512:B_W + tb * 512], start=True, stop=True)
                    X("vector", "tensor_scalar", [RB[b], RC], [po_b.res[tb]], out=pov[:, g, blk(tb)], in0=PB[b],
                      scalar1=V_PSC[:, g:g + 1], scalar2=None, op0=ALU.mult)
            for tb in range(4):
                for m in range(8):
                    b = nb("po", [4, 5, 6, 7])
                    for g in range(4):
                        X("tensor", "matmul", [po_b.res[tb], wop_r], [RB[b]], PB[b], lhsT=wop_v[:, g, m * 128:(m + 1) * 128],
                          rhs=pov[:, g, blk(tb)], start=(g == 0), stop=(g == 3))
                    epilogue(PB[b], RB[b], 0, 2, tb, m)
            A.free(mask_b, pinv_b, t8, po_b, hpre_b, *abuf, *sbuf_, *pld)

        wkv_v, wkv_r = wload(d_win0_kv.rearrange("(c p) n -> p c n", p=128))
        pinned.add(wload.last)
        NKEY = 5120
        ckvn_b = A.alloc("ckvnT", NKEY // 2, nres=10)
        ckvnT = bf(ckvn_b.ap)
        kr_b = A.alloc("KRsq", NKEY, nres=10)
        KRv = bf(kr_b.ap)[:, 0:NKEY]
        KSQv = bf(kr_b.ap)[:, NKEY:2 * NKEY]

        def kv_block(hsrc, rh, kb, rope_tile=None, r_rope=None, ckv32=None, kr32=None, r32=None):
            ks = slice(kb * 512, (kb + 1) * 512)
            rr = slice(64, 96)
            ba = nb("kv", [0, 1, 2, 3])
            for c in range(8):
                X("tensor", "matmul", [rh, wkv_r], [RB[ba]], PB[ba], lhsT=wkv_v[:, c, 0:128], rhs=hsrc[:, c, :],
                  start=(c == 0), stop=(c == 7))
            sqv = bf(sq_b.ap)[:, 0:512]
            X("scalar", "activation", [RB[ba]], [sq_b.r], out=sqv, in_=PB[ba], func=AF.Square)
            bs = nb("kv", [0, 1, 2, 3])
            X("tensor", "matmul", [sq_b.r, RC], [RB[bs]], PB[bs], lhsT=ones_b, rhs=sqv, start=True, stop=True)
            rstd_from(PB[bs], rstd_b.ap, 1.0 / 128, slice(0, 128), RB[bs], rstd_b.r)
            X("vector", "scalar_tensor_tensor", [RB[ba], rstd_b.r, RC], [ckvn_b.res[kb]], out=ckvnT[:, ks], in0=PB[ba],
              scalar=V_KVL, in1=rstd_b.ap, op0=ALU.mult, op1=ALU.mult)
            if ckv32 is not None:
                X("vector", "scalar_tensor_tensor", [RB[ba], rstd_b.r, RC], [r32], out=ckv32, in0=PB[ba],
                  scalar=V_KVL, in1=rstd_b.ap, op0=ALU.mult, op1=ALU.mult)
            bb = nb("kv", [0, 1, 2, 3])
            for c in range(8):
                X("tensor", "matmul", [rh, wkv_r], [RB[bb]], PB[bb][0:96, :], lhsT=wkv_v[:, c, 128:224], rhs=hsrc[:, c, :],
                  start=(c == 0), stop=(c == 7))
            if "k2" not in skip:
                X("scalar", "activation", [RB[bb]], [kr_b.res[kb]], out=KSQv[rr, ks], in_=PB[bb][rr, :], func=AF.Square)
            if kr32 is not None and "k3" not in skip:
                X("vector", "tensor_copy", [RB[bb]], [r32], out=kr32, in_=PB[bb][rr, :])
            if rope_tile is None:
                if "k1" not in skip:
                    X("vector", "tensor_scalar", [RB[bb], RC], [kr_b.res[kb]], out=KRv[rr, ks], in0=PB[bb][rr, :],
                      scalar1=V_GK[rr, :], scalar2=None, op0=ALU.mult)
            else:
                bc = nb("kv", [0, 1, 2, 3])
                for c in range(8):
                    X("tensor", "matmul", [rh, wkv_r], [RB[bc]], PB[bc][0:96, :], lhsT=wkv_v[:, c, 224:320],
                      rhs=hsrc[:, c, :], start=(c == 0), stop=(c == 7))
                t1, t2 = tmpb[0], tmpb[1]
                X("vector", "scalar_tensor_tensor", [RB[bb], r_rope, RC], [t1.r], out=t1.ap[rr, :], in0=PB[bb][rr, :],
                  scalar=V_GK[rr, :], in1=rope_tile[rr, 0:512], op0=ALU.mult, op1=ALU.mult)
                X("vector", "scalar_tensor_tensor", [RB[bc], r_rope, RC], [t2.r], out=t2.ap[rr, :], in0=PB[bc][rr, :],
                  scalar=V_GKS[rr, :], in1=rope_tile[rr, 512:1024], op0=ALU.mult, op1=ALU.mult)
                X("vector", "tensor_tensor", [t1.r, t2.r], [kr_b.res[kb]], out=KRv[rr, ks], in0=t1.ap[rr, :], in1=t2.ap[rr, :],
                  op=ALU.add)

        o32_b = A.alloc("kvout32", 1024)
        ckv32 = o32_b.ap[:, 0:512]
        kr32 = o32_b.ap[:, 512:1024]
        kv_block(hT[:, :, blk(0)], RH[0], 9, ckv32=ckv32, kr32=kr32[64:96, :], r32=o32_b.r)
        if "x4" in skip:
            P.emit(nc, st)
            return nc
        ost_b = A.alloc("ostage", 4 * 160)
        ostv = ost_b.ap.rearrange("p (a d) -> p a d", a=4)
        for a in range(4):
            b = nb("ld", [0, 1, 2, 3])
            X("tensor", "transpose", [o32_b.r, RC], [RB[b]], out=PB[b][:, 0:128], in_=ckv32[:, a * 128:(a + 1) * 128],
              identity=ident_f)
            X("tensor", "matmul", [o32_b.r, RC], [RB[b]], PB[b][:, 128:160], lhsT=kr32[64:96, a * 128:(a + 1) * 128],
              rhs=ident_f[64:96, 64:96], start=True, stop=True)
            X("vector", "tensor_copy", [RB[b]], [ost_b.r], out=ostv[:, a, :], in_=PB[b][:, 0:160])
        DMA("sync", o_ckv.rearrange("(a p) d -> p a d", p=128), ostv[:, :, 0:128], ost_b.r, reads=[ost_b.r])
        DMA("sync", o_kr.rearrange("(a p) d -> p a d", p=128), ostv[:, :, 128:160], ost_b.r, reads=[ost_b.r])
        A.free(o32_b, ost_b)

        if stage >= 3:
            wq_v, wq_r = wload(d_win0_q.rearrange("(c p) n -> p c n", p=128))
            qln_b = A.alloc("qlnT", 2 * 2048 // 2, nres=4)
            qln = bf(qln_b.ap).rearrange("p (j t) -> p j t", j=2)
            sq2 = bf(sq_b.ap)[:, 0:1024].rearrange("p (j t) -> p j t", j=2)
            for tb in range(4):
                bq = [nb("q", [0, 1, 2, 3]), nb("q", [0, 1, 2, 3])]
                for j in range(2):
                    for c in range(8):
                        X("tensor", "matmul", [RH[tb], wq_r], [RB[bq[j]]], PB[bq[j]], lhsT=wq_v[:, c, j * 128:(j + 1) * 128],
                          rhs=hT[:, c, blk(tb)], start=(c == 0), stop=(c == 7))
                    X("scalar", "activation", [RB[bq[j]]], [sq_b.res[j]], out=sq2[:, j, :], in_=PB[bq[j]], func=AF.Square)
                bs = nb("qs", [4, 5])
                for j in range(2):
                    X("tensor", "matmul", [sq_b.res[j], RC], [RB[bs]], PB[bs], lhsT=ones_b, rhs=sq2[:, j, :], start=(j == 0), stop=(j == 1))
                rstd_from(PB[bs], rstd_b.ap, 1.0 / 256, slice(0, 128), RB[bs], rstd_b.r)
                for j in range(2):
                    X("vector", "scalar_tensor_tensor", [RB[bq[j]], rstd_b.r, RC], [qln_b.res[tb]], out=qln[:, j, blk(tb)],
                      in0=PB[bq[j]], scalar=V_QL[:, j:j + 1], in1=rstd_b.ap, op0=ALU.mult, op1=ALU.mult)
        A.free(hT_b)

        if stage >= 3:
            stg = A.alloc("stg", 4096)
            xtmp_b = A.alloc("xtmp", 4096)
            xtmp = xtmp_b.ap.rearrange("p (c t) -> p c t", c=8)
            htmp_b = [A.alloc("htmp%d" % i, 2048) for i in range(1)]
            ropet = [A.alloc("ropet%d" % i, 1024) for i in range(1)]
            for kb in range(8):
                load_xT(d_xb[kb * 512:(kb + 1) * 512, :], 512, xtmp, xtmp_b.r, stg, eng_alt=kb)
                hb_ = htmp_b[0]
                hv_ = bf(hb_.ap).rearrange("p (c t) -> p c t", c=8)
                normmod(xtmp, xtmp_b.r, 512, MOD(0, 1, 1), MOD(0, 0, 1), hv_, hb_.r, sq_b, tmpb, rstd_b)
                rt = ropet[0]
                DMA("sync", rt.ap[64:96, 0:512], d_ropek[0:32, kb * 512:(kb + 1) * 512], rt.r, writes=[rt.r])
                DMA("sync", rt.ap[64:96, 512:1024], d_ropek[32:64, kb * 512:(kb + 1) * 512], rt.r, writes=[rt.r])
                kv_block(hv_, hb_.r, kb, rope_tile=rt.ap, r_rope=rt.r)
                if ada_deferred:
                    for pc in range((kb * 12) // 8, ((kb + 1) * 12) // 8):
                        ada_piece(1, pc)
            DMA("sync", xtmp_b.ap[:, 0:512], d_cckv, xtmp_b.r, writes=[xtmp_b.r])
            X("vector", "tensor_copy", [xtmp_b.r], [ckvn_b.res[8]], out=ckvnT[:, 4096:4608], in_=xtmp_b.ap[:, 0:512])
            DMA("sync", xtmp_b.ap[64:96, 512:1024], d_ckr, xtmp_b.r, writes=[xtmp_b.r])
            X("scalar", "activation", [xtmp_b.r], [kr_b.res[8]], out=KSQv[64:96, 4096:4608], in_=xtmp_b.ap[64:96, 512:1024],
              func=AF.Square)
            X("vector", "tensor_scalar", [xtmp_b.r, RC], [kr_b.res[8]], out=KRv[64:96, 4096:4608], in0=xtmp_b.ap[64:96, 512:1024],
              scalar1=V_GK[64:96, :], scalar2=None, op0=ALU.mult)
            A.free(stg, xtmp_b, *htmp_b, *ropet)
            A.free(sq_b, rstd_b)
            if ada_deferred:
                ada_finish(1)
                A.free(cs_b, bada_b, *mrow)
            pinned.clear()

        if stage >= 3:
            SC0 = 96.0 ** -0.5
            wqup_v, wqup_r = wload(d_wqup.rearrange("(c p) n -> p c n", p=128))
            wkvup_v, wkvup_r = wload(d_wkvup)
            KT_b = A.alloc("KT", NKEY // 2, nres=10)
            KT = bf(KT_b.ap)
            X("vector", "memset", [], KT_b.res, KT[64:128, :], 0.0)
            V_b = A.alloc("Vh", 40 * 128 // 2, nres=10)
            Vv = bf(V_b.ap).rearrange("p (t d) -> p t d", d=128)
            X("vector", "memset", [], V_b.res, Vv[:, :, 64:128], 1.0)
            QT_b = [A.alloc("QT%d" % i, 2048 // 2, nres=4) for i in range(2)]
            for qb_ in QT_b:
                X("vector", "memset", [], qb_.res, bf(qb_.ap)[64:128, :], 0.0)
            PT_b = [A.alloc("PT%d" % i, 256) for i in range(4)]
            rden_l = [A.alloc("rden%d" % i, 512) for i in range(1)]
            osb_l = [A.alloc("osbm%d" % i, 512) for i in range(4)]
            sqs_b = [A.alloc("sqs%d" % i, 256) for i in range(2)]
            r96_b = [A.alloc("r96_%d" % i, 512) for i in range(2)]
            rq_b = [A.alloc("ropeq%d" % i, 1024) for i in range(1)]
            rsk_b = [A.alloc("rsk%d" % i, 40, nres=10) for i in range(2)]
            lnsc_b = A.alloc("lnsc", 2)
            X("vector", "memset", [], [lnsc_b.r], lnsc_b.ap[:, 0:1], float(np.log(SC0)))
            X("vector", "tensor_tensor", [RC], [lnsc_b.r], out=lnsc_b.ap[:, 1:2], in0=V_GQ, in1=V_GK, op=ALU.mult)
            V_GQK = lnsc_b.ap[:, 1:2]
            for kb_ in range(10):
                X("vector", "tensor_copy", [kr_b.res[kb_]], [KT_b.res[kb_]], out=KT[64:96, kb_ * 512:(kb_ + 1) * 512],
                  in_=KRv[64:96, kb_ * 512:(kb_ + 1) * 512])
            attn_b = A.alloc("attnT", 2 * 2048 // 2, nres=4)
            attnT = bf(attn_b.ap).rearrange("p (h t) -> p h t", h=2)
            cnt = {"sq": 0, "r96": 0, "pt": 0, "rden": 0, "rq": 0, "osb": 0}

            def rot(key, lst):
                i = cnt[key]
                cnt[key] = i + 1
                return lst[i % len(lst)]

            def evac(ob, nq):
                ob_ = rot("osb", osb_l)
                X("vector", "tensor_copy", [RB[ob]], [ob_.r], out=ob_.ap[:, 0:nq], in_=PB[ob][:, 0:nq])
                return ob_

            def normalize_sb(ob_, dst, rdst, nq):
                rd = rot("rden", rden_l)
                X("vector", "reciprocal", [ob_.r], [rd.r], out=rd.ap[0:64, 0:nq], in_=ob_.ap[64:128, 0:nq])
                X("vector", "tensor_tensor", [ob_.r, rd.r], [rdst], out=dst, in0=ob_.ap[0:64, 0:nq], in1=rd.ap[0:64, 0:nq],
                  op=ALU.mult)

            def normalize(ob, dst, rdst, nq):
                normalize_sb(evac(ob, nq), dst, rdst, nq)

            def q_a(h, tb, stt_):
                for kc in range(2):
                    X("tensor", "matmul", [qln_b.res[tb], wqup_r], [RB[5]], PB[5][0:96, :], lhsT=wqup_v[:, kc, h * 192:h * 192 + 96],
                      rhs=qln[:, kc, blk(tb)], start=(kc == 0), stop=(kc == 1))
                if tb > 0:
                    for kc in range(2):
                        X("tensor", "matmul", [qln_b.res[tb], wqup_r], [RB[6]], PB[6][0:96, :],
                          lhsT=wqup_v[:, kc, h * 192 + 96:h * 192 + 192], rhs=qln[:, kc, blk(tb)], start=(kc == 0), stop=(kc == 1))
                sq = rot("sq", sqs_b)
                X("scalar", "activation", [RB[5]], [sq.r], out=bf(sq.ap)[0:96, :], in_=PB[5][0:96, :], func=AF.Square)
                stt_["sq"] = sq

            def q_b(h, tb, stt_):
                QB_ = QT_b[h % 2]
                QTv = bf(QB_.ap)
                rq = QB_.res[tb]
                sq = stt_["sq"]
                bq_ = nb("S", [3, 4, 7])
                X("tensor", "matmul", [sq.r, RC], [RB[bq_]], PB[bq_][0:96, :], lhsT=ones_b[0:96, 0:96], rhs=bf(sq.ap)[0:96, :],
                  start=True, stop=True)
                r96 = rot("r96", r96_b)
                rstd_from(PB[bq_][0:96, :], r96.ap[0:96, :], 1.0 / 96, slice(0, 96), RB[bq_], r96.r)
                if tb == 0:
                    X("vector", "scalar_tensor_tensor", [RB[5], r96.r, lnsc_b.r], [rq], out=QTv[0:64, blk(tb)], in0=PB[5][0:64, :],
                      scalar=V_GQK[0:64, :], in1=r96.ap[0:64, :], op0=ALU.mult, op1=ALU.mult)
                    X("vector", "scalar_tensor_tensor", [RB[5], r96.r, RC], [rq], out=QTv[64:96, blk(tb)], in0=PB[5][64:96, :],
                      scalar=V_GQ[64:96, :], in1=r96.ap[64:96, :], op0=ALU.mult, op1=ALU.mult)
                    return
                X("vector", "scalar_tensor_tensor", [RB[5], r96.r, lnsc_b.r], [rq], out=QTv[0:64, blk(tb)], in0=PB[5][0:64, :],
                  scalar=V_GQK[0:64, :], in1=r96.ap[0:64, :], op0=ALU.mult, op1=ALU.mult)
                rt = rot("rq", rq_b)
                ws = (tb - 1) * 512
                DMA("sync", rt.ap[64:96, 0:512], d_ropeq[0:32, ws:ws + 512], rt.r, writes=[rt.r])
                DMA("sync", rt.ap[64:96, 512:1024], d_ropeq[32:64, ws:ws + 512], rt.r, writes=[rt.r])
                rr = slice(64, 96)
                t1, t2 = tmpb[0], tmpb[1]
                X("vector", "scalar_tensor_tensor", [RB[5], rt.r, RC], [t1.r], out=t1.ap[rr, :], in0=PB[5][rr, :],
                  scalar=V_GQ[rr, :], in1=rt.ap[rr, 0:512], op0=ALU.mult, op1=ALU.mult)
                X("vector", "scalar_tensor_tensor", [RB[6], rt.r, RC], [t2.r], out=t2.ap[rr, :], in0=PB[6][rr, :],
                  scalar=V_GQS[rr, :], in1=rt.ap[rr, 512:1024], op0=ALU.mult, op1=ALU.mult)
                X("vector", "tensor_tensor", [t1.r, t2.r], [t1.r], out=t1.ap[rr, :], in0=t1.ap[rr, :], in1=t2.ap[rr, :], op=ALU.add)
                X("vector", "tensor_tensor", [t1.r, r96.r], [rq], out=QTv[rr, blk(tb)], in0=t1.ap[rr, :], in1=r96.ap[rr, :], op=ALU.mult)

            def k_s0(h, kb, stt_):
                ks = slice(kb * 512, (kb + 1) * 512)
                X("tensor", "matmul", [ckvn_b.res[kb], wkvup_r], [RB[5]], PB[5][0:64, :], lhsT=wkvup_v[:, h * 128:h * 128 + 64],
                  rhs=ckvnT[:, ks], start=True, stop=True)
                sq = rot("sq", sqs_b)
                X("scalar", "activation", [RB[5]], [sq.r], out=bf(sq.ap)[0:64, :], in_=PB[5][0:64, :], func=AF.Square)
                X("vector", "tensor_copy", [RB[5]], [KT_b.res[kb]], out=KT[0:64, ks], in_=PB[5][0:64, :])
                stt_["sq"] = sq

            def k_s1(h, kb, stt_):
                sq = stt_["sq"]
                for a in range(4):
                    X("tensor", "matmul", [sq.r, RC], [RB[6]], PB[6][:, a:a + 1], lhsT=bf(sq.ap)[0:64, a * 128:(a + 1) * 128],
                      rhs=ones_b[0:64, 0:1], start=True, stop=False)
                    X("tensor", "matmul", [kr_b.res[kb], RC], [RB[6]], PB[6][:, a:a + 1],
                      lhsT=KSQv[64:96, kb * 512 + a * 128:kb * 512 + (a + 1) * 128], rhs=ones_b[64:96, 0:1], start=False, stop=True)
                rk = rsk_b[h % 2]
                X("scalar", "activation", [RB[6], RC], [rk.res[kb]], out=rk.ap[:, kb * 4:kb * 4 + 4], in_=PB[6][:, 0:4], func=AF.Ln,
                  bias=eps_t[:, 0:1], scale=1.0 / 96)
                X("scalar", "activation", [rk.res[kb], lnsc_b.r], [rk.res[kb]], out=rk.ap[:, kb * 4:kb * 4 + 4],
                  in_=rk.ap[:, kb * 4:kb * 4 + 4], func=AF.Exp, bias=lnsc_b.ap[:, 0:1], scale=-0.5)

            def k_s2(h, kb, stt_):
                pass

            def v_all(h, kb, stt_):
                for a in range(4):
                    t = kb * 4 + a
                    X("tensor", "matmul", [ckvn_b.res[kb], wkvup_r], [RB[6]], PB[6][:, a * 64:(a + 1) * 64],
                      lhsT=ckvnT[:, t * 128:(t + 1) * 128], rhs=wkvup_v[:, h * 128 + 64:h * 128 + 128], start=True, stop=True)
                X("vector", "tensor_copy", [RB[6]], [V_b.res[kb]], out=Vv[:, kb * 4:kb * 4 + 4, 0:64],
                  in_=PB[6][:, 0:256].rearrange("p (a d) -> p a d", a=4))

            items = [(h, kb) for h in range(8) for kb in (list(range(9)) + [9])]
            nitem = len(items)
            sched = [dict() for _ in range(nitem)]

            def add(n, step, fn, *a):
                sched[n].setdefault(step, []).append((fn, a))

            for n, (h, kb) in enumerate(items):
                if n + 1 >= nitem:
                    break
                h2, kb2 = items[n + 1]
                st_ = {}
                if kb == 9:
                    continue
                base = 1
                add(n, base, k_s0, h2, kb2, st_)
                add(n, base + 2, k_s1, h2, kb2, st_)
                add(n, base + 4, k_s2, h2, kb2, st_)
                add(n, base + 5, v_all, h2, kb2, st_)
                if kb == 8 and n + 2 < nitem:
                    h3, kb3 = items[n + 2]
                    st3 = {}
                    add(n, 7, k_s0, h3, kb3, st3)
                    add(n, 9, k_s1, h3, kb3, st3)
                    add(n, 11, k_s2, h3, kb3, st3)
                    add(n, 11, v_all, h3, kb3, st3)
                if 2 <= kb <= 5 and h + 1 < 8:
                    stq = {}
                    add(n, 7, q_a, h + 1, kb - 2, stq)
                    add(n, 9, q_b, h + 1, kb - 2, stq)
            for tb in range(4):
                stq = {}
                q_a(0, tb, stq)
                q_b(0, tb, stq)
            st0 = {}
            k_s0(0, 0, st0)
            k_s1(0, 0, st0)
            k_s2(0, 0, st0)
            v_all(0, 0, st0)

            def run_sched(n, step):
                for fn, a in sched[n].get(step, []):
                    fn(*a)

            woa = {}
            pend = []

            def flush(keep):
                while len(pend) > keep:
                    pt_, kc_, qb_ = pend.pop(0)
                    X("tensor", "matmul", [V_b.res[kc_ // 4], pt_.r], [RB[qb_ - 1]], PB[qb_ - 1],
                      lhsT=Vv[:, kc_, :], rhs=bf(pt_.ap), start=(kc_ == 0), stop=(kc_ == 35))

            for n, (h, kb) in enumerate(items):
                hg, hh = h // 4, h % 4
                po = (hh % 2) * 64
                QB = QT_b[h % 2]
                QTv = bf(QB.ap)
                if kb == 0 and hh == 0:
                    woa[hg] = wload(d_wout0[512 + hg * 256:512 + (hg + 1) * 256, :].rearrange("(c p) n -> p c n", p=128))
                if kb < 9:
                    step = 0
                    for c in range(4):
                        kc = kb * 4 + c
                        for qb in (1, 2, 3):
                            sb = nb("S", [3, 4, 7])
                            X("tensor", "matmul", [KT_b.res[kb], QB.res[qb]], [RB[sb]], PB[sb], lhsT=KT[:, kc * 128:(kc + 1) * 128],
                              rhs=QTv[:, blk(qb)], start=True, stop=True)
                            pt = rot("pt", PT_b)
                            X("scalar", "activation", [RB[sb], rsk_b[h % 2].res[kb]], [pt.r], out=bf(pt.ap), in_=PB[sb], func=AF.Exp,
                              scale=rsk_b[h % 2].ap[:, kc:kc + 1])
                            pend.append((pt, kc, qb))
                            flush(2)
                            run_sched(n, step)
                            step += 1
                    if kb == 8:
                        flush(0)
                        pending_norm = [(evac(qb - 1, 512), attnT[po:po + 64, hh // 2, blk(qb)], attn_b.res[qb]) for qb in (1, 2, 3)]
                else:
                    for s_ in range(2):
                        sb = nb("S", [3, 4, 7])
                        for c in range(2):
                            k0 = 4608 + s_ * 256 + c * 128
                            X("tensor", "matmul", [KT_b.res[9], QB.res[0]], [RB[sb]], PB[sb][:, c * 256:(c + 1) * 256],
                              lhsT=KT[:, k0:k0 + 128], rhs=QTv[:, s_ * 256:(s_ + 1) * 256], start=True, stop=True)
                        pt = rot("pt", PT_b)
                        ptv = bf(pt.ap)
                        for c in range(2):
                            X("scalar", "activation", [RB[sb], rsk_b[h % 2].res[9]], [pt.r], out=ptv[:, c * 256:(c + 1) * 256],
                              in_=PB[sb][:, c * 256:(c + 1) * 256], func=AF.Exp, scale=rsk_b[h % 2].ap[:, 36 + 2 * s_ + c:37 + 2 * s_ + c])
                        for c in range(2):
                            X("tensor", "matmul", [V_b.res[9], pt.r], [RB[0]], PB[0][:, s_ * 256:(s_ + 1) * 256],
                              lhsT=Vv[:, 36 + 2 * s_ + c, :], rhs=ptv[:, c * 256:(c + 1) * 256], start=(c == 0), stop=(c == 1))
                    ev0 = evac(0, 512)
                    todo = pending_norm + [(ev0, attnT[po:po + 64, hh // 2, blk(0)], attn_b.res[0])]
                    if hh == 3 or n + 4 >= nitem:
                        for (ev_, dst_, rr_) in todo:
                            normalize_sb(ev_, dst_, rr_, 512)
                    else:
                        for j_, (ev_, dst_, rr_) in enumerate(todo):
                            add(n + 1 + j_, 10, normalize_sb, ev_, dst_, rr_, 512)
                    if hh == 3:
                        woa_v, woa_r = woa.pop(hg)

                        def oproj_unit(tb, m, b, woa_v=woa_v, woa_r=woa_r):
                            for pr in range(2):
                                X("tensor", "matmul", [attn_b.res[tb], woa_r], [RB[b]], PB[b], lhsT=woa_v[:, pr, m * 128:(m + 1) * 128],
                                  rhs=attnT[:, pr, blk(tb)], start=(pr == 0), stop=(pr == 1))
                            epilogue(PB[b], RB[b], 0, 2, tb, m)
                        units = [(tb, m) for tb in range(4) for m in range(8)]
                        if hg == 0:
                            for ui, (tb, m) in enumerate(units):
                                add(n + 1 + ui // 4, 10 + (ui % 4) // 2, oproj_unit, tb, m, 5 + (ui % 2))
                        else:
                            for (tb, m) in units:
                                oproj_unit(tb, m, nb("po2", [5, 6, 3, 4, 7]))
            A.free(KT_b, V_b, attn_b, qln_b, ckvn_b, kr_b, lnsc_b, *rsk_b, *QT_b, *PT_b, *rden_l, *osb_l, *sqs_b, *r96_b, *rq_b)
        if stage < 3:
            A.free(sq_b, rstd_b)
        A.free(*tmpb)

        if stage >= 4:
            mlp(0, [(tb * 512, [RX[tb]], cond_of(tb)) for tb in range(4)])

        if dbg and stage < 5:
            ostg_d = A.alloc("ostg_dbg", 4096)
            for tb in range(4):
                store_tokens(xT[:, :, blk(tb)], RX[tb], 512, o_dbg[tb * 512:(tb + 1) * 512, :], ostg_d)
            A.free(ostg_d)

        if stage >= 5:
            hT_b = A.alloc("hT1", 8 * 2048 // 2, nres=4)
            hT = bf(hT_b.ap).rearrange("p (c t) -> p c t", c=8)
            RH = hT_b.res
            sq_b = A.alloc("sq1", 2048, nres=8)
            tmpb = [A.alloc("tmp1_%d" % i, 512) for i in range(2)]
            rstd_b = A.alloc("rstd1", 512)
            for tb in range(4):
                normmod(xT[:, :, blk(tb)], RX[tb], 512, MOD(1, 1, cond_of(tb)), MOD(1, 0, cond_of(tb)),
                        hT[:, :, blk(tb)], RH[tb], sq_b, tmpb, rstd_b)
            A.free(sq_b)
            SC1 = 0.125
            QCOL = [0, 768, 1280]
            sqp = [A.alloc("sqp%d" % i, 256) for i in range(2)]
            KTp = [A.alloc("KT1_%d" % i, 1024, nres=4) for i in range(2)]
            QTp = [A.alloc("QT1_%d" % i, 768, nres=3) for i in range(2)]
            Vw_b = A.alloc("Vw", 12 * 2 * 128 // 2, nres=3)
            Vw = bf(Vw_b.ap).rearrange("p (r h d) -> p r h d", r=12, h=2)
            Vp_b = A.alloc("Vp", 4 * 2 * 128 // 2)
            Vp = bf(Vp_b.ap).rearrange("p (r h d) -> p r h d", r=4, h=2)
            Kc_b = A.alloc("Kc", 512)
            Kc = bf(Kc_b.ap).rearrange("p (h k) -> p h k", h=2)
            X("vector", "memset", [], [Kc_b.r], Kc[64:128, :, :], 0.0)
            Vc_b = A.alloc("Vc", 4 * 2 * 128 // 2)
            Vc = bf(Vc_b.ap).rearrange("p (r h d) -> p r h d", r=4, h=2)
            T_b = [A.alloc("Ttab%d" % i, 1024) for i in range(2)]
            cm_b = A.alloc("cmask", 64)
            DMA("sync", cm_b.ap[0:64, :], d_cmask, cm_b.r, writes=[cm_b.r])
            DMA("sync", cm_b.ap[64:128, :], d_cmask, cm_b.r, writes=[cm_b.r])
            for tb_ in T_b:
                X("vector", "memset", [], [tb_.r], tb_.ap, 0.0)
            vT_l = [A.alloc("vT%d" % i, 256) for i in range(2)]
            kf_b = A.alloc("kf32", 512)
            kst_b = [A.alloc("kstage%d" % i, 512) for i in range(1)]
            E32_b = [A.alloc("E32_%d" % i, 768) for i in range(2)]
            PL_b = [A.alloc("PL%d" % i, 384) for i in range(3)]
            PT1_b = [A.alloc("PT1_%d" % i, 256) for i in range(2)]
            rden1 = [A.alloc("rden1_%d" % i, 512) for i in range(1)]
            attn1_b = A.alloc("attn1", 2 * 1536 // 2, nres=3)
            attn1 = bf(attn1_b.ap).rearrange("p (h t) -> p h t", h=2)
            X("vector", "memset", [], Vw_b.res, Vw[:, :, :, 64:128], 1.0)
            X("vector", "memset", [], [Vp_b.r], Vp[:, :, :, 64:128], 1.0)
            X("vector", "memset", [], [Vc_b.r], Vc[:, :, :, 64:128], 1.0)
            cnt1 = {}

            def rot1(key, lst):
                i = cnt1.get(key, 0)
                cnt1[key] = i + 1
                return lst[i % len(lst)]

            def normalize1(ob, dst, rdst, nq):
                rd = rot1("rden", rden1)
                X("scalar", "activation", [RB[ob]], [rd.r], out=rd.ap[0:64, 0:nq], in_=PB[ob][64:128, 0:nq], func=AF.Ln)
                X("scalar", "activation", [rd.r], [rd.r], out=rd.ap[0:64, 0:nq], in_=rd.ap[0:64, 0:nq], func=AF.Exp, scale=-1.0)
                X("vector", "tensor_tensor", [RB[ob], rd.r], [rdst], out=dst, in0=PB[ob][0:64, 0:nq], in1=rd.ap[0:64, 0:nq],
                  op=ALU.mult)

            def headnorm(bp, gvec):
                sq = rot1("sqp", sqp)
                sqv = bf(sq.ap)
                X("scalar", "activation", [RB[bp]], [sq.r], out=sqv, in_=PB[bp], func=AF.Square)
                bs = nb("prep1", [2, 3, 4, 5, 6, 7])
                X("tensor", "matmul", [sq.r, RC], [RB[bs]], PB[bs], lhsT=blk_b, rhs=sqv, start=True, stop=True)
                rb_ = rot1("rstd3", [rstd_b, tmpb[0], tmpb[1]])
                rstd_from(PB[bs], rb_.ap, 1.0 / 64, slice(0, 128), RB[bs], rb_.r)
                return rb_

            deferred = []

            def defer(fn):
                deferred.append(fn)
                while len(deferred) > 1:
                    deferred.pop(0)()

            def drain():
                while deferred:
                    deferred.pop(0)()

            def own_rows(i):
                if i < 4:
                    return i, 12
                if i > 12:
                    return 12, i + 8
                return i, i + 8

            for hh in range(2):
                X("vector", "memset", [], KTp[hh].res, bf(KTp[hh].ap)[64:128, :], 0.0)
                X("vector", "memset", [], QTp[hh].res, bf(QTp[hh].ap)[64:128, :], 0.0)
            for hh in range(2):
                DMA("gpsimd", bf(KTp[hh].ap)[64:80, 512:2048], d_pen, KTp[hh].res[1], writes=KTp[hh].res[1:4])
                DMA("gpsimd", bf(QTp[hh].ap)[64:80, 512:1536], d_qoh, QTp[hh].res[1], writes=QTp[hh].res[1:3])
            for c in range(8):
                w1v, w1r = wload(d_win1[:, c * 384:(c + 1) * 384].rearrange("(kc p) n -> p kc n", p=128))
                KTv = [bf(KTp[i].ap) for i in range(2)]
                QTv = [bf(QTp[i].ap) for i in range(2)]
                DMA("gpsimd", Kc[0:64, :, :], d_ck1[:, c * 1024:(c + 1) * 1024].rearrange("p (h k) -> p h k", h=2), Kc_b.r,
                    writes=[Kc_b.r])
                for t in range(4):
                    DMA("gpsimd", Vc[:, t, :, 0:64], d_cv1[t * 128:(t + 1) * 128, c * 128:(c + 1) * 128].rearrange("p (h d) -> p h d", h=2),
                        Vc_b.r, writes=[Vc_b.r])
                Tv = []
                for hh in range(2):
                    h = 2 * c + hh
                    tb_ = T_b[hh]
                    DMA("sync", tb_.ap[0:64, 0:960], d_rbt[h], tb_.r, writes=[tb_.r])
                    DMA("sync", tb_.ap[64:128, 64:1024], d_rbt[h], tb_.r, writes=[tb_.r])
                    X("vector", "memset", [], [tb_.r], tb_.ap[0:64, 960:1024], 0.0)
                    X("vector", "memset", [], [tb_.r], tb_.ap[64:128, 0:64], 0.0)
                    X("scalar", "activation", [tb_.r], [tb_.r], out=tb_.ap, in_=tb_.ap, func=AF.Exp)
                    t3 = tb_.ap.rearrange("p (r q) -> p r q", r=16)
                    X("vector", "tensor_tensor", [tb_.r, cm_b.r], [tb_.r], out=t3, in0=t3,
                      in1=cm_b.ap.unsqueeze(1).to_broadcast([128, 16, 64]), op=ALU.mult)
                    Tv.append(t3)
                for tb in range(4):
                    bp = nb("prep1", [2, 3, 4, 5, 6, 7])
                    for kc in range(8):
                        X("tensor", "matmul", [RH[tb], w1r], [RB[bp]], PB[bp], lhsT=w1v[:, kc, 128:256], rhs=hT[:, kc, blk(tb)],
                          start=(kc == 0), stop=(kc == 7))
                    def post_k(tb=tb, bp=bp):
                        rb_ = headnorm(bp, V_NGK)
                        for hh in range(2):
                            ps = slice(hh * 64, hh * 64 + 64)
                            X("vector", "scalar_tensor_tensor", [RB[bp], rb_.r, RC], [KTp[hh].res[tb]], out=KTv[hh][0:64, blk(tb)],
                              in0=PB[bp][ps, :], scalar=V_NGK[ps, :], in1=rb_.ap[ps, :], op0=ALU.mult, op1=ALU.mult)
                        if tb == 0:
                            X("vector", "scalar_tensor_tensor", [RB[bp], rb_.r, RC], [kf_b.r], out=kf_b.ap, in0=PB[bp],
                              scalar=V_NGK, in1=rb_.ap, op0=ALU.mult, op1=ALU.mult)
                            ks_ = rot1("kst", kst_b)
                            ksv = ks_.ap.rearrange("p (a d) -> p a d", a=4)
                            bt = nb("prep1", [2, 3, 4, 5, 6, 7])
                            for a in range(4):
                                X("tensor", "transpose", [kf_b.r, RC], [RB[bt]], out=PB[bt][:, a * 128:(a + 1) * 128],
                                  in_=kf_b.ap[:, a * 128:(a + 1) * 128], identity=ident_f)
                            X("vector", "tensor_copy", [RB[bt]], [ks_.r], out=ksv, in_=PB[bt].rearrange("p (a d) -> p a d", a=4))
                            DMA("sync", o_k1[:, c * 128:(c + 1) * 128].rearrange("(a p) d -> p a d", p=128), ksv, ks_.r, reads=[ks_.r])
                    defer(post_k)
                for qi in range(3):
                    q0 = QCOL[qi]
                    rq_ = [RH[0]] if qi == 0 else ([RH[1], RH[2]] if qi == 1 else [RH[2], RH[3]])
                    bp = nb("prep1", [2, 3, 4, 5, 6, 7])
                    for kc in range(8):
                        X("tensor", "matmul", rq_ + [w1r], [RB[bp]], PB[bp], lhsT=w1v[:, kc, 0:128], rhs=hT[:, kc, q0:q0 + 512],
                          start=(kc == 0), stop=(kc == 7))
                    def post_q(qi=qi, bp=bp):
                        rb_ = headnorm(bp, V_NGQ)
                        for hh in range(2):
                            ps = slice(hh * 64, hh * 64 + 64)
                            X("vector", "scalar_tensor_tensor", [RB[bp], rb_.r, RC], [QTp[hh].res[qi]],
                              out=QTv[hh][0:64, qi * 512:(qi + 1) * 512], in0=PB[bp][ps, :], scalar=V_NGQ[ps, :], in1=rb_.ap[ps, :],
                              op0=ALU.mult, op1=ALU.mult)
                    defer(post_q)
                for tb in range(4):
                    bp = nb("prep1", [2, 3, 4, 5, 6, 7])
                    for kc in range(8):
                        X("tensor", "matmul", [RH[tb], w1r], [RB[bp]], PB[bp], lhsT=w1v[:, kc, 256:384], rhs=hT[:, kc, blk(tb)],
                          start=(kc == 0), stop=(kc == 7))

                    def post_v(tb=tb, bp=bp):
                        vT_b = rot1("vT", vT_l)
                        vTv = bf(vT_b.ap)
                        X("scalar", "copy", [RB[bp]], [vT_b.r], out=vTv, in_=PB[bp])
                        bt = nb("prep1", [2, 3, 4, 5, 6, 7])
                        ptb = PB[bt].bitcast(BF16)
                        if tb == 0:
                            X("vector", "tensor_copy", [RB[bp]], [kf_b.r], out=kf_b.ap, in_=PB[bp])
                            for a in range(4):
                                X("tensor", "transpose", [vT_b.r, RC], [RB[bt]], out=ptb[:, a * 128:(a + 1) * 128],
                                  in_=vTv[:, a * 128:(a + 1) * 128], identity=ident_b)
                            X("vector", "tensor_copy", [RB[bt]], [Vp_b.r], out=Vp[:, :, :, 0:64],
                              in_=ptb[:, 0:512].rearrange("p (a h d) -> p a h d", a=4, h=2))
                            ks_ = rot1("kst", kst_b)
                            ksv = ks_.ap.rearrange("p (a d) -> p a d", a=4)
                            bt2 = nb("prep1", [2, 3, 4, 5, 6, 7])
                            for a in range(4):
                                X("tensor", "transpose", [kf_b.r, RC], [RB[bt2]], out=PB[bt2][:, a * 128:(a + 1) * 128],
                                  in_=kf_b.ap[:, a * 128:(a + 1) * 128], identity=ident_f)
                            X("vector", "tensor_copy", [RB[bt2]], [ks_.r], out=ksv, in_=PB[bt2].rearrange("p (a d) -> p a d", a=4))
                            DMA("sync", o_v1[:, c * 128:(c + 1) * 128].rearrange("(a p) d -> p a d", p=128), ksv, ks_.r, reads=[ks_.r])
                        else:
                            for a in range(4):
                                X("tensor", "transpose", [vT_b.r, RC], [RB[bt]], out=ptb[:, a * 128:(a + 1) * 128],
                                  in_=vTv[:, a * 128:(a + 1) * 128], identity=ident_b)
                            X("vector", "tensor_copy", [RB[bt]], [Vw_b.res[tb - 1]], out=Vw[:, (tb - 1) * 4:tb * 4, :, 0:64],
                              in_=ptb[:, 0:512].rearrange("p (a h d) -> p a h d", a=4, h=2))
                    defer(post_v)
                drain()
                for hh in range(2):
                    hidx = (c % 2) * 2 + hh
                    K_, Q_ = KTv[hh], QTv[hh]
                    rK, rQ = KTp[hh].res, QTp[hh].res
                    po1 = (hidx % 2) * 64
                    obp = nb("O1", [0, 1])
                    obs = [nb("O1", [0, 1]), nb("O1", [0, 1])]
                    cp = []

                    def cflush(keep):
                        while len(cp) > keep:
                            fn = cp.pop(0)
                            fn()
                    for s_ in range(2):
                        sb = nb("SC", [2, 3, 4, 5, 6, 7])
                        for cc_ in range(2):
                            k0 = s_ * 256 + cc_ * 128
                            X("tensor", "matmul", [rK[0], rQ[0]], [RB[sb]], PB[sb][:, cc_ * 256:(cc_ + 1) * 256], lhsT=K_[:, k0:k0 + 128],
                              rhs=Q_[:, s_ * 256:(s_ + 1) * 256], start=True, stop=True)
                        pt = rot1("pl", PL_b)
                        ptv = bf(pt.ap)[:, 0:512]
                        X("scalar", "activation", [RB[sb]], [pt.r], out=ptv, in_=PB[sb], func=AF.Exp, scale=SC1)

                        def pv_p(s_=s_, pt=pt, ptv=ptv):
                            for cc_ in range(2):
                                X("tensor", "matmul", [Vp_b.r, pt.r], [RB[obp]], PB[obp][:, s_ * 256:(s_ + 1) * 256],
                                  lhsT=Vp[:, 2 * s_ + cc_, hh, :], rhs=ptv[:, cc_ * 256:(cc_ + 1) * 256], start=(cc_ == 0), stop=(cc_ == 1))
                        cp.append(pv_p)
                        cflush(2)
                    cflush(0)
                    normalize1(obp, attn1[po1:po1 + 64, hidx // 2, 0:512], attn1_b.res[0], 512)
                    obs = [obp, obs[0]] if False else obs
                    for qb in range(2):
                        ob = obs[qb]
                        qs = slice(512 + qb * 512, 1024 + qb * 512)
                        for c4 in range(4):
                            sb = nb("SC", [2, 3, 4, 5, 6, 7])
                            X("tensor", "matmul", [Kc_b.r, rQ[1 + qb]], [RB[sb]], PB[sb], lhsT=Kc[:, hh, c4 * 128:(c4 + 1) * 128],
                              rhs=Q_[:, qs], start=True, stop=True)
                            pt = rot1("pl", PL_b)
                            ptv = bf(pt.ap)[:, 0:512]
                            X("scalar", "activation", [RB[sb]], [pt.r], out=ptv, in_=PB[sb], func=AF.Exp, scale=SC1)

                            def pv_c(ob=ob, c4=c4, pt=pt, ptv=ptv):
                                X("tensor", "matmul", [Vc_b.r, pt.r], [RB[ob]], PB[ob], lhsT=Vc[:, c4, hh, :], rhs=ptv,
                                  start=(c4 == 0), stop=False)
                            cp.append(pv_c)
                            cflush(2)
                    cflush(0)
                    pend = []

                    def rows_of(wr):
                        if wr > 22:
                            return None
                        return (0 if wr <= 11 else wr - 7), (min(15, wr) if wr < 12 else 15)
                    chunks = []
                    for m_ in range(12):
                        ra, rb_ = rows_of(2 * m_), rows_of(2 * m_ + 1)
                        ilo_c = min(ra[0], rb_[0]) if rb_ else ra[0]
                        ihi_c = max(ra[1], rb_[1]) if rb_ else ra[1]
                        chunks.append((m_, ilo_c, ihi_c))
                    last_m = {}
                    for (m_, ilo_c, ihi_c) in chunks:
                        for qb_ in range(2):
                            if max(ilo_c, qb_ * 8) <= min(ihi_c, qb_ * 8 + 7):
                                last_m[qb_] = m_

                    def flush1(keep):
                        while len(pend) > keep:
                            pl_, plv_, m_, ilo_, ihi_ = pend.pop(0)
                            for qb_ in range(2):
                                a_, b_ = max(ilo_, qb_ * 8), min(ihi_, qb_ * 8 + 7)
                                if a_ > b_:
                                    continue
                                X("tensor", "matmul", [Vw_b.res[m_ // 4], pl_.r], [RB[obs[qb_]]],
                                  PB[obs[qb_]][:, (a_ - qb_ * 8) * 64:(b_ - qb_ * 8 + 1) * 64], lhsT=Vw[:, m_, hh, :],
                                  rhs=plv_[:, (a_ - ilo_) * 64:(b_ - ilo_ + 1) * 64], start=False, stop=(m_ == last_m[qb_]))
                    for (m_, ilo, ihi) in chunks:
                        nq = ihi - ilo + 1
                        jj0 = ilo - 2 * m_ + 11
                        pair = nb("SL", [2, 4, 6])
                        sl = psum_t[:, pair * 512:pair * 512 + 1024]
                        kcol = 512 + m_ * 128
                        for part in range(2):
                            a_, b_ = part * 8, min(nq, part * 8 + 8)
                            if a_ >= b_:
                                continue
                            qc0 = 512 + (ilo + a_) * 64
                            rqs = sorted(set([1 + (ilo + a_) // 8, 1 + (ilo + b_ - 1) // 8]))
                            X("tensor", "matmul", [rK[1 + m_ // 4]] + [rQ[x_] for x_ in rqs], [RB[pair + part]],
                              sl[:, a_ * 64:b_ * 64], lhsT=K_[:, kcol:kcol + 128], rhs=Q_[:, qc0:qc0 + (b_ - a_) * 64],
                              start=True, stop=True)
                        e32 = rot1("e32", E32_b)
                        rbs = [RB[pair]] + ([RB[pair + 1]] if nq > 8 else [])
                        X("scalar", "activation", rbs, [e32.r], out=e32.ap[:, 0:nq * 64], in_=sl[:, 0:nq * 64], func=AF.Exp,
                          scale=SC1)
                        pl = rot1("pl", PL_b)
                        plv2 = bf(pl.ap)[:, 0:nq * 64]
                        X("vector", "tensor_tensor", [e32.r, T_b[hh].r], [pl.r], out=plv2.rearrange("p (r q) -> p r q", q=64),
                          in0=e32.ap[:, 0:nq * 64].rearrange("p (r q) -> p r q", q=64), in1=Tv[hh][:, jj0:jj0 + nq, :],
                          op=ALU.mult)
                        pend.append((pl, plv2, m_, ilo, ihi))
                        flush1(2)
                    flush1(0)
                    for qb in range(2):
                        qs = slice(512 + qb * 512, 1024 + qb * 512)
                        normalize1(obs[qb], attn1[po1:po1 + 64, hidx // 2, qs], attn1_b.res[1 + qb], 512)
                if c % 2 == 1:
                    grp = c // 2
                    wo1v, wo1r = wload(d_wout1[grp * 256:(grp + 1) * 256, :].rearrange("(c p) n -> p c n", p=128))
                    for qi in range(3):
                        q0 = QCOL[qi]
                        rxs = [RX[0]] if qi == 0 else ([RX[1], RX[2]] if qi == 1 else [RX[2], RX[3]])
                        cd = 0 if qi == 0 else 1
                        for m in range(8):
                            b = nb("prep1", [2, 3, 4, 5, 6, 7])
                            for hq in range(2):
                                X("tensor", "matmul", [attn1_b.res[qi], wo1r], [RB[b]], PB[b], lhsT=wo1v[:, hq, m * 128:(m + 1) * 128],
                                  rhs=attn1[:, hq, qi * 512:(qi + 1) * 512], start=(hq == 0), stop=(hq == 3 - 2))
                            X("vector", "scalar_tensor_tensor", [RB[b], RM] + rxs, rxs, out=xT[:, m, q0:q0 + 512], in0=PB[b],
                              scalar=MOD(1, 2, cd)[:, m:m + 1], in1=xT[:, m, q0:q0 + 512], op0=ALU.mult, op1=ALU.add)
            A.free(hT_b, rstd_b, Vw_b, Vp_b, Kc_b, Vc_b, cm_b, kf_b, attn1_b, *vT_l, *tmpb, *sqp, *KTp, *QTp, *T_b, *kst_b,
                   *E32_b, *PL_b, *PT1_b, *rden1)

        if dbg and stage >= 5:
            ostg_d = A.alloc("ostg_dbg", 4096)
            for tb in range(4):
                store_tokens(xT[:, :, blk(tb)], RX[tb], 512, o_dbg[tb * 512:(tb + 1) * 512, :], ostg_d)
            A.free(ostg_d)

        if stage >= 6:
            mlp(1, [(0, [RX[0]], 0), (768, [RX[1], RX[2]], 1), (1280, [RX[2], RX[3]], 1)])
            ostg = A.alloc("ostg", 4096)
            store_tokens(xT[:, :, 0:512], RX[0], 512, o_yp, ostg)
            store_tokens(xT[:, :, 768:1280], RX[1], 512, o_ys[0:512, :], ostg, rextra=[RX[2]])
            store_tokens(xT[:, :, 1280:1792], RX[2], 512, o_ys[512:1024, :], ostg, rextra=[RX[3]])
            A.free(ostg)

        P.emit(nc, st)
    return nc


def _pvec(v, n):
    return np.ascontiguousarray(np.asarray(v, np.float32).reshape(n, 128).T)


_SWAP = np.array([(j + 8) if (j % 16) < 8 else (j - 8) for j in range(32)])
_SGN = np.array([-1.0 if (j % 16) < 8 else 1.0 for j in range(32)])


def _rope_tables(rows, cols):
    f = THETA ** (-np.arange(0, 16, 2, dtype=np.float64) / 16.0)
    n = len(rows)
    ang = np.zeros((32, n))
    for j in range(32):
        pos = rows if j < 16 else cols
        ang[j] = pos.astype(np.float32).astype(np.float64) * np.float32(f[j % 8]).astype(np.float64)
    ang = ang.astype(np.float32).astype(np.float64)
    return np.concatenate([np.cos(ang), _SGN[:, None] * np.sin(ang)], axis=0).astype(np.float32)


def _prep_inputs(inp):
    f32 = np.float32
    g = {k: np.asarray(v) for k, v in inp.items()}
    shared = {}
    shared["w_ada0"] = g["w_ada_l0"]
    shared["w_ada1"] = g["w_ada_l1"]
    shared["b_ada0"] = _pvec(g["b_ada_l0"], 48)
    shared["b_ada1"] = _pvec(g["b_ada_l1"], 48)
    shared["w_mlp1_0"] = g["w_mlp1_l0"]
    shared["w_mlp1_1"] = g["w_mlp1_l1"]
    shared["w_mlp2_0"] = g["w_mlp2_l0"]
    shared["w_mlp2_1"] = g["w_mlp2_l1"]
    vec = np.zeros((128, 64), f32)
    vec[:, 0:8] = _pvec(g["norm_mix_l0"], 8)
    vec[:, 8:16] = _pvec(g["norm_mlp_l0"], 8)
    vec[:, 16:24] = _pvec(g["norm_mix_l1"], 8)
    vec[:, 24:32] = _pvec(g["norm_mlp_l1"], 8)
    vec[:, 32:34] = _pvec(g["q_lora_norm_l0"], 2)
    vec[:, 34] = g["kv_lora_norm_l0"]
    gq, gk = g["mla_q_norm_l0"], g["mla_k_norm_l0"]
    vec[0:96, 35] = gq
    vec[64:96, 36] = gq[64 + _SWAP]
    vec[0:96, 37] = gk
    vec[64:96, 38] = gk[64 + _SWAP]
    vec[:, 39:43] = _pvec(g["pool_scale_l0"], 4)
    vec[:, 43] = np.tile(g["na_q_norm_l1"], 2)
    vec[:, 44] = np.tile(g["na_k_norm_l1"], 2)
    shared["vecs"] = vec
    w_in0 = g["w_in_l0"]
    shared["w_in0_pool"] = np.ascontiguousarray(w_in0[:, 0:512])
    shared["w_in0_q"] = np.ascontiguousarray(w_in0[:, 512:768])
    kr = w_in0[:, 896:928]
    dummy = w_in0[:, 768:832]
    shared["w_in0_kv"] = np.ascontiguousarray(np.concatenate([w_in0[:, 768:896], dummy, kr, dummy, kr[:, _SWAP]], axis=1))
    wq = g["w_q_up_l0"].reshape(256, 8, 96)
    wq_aug = np.concatenate([wq, wq[:, :, 0:64], wq[:, :, 64 + _SWAP]], axis=2)
    shared["w_qup_aug"] = np.ascontiguousarray(wq_aug.reshape(256, 1536))
    shared["w_kvup"] = g["w_kv_up_l0"]
    shared["w_pool"] = np.ascontiguousarray(g["w_pool_l0"].transpose(1, 0, 2).reshape(128, 512))
    shared["w_out0"] = g["w_out_l0"]
    t = np.arange(4096)
    shared["rope_k"] = _rope_tables(t // 64, t % 64)
    w1 = g["w_in_l1"]
    pieces = []
    for c in range(8):
        pieces += [w1[:, c * 128:(c + 1) * 128], w1[:, 1024 + c * 128:1024 + (c + 1) * 128],
                   w1[:, 2048 + c * 128:2048 + (c + 1) * 128]]
    shared["w_in1"] = np.ascontiguousarray(np.concatenate(pieces, axis=1))
    shared["w_out1"] = g["w_out_l1"]
    rb = g["rel_bias_l1"]
    kc = np.arange(64)
    dc = np.clip(kc[:, None] - kc[None, :] + 15, 0, 30)
    rbt = rb[:, ::-1, :][:, :, dc]
    shared["rbt"] = np.ascontiguousarray(rbt.transpose(0, 2, 1, 3).reshape(16, 64, 960))
    cs = np.clip(kc - 8, 0, 48)
    cm = (kc[:, None] >= cs[None, :]) & (kc[:, None] < cs[None, :] + 16)
    shared["cmask"] = cm.astype(f32)
    maps = []
    xs = g["x_sample"]
    for core in range(8):
        b, q = core // 4, core % 4
        m = dict(shared)
        m["xp"] = np.ascontiguousarray(g["x_prompt"][2 * core:2 * core + 2].reshape(512, D))
        r0 = 16 * q - 4
        xw = np.zeros((24, 64, D), f32)
        xpre = np.zeros((128, D), f32)
        valid = np.zeros(25, f32)
        xg = xs[b].reshape(64, 64, D)
        for wr in range(24):
            if 0 <= r0 + wr < 64:
                xw[wr] = xg[r0 + wr]
                valid[1 + wr] = 1.0
        if 0 <= r0 - 1 < 64:
            xpre[0:64] = xg[r0 - 1]
            valid[0] = 1.0
        m["xw"] = xw.reshape(1536, D)
        m["xpre"] = xpre
        m["xb"] = np.ascontiguousarray(xs[b])
        cond = np.stack([g["c_ctx"], g["c"][b]], axis=1)
        m["condT"] = np.ascontiguousarray(cond.reshape(8, 128, 2).transpose(1, 0, 2).reshape(128, 16))
        wt = np.arange(1536)
        m["rope_q"] = _rope_tables(r0 + wt // 64, wt % 64)
        m["cache_ckvT"] = np.ascontiguousarray(g["cache_l0_mla_ckv"][b].T)
        m["cache_kropeT"] = np.ascontiguousarray(g["cache_l0_mla_krope"][b].T)
        m["pool_mask"] = np.ascontiguousarray(np.broadcast_to(np.repeat(valid, 64)[None, :], (128, 1600))).astype(f32)
        pinv = np.zeros((4, 4, 8), f32)
        for gi, w in enumerate((2, 4, 8, 16)):
            for bd, (S, first, exists) in enumerate([(256, True, True), (256, False, True),
                                                    (4096, True, q == 0), (4096, False, q == 3)]):
                for j in range(8):
                    tt = j if first else S - 8 + j
                    if exists:
                        lo = np.clip(tt - w // 2, 0, S)
                        hi = np.clip(tt - w // 2 + w, 0, S)
                        pinv[gi, bd, j] = 1.0 / float(hi - lo)
                    else:
                        pinv[gi, bd, j] = 1.0 / w
        m["pool_inv"] = np.ascontiguousarray(np.broadcast_to(pinv.reshape(1, 128), (128, 128))).astype(f32)
        pen = np.full((16, 24), NEGBIG, f32)
        for i in range(16):
            gr = 16 * q + i
            rs = int(np.clip(gr - 4, 0, 56))
            for kr_ in range(rs, rs + 8):
                pen[i, kr_ - r0] = 0.0
        m["pen"] = np.ascontiguousarray(np.repeat(pen, 64, axis=1))
        m["qoh"] = np.ascontiguousarray(np.repeat(np.eye(16, dtype=f32), 64, axis=1))
        m["cache_k1T"] = np.ascontiguousarray(g["cache_l1_na_k"][b].transpose(2, 1, 0).reshape(64, 16 * 512))
        m["cache_v1"] = np.ascontiguousarray(g["cache_l1_na_v"][b].reshape(512, 1024))
        maps.append(m)
    return maps


_NC_CACHE = {}
_BUILD_ARGS = {}


def kernel(**inputs):
    maps = _prep_inputs(inputs)
    if "nc" not in _NC_CACHE:
        _NC_CACHE["nc"] = build_program(**_BUILD_ARGS)
    nc = _NC_CACHE["nc"]
    res = run_bass_kernel_spmd(nc, maps, core_ids=list(range(8)))
    R = res.results
    yp = np.concatenate([R[c]["yp"].reshape(2, 256, D) for c in range(8)], axis=0)
    ys = np.stack([np.concatenate([R[b * 4 + q]["ys"] for q in range(4)], axis=0) for b in range(2)], axis=0)
    ckv = np.concatenate([R[c]["ckv_new"].reshape(2, 256, 128) for c in range(8)], axis=0)
    kr = np.concatenate([R[c]["krope_new"].reshape(2, 256, 32) for c in range(8)], axis=0)
    k1 = np.concatenate([R[c]["k_new"].reshape(2, 256, 16, 64) for c in range(8)], axis=0)
    v1 = np.concatenate([R[c]["v_new"].reshape(2, 256, 16, 64) for c in range(8)], axis=0)
    return (yp.astype(np.float32), ys.astype(np.float32), ckv.astype(np.float32), kr.astype(np.float32),
            k1.astype(np.float32), v1.astype(np.float32))
```

```python
import numpy as np
from contextlib import ExitStack
import concourse.bass as bass
import concourse.mybir as mybir
from concourse.bass_utils import run_bass_kernel_spmd

F32 = mybir.dt.float32
BF16 = mybir.dt.bfloat16
AF = mybir.ActivationFunctionType
ALU = mybir.AluOpType

ENGS = ("tensor", "vector", "scalar", "gpsimd", "sync")
D = 1024
NCH = 8
EPS = 1e-6
THETA = 10000.0
NEGBIG = -30000.0


class Res:
    __slots__ = ("name", "last_w", "readers", "dsem_idx", "dcount", "excl")

    def __init__(self, name, excl=False):
        self.name = name
        self.excl = excl
        self.last_w = None
        self.readers = []
        self.dsem_idx = None
        self.dcount = 0

    def pending(self):
        r = list(self.readers)
        if self.last_w is not None:
            r.append(self.last_w)
        return r


class Op:
    __slots__ = ("eng", "fn", "deps", "signal", "sigval", "is_dma", "dsem_idx", "dval")

    def __init__(self, eng, fn, is_dma=False):
        self.eng = eng
        self.fn = fn
        self.deps = []
        self.signal = False
        self.sigval = 0
        self.is_dma = is_dma
        self.dsem_idx = None
        self.dval = 0


class Prog:
    def __init__(self):
        self.ops = []
        self.n_dsem = 0

    def _track(self, op, reads, writes):
        for r in reads:
            if r.last_w is not None:
                op.deps.append(r.last_w)
            if r.excl:
                for rd in r.readers:
                    if rd.eng != op.eng:
                        op.deps.append(rd)
            r.readers.append(op)
        for w in writes:
            if w.last_w is not None:
                op.deps.append(w.last_w)
            for rd in w.readers:
                if rd is not op:
                    op.deps.append(rd)
            w.last_w = op
            w.readers = []

    def op(self, eng, fn, reads=(), writes=()):
        o = Op(eng, fn)
        self._track(o, reads, writes)
        self.ops.append(o)
        return o

    def dma(self, eng, fn, res, reads=(), writes=()):
        o = Op(eng, fn, is_dma=True)
        if res.dsem_idx is None:
            res.dsem_idx = self.n_dsem
            self.n_dsem += 1
        res.dcount += 1
        o.dsem_idx = res.dsem_idx
        o.dval = 16 * res.dcount
        self._track(o, reads, writes)
        self.ops.append(o)
        return o

    def emit(self, nc, stack, final_eng="sync"):
        fin = Op(final_eng, None)
        last_by_sem = {}
        for o in self.ops:
            if o.is_dma:
                last_by_sem[o.dsem_idx] = o
        fin.deps = list(last_by_sem.values())
        ops = self.ops + [fin]
        for o in ops:
            for d in o.deps:
                if not d.is_dma:
                    if d.eng == "tensor" and o.eng == "tensor" and not o.is_dma:
                        continue
                    d.signal = True
        cnt = {e: 0 for e in ENGS}
        for o in ops:
            if o.signal:
                cnt[o.eng] += 1
                o.sigval = cnt[o.eng]
        esem = {e: stack.enter_context(nc.semaphore("es_" + e)) for e in ENGS}
        dsem = [stack.enter_context(nc.semaphore("ds_%d" % i)) for i in range(self.n_dsem)]
        per_eng = {e: [o for o in ops if o.eng == e] for e in ENGS}
        block = stack.enter_context(nc.Block())

        def run(e, engobj):
            waited = {}
            for o in per_eng[e]:
                need = {}
                for d in o.deps:
                    if d.is_dma:
                        key = ("d", d.dsem_idx)
                        val = d.dval
                    else:
                        if d.eng == "tensor" and e == "tensor" and not o.is_dma:
                            continue
                        key = ("e", d.eng)
                        val = d.sigval
                    if val > need.get(key, 0):
                        need[key] = val
                for key, val in need.items():
                    if waited.get(key, 0) >= val:
                        continue
                    waited[key] = val
                    s = dsem[key[1]] if key[0] == "d" else esem[key[1]]
                    engobj.wait_ge(s, val)
                if o.fn is None:
                    continue
                inst = o.fn(engobj)
                if o.is_dma:
                    inst.then_inc(dsem[o.dsem_idx], 16)
                elif o.signal:
                    inst.then_inc(esem[e], 1)

        @block.tensor
        def _(eng):
            run("tensor", eng)

        @block.vector
        def _(eng):
            run("vector", eng)

        @block.scalar
        def _(eng):
            run("scalar", eng)

        @block.gpsimd
        def _(eng):
            run("gpsimd", eng)

        @block.sync
        def _(eng):
            run("sync", eng)


class Buf:
    def __init__(self, ap, res, c0, c1):
        self.ap = ap
        self.res = res
        self.c0 = c0
        self.c1 = c1

    @property
    def r(self):
        return self.res[0]


class Arena:
    def __init__(self, ap, ncols):
        self.ap = ap
        self.ncols = ncols
        self.live = []
        self.freed = []

    def alloc(self, name, ncols, nres=1):
        ivs = sorted((c0, c1) for c0, c1, _ in self.live)
        pos = 0
        for c0, c1 in ivs:
            if c0 - pos >= ncols:
                break
            pos = max(pos, c1)
        if pos + ncols > self.ncols:
            lay = sorted((c0, c1, b.res[0].name) for c0, c1, b in self.live)
            raise AssertionError("SBUF arena overflow allocating %s (%d cols at %d); live=%s" % (name, ncols, pos, lay))
        res = [Res("%s_%d" % (name, i)) for i in range(nres)]
        keep = []
        for c0, c1, ops in self.freed:
            if c0 < pos + ncols and pos < c1:
                for r in res:
                    r.readers.extend(ops)
                if c0 < pos:
                    keep.append((c0, pos, ops))
                if c1 > pos + ncols:
                    keep.append((pos + ncols, c1, ops))
            else:
                keep.append((c0, c1, ops))
        self.freed = keep
        b = Buf(self.ap[:, pos:pos + ncols], res, pos, pos + ncols)
        self.live.append((pos, pos + ncols, b))
        return b

    def free(self, *bufs):
        for b in bufs:
            self.live = [t for t in self.live if t[2] is not b]
            ops = []
            for r in b.res:
                ops.extend(r.pending())
            self.freed.append((b.c0, b.c1, ops))


def build_program(stage=99, dbg=False, mini=None, skip=()):
    nc = bass.Bass("TRN2", target_bir_lowering=False)
    P = Prog()

    def din(name, shape, dt=F32):
        if mini is not None and name not in mini:
            shape = [1, 1]
        return nc.dram_tensor(name, list(shape), dt, kind="ExternalInput").ap()

    def dout(name, shape):
        return nc.dram_tensor(name, list(shape), F32, kind="ExternalOutput").ap()

    d_xp = din("xp", [512, D])
    d_xw = din("xw", [1536, D])
    d_xpre = din("xpre", [128, D])
    d_xb = din("xb", [4096, D])
    d_cond = din("condT", [128, 16])
    d_wada = [din("w_ada0", [D, 6 * D]), din("w_ada1", [D, 6 * D])]
    d_bada = [din("b_ada0", [128, 48]), din("b_ada1", [128, 48])]
    d_vec = din("vecs", [128, 64])
    d_win0_pool = din("w_in0_pool", [D, 512])
    d_win0_q = din("w_in0_q", [D, 256])
    d_win0_kv = din("w_in0_kv", [D, 320])
    d_wqup = din("w_qup_aug", [256, 1536])
    d_wkvup = din("w_kvup", [128, 1024])
    d_wpool = din("w_pool", [128, 512])
    d_wout0 = din("w_out0", [D, D])
    d_mlp1 = [din("w_mlp1_0", [D, 4 * D]), din("w_mlp1_1", [D, 4 * D])]
    d_mlp2 = [din("w_mlp2_0", [4 * D, D]), din("w_mlp2_1", [4 * D, D])]
    d_ropek = din("rope_k", [64, 4096])
    d_ropeq = din("rope_q", [64, 1536])
    d_cckv = din("cache_ckvT", [128, 512])
    d_ckr = din("cache_kropeT", [32, 512])
    d_pmask = din("pool_mask", [128, 1600])
    d_pinv = din("pool_inv", [128, 128])
    d_win1 = din("w_in1", [D, 8 * 384])
    d_wout1 = din("w_out1", [D, D])
    d_rbt = din("rbt", [16, 64, 960])
    d_cmask = din("cmask", [64, 64])
    d_pen = din("pen", [16, 1536])
    d_qoh = din("qoh", [16, 1024])
    d_ck1 = din("cache_k1T", [64, 16 * 512])
    d_cv1 = din("cache_v1", [512, 1024])

    o_yp = dout("yp", [512, D])
    o_ys = dout("ys", [1024, D])
    o_ckv = dout("ckv_new", [512, 128])
    o_kr = dout("krope_new", [512, 32])
    o_k1 = dout("k_new", [512, D])
    o_v1 = dout("v_new", [512, D])
    o_dbg = dout("dbg", [2048, D]) if dbg else None

    with ExitStack() as st:
        NCOLS = 53000
        arena_t = st.enter_context(nc.sbuf_tensor("arena", [128, NCOLS], F32))
        psum_t = st.enter_context(nc.psum_tensor("psum", [128, 8 * 512], F32))
        A = Arena(arena_t, NCOLS)
        PB = [psum_t[:, i * 512:(i + 1) * 512] for i in range(8)]
        RB = [Res("bank%d" % i, excl=True) for i in range(8)]
        bank_rr = {}

        def nb(group, banks):
            i = bank_rr.get(group, 0)
            bank_rr[group] = i + 1
            return banks[i % len(banks)]

        def X(eng, meth, reads, writes, *a, **kw):
            def fn(e):
                try:
                    return getattr(e, meth)(*a, **kw)
                except Exception as ex:
                    desc = [getattr(v, "shape", v) for v in a] + ["%s=%s" % (k, getattr(v, "shape", v)) for k, v in kw.items()]
                    raise RuntimeError("op %s.%s failed: %s | %s" % (eng, meth, desc, ex))
            return P.op(eng, fn, reads, writes)

        def DMA(eng, out, in_, res, reads=(), writes=()):
            return P.dma(eng, lambda e: e.dma_start(out=out, in_=in_), res, reads, writes)

        def bf(buf_ap):
            return buf_ap.bitcast(BF16)

        cst = A.alloc("consts", 128 + 64 + 64 + 64 + 128 + 1 + 64 + 16 + 48 * 2 + 48 * 2 + 2 * 6 * 16)
        cc = [0]

        def ccarve(n):
            a = cst.ap[:, cc[0]:cc[0] + n]
            cc[0] += n
            return a
        ident_f = ccarve(128)
        ident_b = bf(ccarve(64))
        ones_b = bf(ccarve(64))
        blk_b = bf(ccarve(64))
        ones_f = ccarve(128)
        eps_t = ccarve(1)
        vec = ccarve(64)
        cond_f = ccarve(16)
        RC = cst.r
        X("gpsimd", "memset", [], [RC], ident_f, 1.0)
        X("gpsimd", "affine_select", [RC], [RC], out=ident_f, in_=ident_f, pattern=[[-1, 128]],
          compare_op=ALU.is_equal, fill=0.0, base=0, channel_multiplier=1)
        X("gpsimd", "memset", [], [RC], ones_f, 1.0)
        X("gpsimd", "memset", [], [RC], eps_t, EPS)
        X("vector", "tensor_copy", [RC], [RC], out=ident_b, in_=ident_f)
        X("vector", "tensor_copy", [RC], [RC], out=ones_b, in_=ones_f)
        X("gpsimd", "memset", [], [RC], blk_b, 0.0)
        X("gpsimd", "memset", [], [RC], blk_b[0:64, 0:64], 1.0)
        X("gpsimd", "memset", [], [RC], blk_b[64:128, 64:128], 1.0)
        DMA("sync", vec, d_vec, RC, writes=[RC])
        DMA("sync", cond_f, d_cond, RC, writes=[RC])
        V_NMIX = [vec[:, 0:8], vec[:, 16:24]]
        V_NMLP = [vec[:, 8:16], vec[:, 24:32]]
        V_QL = vec[:, 32:34]
        V_KVL = vec[:, 34:35]
        V_GQ, V_GQS, V_GK, V_GKS = vec[:, 35:36], vec[:, 36:37], vec[:, 37:38], vec[:, 38:39]
        V_PSC = vec[:, 39:43]
        V_NGQ, V_NGK = vec[:, 43:44], vec[:, 44:45]

        if stage == -3:
            P.emit(nc, st)
            return nc
        NSLOT = 4
        slots = [A.alloc("wslot%d" % i, 2048) for i in range(NSLOT)]
        slot_rr = [0]

        pinned = set()

        def wload(dram_ap, shape_str=None, parts=128, **kw):
            while (slot_rr[0] % NSLOT) in pinned:
                slot_rr[0] += 1
            wload.last = slot_rr[0] % NSLOT
            s = slots[slot_rr[0] % NSLOT]
            slot_rr[0] += 1
            shp = list(dram_ap.shape)
            n = int(np.prod(shp[1:]))
            assert n <= 4096 and shp[0] == parts
            flat = bf(s.ap)[0:parts, 0:n]
            if len(shp) == 3:
                view = flat.rearrange("p (a b) -> p a b", a=shp[1])
            else:
                view = flat
            DMA("gpsimd", view, dram_ap, s.r, writes=[s.r])
            return view, s.r

        mods_b = A.alloc("mods", 2 * 6 * 16)
        RM = mods_b.r
        xT_b = A.alloc("xT", 8 * 2048, nres=4)
        xT = xT_b.ap.rearrange("p (c t) -> p c t", c=8)
        RX = xT_b.res
        hT_b = A.alloc("hT", 8 * 2048 // 2, nres=4)
        hT = bf(hT_b.ap).rearrange("p (c t) -> p c t", c=8)
        RH = hT_b.res

        def blk(tb):
            return slice(tb * 512, (tb + 1) * 512)

        def rstd_from(ps_ap, out_ap, inv_n, rows, rps, rout):
            X("scalar", "activation", [rps, RC], [rout], out=out_ap, in_=ps_ap, func=AF.Ln,
              bias=eps_t[rows, 0:1], scale=inv_n)
            X("scalar", "activation", [rout], [rout], out=out_ap, in_=out_ap, func=AF.Exp, scale=-0.5)

        def load_xT(dram_ap, n, dst, rdst, stg, eng_alt=0):
            nt = max(1, n // 128)
            pp = min(n, 128)
            sv = stg.ap.rearrange("p (a d) -> p a d", a=4)
            if n >= 128:
                DMA("sync", sv[:, 0:nt, :], dram_ap.rearrange("(a p) d -> p a d", p=128), stg.r, writes=[stg.r])
            else:
                DMA("sync", sv[0:pp, 0, :], dram_ap, stg.r, writes=[stg.r])
            for c in range(8):
                b = nb("ld", [0, 1, 2, 3])
                for a in range(nt):
                    X("tensor", "transpose", [stg.r, RC], [RB[b]], out=PB[b][:, a * pp:(a + 1) * pp],
                      in_=sv[0:pp, a, c * 128:(c + 1) * 128], identity=ident_f[0:pp, 0:pp])
                if (c + eng_alt) % 2 == 0:
                    X("vector", "tensor_copy", [RB[b]], [rdst], out=dst[:, c, 0:n], in_=PB[b][:, 0:n])
                else:
                    X("scalar", "copy", [RB[b]], [rdst], out=dst[:, c, 0:n], in_=PB[b][:, 0:n])

        def normmod(xsrc, rx, n, Avec, shvec, hdst, rh, sq, tmp, rstd):
            sqv = bf(sq.ap)[:, 0:8 * n].rearrange("p (c t) -> p c t", c=8)
            rxl = list(rx) if isinstance(rx, (list, tuple)) else [rx]
            b = nb("nm", [4, 5])
            for c in range(8):
                X("scalar", "activation", rxl, [sq.res[c % len(sq.res)]], out=sqv[:, c, :], in_=xsrc[:, c, :], func=AF.Square)
                X("tensor", "matmul", [sq.res[c % len(sq.res)], RC], [RB[b]], PB[b][:, 0:n], lhsT=ones_b, rhs=sqv[:, c, :],
                  start=(c == 0), stop=(c == 7))
            rs = rstd.ap[:, 0:n]
            rstd_from(PB[b][:, 0:n], rs, 1.0 / D, slice(0, 128), RB[b], rstd.r)
            for c in range(8):
                t = tmp[c % len(tmp)]
                X("vector", "scalar_tensor_tensor", rxl + [rstd.r, RM], [t.r], out=t.ap[:, 0:n], in0=xsrc[:, c, :],
                  scalar=Avec[:, c:c + 1], in1=rs, op0=ALU.mult, op1=ALU.mult)
                X("scalar", "activation", [t.r, RM], [rh], out=hdst[:, c, :], in_=t.ap[:, 0:n], func=AF.Identity,
                  bias=shvec[:, c:c + 1], scale=1.0)

        modT = [mods_b.ap[:, l * 96:(l + 1) * 96].rearrange("p (j k) -> p j k", k=2) for l in range(2)]
        cs_b = A.alloc("cond_silu", 8)
        csb = bf(cs_b.ap).rearrange("p (c k) -> p c k", k=2)
        X("scalar", "activation", [RC], [cs_b.r], out=csb, in_=cond_f.rearrange("p (c k) -> p c k", k=2), func=AF.Silu)
        mrow = [A.alloc("mrow%d" % i, 512) for i in range(2)]
        bada_b = A.alloc("bada", 96)
        for l in range(2):
            DMA("sync", bada_b.ap[:, l * 48:(l + 1) * 48], d_bada[l], bada_b.r, writes=[bada_b.r])
        if mini is not None:
            X("vector", "memset", [], [RM], mods_b.ap, 0.5)
        def ada_piece(l, pc):
            tbk = 6 + l
            wv, wr = wload(d_wada[l][:, pc * 512:(pc + 1) * 512].rearrange("(c p) n -> p c n", p=128))
            b = nb("ada", [0, 1, 2, 3])
            for kc in range(8):
                X("tensor", "matmul", [cs_b.r, wr], [RB[b]], PB[b][0:2, :], lhsT=csb[:, kc, :], rhs=wv[:, kc, :],
                  start=(kc == 0), stop=(kc == 7))
            mr = mrow[pc % 2]
            X("vector", "tensor_copy", [RB[b]], [mr.r], out=mr.ap[0:2, :], in_=PB[b][0:2, :])
            for j in range(4):
                jj = pc * 4 + j
                X("tensor", "matmul", [mr.r, RC], [RB[tbk]], PB[tbk][:, jj * 2:jj * 2 + 2],
                  lhsT=mr.ap[0:2, j * 128:(j + 1) * 128], rhs=ident_f[0:2, 0:2], start=True, stop=True)

        def ada_finish(l):
            tbk = 6 + l
            for k in range(2):
                X("vector", "tensor_tensor", [RB[tbk], bada_b.r], [RM], out=modT[l][:, :, k],
                  in0=PB[tbk][:, 0:96].rearrange("p (j k) -> p j k", k=2)[:, :, k], in1=bada_b.ap[:, l * 48:(l + 1) * 48],
                  op=ALU.add)
            for k in range(2):
                X("vector", "scalar_tensor_tensor", [RM, RC], [RM], out=modT[l][:, 8:16, k], in0=modT[l][:, 8:16, k],
                  scalar=1.0, in1=V_NMIX[l], op0=ALU.add, op1=ALU.mult)
                X("vector", "scalar_tensor_tensor", [RM, RC], [RM], out=modT[l][:, 32:40, k], in0=modT[l][:, 32:40, k],
                  scalar=1.0, in1=V_NMLP[l], op0=ALU.add, op1=ALU.mult)

        ada_deferred = (mini is None) and stage >= 3

        def MOD(l, kind, cond):
            return modT[l][:, kind * 8:(kind + 1) * 8, cond]

        def cond_of(tb):
            return 0 if tb == 0 else 1

        def store_tokens(xsrc, rsrc, n, dram_ap, ostg, rextra=()):
            ov = ostg.ap.rearrange("p (a d) -> p a d", a=4)
            nt = n // 128
            assert nt <= 4
            for a in range(nt):
                for half in range(2):
                    b = nb("st", [0, 1, 2, 3])
                    for cc_ in range(4):
                        c = half * 4 + cc_
                        X("tensor", "transpose", [rsrc, RC] + list(rextra), [RB[b]], out=PB[b][:, cc_ * 128:(cc_ + 1) * 128],
                          in_=xsrc[:, c, a * 128:(a + 1) * 128], identity=ident_f)
                    if half == 0:
                        X("vector", "tensor_copy", [RB[b]], [ostg.r], out=ov[:, a, 0:512], in_=PB[b])
                    else:
                        X("scalar", "copy", [RB[b]], [ostg.r], out=ov[:, a, 512:1024], in_=PB[b])
            DMA("sync", dram_ap.rearrange("(a p) d -> p a d", p=128), ov[:, 0:nt, :], ostg.r, reads=[ostg.r])

        def epilogue(ps, rps, l, kind, tb, m):
            X("vector", "scalar_tensor_tensor", [rps, RM, RX[tb]], [RX[tb]], out=xT[:, m, blk(tb)], in0=ps,
              scalar=MOD(l, kind, cond_of(tb))[:, m:m + 1], in1=xT[:, m, blk(tb)], op0=ALU.mult, op1=ALU.add)

        def mlp(l, segs):
            ns = len(segs)
            hb = A.alloc("hT_mlp", 8 * 512 * ns // 2, nres=ns)
            hv = bf(hb.ap).rearrange("p (c t) -> p c t", c=8)
            sq = A.alloc("sq_m", 2048, nres=8)
            tmps = [A.alloc("tmp_m%d" % i, 512) for i in range(2)]
            tmps3 = [A.alloc("tmp_n%d" % i, 512) for i in range(2)]
            rstd = A.alloc("rstd_m", 512)
            ag = [A.alloc("ag%d" % i, 4 * 512 * ns // 2, nres=ns) for i in range(2)]
            agv = [bf(a_.ap).rearrange("p (j t) -> p j t", j=4) for a_ in ag]
            wts = {}

            def ph1(g, with_norm=False):
                w1, r1 = wload(d_mlp1[l][:, g * 512:(g + 1) * 512].rearrange("(c p) n -> p c n", p=128))
                w2, r2 = wload(d_mlp2[l][g * 512:(g + 1) * 512, :].rearrange("(j p) n -> p j n", p=128))
                wts[g] = (w2, r2)
                if with_norm:
                    c0_, rl_, cd_ = segs[0]
                    normmod(xT[:, :, c0_:c0_ + 512], rl_, 512, MOD(l, 4, cd_), MOD(l, 3, cd_), hv[:, :, 0:512], hb.res[0], sq, tmps3, rstd)
                for si in range(ns):
                    if with_norm and si + 1 < ns:
                        c0_, rl_, cd_ = segs[si + 1]
                        normmod(xT[:, :, c0_:c0_ + 512], rl_, 512, MOD(l, 4, cd_), MOD(l, 3, cd_),
                                hv[:, :, (si + 1) * 512:(si + 2) * 512], hb.res[si + 1], sq, tmps3, rstd)
                    ss = slice(si * 512, (si + 1) * 512)
                    for j in range(4):
                        b = nb("m1", [0, 1, 2, 3])
                        for c in range(8):
                            X("tensor", "matmul", [hb.res[si], r1], [RB[b]], PB[b], lhsT=w1[:, c, j * 128:(j + 1) * 128],
                              rhs=hv[:, c, ss], start=(c == 0), stop=(c == 7))
                        t = tmps[(si * 4 + j) % 2]
                        X("scalar", "activation", [RB[b]], [t.r], out=t.ap, in_=PB[b], func=AF.Relu)
                        X("vector", "tensor_tensor", [t.r], [ag[g % 2].res[si]], out=agv[g % 2][:, j, ss], in0=t.ap,
                          in1=t.ap, op=ALU.mult)

            def ph2(g):
                w2, r2 = wts.pop(g)
                for si, (c0, rl, cd) in enumerate(segs):
                    ss = slice(si * 512, (si + 1) * 512)
                    for m in range(8):
                        b = nb("m2", [4, 5, 6, 7])
                        for j in range(4):
                            X("tensor", "matmul", [ag[g % 2].res[si], r2], [RB[b]], PB[b], lhsT=w2[:, j, m * 128:(m + 1) * 128],
                              rhs=agv[g % 2][:, j, ss], start=(j == 0), stop=(j == 3))
                        X("vector", "scalar_tensor_tensor", [RB[b], RM] + list(rl), list(rl), out=xT[:, m, c0:c0 + 512], in0=PB[b],
                          scalar=MOD(l, 5, cd)[:, m:m + 1], in1=xT[:, m, c0:c0 + 512], op0=ALU.mult, op1=ALU.add)
            ph1(0, with_norm=True)
            for g in range(1, 8):
                ph1(g)
                ph2(g - 1)
            ph2(7)
            A.free(hb, sq, rstd, *tmps, *tmps3, *ag)

        stg = A.alloc("stg", 4096)
        load_xT(d_xp, 512, xT[:, :, blk(0)], RX[0], stg)
        for i in range(3):
            load_xT(d_xw[i * 512:(i + 1) * 512, :], 512, xT[:, :, blk(i + 1)], RX[i + 1], stg, eng_alt=i)
        if "x1" in skip:
            P.emit(nc, st)
            return nc
        xpre_b = A.alloc("xpreT", 8 * 128)
        xpreT = xpre_b.ap.rearrange("p (c t) -> p c t", c=8)
        load_xT(d_xpre, 128, xpreT, xpre_b.r, stg)
        hpre_b = A.alloc("hpreT", 8 * 128 // 2)
        hpreT = bf(hpre_b.ap).rearrange("p (c t) -> p c t", c=8)
        if "x2" in skip:
            P.emit(nc, st)
            return nc
        if mini is None:
            for pc in range(12):
                ada_piece(0, pc)
            ada_finish(0)
            if not ada_deferred:
                for pc in range(12):
                    ada_piece(1, pc)
                ada_finish(1)
        if not ada_deferred:
            A.free(cs_b, bada_b, *mrow)

        sq_b = A.alloc("sq", 2048, nres=8)
        tmpb = [A.alloc("tmp%d" % i, 512) for i in range(2)]
        rstd_b = A.alloc("rstd", 512)
        for tb in range(4):
            normmod(xT[:, :, blk(tb)], RX[tb], 512, MOD(0, 1, cond_of(tb)), MOD(0, 0, cond_of(tb)),
                    hT[:, :, blk(tb)], RH[tb], sq_b, tmpb, rstd_b)
        normmod(xpreT, xpre_b.r, 128, MOD(0, 1, 1), MOD(0, 0, 1), hpreT, hpre_b.r, sq_b, tmpb, rstd_b)
        if "x3" in skip:
            P.emit(nc, st)
            return nc
        A.free(stg, xpre_b)

        if stage >= 2:
            LA = 2144
            B_S0, B_S1, B_PRE, B_W = 8, 272, 536, 600
            wpin_v, wpin_r = wload(d_win0_pool.rearrange("(c p) n -> p c n", p=128))
            wpl_v, wpl_r = wload(d_wpool)
            wop_v, wop_r = wload(d_wout0[0:512, :].rearrange("(c p) n -> p c n", p=128))
            mask_b = A.alloc("pmask", 1600)
            pinv_b = A.alloc("pinv", 128)
            DMA("sync", mask_b.ap, d_pmask, mask_b.r, writes=[mask_b.r])
            DMA("sync", pinv_b.ap, d_pinv, pinv_b.r, writes=[pinv_b.r])
            abuf = [A.alloc("abuf%d" % i, LA) for i in range(1)]
            sbuf_ = [A.alloc("sbuf%d" % i, LA) for i in range(2)]
            pld = [A.alloc("pooled%d" % i, LA // 2) for i in range(1)]
            t8 = A.alloc("t8", 8)
            po_b = A.alloc("poolout", 4 * 2048 // 2, nres=4)
            pov = bf(po_b.ap).rearrange("p (g t) -> p g t", g=4)
            for b_ in abuf + sbuf_:
                X("vector", "memset", [], [b_.r], b_.ap, 0.0)
            for g in range(4):
                w = (2, 4, 8, 16)[g]
                ab = abuf[0]
                av = ab.ap
                for tb in range(4):
                    b = nb("pl", [0, 1, 2, 3])
                    for c in range(8):
                        X("tensor", "matmul", [RH[tb], wpin_r], [RB[b]], PB[b], lhsT=wpin_v[:, c, g * 128:(g + 1) * 128],
                          rhs=hT[:, c, blk(tb)], start=(c == 0), stop=(c == 7))
                    if tb == 0:
                        X("vector", "tensor_copy", [RB[b]], [ab.r], out=av[:, B_S0:B_S0 + 256], in_=PB[b][:, 0:256])
                        X("vector", "tensor_copy", [RB[b]], [ab.r], out=av[:, B_S1:B_S1 + 256], in_=PB[b][:, 256:512])
                    else:
                        X("vector", "tensor_tensor", [RB[b], mask_b.r], [ab.r],
                          out=av[:, B_W + (tb - 1) * 512:B_W + tb * 512], in0=PB[b],
                          in1=mask_b.ap[:, 64 + (tb - 1) * 512:64 + tb * 512], op=ALU.mult)
                b = nb("pl", [0, 1, 2, 3])
                for c in range(8):
                    X("tensor", "matmul", [hpre_b.r, wpin_r], [RB[b]], PB[b][:, 0:64], lhsT=wpin_v[:, c, g * 128:(g + 1) * 128],
                      rhs=hpreT[:, c, 0:64], start=(c == 0), stop=(c == 7))
                X("vector", "tensor_tensor", [RB[b], mask_b.r], [ab.r], out=av[:, B_PRE:B_PRE + 64], in0=PB[b][:, 0:64],
                  in1=mask_b.ap[:, 0:64], op=ALU.mult)
                cur, rcur = av, ab.r
                sh = 1
                k = 0
                while sh < w:
                    dstb = sbuf_[k % 2]
                    X("vector", "tensor_tensor", [rcur], [dstb.r], out=dstb.ap[:, sh:LA], in0=cur[:, sh:LA], in1=cur[:, 0:LA - sh],
                      op=ALU.add)
                    cur, rcur = dstb.ap, dstb.r
                    sh *= 2
                    k += 1
                pb_ = pld[0]
                pv = bf(pb_.ap)
                off = w // 2 - 1
                for (base, L) in ((B_S0, 256), (B_S1, 256), (B_W, 1536)):
                    X("vector", "scalar_tensor_tensor", [rcur, ab.r], [pb_.r], out=pv[:, base:base + L],
                      in0=cur[:, base + off:base + off + L], scalar=1.0 / w, in1=av[:, base:base + L], op0=ALU.mult,
                      op1=ALU.subtract)
                for (pos, bd) in ((B_S0, 0), (B_S0 + 248, 1), (B_S1, 0), (B_S1 + 248, 1), (B_W + 256, 2), (B_W + 1272, 3)):
                    X("vector", "tensor_tensor", [rcur, pinv_b.r], [t8.r], out=t8.ap, in0=cur[:, pos + off:pos + off + 8],
                      in1=pinv_b.ap[:, g * 32 + bd * 8:g * 32 + bd * 8 + 8], op=ALU.mult)
                    X("vector", "tensor_tensor", [t8.r, ab.r], [pb_.r], out=pv[:, pos:pos + 8], in0=t8.ap, in1=av[:, pos:pos + 8],
                      op=ALU.subtract)
                for tb in range(4):
                    b = nb("pl", [0, 1, 2, 3])
                    if tb == 0:
                        for s_, base in enumerate((B_S0, B_S1)):
                            X("tensor", "matmul", [pb_.r, wpl_r], [RB[b]], PB[b][:, s_ * 256:(s_ + 1) * 256],
                              lhsT=wpl_v[:, g * 128:(g + 1) * 128], rhs=pv[:, base:base + 256], start=True, stop=True)
                    else:
                        X("tensor", "matmul", [pb_.r, wpl_r], [RB[b]], PB[b], lhsT=wpl_v[:, g * 128:(g + 1) * 128],
                          rhs=pv[:, B_W + (tb - 1) * 512:B_W + tb * 512], start=True, stop=True)
                    X("vector", "tensor_scalar", [RB[b], RC], [po_b.res[tb]], out=pov[:, g, blk(tb)], in0=PB[b],
                      scalar1=V_PSC[:, g:g + 1], scalar2=None, op0=ALU.mult)
            for tb in range(4):
                for m in range(8):
                    b = nb("po", [4, 5, 6, 7])
                    for g in range(4):
                        X("tensor", "matmul", [po_b.res[tb], wop_r], [RB[b]], PB[b], lhsT=wop_v[:, g, m * 128:(m + 1) * 128],
                          rhs=pov[:, g, blk(tb)], start=(g == 0), stop=(g == 3))
                    epilogue(PB[b], RB[b], 0, 2, tb, m)
            A.free(mask_b, pinv_b, t8, po_b, hpre_b, *abuf, *sbuf_, *pld)

        wkv_v, wkv_r = wload(d_win0_kv.rearrange("(c p) n -> p c n", p=128))
        pinned.add(wload.last)
        NKEY = 5120
        ckvn_b = A.alloc("ckvnT", NKEY // 2, nres=10)
        ckvnT = bf(ckvn_b.ap)
        kr_b = A.alloc("KRsq", NKEY, nres=10)
        KRv = bf(kr_b.ap)[:, 0:NKEY]
        KSQv = bf(kr_b.ap)[:, NKEY:2 * NKEY]

        def kv_block(hsrc, rh, kb, rope_tile=None, r_rope=None, ckv32=None, kr32=None, r32=None):
            ks = slice(kb * 512, (kb + 1) * 512)
            rr = slice(64, 96)
            ba = nb("kv", [0, 1, 2, 3])
            for c in range(8):
                X("tensor", "matmul", [rh, wkv_r], [RB[ba]], PB[ba], lhsT=wkv_v[:, c, 0:128], rhs=hsrc[:, c, :],
                  start=(c == 0), stop=(c == 7))
            sqv = bf(sq_b.ap)[:, 0:512]
            X("scalar", "activation", [RB[ba]], [sq_b.r], out=sqv, in_=PB[ba], func=AF.Square)
            bs = nb("kv", [0, 1, 2, 3])
            X("tensor", "matmul", [sq_b.r, RC], [RB[bs]], PB[bs], lhsT=ones_b, rhs=sqv, start=True, stop=True)
            rstd_from(PB[bs], rstd_b.ap, 1.0 / 128, slice(0, 128), RB[bs], rstd_b.r)
            X("vector", "scalar_tensor_tensor", [RB[ba], rstd_b.r, RC], [ckvn_b.res[kb]], out=ckvnT[:, ks], in0=PB[ba],
              scalar=V_KVL, in1=rstd_b.ap, op0=ALU.mult, op1=ALU.mult)
            if ckv32 is not None:
                X("vector", "scalar_tensor_tensor", [RB[ba], rstd_b.r, RC], [r32], out=ckv32, in0=PB[ba],
                  scalar=V_KVL, in1=rstd_b.ap, op0=ALU.mult, op1=ALU.mult)
            bb = nb("kv", [0, 1, 2, 3])
            for c in range(8):
                X("tensor", "matmul", [rh, wkv_r], [RB[bb]], PB[bb][0:96, :], lhsT=wkv_v[:, c, 128:224], rhs=hsrc[:, c, :],
                  start=(c == 0), stop=(c == 7))
            if "k2" not in skip:
                X("scalar", "activation", [RB[bb]], [kr_b.res[kb]], out=KSQv[rr, ks], in_=PB[bb][rr, :], func=AF.Square)
            if kr32 is not None and "k3" not in skip:
                X("vector", "tensor_copy", [RB[bb]], [r32], out=kr32, in_=PB[bb][rr, :])
            if rope_tile is None:
                if "k1" not in skip:
                    X("vector", "tensor_scalar", [RB[bb], RC], [kr_b.res[kb]], out=KRv[rr, ks], in0=PB[bb][rr, :],
                      scalar1=V_GK[rr, :], scalar2=None, op0=ALU.mult)
            else:
                bc = nb("kv", [0, 1, 2, 3])
                for c in range(8):
                    X("tensor", "matmul", [rh, wkv_r], [RB[bc]], PB[bc][0:96, :], lhsT=wkv_v[:, c, 224:320],
                      rhs=hsrc[:, c, :], start=(c == 0), stop=(c == 7))
                t1, t2 = tmpb[0], tmpb[1]
                X("vector", "scalar_tensor_tensor", [RB[bb], r_rope, RC], [t1.r], out=t1.ap[rr, :], in0=PB[bb][rr, :],
                  scalar=V_GK[rr, :], in1=rope_tile[rr, 0:512], op0=ALU.mult, op1=ALU.mult)
                X("vector", "scalar_tensor_tensor", [RB[bc], r_rope, RC], [t2.r], out=t2.ap[rr, :], in0=PB[bc][rr, :],
                  scalar=V_GKS[rr, :], in1=rope_tile[rr, 512:1024], op0=ALU.mult, op1=ALU.mult)
                X("vector", "tensor_tensor", [t1.r, t2.r], [kr_b.res[kb]], out=KRv[rr, ks], in0=t1.ap[rr, :], in1=t2.ap[rr, :],
                  op=ALU.add)

        o32_b = A.alloc("kvout32", 1024)
        ckv32 = o32_b.ap[:, 0:512]
        kr32 = o32_b.ap[:, 512:1024]
        kv_block(hT[:, :, blk(0)], RH[0], 9, ckv32=ckv32, kr32=kr32[64:96, :], r32=o32_b.r)
        if "x4" in skip:
            P.emit(nc, st)
            return nc
        ost_b = A.alloc("ostage", 4 * 160)
        ostv = ost_b.ap.rearrange("p (a d) -> p a d", a=4)
        for a in range(4):
            b = nb("ld", [0, 1, 2, 3])
            X("tensor", "transpose", [o32_b.r, RC], [RB[b]], out=PB[b][:, 0:128], in_=ckv32[:, a * 128:(a + 1) * 128],
              identity=ident_f)
            X("tensor", "matmul", [o32_b.r, RC], [RB[b]], PB[b][:, 128:160], lhsT=kr32[64:96, a * 128:(a + 1) * 128],
              rhs=ident_f[64:96, 64:96], start=True, stop=True)
            X("vector", "tensor_copy", [RB[b]], [ost_b.r], out=ostv[:, a, :], in_=PB[b][:, 0:160])
        DMA("sync", o_ckv.rearrange("(a p) d -> p a d", p=128), ostv[:, :, 0:128], ost_b.r, reads=[ost_b.r])
        DMA("sync", o_kr.rearrange("(a p) d -> p a d", p=128), ostv[:, :, 128:160], ost_b.r, reads=[ost_b.r])
        A.free(o32_b, ost_b)

        if stage >= 3:
            wq_v, wq_r = wload(d_win0_q.rearrange("(c p) n -> p c n", p=128))
            qln_b = A.alloc("qlnT", 2 * 2048 // 2, nres=4)
            qln = bf(qln_b.ap).rearrange("p (j t) -> p j t", j=2)
            sq2 = bf(sq_b.ap)[:, 0:1024].rearrange("p (j t) -> p j t", j=2)
            for tb in range(4):
                bq = [nb("q", [0, 1, 2, 3]), nb("q", [0, 1, 2, 3])]
                for j in range(2):
                    for c in range(8):
                        X("tensor", "matmul", [RH[tb], wq_r], [RB[bq[j]]], PB[bq[j]], lhsT=wq_v[:, c, j * 128:(j + 1) * 128],
                          rhs=hT[:, c, blk(tb)], start=(c == 0), stop=(c == 7))
                    X("scalar", "activation", [RB[bq[j]]], [sq_b.res[j]], out=sq2[:, j, :], in_=PB[bq[j]], func=AF.Square)
                bs = nb("qs", [4, 5])
                for j in range(2):
                    X("tensor", "matmul", [sq_b.res[j], RC], [RB[bs]], PB[bs], lhsT=ones_b, rhs=sq2[:, j, :], start=(j == 0), stop=(j == 1))
                rstd_from(PB[bs], rstd_b.ap, 1.0 / 256, slice(0, 128), RB[bs], rstd_b.r)
                for j in range(2):
                    X("vector", "scalar_tensor_tensor", [RB[bq[j]], rstd_b.r, RC], [qln_b.res[tb]], out=qln[:, j, blk(tb)],
                      in0=PB[bq[j]], scalar=V_QL[:, j:j + 1], in1=rstd_b.ap, op0=ALU.mult, op1=ALU.mult)
        A.free(hT_b)

        if stage >= 3:
            stg = A.alloc("stg", 4096)
            xtmp_b = A.alloc("xtmp", 4096)
            xtmp = xtmp_b.ap.rearrange("p (c t) -> p c t", c=8)
            htmp_b = [A.alloc("htmp%d" % i, 2048) for i in range(1)]
            ropet = [A.alloc("ropet%d" % i, 1024) for i in range(1)]
            for kb in range(8):
                load_xT(d_xb[kb * 512:(kb + 1) * 512, :], 512, xtmp, xtmp_b.r, stg, eng_alt=kb)
                hb_ = htmp_b[0]
                hv_ = bf(hb_.ap).rearrange("p (c t) -> p c t", c=8)
                normmod(xtmp, xtmp_b.r, 512, MOD(0, 1, 1), MOD(0, 0, 1), hv_, hb_.r, sq_b, tmpb, rstd_b)
                rt = ropet[0]
                DMA("sync", rt.ap[64:96, 0:512], d_ropek[0:32, kb * 512:(kb + 1) * 512], rt.r, writes=[rt.r])
                DMA("sync", rt.ap[64:96, 512:1024], d_ropek[32:64, kb * 512:(kb + 1) * 512], rt.r, writes=[rt.r])
                kv_block(hv_, hb_.r, kb, rope_tile=rt.ap, r_rope=rt.r)
                if ada_deferred:
                    for pc in range((kb * 12) // 8, ((kb + 1) * 12) // 8):
                        ada_piece(1, pc)
            DMA("sync", xtmp_b.ap[:, 0:512], d_cckv, xtmp_b.r, writes=[xtmp_b.r])
            X("vector", "tensor_copy", [xtmp_b.r], [ckvn_b.res[8]], out=ckvnT[:, 4096:4608], in_=xtmp_b.ap[:, 0:512])
            DMA("sync", xtmp_b.ap[64:96, 512:1024], d_ckr, xtmp_b.r, writes=[xtmp_b.r])
            X("scalar", "activation", [xtmp_b.r], [kr_b.res[8]], out=KSQv[64:96, 4096:4608], in_=xtmp_b.ap[64:96, 512:1024],
              func=AF.Square)
            X("vector", "tensor_scalar", [xtmp_b.r, RC], [kr_b.res[8]], out=KRv[64:96, 4096:4608], in0=xtmp_b.ap[64:96, 512:1024],
              scalar1=V_GK[64:96, :], scalar2=None, op0=ALU.mult)
            A.free(stg, xtmp_b, *htmp_b, *ropet)
            A.free(sq_b, rstd_b)
            if ada_deferred:
                ada_finish(1)
                A.free(cs_b, bada_b, *mrow)
            pinned.clear()

        if stage >= 3:
            SC0 = 96.0 ** -0.5
            wqup_v, wqup_r = wload(d_wqup.rearrange("(c p) n -> p c n", p=128))
            wkvup_v, wkvup_r = wload(d_wkvup)
            KT_b = A.alloc("KT", NKEY // 2, nres=10)
            KT = bf(KT_b.ap)
            X("vector", "memset", [], KT_b.res, KT[64:128, :], 0.0)
            V_b = A.alloc("Vh", 40 * 128 // 2, nres=10)
            Vv = bf(V_b.ap).rearrange("p (t d) -> p t d", d=128)
            X("vector", "memset", [], V_b.res, Vv[:, :, 64:128], 1.0)
            QT_b = [A.alloc("QT%d" % i, 2048 // 2, nres=4) for i in range(2)]
            for qb_ in QT_b:
                X("vector", "memset", [], qb_.res, bf(qb_.ap)[64:128, :], 0.0)
            PT_b = [A.alloc("PT%d" % i, 256) for i in range(4)]
            rden_l = [A.alloc("rden%d" % i, 512) for i in range(1)]
            osb_l = [A.alloc("osbm%d" % i, 512) for i in range(4)]
            sqs_b = [A.alloc("sqs%d" % i, 256) for i in range(2)]
            r96_b = [A.alloc("r96_%d" % i, 512) for i in range(2)]
            rq_b = [A.alloc("ropeq%d" % i, 1024) for i in range(1)]
            rsk_b = [A.alloc("rsk%d" % i, 40, nres=10) for i in range(2)]
            lnsc_b = A.alloc("lnsc", 2)
            X("vector", "memset", [], [lnsc_b.r], lnsc_b.ap[:, 0:1], float(np.log(SC0)))
            X("vector", "tensor_tensor", [RC], [lnsc_b.r], out=lnsc_b.ap[:, 1:2], in0=V_GQ, in1=V_GK, op=ALU.mult)
            V_GQK = lnsc_b.ap[:, 1:2]
            for kb_ in range(10):
                X("vector", "tensor_copy", [kr_b.res[kb_]], [KT_b.res[kb_]], out=KT[64:96, kb_ * 512:(kb_ + 1) * 512],
                  in_=KRv[64:96, kb_ * 512:(kb_ + 1) * 512])
            attn_b = A.alloc("attnT", 2 * 2048 // 2, nres=4)
            attnT = bf(attn_b.ap).rearrange("p (h t) -> p h t", h=2)
            cnt = {"sq": 0, "r96": 0, "pt": 0, "rden": 0, "rq": 0, "osb": 0}

            def rot(key, lst):
                i = cnt[key]
                cnt[key] = i + 1
                return lst[i % len(lst)]

            def evac(ob, nq):
                ob_ = rot("osb", osb_l)
                X("vector", "tensor_copy", [RB[ob]], [ob_.r], out=ob_.ap[:, 0:nq], in_=PB[ob][:, 0:nq])
                return ob_

            def normalize_sb(ob_, dst, rdst, nq):
                rd = rot("rden", rden_l)
                X("vector", "reciprocal", [ob_.r], [rd.r], out=rd.ap[0:64, 0:nq], in_=ob_.ap[64:128, 0:nq])
                X("vector", "tensor_tensor", [ob_.r, rd.r], [rdst], out=dst, in0=ob_.ap[0:64, 0:nq], in1=rd.ap[0:64, 0:nq],
                  op=ALU.mult)

            def normalize(ob, dst, rdst, nq):
                normalize_sb(evac(ob, nq), dst, rdst, nq)

            def q_a(h, tb, stt_):
                for kc in range(2):
                    X("tensor", "matmul", [qln_b.res[tb], wqup_r], [RB[5]], PB[5][0:96, :], lhsT=wqup_v[:, kc, h * 192:h * 192 + 96],
                      rhs=qln[:, kc, blk(tb)], start=(kc == 0), stop=(kc == 1))
                if tb > 0:
                    for kc in range(2):
                        X("tensor", "matmul", [qln_b.res[tb], wqup_r], [RB[6]], PB[6][0:96, :],
                          lhsT=wqup_v[:, kc, h * 192 + 96:h * 192 + 192], rhs=qln[:, kc, blk(tb)], start=(kc == 0), stop=(kc == 1))
                sq = rot("sq", sqs_b)
                X("scalar", "activation", [RB[5]], [sq.r], out=bf(sq.ap)[0:96, :], in_=PB[5][0:96, :], func=AF.Square)
                stt_["sq"] = sq

            def q_b(h, tb, stt_):
                QB_ = QT_b[h % 2]
                QTv = bf(QB_.ap)
                rq = QB_.res[tb]
                sq = stt_["sq"]
                bq_ = nb("S", [3, 4, 7])
                X("tensor", "matmul", [sq.r, RC], [RB[bq_]], PB[bq_][0:96, :], lhsT=ones_b[0:96, 0:96], rhs=bf(sq.ap)[0:96, :],
                  start=True, stop=True)
                r96 = rot("r96", r96_b)
                rstd_from(PB[bq_][0:96, :], r96.ap[0:96, :], 1.0 / 96, slice(0, 96), RB[bq_], r96.r)
                if tb == 0:
                    X("vector", "scalar_tensor_tensor", [RB[5], r96.r, lnsc_b.r], [rq], out=QTv[0:64, blk(tb)], in0=PB[5][0:64, :],
                      scalar=V_GQK[0:64, :], in1=r96.ap[0:64, :], op0=ALU.mult, op1=ALU.mult)
                    X("vector", "scalar_tensor_tensor", [RB[5], r96.r, RC], [rq], out=QTv[64:96, blk(tb)], in0=PB[5][64:96, :],
                      scalar=V_GQ[64:96, :], in1=r96.ap[64:96, :], op0=ALU.mult, op1=ALU.mult)
                    return
                X("vector", "scalar_tensor_tensor", [RB[5], r96.r, lnsc_b.r], [rq], out=QTv[0:64, blk(tb)], in0=PB[5][0:64, :],
                  scalar=V_GQK[0:64, :], in1=r96.ap[0:64, :], op0=ALU.mult, op1=ALU.mult)
                rt = rot("rq", rq_b)
                ws = (tb - 1) * 512
                DMA("sync", rt.ap[64:96, 0:512], d_ropeq[0:32, ws:ws + 512], rt.r, writes=[rt.r])
                DMA("sync", rt.ap[64:96, 512:1024], d_ropeq[32:64, ws:ws + 512], rt.r, writes=[rt.r])
                rr = slice(64, 96)
                t1, t2 = tmpb[0], tmpb[1]
                X("vector", "scalar_tensor_tensor", [RB[5], rt.r, RC], [t1.r], out=t1.ap[rr, :], in0=PB[5][rr, :],
                  scalar=V_GQ[rr, :], in1=rt.ap[rr, 0:512], op0=ALU.mult, op1=ALU.mult)
                X("vector", "scalar_tensor_tensor", [RB[6], rt.r, RC], [t2.r], out=t2.ap[rr, :], in0=PB[6][rr, :],
                  scalar=V_GQS[rr, :], in1=rt.ap[rr, 512:1024], op0=ALU.mult, op1=ALU.mult)
                X("vector", "tensor_tensor", [t1.r, t2.r], [t1.r], out=t1.ap[rr, :], in0=t1.ap[rr, :], in1=t2.ap[rr, :], op=ALU.add)
                X("vector", "tensor_tensor", [t1.r, r96.r], [rq], out=QTv[rr, blk(tb)], in0=t1.ap[rr, :], in1=r96.ap[rr, :], op=ALU.mult)

            def k_s0(h, kb, stt_):
                ks = slice(kb * 512, (kb + 1) * 512)
                X("tensor", "matmul", [ckvn_b.res[kb], wkvup_r], [RB[5]], PB[5][0:64, :], lhsT=wkvup_v[:, h * 128:h * 128 + 64],
                  rhs=ckvnT[:, ks], start=True, stop=True)
                sq = rot("sq", sqs_b)
                X("scalar", "activation", [RB[5]], [sq.r], out=bf(sq.ap)[0:64, :], in_=PB[5][0:64, :], func=AF.Square)
                X("vector", "tensor_copy", [RB[5]], [KT_b.res[kb]], out=KT[0:64, ks], in_=PB[5][0:64, :])
                stt_["sq"] = sq

            def k_s1(h, kb, stt_):
                sq = stt_["sq"]
                for a in range(4):
                    X("tensor", "matmul", [sq.r, RC], [RB[6]], PB[6][:, a:a + 1], lhsT=bf(sq.ap)[0:64, a * 128:(a + 1) * 128],
                      rhs=ones_b[0:64, 0:1], start=True, stop=False)
                    X("tensor", "matmul", [kr_b.res[kb], RC], [RB[6]], PB[6][:, a:a + 1],
                      lhsT=KSQv[64:96, kb * 512 + a * 128:kb * 512 + (a + 1) * 128], rhs=ones_b[64:96, 0:1], start=False, stop=True)
                rk = rsk_b[h % 2]
                X("scalar", "activation", [RB[6], RC], [rk.res[kb]], out=rk.ap[:, kb * 4:kb * 4 + 4], in_=PB[6][:, 0:4], func=AF.Ln,
                  bias=eps_t[:, 0:1], scale=1.0 / 96)
                X("scalar", "activation", [rk.res[kb], lnsc_b.r], [rk.res[kb]], out=rk.ap[:, kb * 4:kb * 4 + 4],
                  in_=rk.ap[:, kb * 4:kb * 4 + 4], func=AF.Exp, bias=lnsc_b.ap[:, 0:1], scale=-0.5)

            def k_s2(h, kb, stt_):
                pass

            def v_all(h, kb, stt_):
                for a in range(4):
                    t = kb * 4 + a
                    X("tensor", "matmul", [ckvn_b.res[kb], wkvup_r], [RB[6]], PB[6][:, a * 64:(a + 1) * 64],
                      lhsT=ckvnT[:, t * 128:(t + 1) * 128], rhs=wkvup_v[:, h * 128 + 64:h * 128 + 128], start=True, stop=True)
                X("vector", "tensor_copy", [RB[6]], [V_b.res[kb]], out=Vv[:, kb * 4:kb * 4 + 4, 0:64],
                  in_=PB[6][:, 0:256].rearrange("p (a d) -> p a d", a=4))

            items = [(h, kb) for h in range(8) for kb in (list(range(9)) + [9])]
            nitem = len(items)
            sched = [dict() for _ in range(nitem)]

            def add(n, step, fn, *a):
                sched[n].setdefault(step, []).append((fn, a))

            for n, (h, kb) in enumerate(items):
                if n + 1 >= nitem:
                    break
                h2, kb2 = items[n + 1]
                st_ = {}
                if kb == 9:
                    continue
                base = 1
                add(n, base, k_s0, h2, kb2, st_)
                add(n, base + 2, k_s1, h2, kb2, st_)
                add(n, base + 4, k_s2, h2, kb2, st_)
                add(n, base + 5, v_all, h2, kb2, st_)
                if kb == 8 and n + 2 < nitem:
                    h3, kb3 = items[n + 2]
                    st3 = {}
                    add(n, 7, k_s0, h3, kb3, st3)
                    add(n, 9, k_s1, h3, kb3, st3)
                    add(n, 11, k_s2, h3, kb3, st3)
                    add(n, 11, v_all, h3, kb3, st3)
                if 2 <= kb <= 5 and h + 1 < 8:
                    stq = {}
                    add(n, 7, q_a, h + 1, kb - 2, stq)
                    add(n, 9, q_b, h + 1, kb - 2, stq)
            for tb in range(4):
                stq = {}
                q_a(0, tb, stq)
                q_b(0, tb, stq)
            st0 = {}
            k_s0(0, 0, st0)
            k_s1(0, 0, st0)
            k_s2(0, 0, st0)
            v_all(0, 0, st0)

            def run_sched(n, step):
                for fn, a in sched[n].get(step, []):
                    fn(*a)

            woa = {}
            pend = []

            def flush(keep):
                while len(pend) > keep:
                    pt_, kc_, qb_ = pend.pop(0)
                    X("tensor", "matmul", [V_b.res[kc_ // 4], pt_.r], [RB[qb_ - 1]], PB[qb_ - 1],
                      lhsT=Vv[:, kc_, :], rhs=bf(pt_.ap), start=(kc_ == 0), stop=(kc_ == 35))

            for n, (h, kb) in enumerate(items):
                hg, hh = h // 4, h % 4
                po = (hh % 2) * 64
                QB = QT_b[h % 2]
                QTv = bf(QB.ap)
                if kb == 0 and hh == 0:
                    woa[hg] = wload(d_wout0[512 + hg * 256:512 + (hg + 1) * 256, :].rearrange("(c p) n -> p c n", p=128))
                if kb < 9:
                    step = 0
                    for c in range(4):
                        kc = kb * 4 + c
                        for qb in (1, 2, 3):
                            sb = nb("S", [3, 4, 7])
                            X("tensor", "matmul", [KT_b.res[kb], QB.res[qb]], [RB[sb]], PB[sb], lhsT=KT[:, kc * 128:(kc + 1) * 128],
                              rhs=QTv[:, blk(qb)], start=True, stop=True)
                            pt = rot("pt", PT_b)
                            X("scalar", "activation", [RB[sb], rsk_b[h % 2].res[kb]], [pt.r], out=bf(pt.ap), in_=PB[sb], func=AF.Exp,
                              scale=rsk_b[h % 2].ap[:, kc:kc + 1])
                            pend.append((pt, kc, qb))
                            flush(2)
                            run_sched(n, step)
                            step += 1
                    if kb == 8:
                        flush(0)
                        pending_norm = [(evac(qb - 1, 512), attnT[po:po + 64, hh // 2, blk(qb)], attn_b.res[qb]) for qb in (1, 2, 3)]
                else:
                    for s_ in range(2):
                        sb = nb("S", [3, 4, 7])
                        for c in range(2):
                            k0 = 4608 + s_ * 256 + c * 128
                            X("tensor", "matmul", [KT_b.res[9], QB.res[0]], [RB[sb]], PB[sb][:, c * 256:(c + 1) * 256],
                              lhsT=KT[:, k0:k0 + 128], rhs=QTv[:, s_ * 256:(s_ + 1) * 256], start=True, stop=True)
                        pt = rot("pt", PT_b)
                        ptv = bf(pt.ap)
                        for c in range(2):
                            X("scalar", "activation", [RB[sb], rsk_b[h % 2].res[9]], [pt.r], out=ptv[:, c * 256:(c + 1) * 256],
                              in_=PB[sb][:, c * 256:(c + 1) * 256], func=AF.Exp, scale=rsk_b[h % 2].ap[:, 36 + 2 * s_ + c:37 + 2 * s_ + c])
                        for c in range(2):
                            X("tensor", "matmul", [V_b.res[9], pt.r], [RB[0]], PB[0][:, s_ * 256:(s_ + 1) * 256],
                              lhsT=Vv[:, 36 + 2 * s_ + c, :], rhs=ptv[:, c * 256:(c + 1) * 256], start=(c == 0), stop=(c == 1))
                    ev0 = evac(0, 512)
                    todo = pending_norm + [(ev0, attnT[po:po + 64, hh // 2, blk(0)], attn_b.res[0])]
                    if hh == 3 or n + 4 >= nitem:
                        for (ev_, dst_, rr_) in todo:
                            normalize_sb(ev_, dst_, rr_, 512)
                    else:
                        for j_, (ev_, dst_, rr_) in enumerate(todo):
                            add(n + 1 + j_, 10, normalize_sb, ev_, dst_, rr_, 512)
                    if hh == 3:
                        woa_v, woa_r = woa.pop(hg)
                        for tb in range(4):
                            for m in range(8):
                                b = nb("po2", [5, 6, 3, 4, 7])
                                for pr in range(2):
                                    X("tensor", "matmul", [attn_b.res[tb], woa_r], [RB[b]], PB[b], lhsT=woa_v[:, pr, m * 128:(m + 1) * 128],
                                      rhs=attnT[:, pr, blk(tb)], start=(pr == 0), stop=(pr == 1))
                                epilogue(PB[b], RB[b], 0, 2, tb, m)
            A.free(KT_b, V_b, attn_b, qln_b, ckvn_b, kr_b, lnsc_b, *rsk_b, *QT_b, *PT_b, *rden_l, *osb_l, *sqs_b, *r96_b, *rq_b)
        if stage < 3:
            A.free(sq_b, rstd_b)
        A.free(*tmpb)

        if stage >= 4:
            mlp(0, [(tb * 512, [RX[tb]], cond_of(tb)) for tb in range(4)])

        if dbg and stage < 5:
            ostg_d = A.alloc("ostg_dbg", 4096)
            for tb in range(4):
                store_tokens(xT[:, :, blk(tb)], RX[tb], 512, o_dbg[tb * 512:(tb + 1) * 512, :], ostg_d)
            A.free(ostg_d)

        if stage >= 5:
            hT_b = A.alloc("hT1", 8 * 2048 // 2, nres=4)
            hT = bf(hT_b.ap).rearrange("p (c t) -> p c t", c=8)
            RH = hT_b.res
            sq_b = A.alloc("sq1", 2048, nres=8)
            tmpb = [A.alloc("tmp1_%d" % i, 512) for i in range(2)]
            rstd_b = A.alloc("rstd1", 512)
            for tb in range(4):
                normmod(xT[:, :, blk(tb)], RX[tb], 512, MOD(1, 1, cond_of(tb)), MOD(1, 0, cond_of(tb)),
                        hT[:, :, blk(tb)], RH[tb], sq_b, tmpb, rstd_b)
            A.free(sq_b)
            SC1 = 0.125
            QCOL = [0, 768, 1280]
            sqp = [A.alloc("sqp%d" % i, 256) for i in range(2)]
            KTp = [A.alloc("KT1_%d" % i, 1024, nres=4) for i in range(2)]
            QTp = [A.alloc("QT1_%d" % i, 768, nres=3) for i in range(2)]
            Vw_b = A.alloc("Vw", 12 * 2 * 128 // 2, nres=3)
            Vw = bf(Vw_b.ap).rearrange("p (r h d) -> p r h d", r=12, h=2)
            Vp_b = A.alloc("Vp", 4 * 2 * 128 // 2)
            Vp = bf(Vp_b.ap).rearrange("p (r h d) -> p r h d", r=4, h=2)
            Kc_b = A.alloc("Kc", 512)
            Kc = bf(Kc_b.ap).rearrange("p (h k) -> p h k", h=2)
            X("vector", "memset", [], [Kc_b.r], Kc[64:128, :, :], 0.0)
            Vc_b = A.alloc("Vc", 4 * 2 * 128 // 2)
            Vc = bf(Vc_b.ap).rearrange("p (r h d) -> p r h d", r=4, h=2)
            T_b = [A.alloc("Ttab%d" % i, 1024) for i in range(2)]
            cm_b = A.alloc("cmask", 64)
            DMA("sync", cm_b.ap[0:64, :], d_cmask, cm_b.r, writes=[cm_b.r])
            DMA("sync", cm_b.ap[64:128, :], d_cmask, cm_b.r, writes=[cm_b.r])
            for tb_ in T_b:
                X("vector", "memset", [], [tb_.r], tb_.ap, 0.0)
            vT_l = [A.alloc("vT%d" % i, 256) for i in range(2)]
            kf_b = A.alloc("kf32", 512)
            kst_b = [A.alloc("kstage%d" % i, 512) for i in range(1)]
            E32_b = [A.alloc("E32_%d" % i, 768) for i in range(2)]
            PL_b = [A.alloc("PL%d" % i, 384) for i in range(3)]
            PT1_b = [A.alloc("PT1_%d" % i, 256) for i in range(2)]
            rden1 = [A.alloc("rden1_%d" % i, 512) for i in range(1)]
            attn1_b = A.alloc("attn1", 2 * 1536 // 2, nres=3)
            attn1 = bf(attn1_b.ap).rearrange("p (h t) -> p h t", h=2)
            X("vector", "memset", [], Vw_b.res, Vw[:, :, :, 64:128], 1.0)
            X("vector", "memset", [], [Vp_b.r], Vp[:, :, :, 64:128], 1.0)
            X("vector", "memset", [], [Vc_b.r], Vc[:, :, :, 64:128], 1.0)
            cnt1 = {}

            def rot1(key, lst):
                i = cnt1.get(key, 0)
                cnt1[key] = i + 1
                return lst[i % len(lst)]

            def normalize1(ob, dst, rdst, nq):
                rd = rot1("rden", rden1)
                X("scalar", "activation", [RB[ob]], [rd.r], out=rd.ap[0:64, 0:nq], in_=PB[ob][64:128, 0:nq], func=AF.Ln)
                X("scalar", "activation", [rd.r], [rd.r], out=rd.ap[0:64, 0:nq], in_=rd.ap[0:64, 0:nq], func=AF.Exp, scale=-1.0)
                X("vector", "tensor_tensor", [RB[ob], rd.r], [rdst], out=dst, in0=PB[ob][0:64, 0:nq], in1=rd.ap[0:64, 0:nq],
                  op=ALU.mult)

            def headnorm(bp, gvec):
                sq = rot1("sqp", sqp)
                sqv = bf(sq.ap)
                X("scalar", "activation", [RB[bp]], [sq.r], out=sqv, in_=PB[bp], func=AF.Square)
                bs = nb("prep1", [2, 3, 4, 5, 6, 7])
                X("tensor", "matmul", [sq.r, RC], [RB[bs]], PB[bs], lhsT=blk_b, rhs=sqv, start=True, stop=True)
                rb_ = rot1("rstd3", [rstd_b, tmpb[0], tmpb[1]])
                rstd_from(PB[bs], rb_.ap, 1.0 / 64, slice(0, 128), RB[bs], rb_.r)
                return rb_

            deferred = []

            def defer(fn):
                deferred.append(fn)
                while len(deferred) > 1:
                    deferred.pop(0)()

            def drain():
                while deferred:
                    deferred.pop(0)()

            def own_rows(i):
                if i < 4:
                    return i, 12
                if i > 12:
                    return 12, i + 8
                return i, i + 8

            for hh in range(2):
                X("vector", "memset", [], KTp[hh].res, bf(KTp[hh].ap)[64:128, :], 0.0)
                X("vector", "memset", [], QTp[hh].res, bf(QTp[hh].ap)[64:128, :], 0.0)
            for hh in range(2):
                DMA("gpsimd", bf(KTp[hh].ap)[64:80, 512:2048], d_pen, KTp[hh].res[1], writes=KTp[hh].res[1:4])
                DMA("gpsimd", bf(QTp[hh].ap)[64:80, 512:1536], d_qoh, QTp[hh].res[1], writes=QTp[hh].res[1:3])
            for c in range(8):
                w1v, w1r = wload(d_win1[:, c * 384:(c + 1) * 384].rearrange("(kc p) n -> p kc n", p=128))
                KTv = [bf(KTp[i].ap) for i in range(2)]
                QTv = [bf(QTp[i].ap) for i in range(2)]
                DMA("gpsimd", Kc[0:64, :, :], d_ck1[:, c * 1024:(c + 1) * 1024].rearrange("p (h k) -> p h k", h=2), Kc_b.r,
                    writes=[Kc_b.r])
                for t in range(4):
                    DMA("gpsimd", Vc[:, t, :, 0:64], d_cv1[t * 128:(t + 1) * 128, c * 128:(c + 1) * 128].rearrange("p (h d) -> p h d", h=2),
                        Vc_b.r, writes=[Vc_b.r])
                Tv = []
                for hh in range(2):
                    h = 2 * c + hh
                    tb_ = T_b[hh]
                    DMA("sync", tb_.ap[0:64, 0:960], d_rbt[h], tb_.r, writes=[tb_.r])
                    DMA("sync", tb_.ap[64:128, 64:1024], d_rbt[h], tb_.r, writes=[tb_.r])
                    X("vector", "memset", [], [tb_.r], tb_.ap[0:64, 960:1024], 0.0)
                    X("vector", "memset", [], [tb_.r], tb_.ap[64:128, 0:64], 0.0)
                    X("scalar", "activation", [tb_.r], [tb_.r], out=tb_.ap, in_=tb_.ap, func=AF.Exp)
                    t3 = tb_.ap.rearrange("p (r q) -> p r q", r=16)
                    X("vector", "tensor_tensor", [tb_.r, cm_b.r], [tb_.r], out=t3, in0=t3,
                      in1=cm_b.ap.unsqueeze(1).to_broadcast([128, 16, 64]), op=ALU.mult)
                    Tv.append(t3)
                for tb in range(4):
                    bp = nb("prep1", [2, 3, 4, 5, 6, 7])
                    for kc in range(8):
                        X("tensor", "matmul", [RH[tb], w1r], [RB[bp]], PB[bp], lhsT=w1v[:, kc, 128:256], rhs=hT[:, kc, blk(tb)],
                          start=(kc == 0), stop=(kc == 7))
                    def post_k(tb=tb, bp=bp):
                        rb_ = headnorm(bp, V_NGK)
                        for hh in range(2):
                            ps = slice(hh * 64, hh * 64 + 64)
                            X("vector", "scalar_tensor_tensor", [RB[bp], rb_.r, RC], [KTp[hh].res[tb]], out=KTv[hh][0:64, blk(tb)],
                              in0=PB[bp][ps, :], scalar=V_NGK[ps, :], in1=rb_.ap[ps, :], op0=ALU.mult, op1=ALU.mult)
                        if tb == 0:
                            X("vector", "scalar_tensor_tensor", [RB[bp], rb_.r, RC], [kf_b.r], out=kf_b.ap, in0=PB[bp],
                              scalar=V_NGK, in1=rb_.ap, op0=ALU.mult, op1=ALU.mult)
                            ks_ = rot1("kst", kst_b)
                            ksv = ks_.ap.rearrange("p (a d) -> p a d", a=4)
                            bt = nb("prep1", [2, 3, 4, 5, 6, 7])
                            for a in range(4):
                                X("tensor", "transpose", [kf_b.r, RC], [RB[bt]], out=PB[bt][:, a * 128:(a + 1) * 128],
                                  in_=kf_b.ap[:, a * 128:(a + 1) * 128], identity=ident_f)
                            X("vector", "tensor_copy", [RB[bt]], [ks_.r], out=ksv, in_=PB[bt].rearrange("p (a d) -> p a d", a=4))
                            DMA("sync", o_k1[:, c * 128:(c + 1) * 128].rearrange("(a p) d -> p a d", p=128), ksv, ks_.r, reads=[ks_.r])
                    defer(post_k)
                for qi in range(3):
                    q0 = QCOL[qi]
                    rq_ = [RH[0]] if qi == 0 else ([RH[1], RH[2]] if qi == 1 else [RH[2], RH[3]])
                    bp = nb("prep1", [2, 3, 4, 5, 6, 7])
                    for kc in range(8):
                        X("tensor", "matmul", rq_ + [w1r], [RB[bp]], PB[bp], lhsT=w1v[:, kc, 0:128], rhs=hT[:, kc, q0:q0 + 512],
                          start=(kc == 0), stop=(kc == 7))
                    def post_q(qi=qi, bp=bp):
                        rb_ = headnorm(bp, V_NGQ)
                        for hh in range(2):
                            ps = slice(hh * 64, hh * 64 + 64)
                            X("vector", "scalar_tensor_tensor", [RB[bp], rb_.r, RC], [QTp[hh].res[qi]],
                              out=QTv[hh][0:64, qi * 512:(qi + 1) * 512], in0=PB[bp][ps, :], scalar=V_NGQ[ps, :], in1=rb_.ap[ps, :],
                              op0=ALU.mult, op1=ALU.mult)
                    defer(post_q)
                for tb in range(4):
                    bp = nb("prep1", [2, 3, 4, 5, 6, 7])
                    for kc in range(8):
                        X("tensor", "matmul", [RH[tb], w1r], [RB[bp]], PB[bp], lhsT=w1v[:, kc, 256:384], rhs=hT[:, kc, blk(tb)],
                          start=(kc == 0), stop=(kc == 7))

                    def post_v(tb=tb, bp=bp):
                        vT_b = rot1("vT", vT_l)
                        vTv = bf(vT_b.ap)
                        X("scalar", "copy", [RB[bp]], [vT_b.r], out=vTv, in_=PB[bp])
                        bt = nb("prep1", [2, 3, 4, 5, 6, 7])
                        ptb = PB[bt].bitcast(BF16)
                        if tb == 0:
                            X("vector", "tensor_copy", [RB[bp]], [kf_b.r], out=kf_b.ap, in_=PB[bp])
                            for a in range(4):
                                X("tensor", "transpose", [vT_b.r, RC], [RB[bt]], out=ptb[:, a * 128:(a + 1) * 128],
                                  in_=vTv[:, a * 128:(a + 1) * 128], identity=ident_b)
                            X("vector", "tensor_copy", [RB[bt]], [Vp_b.r], out=Vp[:, :, :, 0:64],
                              in_=ptb[:, 0:512].rearrange("p (a h d) -> p a h d", a=4, h=2))
                            ks_ = rot1("kst", kst_b)
                            ksv = ks_.ap.rearrange("p (a d) -> p a d", a=4)
                            bt2 = nb("prep1", [2, 3, 4, 5, 6, 7])
                            for a in range(4):
                                X("tensor", "transpose", [kf_b.r, RC], [RB[bt2]], out=PB[bt2][:, a * 128:(a + 1) * 128],
                                  in_=kf_b.ap[:, a * 128:(a + 1) * 128], identity=ident_f)
                            X("vector", "tensor_copy", [RB[bt2]], [ks_.r], out=ksv, in_=PB[bt2].rearrange("p (a d) -> p a d", a=4))
                            DMA("sync", o_v1[:, c * 128:(c + 1) * 128].rearrange("(a p) d -> p a d", p=128), ksv, ks_.r, reads=[ks_.r])
                        else:
                            for a in range(4):
                                X("tensor", "transpose", [vT_b.r, RC], [RB[bt]], out=ptb[:, a * 128:(a + 1) * 128],
                                  in_=vTv[:, a * 128:(a + 1) * 128], identity=ident_b)
                            X("vector", "tensor_copy", [RB[bt]], [Vw_b.res[tb - 1]], out=Vw[:, (tb - 1) * 4:tb * 4, :, 0:64],
                              in_=ptb[:, 0:512].rearrange("p (a h d) -> p a h d", a=4, h=2))
                    defer(post_v)
                drain()
                for hh in range(2):
                    hidx = (c % 2) * 2 + hh
                    K_, Q_ = KTv[hh], QTv[hh]
                    rK, rQ = KTp[hh].res, QTp[hh].res
                    po1 = (hidx % 2) * 64
                    obp = nb("O1", [0, 1])
                    obs = [nb("O1", [0, 1]), nb("O1", [0, 1])]
                    cp = []

                    def cflush(keep):
                        while len(cp) > keep:
                            fn = cp.pop(0)
                            fn()
                    for s_ in range(2):
                        sb = nb("SC", [2, 3, 4, 5, 6, 7])
                        for cc_ in range(2):
                            k0 = s_ * 256 + cc_ * 128
                            X("tensor", "matmul", [rK[0], rQ[0]], [RB[sb]], PB[sb][:, cc_ * 256:(cc_ + 1) * 256], lhsT=K_[:, k0:k0 + 128],
                              rhs=Q_[:, s_ * 256:(s_ + 1) * 256], start=True, stop=True)
                        pt = rot1("pl", PL_b)
                        ptv = bf(pt.ap)[:, 0:512]
                        X("scalar", "activation", [RB[sb]], [pt.r], out=ptv, in_=PB[sb], func=AF.Exp, scale=SC1)

                        def pv_p(s_=s_, pt=pt, ptv=ptv):
                            for cc_ in range(2):
                                X("tensor", "matmul", [Vp_b.r, pt.r], [RB[obp]], PB[obp][:, s_ * 256:(s_ + 1) * 256],
                                  lhsT=Vp[:, 2 * s_ + cc_, hh, :], rhs=ptv[:, cc_ * 256:(cc_ + 1) * 256], start=(cc_ == 0), stop=(cc_ == 1))
                        cp.append(pv_p)
                        cflush(2)
                    cflush(0)
                    normalize1(obp, attn1[po1:po1 + 64, hidx // 2, 0:512], attn1_b.res[0], 512)
                    obs = [obp, obs[0]] if False else obs
                    for qb in range(2):
                        ob = obs[qb]
                        qs = slice(512 + qb * 512, 1024 + qb * 512)
                        for c4 in range(4):
                            sb = nb("SC", [2, 3, 4, 5, 6, 7])
                            X("tensor", "matmul", [Kc_b.r, rQ[1 + qb]], [RB[sb]], PB[sb], lhsT=Kc[:, hh, c4 * 128:(c4 + 1) * 128],
                              rhs=Q_[:, qs], start=True, stop=True)
                            pt = rot1("pl", PL_b)
                            ptv = bf(pt.ap)[:, 0:512]
                            X("scalar", "activation", [RB[sb]], [pt.r], out=ptv, in_=PB[sb], func=AF.Exp, scale=SC1)

                            def pv_c(ob=ob, c4=c4, pt=pt, ptv=ptv):
                                X("tensor", "matmul", [Vc_b.r, pt.r], [RB[ob]], PB[ob], lhsT=Vc[:, c4, hh, :], rhs=ptv,
                                  start=(c4 == 0), stop=False)
                            cp.append(pv_c)
                            cflush(2)
                    cflush(0)
                    pend = []

                    def rows_of(wr):
                        if wr > 22:
                            return None
                        return (0 if wr <= 11 else wr - 7), (min(15, wr) if wr < 12 else 15)
                    chunks = []
                    for m_ in range(12):
                        ra, rb_ = rows_of(2 * m_), rows_of(2 * m_ + 1)
                        ilo_c = min(ra[0], rb_[0]) if rb_ else ra[0]
                        ihi_c = max(ra[1], rb_[1]) if rb_ else ra[1]
                        chunks.append((m_, ilo_c, ihi_c))
                    last_m = {}
                    for (m_, ilo_c, ihi_c) in chunks:
                        for qb_ in range(2):
                            if max(ilo_c, qb_ * 8) <= min(ihi_c, qb_ * 8 + 7):
                                last_m[qb_] = m_

                    def flush1(keep):
                        while len(pend) > keep:
                            pl_, plv_, m_, ilo_, ihi_ = pend.pop(0)
                            for qb_ in range(2):
                                a_, b_ = max(ilo_, qb_ * 8), min(ihi_, qb_ * 8 + 7)
                                if a_ > b_:
                                    continue
                                X("tensor", "matmul", [Vw_b.res[m_ // 4], pl_.r], [RB[obs[qb_]]],
                                  PB[obs[qb_]][:, (a_ - qb_ * 8) * 64:(b_ - qb_ * 8 + 1) * 64], lhsT=Vw[:, m_, hh, :],
                                  rhs=plv_[:, (a_ - ilo_) * 64:(b_ - ilo_ + 1) * 64], start=False, stop=(m_ == last_m[qb_]))
                    for (m_, ilo, ihi) in chunks:
                        nq = ihi - ilo + 1
                        jj0 = ilo - 2 * m_ + 11
                        pair = nb("SL", [2, 4, 6])
                        sl = psum_t[:, pair * 512:pair * 512 + 1024]
                        kcol = 512 + m_ * 128
                        for part in range(2):
                            a_, b_ = part * 8, min(nq, part * 8 + 8)
                            if a_ >= b_:
                                continue
                            qc0 = 512 + (ilo + a_) * 64
                            rqs = sorted(set([1 + (ilo + a_) // 8, 1 + (ilo + b_ - 1) // 8]))
                            X("tensor", "matmul", [rK[1 + m_ // 4]] + [rQ[x_] for x_ in rqs], [RB[pair + part]],
                              sl[:, a_ * 64:b_ * 64], lhsT=K_[:, kcol:kcol + 128], rhs=Q_[:, qc0:qc0 + (b_ - a_) * 64],
                              start=True, stop=True)
                        e32 = rot1("e32", E32_b)
                        rbs = [RB[pair]] + ([RB[pair + 1]] if nq > 8 else [])
                        X("scalar", "activation", rbs, [e32.r], out=e32.ap[:, 0:nq * 64], in_=sl[:, 0:nq * 64], func=AF.Exp,
                          scale=SC1)
                        pl = rot1("pl", PL_b)
                        plv2 = bf(pl.ap)[:, 0:nq * 64]
                        X("vector", "tensor_tensor", [e32.r, T_b[hh].r], [pl.r], out=plv2.rearrange("p (r q) -> p r q", q=64),
                          in0=e32.ap[:, 0:nq * 64].rearrange("p (r q) -> p r q", q=64), in1=Tv[hh][:, jj0:jj0 + nq, :],
                          op=ALU.mult)
                        pend.append((pl, plv2, m_, ilo, ihi))
                        flush1(2)
                    flush1(0)
                    for qb in range(2):
                        qs = slice(512 + qb * 512, 1024 + qb * 512)
                        normalize1(obs[qb], attn1[po1:po1 + 64, hidx // 2, qs], attn1_b.res[1 + qb], 512)
                if c % 2 == 1:
                    grp = c // 2
                    wo1v, wo1r = wload(d_wout1[grp * 256:(grp + 1) * 256, :].rearrange("(c p) n -> p c n", p=128))
                    for qi in range(3):
                        q0 = QCOL[qi]
                        rxs = [RX[0]] if qi == 0 else ([RX[1], RX[2]] if qi == 1 else [RX[2], RX[3]])
                        cd = 0 if qi == 0 else 1
                        for m in range(8):
                            b = nb("prep1", [2, 3, 4, 5, 6, 7])
                            for hq in range(2):
                                X("tensor", "matmul", [attn1_b.res[qi], wo1r], [RB[b]], PB[b], lhsT=wo1v[:, hq, m * 128:(m + 1) * 128],
                                  rhs=attn1[:, hq, qi * 512:(qi + 1) * 512], start=(hq == 0), stop=(hq == 3 - 2))
                            X("vector", "scalar_tensor_tensor", [RB[b], RM] + rxs, rxs, out=xT[:, m, q0:q0 + 512], in0=PB[b],
                              scalar=MOD(1, 2, cd)[:, m:m + 1], in1=xT[:, m, q0:q0 + 512], op0=ALU.mult, op1=ALU.add)
            A.free(hT_b, rstd_b, Vw_b, Vp_b, Kc_b, Vc_b, cm_b, kf_b, attn1_b, *vT_l, *tmpb, *sqp, *KTp, *QTp, *T_b, *kst_b,
                   *E32_b, *PL_b, *PT1_b, *rden1)

        if dbg and stage >= 5:
            ostg_d = A.alloc("ostg_dbg", 4096)
            for tb in range(4):
                store_tokens(xT[:, :, blk(tb)], RX[tb], 512, o_dbg[tb * 512:(tb + 1) * 512, :], ostg_d)
            A.free(ostg_d)

        if stage >= 6:
            mlp(1, [(0, [RX[0]], 0), (768, [RX[1], RX[2]], 1), (1280, [RX[2], RX[3]], 1)])
            ostg = A.alloc("ostg", 4096)
            store_tokens(xT[:, :, 0:512], RX[0], 512, o_yp, ostg)
            store_tokens(xT[:, :, 768:1280], RX[1], 512, o_ys[0:512, :], ostg, rextra=[RX[2]])
            store_tokens(xT[:, :, 1280:1792], RX[2], 512, o_ys[512:1024, :], ostg, rextra=[RX[3]])
            A.free(ostg)

        P.emit(nc, st)
    return nc


def _pvec(v, n):
    return np.ascontiguousarray(np.asarray(v, np.float32).reshape(n, 128).T)


_SWAP = np.array([(j + 8) if (j % 16) < 8 else (j - 8) for j in range(32)])
_SGN = np.array([-1.0 if (j % 16) < 8 else 1.0 for j in range(32)])


def _rope_tables(rows, cols):
    f = THETA ** (-np.arange(0, 16, 2, dtype=np.float64) / 16.0)
    n = len(rows)
    ang = np.zeros((32, n))
    for j in range(32):
        pos = rows if j < 16 else cols
        ang[j] = pos.astype(np.float32).astype(np.float64) * np.float32(f[j % 8]).astype(np.float64)
    ang = ang.astype(np.float32).astype(np.float64)
    return np.concatenate([np.cos(ang), _SGN[:, None] * np.sin(ang)], axis=0).astype(np.float32)


def _prep_inputs(inp):
    f32 = np.float32
    g = {k: np.asarray(v) for k, v in inp.items()}
    shared = {}
    shared["w_ada0"] = g["w_ada_l0"]
    shared["w_ada1"] = g["w_ada_l1"]
    shared["b_ada0"] = _pvec(g["b_ada_l0"], 48)
    shared["b_ada1"] = _pvec(g["b_ada_l1"], 48)
    shared["w_mlp1_0"] = g["w_mlp1_l0"]
    shared["w_mlp1_1"] = g["w_mlp1_l1"]
    shared["w_mlp2_0"] = g["w_mlp2_l0"]
    shared["w_mlp2_1"] = g["w_mlp2_l1"]
    vec = np.zeros((128, 64), f32)
    vec[:, 0:8] = _pvec(g["norm_mix_l0"], 8)
    vec[:, 8:16] = _pvec(g["norm_mlp_l0"], 8)
    vec[:, 16:24] = _pvec(g["norm_mix_l1"], 8)
    vec[:, 24:32] = _pvec(g["norm_mlp_l1"], 8)
    vec[:, 32:34] = _pvec(g["q_lora_norm_l0"], 2)
    vec[:, 34] = g["kv_lora_norm_l0"]
    gq, gk = g["mla_q_norm_l0"], g["mla_k_norm_l0"]
    vec[0:96, 35] = gq
    vec[64:96, 36] = gq[64 + _SWAP]
    vec[0:96, 37] = gk
    vec[64:96, 38] = gk[64 + _SWAP]
    vec[:, 39:43] = _pvec(g["pool_scale_l0"], 4)
    vec[:, 43] = np.tile(g["na_q_norm_l1"], 2)
    vec[:, 44] = np.tile(g["na_k_norm_l1"], 2)
    shared["vecs"] = vec
    w_in0 = g["w_in_l0"]
    shared["w_in0_pool"] = np.ascontiguousarray(w_in0[:, 0:512])
    shared["w_in0_q"] = np.ascontiguousarray(w_in0[:, 512:768])
    kr = w_in0[:, 896:928]
    dummy = w_in0[:, 768:832]
    shared["w_in0_kv"] = np.ascontiguousarray(np.concatenate([w_in0[:, 768:896], dummy, kr, dummy, kr[:, _SWAP]], axis=1))
    wq = g["w_q_up_l0"].reshape(256, 8, 96)
    wq_aug = np.concatenate([wq, wq[:, :, 0:64], wq[:, :, 64 + _SWAP]], axis=2)
    shared["w_qup_aug"] = np.ascontiguousarray(wq_aug.reshape(256, 1536))
    shared["w_kvup"] = g["w_kv_up_l0"]
    shared["w_pool"] = np.ascontiguousarray(g["w_pool_l0"].transpose(1, 0, 2).reshape(128, 512))
    shared["w_out0"] = g["w_out_l0"]
    t = np.arange(4096)
    shared["rope_k"] = _rope_tables(t // 64, t % 64)
    w1 = g["w_in_l1"]
    pieces = []
    for c in range(8):
        pieces += [w1[:, c * 128:(c + 1) * 128], w1[:, 1024 + c * 128:1024 + (c + 1) * 128],
                   w1[:, 2048 + c * 128:2048 + (c + 1) * 128]]
    shared["w_in1"] = np.ascontiguousarray(np.concatenate(pieces, axis=1))
    shared["w_out1"] = g["w_out_l1"]
    rb = g["rel_bias_l1"]
    kc = np.arange(64)
    dc = np.clip(kc[:, None] - kc[None, :] + 15, 0, 30)
    rbt = rb[:, ::-1, :][:, :, dc]
    shared["rbt"] = np.ascontiguousarray(rbt.transpose(0, 2, 1, 3).reshape(16, 64, 960))
    cs = np.clip(kc - 8, 0, 48)
    cm = (kc[:, None] >= cs[None, :]) & (kc[:, None] < cs[None, :] + 16)
    shared["cmask"] = cm.astype(f32)
    maps = []
    xs = g["x_sample"]
    for core in range(8):
        b, q = core // 4, core % 4
        m = dict(shared)
        m["xp"] = np.ascontiguousarray(g["x_prompt"][2 * core:2 * core + 2].reshape(512, D))
        r0 = 16 * q - 4
        xw = np.zeros((24, 64, D), f32)
        xpre = np.zeros((128, D), f32)
        valid = np.zeros(25, f32)
        xg = xs[b].reshape(64, 64, D)
        for wr in range(24):
            if 0 <= r0 + wr < 64:
                xw[wr] = xg[r0 + wr]
                valid[1 + wr] = 1.0
        if 0 <= r0 - 1 < 64:
            xpre[0:64] = xg[r0 - 1]
            valid[0] = 1.0
        m["xw"] = xw.reshape(1536, D)
        m["xpre"] = xpre
        m["xb"] = np.ascontiguousarray(xs[b])
        cond = np.stack([g["c_ctx"], g["c"][b]], axis=1)
        m["condT"] = np.ascontiguousarray(cond.reshape(8, 128, 2).transpose(1, 0, 2).reshape(128, 16))
        wt = np.arange(1536)
        m["rope_q"] = _rope_tables(r0 + wt // 64, wt % 64)
        m["cache_ckvT"] = np.ascontiguousarray(g["cache_l0_mla_ckv"][b].T)
        m["cache_kropeT"] = np.ascontiguousarray(g["cache_l0_mla_krope"][b].T)
        m["pool_mask"] = np.ascontiguousarray(np.broadcast_to(np.repeat(valid, 64)[None, :], (128, 1600))).astype(f32)
        pinv = np.zeros((4, 4, 8), f32)
        for gi, w in enumerate((2, 4, 8, 16)):
            for bd, (S, first, exists) in enumerate([(256, True, True), (256, False, True),
                                                    (4096, True, q == 0), (4096, False, q == 3)]):
                for j in range(8):
                    tt = j if first else S - 8 + j
                    if exists:
                        lo = np.clip(tt - w // 2, 0, S)
                        hi = np.clip(tt - w // 2 + w, 0, S)
                        pinv[gi, bd, j] = 1.0 / float(hi - lo)
                    else:
                        pinv[gi, bd, j] = 1.0 / w
        m["pool_inv"] = np.ascontiguousarray(np.broadcast_to(pinv.reshape(1, 128), (128, 128))).astype(f32)
        pen = np.full((16, 24), NEGBIG, f32)
        for i in range(16):
            gr = 16 * q + i
            rs = int(np.clip(gr - 4, 0, 56))
            for kr_ in range(rs, rs + 8):
                pen[i, kr_ - r0] = 0.0
        m["pen"] = np.ascontiguousarray(np.repeat(pen, 64, axis=1))
        m["qoh"] = np.ascontiguousarray(np.repeat(np.eye(16, dtype=f32), 64, axis=1))
        m["cache_k1T"] = np.ascontiguousarray(g["cache_l1_na_k"][b].transpose(2, 1, 0).reshape(64, 16 * 512))
        m["cache_v1"] = np.ascontiguousarray(g["cache_l1_na_v"][b].reshape(512, 1024))
        maps.append(m)
    return maps


_NC_CACHE = {}
_BUILD_ARGS = {}


def kernel(**inputs):
    maps = _prep_inputs(inputs)
    if "nc" not in _NC_CACHE:
        _NC_CACHE["nc"] = build_program(**_BUILD_ARGS)
    nc = _NC_CACHE["nc"]
    res = run_bass_kernel_spmd(nc, maps, core_ids=list(range(8)))
    R = res.results
    yp = np.concatenate([R[c]["yp"].reshape(2, 256, D) for c in range(8)], axis=0)
    ys = np.stack([np.concatenate([R[b * 4 + q]["ys"] for q in range(4)], axis=0) for b in range(2)], axis=0)
    ckv = np.concatenate([R[c]["ckv_new"].reshape(2, 256, 128) for c in range(8)], axis=0)
    kr = np.concatenate([R[c]["krope_new"].reshape(2, 256, 32) for c in range(8)], axis=0)
    k1 = np.concatenate([R[c]["k_new"].reshape(2, 256, 16, 64) for c in range(8)], axis=0)
    v1 = np.concatenate([R[c]["v_new"].reshape(2, 256, 16, 64) for c in range(8)], axis=0)
    return (yp.astype(np.float32), ys.astype(np.float32), ckv.astype(np.float32), kr.astype(np.float32),
            k1.astype(np.float32), v1.astype(np.float32))
```
